# Optimizing a Trainium2 kernel written in Bass

```python
import math
import jax, jax.numpy as jnp
from jax import lax
import numpy as np

D_MODEL = 2048
BATCH = 4
SEQ = 2048
DEPTH = 1
DEC_BATCH = 128
DEC_SEQ = 8
PAST_LEN = 16384
PAGE_SIZE = 128

N_HEADS = 16
HEAD_K = 128
HEAD_V = 128
KEY_DIM = N_HEADS * HEAD_K
VAL_DIM = N_HEADS * HEAD_V
QKV_DIM = 2 * KEY_DIM + VAL_DIM
SHORT_CONV = 4
CHUNK = 64
CONV_CH = D_MODEL
CONV_WIDTH = 31
D_FF = 5632
PLE_DIM = 256
EPS = 1e-6

O_Z = QKV_DIM
O_BETA = O_Z + VAL_DIM
O_A = O_BETA + N_HEADS
O_GLU = O_A + N_HEADS
O_GATE = O_GLU + 2 * CONV_CH
IN_DIM = O_GATE + 2 * D_MODEL

kernel_name = 'hybrid_gdn_conformer_decoder_step'


def rmsnorm(x, g):
    xf = x.astype(jnp.float32)
    y = xf * lax.rsqrt(jnp.mean(xf * xf, axis=-1, keepdims=True) + EPS)
    return (y * g.astype(jnp.float32)).astype(x.dtype)


def layernorm(x, g, b):
    xf = x.astype(jnp.float32)
    mu = jnp.mean(xf, axis=-1, keepdims=True)
    xc = xf - mu
    y = xc * lax.rsqrt(jnp.mean(xc * xc, axis=-1, keepdims=True) + EPS)
    return (y * g.astype(jnp.float32) + b.astype(jnp.float32)).astype(x.dtype)


def l2norm(x):
    return x * lax.rsqrt(jnp.sum(x * x, axis=-1, keepdims=True) + EPS)


def swiglu(x, w_gu, w_down):
    gu = x @ w_gu
    return (jax.nn.silu(gu[..., :D_FF]) * gu[..., D_FF:]) @ w_down


def causal_dwconv(x_ext, w):
    c = w.shape[-1]
    return lax.conv_general_dilated(
        x_ext, w[:, None, :].astype(x_ext.dtype), window_strides=(1,), padding='VALID',
        dimension_numbers=('NWC', 'WIO', 'NWC'), feature_group_count=c)


def gated_delta_rule(q, k, v, beta, g, s0):
    b, l, h, _ = q.shape
    dv = v.shape[-1]
    n_chunks = -(-l // CHUNK)
    pad = n_chunks * CHUNK - l

    def prep(t):
        t = jnp.pad(t, [(0, 0), (0, pad)] + [(0, 0)] * (t.ndim - 2))
        t = t.reshape((b, n_chunks, CHUNK) + t.shape[2:])
        return jnp.moveaxis(t, 3, 2)

    qc, kc, vc, bc, gc = prep(q), prep(k), prep(v), prep(beta), prep(g)
    gcum = jnp.cumsum(gc, axis=-1)
    idx = jnp.arange(CHUNK)
    causal = idx[:, None] >= idx[None, :]
    strict = idx[:, None] > idx[None, :]
    decay = jnp.exp(jnp.where(causal, gcum[..., :, None] - gcum[..., None, :], -jnp.inf))
    kk = jnp.einsum('bnhtd,bnhsd->bnhts', kc, kc)
    m = jnp.where(strict, bc[..., :, None] * decay * kk, 0.0)
    eye = jnp.eye(CHUNK, dtype=jnp.float32)
    gamma = jnp.exp(gcum)
    rhs = jnp.concatenate([bc[..., None] * vc, (bc * gamma)[..., None] * kc], axis=-1)
    sol = lax.linalg.triangular_solve(eye + m, rhs, left_side=True, lower=True,
                                      unit_diagonal=True)
    uv, wk = sol[..., :dv], sol[..., dv:]
    qk = jnp.einsum('bnhtd,bnhsd->bnhts', qc, kc) * decay
    qg = qc * gamma[..., None]
    kdec = kc * jnp.exp(gcum[..., -1:] - gcum)[..., None]
    g_last = gamma[..., -1]
    xs = tuple(jnp.moveaxis(t, 1, 0) for t in (uv, wk, qk, qg, kdec, g_last))

    def step(s, inp):
        uv_c, wk_c, qk_c, qg_c, kdec_c, gl_c = inp
        u = uv_c - jnp.einsum('bhtk,bhkv->bhtv', wk_c, s)
        o = jnp.einsum('bhtk,bhkv->bhtv', qg_c, s) + jnp.einsum('bhts,bhsv->bhtv', qk_c, u)
        s = gl_c[..., None, None] * s + jnp.einsum('bhtk,bhtv->bhkv', kdec_c, u)
        return s, o

    s_final, o = lax.scan(step, s0, xs)
    o = jnp.transpose(o, (1, 0, 3, 2, 4)).reshape(b, n_chunks * CHUNK, h, dv)[:, :l]
    return o, s_final


def token_mixer(u, s0, qkv_buf, glu_buf, w_in, w_short_conv, a_log, dt_bias, o_norm,
                w_dw_conv, b_dw_conv, ln_g, ln_b, w_branch_a, w_branch_b, w_out):
    b, l, _ = u.shape
    dt = u.dtype
    proj = u @ w_in
    qkv_raw, z, b_raw, a_raw, glu_in, gate_raw = jnp.split(
        proj, [O_Z, O_BETA, O_A, O_GLU, O_GATE], axis=-1)
    qkv_ext = jnp.concatenate([qkv_buf.astype(dt), qkv_raw], axis=1)
    qkv = jax.nn.silu(causal_dwconv(qkv_ext, w_short_conv)).astype(jnp.float32)
    q, k, v = jnp.split(qkv, [KEY_DIM, 2 * KEY_DIM], axis=-1)
    q = l2norm(q.reshape(b, l, N_HEADS, HEAD_K)) * (HEAD_K ** -0.5)
    k = l2norm(k.reshape(b, l, N_HEADS, HEAD_K))
    v = v.reshape(b, l, N_HEADS, HEAD_V)
    beta = jax.nn.sigmoid(b_raw.astype(jnp.float32))
    g = -jnp.exp(a_log.astype(jnp.float32)) * jax.nn.softplus(
        a_raw.astype(jnp.float32) + dt_bias.astype(jnp.float32))
    o, s_new = gated_delta_rule(q, k, v, beta, g, s0.astype(jnp.float32))
    o = rmsnorm(o, o_norm) * jax.nn.silu(z.astype(jnp.float32).reshape(b, l, N_HEADS, HEAD_V))
    branch_a = o.reshape(b, l, VAL_DIM).astype(dt) @ w_branch_a
    glu = glu_in[..., :CONV_CH] * jax.nn.sigmoid(glu_in[..., CONV_CH:])
    glu_ext = jnp.concatenate([glu_buf.astype(dt), glu], axis=1)
    c = causal_dwconv(glu_ext, w_dw_conv) + b_dw_conv
    c = jax.nn.silu(layernorm(c, ln_g, ln_b))
    branch_b = c @ w_branch_b
    gate_a, gate_b = jnp.split(gate_raw, 2, axis=-1)
    merged = jax.nn.sigmoid(gate_a) * branch_a + jax.nn.sigmoid(gate_b) * branch_b
    return (merged @ w_out, s_new.astype(dt),
            qkv_ext[:, -(SHORT_CONV - 1):], glu_ext[:, -(CONV_WIDTH - 1):])


def _layer(x, p, s0, qkv_buf, glu_buf, w):
    (ffn1_pre, ffn1_w_gu, ffn1_w_down, ffn1_post, mix_pre, w_in, w_short_conv, a_log, dt_bias,
     o_norm, w_dw_conv, b_dw_conv, ln_g, ln_b, w_branch_a, w_branch_b, w_out, mix_post,
     ffn2_pre, ffn2_w_gu, ffn2_w_down, ffn2_post, ple_pre, w_ple_gate, w_ple_proj,
     ple_post) = w
    h = x + 0.5 * rmsnorm(swiglu(rmsnorm(x, ffn1_pre), ffn1_w_gu, ffn1_w_down), ffn1_post)
    mix, s_new, qkv_new, glu_new = token_mixer(
        rmsnorm(h, mix_pre), s0, qkv_buf, glu_buf, w_in, w_short_conv, a_log, dt_bias, o_norm,
        w_dw_conv, b_dw_conv, ln_g, ln_b, w_branch_a, w_branch_b, w_out)
    h = h + rmsnorm(mix, mix_post)
    h = h + 0.5 * rmsnorm(swiglu(rmsnorm(h, ffn2_pre), ffn2_w_gu, ffn2_w_down), ffn2_post)
    gate = jax.nn.sigmoid(rmsnorm(h, ple_pre) @ w_ple_gate)
    h = h + rmsnorm(gate * (p.astype(h.dtype) @ w_ple_proj), ple_post)
    return h, s_new, qkv_new, glu_new


def setup_inputs(seed: int = 0) -> dict:
    key = jax.random.key(seed)
    ks = iter(list(jax.random.split(key, 48)))

    def nrm(shape, scale):
        return scale * jax.random.normal(next(ks), shape, jnp.float32)

    def gain(n):
        return 1.0 + nrm((DEPTH, n), 0.02)

    def ffn_w():
        return (nrm((DEPTH, D_MODEL, 2 * D_FF), D_MODEL ** -0.5),
                nrm((DEPTH, D_FF, D_MODEL), D_FF ** -0.5))

    inp = {}
    inp['x_prompt'] = nrm((BATCH, SEQ, D_MODEL), 1.0)
    inp['x_sample'] = nrm((DEC_BATCH, DEC_SEQ, D_MODEL), 1.0)
    inp['p_prompt'] = nrm((DEPTH, BATCH, SEQ, PLE_DIM), 1.0)
    inp['p_sample'] = nrm((DEPTH, DEC_BATCH, DEC_SEQ, PLE_DIM), 1.0)
    inp['state_delta'] = nrm((DEPTH, DEC_BATCH, N_HEADS, HEAD_K, HEAD_V), 0.1)
    inp['state_qkv_conv'] = nrm((DEPTH, DEC_BATCH, SHORT_CONV - 1, QKV_DIM), 1.0)
    inp['state_glu_conv'] = nrm((DEPTH, DEC_BATCH, CONV_WIDTH - 1, CONV_CH), 0.5)
    inp['ffn1_pre'] = gain(D_MODEL)
    inp['ffn1_w_gu'], inp['ffn1_w_down'] = ffn_w()
    inp['ffn1_post'] = gain(D_MODEL)
    inp['mix_pre'] = gain(D_MODEL)
    inp['w_in'] = nrm((DEPTH, D_MODEL, IN_DIM), D_MODEL ** -0.5)
    inp['w_short_conv'] = nrm((DEPTH, SHORT_CONV, QKV_DIM), SHORT_CONV ** -0.5)
    inp['a_log'] = jnp.log(jax.random.uniform(next(ks), (DEPTH, N_HEADS), jnp.float32,
                                              minval=1.0, maxval=16.0))
    dt0 = jnp.exp(jax.random.uniform(next(ks), (DEPTH, N_HEADS), jnp.float32,
                                     minval=math.log(1e-3), maxval=math.log(1e-1)))
    inp['dt_bias'] = dt0 + jnp.log(-jnp.expm1(-dt0))
    inp['o_norm'] = gain(HEAD_V)
    inp['w_dw_conv'] = nrm((DEPTH, CONV_WIDTH, CONV_CH), CONV_WIDTH ** -0.5)
    inp['b_dw_conv'] = nrm((DEPTH, CONV_CH), 0.02)
    inp['ln_g'] = gain(CONV_CH)
    inp['ln_b'] = nrm((DEPTH, CONV_CH), 0.02)
    inp['w_branch_a'] = nrm((DEPTH, VAL_DIM, D_MODEL), VAL_DIM ** -0.5)
    inp['w_branch_b'] = nrm((DEPTH, CONV_CH, D_MODEL), CONV_CH ** -0.5)
    inp['w_out'] = nrm((DEPTH, D_MODEL, D_MODEL), D_MODEL ** -0.5)
    inp['mix_post'] = gain(D_MODEL)
    inp['ffn2_pre'] = gain(D_MODEL)
    inp['ffn2_w_gu'], inp['ffn2_w_down'] = ffn_w()
    inp['ffn2_post'] = gain(D_MODEL)
    inp['ple_pre'] = gain(D_MODEL)
    inp['w_ple_gate'] = nrm((DEPTH, D_MODEL, D_MODEL), D_MODEL ** -0.5)
    inp['w_ple_proj'] = nrm((DEPTH, PLE_DIM, D_MODEL), PLE_DIM ** -0.5)
    inp['ple_post'] = gain(D_MODEL)
    return inp


def reference(x_prompt, x_sample, p_prompt, p_sample, state_delta, state_qkv_conv,
              state_glu_conv, ffn1_pre, ffn1_w_gu, ffn1_w_down, ffn1_post, mix_pre, w_in,
              w_short_conv, a_log, dt_bias, o_norm, w_dw_conv, b_dw_conv, ln_g, ln_b,
              w_branch_a, w_branch_b, w_out, mix_post, ffn2_pre, ffn2_w_gu, ffn2_w_down,
              ffn2_post, ple_pre, w_ple_gate, w_ple_proj, ple_post):
    weights = (ffn1_pre, ffn1_w_gu, ffn1_w_down, ffn1_post, mix_pre, w_in, w_short_conv, a_log,
               dt_bias, o_norm, w_dw_conv, b_dw_conv, ln_g, ln_b, w_branch_a, w_branch_b,
               w_out, mix_post, ffn2_pre, ffn2_w_gu, ffn2_w_down, ffn2_post, ple_pre,
               w_ple_gate, w_ple_proj, ple_post)
    b = x_prompt.shape[0]
    dt = x_prompt.dtype
    zero_s = jnp.zeros((b, N_HEADS, HEAD_K, HEAD_V), dt)
    zero_qkv = jnp.zeros((b, SHORT_CONV - 1, QKV_DIM), dt)
    zero_glu = jnp.zeros((b, CONV_WIDTH - 1, CONV_CH), dt)
    yp, ys = x_prompt, x_sample
    sp_l, qp_l, gp_l, ss_l, qs_l, gs_l = [], [], [], [], [], []
    for i in range(DEPTH):
        wi = tuple(w[i] for w in weights)
        yp, sp, qp, gp = _layer(yp, p_prompt[i], zero_s, zero_qkv, zero_glu, wi)
        ys, ss, qs, gs = _layer(ys, p_sample[i], state_delta[i], state_qkv_conv[i],
                                state_glu_conv[i], wi)
        sp_l.append(sp); qp_l.append(qp); gp_l.append(gp)
        ss_l.append(ss); qs_l.append(qs); gs_l.append(gs)
    new_delta_prompt = jnp.stack(sp_l)
    new_qkv_conv_prompt = jnp.stack(qp_l)
    new_glu_conv_prompt = jnp.stack(gp_l)
    new_delta_sample = jnp.stack(ss_l)
    new_qkv_conv_sample = jnp.stack(qs_l)
    new_glu_conv_sample = jnp.stack(gs_l)
    return (yp, ys, new_delta_prompt, new_qkv_conv_prompt, new_glu_conv_prompt,
            new_delta_sample, new_qkv_conv_sample, new_glu_conv_sample)
```

```python
import numpy as np
import concourse.bass as bass
import concourse.mybir as mybir
from concourse.bass_utils import run_bass_kernel_spmd

F32 = mybir.dt.float32
BF16 = mybir.dt.bfloat16
AF = mybir.ActivationFunctionType
ALU = mybir.AluOpType

ENGS = ("pe", "act", "dve", "pool", "sp")
EPOCH = 30000


class _Op:
    __slots__ = ("eng", "fn", "reads", "writes", "dma", "deps", "sig", "idx", "n", "bar")

    def __init__(self, eng, fn, reads, writes, dma):
        self.eng = eng
        self.fn = fn
        self.reads = reads
        self.writes = writes
        self.dma = dma
        self.deps = None
        self.sig = False
        self.idx = None
        self.n = 0
        self.bar = False


class Prog:
    def __init__(self, nc):
        self.nc = nc
        self.ops = []
        self.streams = {e: [] for e in ENGS}

    def add(self, eng, fn, reads=(), writes=(), dma=None):
        op = _Op(eng, fn, tuple(reads), tuple(writes), dma)
        op.n = len(self.ops)
        self.ops.append(op)
        self.streams[eng].append(op)
        return op

    def barrier(self, fn):
        op = self.add("sp", fn, dma=("bar",))
        op.bar = True
        return op

    def _analyze(self):
        last_w = {}
        readers = {}
        last_eng = {}
        last_dma = {}
        cur_bar = None
        need_bar = set()
        for op in self.ops:
            deps = {}
            if op.bar:
                for d in last_eng.values():
                    deps[d.n] = d
                for d in last_dma.values():
                    deps[d.n] = d
                last_w = {}
                readers = {}
                cur_bar = op
                need_bar = set(ENGS)
            else:
                if cur_bar is not None and op.eng in need_bar:
                    deps[cur_bar.n] = cur_bar
                    need_bar.discard(op.eng)
                for k in op.reads:
                    w = last_w.get(k)
                    if w is not None:
                        deps[w.n] = w
                for k in op.writes:
                    w = last_w.get(k)
                    if w is not None:
                        deps[w.n] = w
                    for r in readers.get(k, ()):
                        deps[r.n] = r
                for k in op.reads:
                    readers.setdefault(k, []).append(op)
                for k in op.writes:
                    last_w[k] = op
                    readers[k] = []
            if op.dma is not None:
                last_dma[op.dma] = op
            else:
                last_eng[op.eng] = op
            deps.pop(op.n, None)
            dl = []
            for d in deps.values():
                if d.dma is None and op.dma is None and d.eng == "pe" and op.eng == "pe":
                    continue
                dl.append(d)
            op.deps = dl
            for d in dl:
                d.sig = True
        cnt = {e: 0 for e in ENGS}
        dcnt = {}
        for op in self.ops:
            if op.dma is not None:
                dcnt[op.dma] = dcnt.get(op.dma, 0) + 16
                op.idx = ("d", op.dma, dcnt[op.dma])
            elif op.sig:
                c = cnt[op.eng]
                cnt[op.eng] += 1
                op.idx = ("e", (op.eng, c // EPOCH), c % EPOCH + 1)
        self.dma_final = dcnt

    def emit(self):
        nc = self.nc
        self._analyze()
        sems = {}

        def sem(kind, key):
            k = (kind, key)
            if k not in sems:
                sems[k] = nc.alloc_semaphore(name="s%d" % len(sems))
            return sems[k]

        waits = {}
        for e in ENGS:
            seen = {}
            for op in self.streams[e]:
                wl = {}
                for d in op.deps:
                    kind, key, val = d.idx
                    k = (kind, key)
                    if seen.get(k, 0) >= val:
                        continue
                    if wl.get(k, 0) < val:
                        wl[k] = val
                for k, v in wl.items():
                    seen[k] = v
                waits[op.n] = [(sem(*k), v) for k, v in wl.items()]
        final_waits = [(sem("d", k), v) for k, v in self.dma_final.items()]
        self.nsem = len(sems)

        def run_stream(engname, eng):
            for op in self.streams[engname]:
                for s, v in waits[op.n]:
                    eng.wait_ge(s, v)
                ins = op.fn(eng)
                if op.idx is not None:
                    kind, key, val = op.idx
                    ins.then_inc(sem(kind, key), 16 if kind == "d" else 1)
            if engname == "sp":
                for s, v in final_waits:
                    eng.wait_ge(s, v)

        with nc.Block() as block:
            @block.tensor
            def _(e):
                run_stream("pe", e)

            @block.scalar
            def _(e):
                run_stream("act", e)

            @block.vector
            def _(e):
                run_stream("dve", e)

            @block.gpsimd
            def _(e):
                run_stream("pool", e)

            @block.sync
            def _(e):
                run_stream("sp", e)


D = 2048
KC = 16
DFF = 5632
JC = 44
NH = 16
QKV = 6144
O_Z = QKV
O_BETA = O_Z + 2048
O_A = O_BETA + NH
O_GLU = O_A + NH
O_GATE = O_GLU + 4096
IN_DIM = O_GATE + 4096
PLE = 256
EPS = 1e-6
NPRE = 1024
NMAIN = 1024
NTP = NPRE + NMAIN
NSEQ = 16
LS = 8
NS = NSEQ * LS
NT = NTP + NS
NM = NMAIN + NS
M0 = NPRE

ARENA_BYTES = 206000
WR_OFF = 16384
WSLOT = 11264
NWS = 3
PH_OFF = WR_OFF + NWS * WSLOT

CV_G = {n: 16 * i for i, n in enumerate(
    ["ffn1_pre", "ffn1_post", "mix_pre", "mix_post", "ffn2_pre", "ffn2_post", "ple_pre", "ple_post",
     "b_dw", "ln_g", "ln_b"])}
CV_SC = 176
CV_DW = CV_SC + 192
CV_ON = CV_DW + 496
CV_AL = CV_ON + 1
CV_DT = CV_AL + 1
CV_H1 = CV_DT + 1
CV_H2 = CV_H1 + 16
CV_NA = CV_H2 + 16
NCV = CV_NA + 1
CM_ID = 0
CM_ONE = 128
CM_TRI = 256
CM_UI = 320
CM_US = 384
CM_LS = 448
CM_TRI8 = 512
CM_UI8 = 640
CM_US8 = 768
CM_LS8 = 896
CM_SAME8 = 1024
CM_SEG = 1152
NCM = CM_SEG + 16


def _tiles(t0, t1, n=512):
    return [(a, min(n, t1 - a)) for a in range(t0, t1, n)]


class Builder:
    def __init__(self, debug=False, stop_after=None):
        self.debug = debug
        self.stop_after = stop_after
        nc = bass.Bass("TRN2", target_bir_lowering=False)
        self.nc = nc
        self.pg = Prog(nc)
        self.arena = nc.alloc_sbuf_tensor("arena", [128, ARENA_BYTES // 4], F32)
        self.ps = nc.alloc_psum_tensor("ps", [128, 8, 512], F32)
        self.rot = list(range(8))
        self.rot_i = 0
        self.ws_i = 0
        self.ring_i = {}
        self._decl()

    def _in(self, name, shape, dt=F32):
        return self.nc.dram_tensor(name, list(shape), dt, kind="ExternalInput").ap()

    def _out(self, name, shape, dt=F32):
        return self.nc.dram_tensor(name, list(shape), dt, kind="ExternalOutput").ap()

    def _scr(self, name, shape, dt=F32):
        kind = "ExternalOutput" if self.debug else "Internal"
        return self.nc.dram_tensor(name, list(shape), dt, kind=kind).ap()

    def _decl(self):
        self.xin = self._in("xin", [NT, D])
        self.pin = self._in("pin", [NM, PLE])
        self.sdelta = self._in("sdelta", [NSEQ, NH, 128, 128])
        self.sqkv = self._in("sqkv", [NSEQ * 3, QKV])
        self.sglu = self._in("sglu", [NSEQ * 30, D])
        self.cvec = self._in("cvec", [128, NCV])
        self.cmask = self._in("cmask", [128, NCM])
        self.w_gu1 = self._in("w_gu1", [D, 2 * DFF])
        self.w_dn1 = self._in("w_dn1", [DFF, D])
        self.w_in = self._in("w_in", [D, IN_DIM])
        self.w_ba = self._in("w_ba", [D, D])
        self.w_bb = self._in("w_bb", [D, D])
        self.w_out = self._in("w_out", [D, D])
        self.w_gu2 = self._in("w_gu2", [D, 2 * DFF])
        self.w_dn2 = self._in("w_dn2", [DFF, D])
        self.w_pg = self._in("w_pg", [D, D])
        self.w_pp = self._in("w_pp", [PLE, D])
        self.y = self._out("y", [NM, D])
        self.nd_p = self._out("nd_p", [NH, 128, 128])
        self.nq_p = self._out("nq_p", [3, QKV])
        self.ng_p = self._out("ng_p", [30, D])
        self.nd_s = self._out("nd_s", [NSEQ, NH, 128, 128])
        self.nq_s = self._out("nq_s", [NSEQ * 3, QKV])
        self.ng_s = self._out("ng_s", [NSEQ * 30, D])
        self.xT = self._scr("xT", [KC, 128, NT])
        self.h1 = self._scr("h1", [KC, 128, NT])
        self.fT = self._scr("fT", [KC, 128, NT])
        self.qkvF = self._scr("qkvF", [48, 128, NT], BF16)
        self.cT = self._scr("cT", [KC, 128, NM])
        self.mgT = self._scr("mgT", [KC, 128, NM], BF16)
        self.h2 = self._scr("h2", [KC, 128, NT])
        self.h3 = self._scr("h3", [KC, 128, NT])
        self.h4 = self._scr("h4", [KC, 128, NT])
        self.pT = self._scr("pT", [2, 128, NM])
        if self.debug:
            self.dbg_bg = self._scr("dbg_bg", [2, 16, NT])
            self.dbg_of = self._scr("dbg_of", [NH, 128, NM], BF16)
        self.bar_a = self.nc.dram_tensor("bar_a", [1, 16], F32, kind="Internal").ap()
        self.bar_b = self.nc.dram_tensor("bar_b", [1, 16], F32, kind="Internal").ap()

    def v(self, off, shape, dt=F32):
        esz = 4 if dt == F32 else 2
        n = 1
        for s in shape[1:]:
            n *= s
        nb = n * esz
        assert off % 4 == 0 and nb % 4 == 0, (off, nb)
        assert off + nb <= ARENA_BYTES, (off, nb)
        a = self.arena[0:shape[0], off // 4:(off + nb) // 4]
        if dt != F32:
            a = a.bitcast(dt)
        if len(shape) == 3:
            a = a.rearrange("p (a b) -> p a b", a=shape[1])
        elif len(shape) == 4:
            a = a.rearrange("p (a b c) -> p a b c", a=shape[1], b=shape[2])
        return a

    def cv(self, col, n=1, rows=128):
        return self.arena[0:rows, col:col + n]

    def cm(self, col, n, rows=128, dt=F32):
        return self.arena[0:rows, 1024 + col:1024 + col + n]

    def bank(self):
        b = self.rot[self.rot_i % len(self.rot)]
        self.rot_i += 1
        return b

    def ring(self, name, n):
        i = self.ring_i.get(name, 0)
        self.ring_i[name] = i + 1
        return i % n

    def mm(self, out, lhsT, rhs, start, stop, reads, writes):
        return self.pg.add("pe", lambda e: e.matmul(out, lhsT, rhs, start=start, stop=stop), reads, writes)

    def tr(self, out, in_, ident, reads, writes):
        return self.pg.add("pe", lambda e: e.transpose(out, in_, ident), reads, writes)

    def act(self, out, in_, func, reads, writes, bias=None, scale=None):
        kw = {}
        if bias is not None:
            kw["bias"] = bias
        if scale is not None:
            kw["scale"] = scale
        return self.pg.add("act", lambda e: e.activation(out=out, in_=in_, func=func, **kw), reads, writes)

    def tt(self, out, a, b, op, reads, writes, eng="dve"):
        return self.pg.add(eng, lambda e: e.tensor_tensor(out, a, b, op), reads, writes)

    def ts(self, out, a, s1, s2, op0, op1, reads, writes, eng="dve"):
        if op1 is None:
            return self.pg.add(eng, lambda e: e.tensor_single_scalar(out, a, s1, op0), reads, writes)
        return self.pg.add(eng, lambda e: e.tensor_scalar(out, a, s1, s2, op0, op1), reads, writes)

    def stt(self, out, in0, scalar, in1, op0, op1, reads, writes, eng="dve"):
        return self.pg.add(eng, lambda e: e.scalar_tensor_tensor(out, in0, scalar, in1, op0, op1), reads, writes)

    def cp(self, out, in_, reads, writes, eng="dve"):
        return self.pg.add(eng, lambda e: e.tensor_copy(out, in_), reads, writes)

    def rstd(self, out, in_, scale, reads, writes):
        self.act(out, in_, AF.Ln, reads, writes, bias=EPS, scale=scale)
        return self.act(out, out, AF.Exp, writes, writes, scale=-0.5)

    def recip(self, out, in_, reads, writes):
        return self.pg.add("dve", lambda e: e.reciprocal(out, in_), reads, writes)

    def memset(self, ap, val, writes, eng="dve"):
        return self.pg.add(eng, lambda e: e.memset(ap, val), (), writes)

    def dma(self, eng, out, in_, key, reads, writes):
        return self.pg.add(eng, lambda e: e.dma_start(out=out, in_=in_), reads, writes, dma=key)

    def barrier(self):
        a, b = self.bar_a, self.bar_b
        self.bar_a, self.bar_b = b, a
        self.pg.barrier(lambda e: e.dma_start(out=b, in_=a))
        self.ring_i = {}

    def wslot(self):
        s = self.ws_i % NWS
        self.ws_i += 1
        return s

    def wview(self, s, shape):
        return self.v(WR_OFF + s * WSLOT, shape, BF16)

    def wload(self, s, dst, src):
        return self.dma("pool", dst, src, ("w", s), (), (("w", s),))

    def load_consts(self):
        self.dma("sp", self.arena[:, 0:NCV], self.cvec, ("c", 0), (), ("cv",))
        self.dma("sp", self.arena[:, 1024:1024 + NCM], self.cmask, ("c", 1), (), ("cm",))
        self.ident_bf = self.v(12288, [128, 128], BF16)
        self.ones_bf = self.v(12288 + 256, [128, 128], BF16)
        self.cp(self.ident_bf, self.cm(CM_ID, 128), ("cm",), ("cb",))
        self.cp(self.ones_bf, self.cm(CM_ONE, 128), ("cm",), ("cb",))
        self.ts(self.cv(CV_H1, 16), self.cv(CV_G["ffn1_post"], 16), 0.5, None, ALU.mult, None, ("cv",), ("cv2",))
        self.ts(self.cv(CV_H2, 16), self.cv(CV_G["ffn2_post"], 16), 0.5, None, ALU.mult, None, ("cv",), ("cv2",))
        self.act(self.cv(CV_NA, 1, 16), self.cv(CV_AL, 1, 16), AF.Exp, ("cv",), ("cv3",))
        self.ts(self.cv(CV_NA, 1, 16), self.cv(CV_NA, 1, 16), -1.0, None, ALU.mult, None, ("cv3",), ("cv3",))

    def transpose_in(self, src_tok, dst_fm, ntok, nfc, tok_off=0):
        self.barrier()
        XIN = self.v(PH_OFF, [128, 2, nfc * 128])
        XST = self.v(PH_OFF + 2 * nfc * 512, [128, 2, 4, 128])
        ident = self.cm(CM_ID, 128)
        for tb in range(ntok // 128):
            s = self.ring("xin", 2)
            self.dma("sp", XIN[:, s, :], src_tok[tb * 128:(tb + 1) * 128, :], ("xin", s), (), (("xin", s),))
            gsz = min(4, nfc)
            for g in range(nfc // gsz):
                b = self.bank()
                for q in range(gsz):
                    fc = g * gsz + q
                    self.tr(self.ps[:, b, q * 128:(q + 1) * 128], XIN[:, s, fc * 128:(fc + 1) * 128], ident,
                            (("xin", s), "cm"), (("ps", b),))
                k = self.ring("xst", 2)
                self.act(XST[:, k, 0:gsz].rearrange("p a b -> p (a b)"), self.ps[:, b, 0:gsz * 128], AF.Copy,
                         (("ps", b),), (("xst", k),))
                self.dma("act", dst_fm[g * gsz:(g + 1) * gsz, :, tok_off + tb * 128: tok_off + (tb + 1) * 128]
                         .rearrange("c p t -> p c t"), XST[:, k, 0:gsz], ("xst", k), (("xst", k),), ())

    def norm_load(self, src, t0, t1, gcol, XN, RSTD, XT, SQ):
        G = t1 - t0
        for (a, n) in _tiles(0, G):
            b = self.bank()
            for fc in range(KC):
                s = self.ring("xt", 4)
                self.dma("sp" if fc % 2 == 0 else "act", XT[:, s, :n], src[fc, :, t0 + a:t0 + a + n], ("xt", s), (), (("xt", s),))
                q = self.ring("sq", 2)
                self.act(SQ[:, q, :n], XT[:, s, :n], AF.Square, (("xt", s),), (("sq", q),))
                self.mm(self.ps[:, b, :n], self.ones_bf, SQ[:, q, :n], fc == 0, fc == KC - 1,
                        (("sq", q), "cb"), (("ps", b),))
            self.rstd(RSTD[:, a:a + n], self.ps[:, b, :n], 1.0 / D, (("ps", b),), (("rstd", a),))
        for (a, n) in _tiles(0, G):
            for fc in range(KC):
                s = self.ring("xt", 4)
                self.dma("sp" if fc % 2 == 0 else "act", XT[:, s, :n], src[fc, :, t0 + a:t0 + a + n], ("xt", s), (), (("xt", s),))
                self.stt(XN[:, fc, a:a + n], XT[:, s, :n], self.cv(gcol + fc), RSTD[:, a:a + n], ALU.mult, ALU.mult,
                         (("xt", s), ("rstd", a), "cv"), (("xn", fc, a),))

    def post_residual(self, fsrc, rsrc, dst, t0, t1, gcol, ssq_banks, RSTD, XT):
        G = t1 - t0
        tl = _tiles(0, G)
        self.barrier()
        for ti, (a, n) in enumerate(tl):
            b = ssq_banks[ti]
            self.rstd(RSTD[:, a:a + n], self.ps[:, b, :n], 1.0 / D, (("ps", b),), (("rstd", a),))
        its = [(ti, a, n, fc) for ti, (a, n) in enumerate(tl) for fc in range(KC)]

        def load(i):
            ti, a, n, fc = its[i]
            s = self.ring("xt", 4)
            s2 = self.ring("xt", 4)
            self.dma("sp", XT[:, s, :n], fsrc[fc, :, t0 + a:t0 + a + n], ("xt", s), ("fscr",), (("xt", s),))
            self.dma("act", XT[:, s2, :n], rsrc[fc, :, t0 + a:t0 + a + n], ("xt", s2), (), (("xt", s2),))
            return s, s2
        nxt = load(0)
        for i, (ti, a, n, fc) in enumerate(its):
            s, s2 = nxt
            self.tt(XT[:, s, :n], XT[:, s, :n], RSTD[:, a:a + n], ALU.mult, (("xt", s), ("rstd", a)), (("xt", s),))
            self.stt(XT[:, s, :n], XT[:, s, :n], self.cv(gcol + fc), XT[:, s2, :n], ALU.mult, ALU.add,
                     (("xt", s), ("xt", s2), "cv", "cv2"), (("xt", s),))
            if i + 1 < len(its):
                nxt = load(i + 1)
            self.dma("sp", dst[fc, :, t0 + a:t0 + a + n], XT[:, s, :n], ("xt", s), (("xt", s),), ())

    def ffn(self, src, dst, t0, t1, w_gu, w_dn, g_pre, g_post_half):
        G = t1 - t0
        XN = self.v(PH_OFF, [128, KC, G], BF16)
        ACTB = self.v(PH_OFF + 36864, [128, JC, G], BF16)
        MO = PH_OFF + 36864 + 101376
        RSTD = self.v(MO, [128, 1152])
        XT = self.v(MO + 4608, [128, 4, 512])
        SQ = self.v(MO + 4608 + 8192, [128, 2, 512], BF16)
        FT = self.v(PH_OFF, [128, 2, 512])
        tl = _tiles(0, G)
        self.barrier()
        self.rot = list(range(8))
        self.norm_load(src, t0, t1, g_pre, XN, RSTD, XT, SQ)
        self.barrier()
        wgu = w_gu.rearrange("(kc p) n -> p kc n", p=128)
        for j in range(JC):
            s = self.wslot()
            wv = self.wview(s, [128, KC, 256])
            self.wload(s, wv[:, :, 0:128], wgu[:, :, j * 128:(j + 1) * 128])
            self.wload(s, wv[:, :, 128:256], wgu[:, :, DFF + j * 128:DFF + (j + 1) * 128])
            for ti, (a, n) in enumerate(tl):
                bg = self.bank()
                for kc in range(KC):
                    self.mm(self.ps[:, bg, :n], wv[:, kc, 0:128], XN[:, kc, a:a + n], kc == 0, kc == KC - 1,
                            (("w", s),), (("ps", bg),))
                bu = self.bank()
                for kc in range(KC):
                    self.mm(self.ps[:, bu, :n], wv[:, kc, 128:256], XN[:, kc, a:a + n], kc == 0, kc == KC - 1,
                            (("w", s),), (("ps", bu),))
                q = self.ring("sq", 2)
                self.act(SQ[:, q, :n], self.ps[:, bg, :n], AF.Silu, (("ps", bg),), (("sq", q),))
                self.tt(ACTB[:, j, a:a + n], SQ[:, q, :n], self.ps[:, bu, :n], ALU.mult,
                        (("sq", q), ("ps", bu)), ())
        self.barrier()
        nt = len(tl)
        ssq = list(range(8 - nt, 8))
        self.rot = list(range(8 - nt))
        wdn = w_dn.rearrange("(kc p) n -> p kc n", p=128)
        for oc in range(KC):
            s = self.wslot()
            wv = self.wview(s, [128, JC, 128])
            self.wload(s, wv[:, 0:22, :], wdn[:, 0:22, oc * 128:(oc + 1) * 128])
            self.wload(s, wv[:, 22:44, :], wdn[:, 22:44, oc * 128:(oc + 1) * 128])
            for ti, (a, n) in enumerate(tl):
                b = self.bank()
                for kc in range(JC):
                    self.mm(self.ps[:, b, :n], wv[:, kc, :], ACTB[:, kc, a:a + n], kc == 0, kc == JC - 1,
                            (("w", s),), (("ps", b),))
                k = self.ring("ft", 2)
                self.act(FT[:, k, :n], self.ps[:, b, :n], AF.Copy, (("ps", b),), (("ft", k),))
                self.dma("act", self.fT[oc, :, t0 + a:t0 + a + n], FT[:, k, :n], ("ft", k), (("ft", k),), ("fscr",))
                q = self.ring("sq", 2)
                self.tt(SQ[:, q, :n], FT[:, k, :n], FT[:, k, :n], ALU.mult, (("ft", k),), (("sq", q),))
                self.mm(self.ps[:, ssq[ti], :n], self.ones_bf, SQ[:, q, :n], oc == 0, oc == KC - 1,
                        (("sq", q), "cb"), (("ps", ssq[ti]),))
        self.post_residual(self.fT, src, dst, t0, t1, g_post_half, ssq, RSTD, XT)
        self.rot = list(range(8))

    def mix_layout(self):
        o = PH_OFF
        self.BETA = self.v(o, [16, NT]); o += NT * 4
        self.GG = self.v(o, [16, NT]); o += NT * 4
        self.UN = self.v(o, [128, KC, NT], BF16); o += KC * NT * 2
        self.mix_free = o

    def mixer_qkv(self):
        self.mix_layout()
        o = self.mix_free
        RAW = self.v(o, [128, 2, 180]); o += 2 * 180 * 4
        RAWB = self.v(o, [128, 2, 2052], BF16); o += 2 * 2052 * 2
        DG = self.v(o, [128, 2, 4, 128], BF16); o += 2 * 4 * 128 * 2
        CVO = self.v(o, [128, 2, NT]); o += 2 * NT * 4
        QO = self.v(o, [128, 2, NT], BF16); o += 2 * NT * 2
        RSTD = self.v(o, [128, NT]); o += NT * 4
        XT = self.v(o, [128, 4, 512]); o += 8192
        SQ = self.v(o, [128, 2, 512], BF16); o += 2048
        ST = self.v(o, [64, 2, 256]); o += 2048
        CT = self.v(o, [128, 2, 48]); o += 384
        SSQ1 = self.v(o, [128, NT]); o += NT * 4
        SSQ = [RSTD, SSQ1]
        UN = self.UN
        ident = self.cm(CM_ID, 128)
        self.barrier()
        self.rot = list(range(8))
        self.norm_load(self.h1, 0, NT, CV_G["mix_pre"], UN, RSTD, XT, SQ)
        self.barrier()
        win = self.w_in.rearrange("(kc p) n -> p kc n", p=128)
        tl = _tiles(0, NT)
        tlp = [(a, n) for (a, n) in tl if a < NTP]
        allk = lambda nm, cc_: [(nm, cc_, a_) for (a_, _n) in tl]

        def emit_l2(cc_, a_, n_, q_):
            b3 = self.bank()
            self.mm(self.ps[:, b3, :n_], self.ones_bf, SQ[:, q_, :n_], True, True, (("sq", q_), "cb"), (("ps", b3),))
            self.act(SSQ[cc_][:, a_:a_ + n_], self.ps[:, b3, :n_], AF.Copy, (("ps", b3),), (("ssq", cc_, a_),))

        def emit_conv(c_, cc_, a_, n_):
            b2 = self.bank()
            for j in range(4):
                self.mm(self.ps[:, b2, :n_], DG[:, cc_, j, :], RAWB[:, cc_, a_ + j:a_ + j + n_], j == 0, j == 3,
                        (("dg", cc_), ("rawb", cc_, a_), ("rawb", cc_, a_ - 512)), (("ps", b2),))
            self.act(CVO[:, cc_, a_:a_ + n_], self.ps[:, b2, :n_], AF.Silu, (("ps", b2),), (("cvo", cc_, a_),))
            if c_ < 32:
                q_ = self.ring("sq", 2)
                self.act(SQ[:, q_, :n_], CVO[:, cc_, a_:a_ + n_], AF.Square, (("cvo", cc_, a_),), (("sq", q_),))
                return (cc_, a_, n_, q_)
            self.cp(QO[:, cc_, a_:a_ + n_], CVO[:, cc_, a_:a_ + n_], (("cvo", cc_, a_),), (("qo", cc_, a_),))
            return None

        def epilogue(c_, cc_, pend, pend_l2):
            RS = RAW[:, cc_, 0:176].rearrange("p (s j) -> p s j", j=11)
            if pend_l2 is not None:
                emit_l2(*pend_l2)
            if pend is not None:
                p2 = emit_conv(c_, *pend)
                if p2 is not None:
                    emit_l2(*p2)
            CS = CVO[:, cc_, NTP:NT].rearrange("p (s j) -> p s j", j=8)
            for j in range(4):
                wcol = self.cv(CV_SC + c_ * 4 + j)
                if j == 0:
                    self.ts(CS, RS[:, :, 0:8], wcol, None, ALU.mult, None, (("raw", cc_), "cv"), (("cvo", cc_, NTP),))
                else:
                    self.stt(CS, RS[:, :, j:j + 8], wcol, CS, ALU.mult, ALU.add,
                             (("raw", cc_), ("cvo", cc_, NTP), "cv"), (("cvo", cc_, NTP),))
            self.act(CVO[:, cc_, NTP:NT], CVO[:, cc_, NTP:NT], AF.Silu, (("cvo", cc_, NTP),), (("cvo", cc_, NTP),))
            if c_ < 32:
                q = self.ring("sq", 2)
                self.act(SQ[:, q, :NS], CVO[:, cc_, NTP:NT], AF.Square, (("cvo", cc_, NTP),), (("sq", q),))
                emit_l2(cc_, NTP, NS, q)
                keys = allk("ssq", cc_)
                self.rstd(SSQ[cc_][:, 0:NT], SSQ[cc_][:, 0:NT], 1.0, keys, keys)
                if c_ < 16:
                    self.stt(QO[:, cc_, :], CVO[:, cc_, :], float(128 ** -0.5), SSQ[cc_][:, 0:NT], ALU.mult, ALU.mult,
                             keys + allk("cvo", cc_), allk("qo", cc_))
                else:
                    self.tt(QO[:, cc_, :], CVO[:, cc_, :], SSQ[cc_][:, 0:NT], ALU.mult,
                            keys + allk("cvo", cc_), allk("qo", cc_))
            else:
                self.cp(QO[:, cc_, NTP:NT], CVO[:, cc_, NTP:NT], (("cvo", cc_, NTP),), (("qo", cc_, NTP),))
            self.dma("sp", self.qkvF[c_], QO[:, cc_, :], ("qo", cc_), allk("qo", cc_), ())
            b = self.bank()
            self.tr(self.ps[0:3, b, 0:128], RAW[:, cc_, 176:179], ident, (("raw", cc_), "cm"), (("ps", b),))
            kc_ = self.ring("ct", 2)
            self.cp(CT[:, kc_, :].rearrange("p (s j) -> p s j", j=3), RS[:, :, 8:11], (("raw", cc_),), (("ct", kc_),), eng="pool")
            self.tr(self.ps[0:48, b, 128:256], CT[:, kc_, :], ident, (("ct", kc_), "cm"), (("ps", b),))
            k = self.ring("st", 2)
            self.act(ST[0:3, k, 0:128], self.ps[0:3, b, 0:128], AF.Copy, (("ps", b),), (("st", k),))
            self.act(ST[0:48, k, 128:256], self.ps[0:48, b, 128:256], AF.Copy, (("ps", b),), (("st", k),))
            self.dma("sp", self.nq_p[:, c_ * 128:(c_ + 1) * 128], ST[0:3, k, 0:128], ("st", k), (("st", k),), ())
            self.dma("sp", self.nq_s[:, c_ * 128:(c_ + 1) * 128], ST[0:48, k, 128:256], ("st", k), (("st", k),), ())
        deferred = None
        for bi in range(24):
            s = self.wslot()
            wv = self.wview(s, [128, KC, 256])
            self.wload(s, wv, win[:, :, bi * 256:(bi + 1) * 256])
            xs = self.ring("xt", 4)
            self.dma("sp", XT[0:48, xs, 0:256], self.sqkv[:, bi * 256:(bi + 1) * 256], ("xt", xs), (), (("xt", xs),))
            for cc in range(2):
                c = bi * 2 + cc
                RS = RAW[:, cc, 0:176].rearrange("p (s j) -> p s j", j=11)
                self.memset(RAWB[:, cc, 0:3], 0.0, (("rawb", cc, -512),), eng="pool")
                b = self.bank()
                self.tr(self.ps[:, b, 0:48], XT[0:48, xs, cc * 128:(cc + 1) * 128], ident[0:48, 0:48],
                        (("xt", xs), "cm"), (("ps", b),))
                self.act(RS[:, :, 0:3], self.ps[:, b, 0:48].rearrange("p (s j) -> p s j", j=3), AF.Copy,
                         (("ps", b),), (("raw", cc),))
                for j in range(4):
                    self.ts(DG[:, cc, j, :], self.ident_bf, self.cv(CV_SC + c * 4 + j), None, ALU.mult, None,
                            ("cb", "cv"), (("dg", cc),))
                pend = None
                pend_l2 = None
                for ti, (a, n) in enumerate(tl):
                    b = self.bank()
                    for kc in range(KC):
                        self.mm(self.ps[:, b, :n], wv[:, kc, cc * 128:(cc + 1) * 128], UN[:, kc, a:a + n],
                                kc == 0, kc == KC - 1, (("w", s),), (("ps", b),))
                    if a < NTP:
                        self.act(RAWB[:, cc, 3 + a:3 + a + n], self.ps[:, b, :n], AF.Copy, (("ps", b),), (("rawb", cc, a),))
                        if a + n == NTP:
                            self.act(RAW[:, cc, 176:179], self.ps[:, b, n - 3:n], AF.Copy, (("ps", b),), (("raw", cc),))
                    else:
                        self.act(RS[:, :, 3:11], self.ps[:, b, 0:128].rearrange("p (s j) -> p s j", j=8), AF.Copy,
                                 (("ps", b),), (("raw", cc),))
                    if ti == 1 and deferred is not None:
                        epilogue(*deferred)
                        deferred = None
                    if pend_l2 is not None:
                        emit_l2(*pend_l2)
                        pend_l2 = None
                    if pend is not None:
                        pend_l2 = emit_conv(c, *pend)
                    pend = (cc, a, n) if a < NTP else None
                deferred = (c, cc, pend, pend_l2)
        epilogue(*deferred)
        s = self.wslot()
        wv = self.wview(s, [128, KC, 32])
        self.wload(s, wv, win[:, :, O_BETA:O_BETA + 32])
        for (a, n) in tl:
            bb = self.bank()
            for kc in range(KC):
                self.mm(self.ps[0:16, bb, :n], wv[:, kc, 0:16], UN[:, kc, a:a + n], kc == 0, kc == KC - 1,
                        (("w", s),), (("ps", bb),))
            ba = self.bank()
            for kc in range(KC):
                self.mm(self.ps[0:16, ba, :n], wv[:, kc, 16:32], UN[:, kc, a:a + n], kc == 0, kc == KC - 1,
                        (("w", s),), (("ps", ba),))
            self.act(self.BETA[:, a:a + n], self.ps[0:16, bb, :n], AF.Sigmoid, (("ps", bb),), (("bg", a),))
            r = self.ring("xt", 4)
            self.act(XT[0:16, r, :n], self.ps[0:16, ba, :n], AF.Exp, (("ps", ba), "cv"), (("xt", r),),
                     bias=self.cv(CV_DT, 1, 16), scale=1.0)
            self.act(XT[0:16, r, :n], XT[0:16, r, :n], AF.Ln, (("xt", r),), (("xt", r),), bias=1.0, scale=1.0)
            self.ts(self.GG[:, a:a + n], XT[0:16, r, :n], self.cv(CV_NA, 1, 16), None, ALU.mult, None,
                    (("xt", r), "cv3"), (("bg", a),))

    def delta_layout(self):
        o = PH_OFF + 2 * NT * 4
        self.OF = self.v(o, [128, NH, NM], BF16); o += NH * NM * 2
        self.dl_free = o
        L = {}

        def al(name, shape, dt=F32):
            nonlocal o
            L[name] = self.v(o, shape, dt)
            n = 1
            for s in shape[1:]:
                n *= s
            o += ((n * (4 if dt == F32 else 2) + 31) // 32) * 32
        al("S", [128, NH, 128]); al("SB", [128, NH, 128], BF16)
        al("QKVC", [128, 2, 48 * 64], BF16)
        for nm in ("GROW", "NBROW", "GAM", "E1", "W2", "E2", "RM"):
            al(nm, [128, 512])
        self.alias_off = o
        for pb_ in range(2):
            for nm in ("GROW", "NBROW", "GAM", "E1", "W2", "E2", "RM"):
                if pb_ == 0:
                    L["%s_0" % nm] = L[nm]
                else:
                    al("%s_1" % nm, [128, 512])
        for nm in ("P0", "P1", "PT0", "PT1", "RB", "TBT0", "QKM0", "KG0", "QG0", "TBT1", "QKM1", "KG1", "QG1"):
            al(nm, [128, 512], BF16)
        al("KDEC0", [128, 1024], BF16); al("KDEC1", [128, 1024], BF16)
        al("DDB", [128, 1024], BF16); al("UUB", [128, 1024], BF16); al("OTB", [128, 512])
        al("TOK", [128, 32]); al("NBTOK", [128, 16]); al("COLS", [128, 32]); al("DCOL", [128, 16])
        al("TOK_1", [128, 32]); al("NBTOK_1", [128, 16]); al("COLS_1", [128, 32]); al("DCOL_1", [128, 16])
        for nm in ("TOK", "NBTOK", "COLS", "DCOL"):
            L[nm + "_0"] = L[nm]
        for nm in ("P0", "P1", "PT0", "PT1", "RB"):
            L[nm + "_0"] = L[nm]
            al(nm + "_1", [128, 512], BF16)
        for pb_ in range(2):
            al("KK_%d" % pb_, [128, 512], BF16); al("QK_%d" % pb_, [128, 512], BF16)
        al("GLR0", [128, 128]); al("GLR1", [128, 128]); al("RHS2", [128, 128])
        for nm in ("QKM2", "KG2", "QG2"):
            al(nm, [128, 512], BF16)
        al("KDEC2", [128, 1024], BF16); al("GLR2", [128, 128])
        for nm in ("TBT", "QKM", "KG", "QG", "KDEC", "GLR"):
            L[nm] = L[nm + "0"]
        al("DT", [128, 4, 128], BF16); al("DD", [128, 4, 128], BF16); al("UU", [128, 4, 128], BF16)
        al("OT", [128, 4, 128])
        o_save = o
        o = self.alias_off
        al("S2", [128, 16, 128])
        al("UBD", [128, 16, 128], BF16)
        assert o <= o_save
        o = o_save
        L["RHS"] = L["RM_1"]
        self.L = L
        print("delta layout end", o)
        assert o <= ARENA_BYTES, o

    def delta_block(self, C, HB, nseg, tok0, qkvc, need_o, otok0, mk, nlev, sample=False, qk="qkvc"):
        L = self.L
        ones_f = self.cm(CM_ONE, 128)
        ident_f = self.cm(CM_ID, 128)
        W = C * HB

        def t3(name, rows=128, dt=None):
            return L[name][0:rows, 0:W].rearrange("p (h t) -> p h t", h=HB)
        b = self.bank()
        self.tr(self.ps[0:C, b, 0:16], self.BETA[0:16, tok0:tok0 + C], ident_f[0:16, 0:16], ("cm",), (("ps", b),))
        self.tr(self.ps[0:C, b, 16:32], self.GG[0:16, tok0:tok0 + C], ident_f[0:16, 0:16], ("cm",), (("ps", b),))
        TOK = L["TOK"][0:C]
        self.act(TOK, self.ps[0:C, b, 0:32], AF.Copy, (("ps", b),), ("tok",))
        NBTOK = L["NBTOK"][0:C]
        self.ts(NBTOK, TOK[:, 0:16], -1.0, None, ALU.mult, None, ("tok",), ("nbtok",))
        b = self.bank()
        self.mm(self.ps[0:C, b, 0:16], mk["TRI"], TOK[:, 16:32], True, True, ("tok", "cm"), (("ps", b),))
        self.mm(self.ps[0:C, b, 16:32], mk["SAME"], TOK[:, 16:32], True, True, ("tok", "cm"), (("ps", b),))
        COLS = L["COLS"][0:C]
        self.act(COLS, self.ps[0:C, b, 0:32], AF.Copy, (("ps", b),), ("cols",))
        DCOL = L["DCOL"][0:C]
        self.tt(DCOL, COLS[:, 16:32], COLS[:, 0:16], ALU.subtract, ("cols",), ("dcol",))
        self.act(DCOL, DCOL, AF.Exp, ("dcol",), ("dcol",))
        for hb in range(NH // HB):
            h0 = hb * HB
            gt = TOK[:, 16 + h0:16 + h0 + HB]
            RHS = t3("RHS", C)

            def bc_h(m):
                return m.unsqueeze(1).to_broadcast([C, HB, C])

            def bc_t(col, n=C):
                return col.unsqueeze(2).to_broadcast([C, HB, n])
            self.tt(RHS, bc_h(mk["TRI"]), bc_t(gt), ALU.mult, ("tok", "cm"), ("rhs",))
            b = self.bank()
            self.mm(self.ps[:, b, 0:W], ones_f[0:C, :], L["RHS"][0:C, 0:W], True, True, ("rhs", "cm"), (("ps", b),))
            self.act(L["GROW"][:, 0:W], self.ps[:, b, 0:W], AF.Copy, (("ps", b),), ("grow",))
            self.tt(RHS, bc_h(mk["ID"]), bc_t(NBTOK[:, h0:h0 + HB]), ALU.mult, ("nbtok", "cm", "rhs"), ("rhs",))
            b = self.bank()
            self.mm(self.ps[:, b, 0:W], ones_f[0:C, :], L["RHS"][0:C, 0:W], True, True, ("rhs", "cm"), (("ps", b),))
            self.act(L["NBROW"][:, 0:W], self.ps[:, b, 0:W], AF.Copy, (("ps", b),), ("nbrow",))
            R2 = L["RHS2"][0:C, 0:HB * nseg].rearrange("p (h s) -> p h s", h=HB)
            self.tt(R2, mk["SEG"].unsqueeze(1).to_broadcast([C, HB, nseg]), bc_t(gt, nseg), ALU.mult,
                    ("tok", "cm"), ("rhs2",))
            b = self.bank()
            self.mm(self.ps[:, b, 0:HB * nseg], ones_f[0:C, :], L["RHS2"][0:C, 0:HB * nseg], True, True,
                    ("rhs2", "cm"), (("ps", b),))
            GLR = L["GLR"][:, 0:HB * nseg]
            self.act(GLR, self.ps[:, b, 0:HB * nseg], AF.Exp, (("ps", b),), ("glr",))
            GLR3 = GLR.rearrange("p (h s) -> p h s", h=HB)
            self.act(L["GAM"][:, 0:W], L["GROW"][:, 0:W], AF.Exp, ("grow",), ("gam",))
            GAM = t3("GAM")
            kq = qkvc.rearrange("p (c t) -> p c t", c=48)
            self.tt(t3("KG"), kq[:, 16 + h0:16 + h0 + HB, :], GAM, ALU.mult, ("gam", qk), ("kg",))
            if need_o:
                self.tt(t3("QG"), kq[:, h0:h0 + HB, :], GAM, ALU.mult, ("gam", qk), ("qg",), eng="pool")
            GROWc = t3("GROW", C)
            gcol = COLS[:, h0:h0 + HB]
            E1 = t3("E1", C); W2 = t3("W2", C); E2 = t3("E2", C)
            self.tt(E1, GROWc, bc_t(gcol), ALU.subtract, ("grow", "cols"), ("e1",))
            self.ts(E1, E1, 0.0, None, ALU.min, None, ("e1",), ("e1",))
            self.act(E1, E1, AF.Exp, ("e1",), ("e1",))
            self.tt(W2, E1, bc_h(mk["US"]), ALU.mult, ("e1", "cm"), ("w2",))
            self.tt(W2, W2, t3("NBROW", C), ALU.mult, ("w2", "nbrow"), ("w2",))
            self.tt(E1, E1, bc_h(mk["UI"]), ALU.mult, ("e1", "cm"), ("e1",))
            self.tt(E2, GROWc, bc_t(gcol), ALU.subtract, ("grow", "cols"), ("e2",))
            self.ts(E2, E2, -1.0, 0.0, ALU.mult, ALU.min, ("e2",), ("e2",))
            self.act(E2, E2, AF.Exp, ("e2",), ("e2",))
            self.tt(E2, E2, bc_h(mk["LS"]), ALU.mult, ("e2", "cm"), ("e2",))
            self.tt(E2, E2, bc_t(NBTOK[:, h0:h0 + HB]), ALU.mult, ("e2", "nbtok"), ("e2",))
            bk = self.bank()
            bq = self.bank()
            bt = self.bank()
            psT = self.ps[:, bt, :].bitcast(BF16)
            for j in range(HB):
                kf = kq[:, 16 + h0 + j, :]
                qf = kq[:, h0 + j, :]
                self.mm(self.ps[0:C, bk, j * C:(j + 1) * C], kf, kf, True, True, (qk,), (("ps", bk),))
                if need_o:
                    self.mm(self.ps[0:C, bq, j * C:(j + 1) * C], kf, qf, True, True, (qk,), (("ps", bq),))
                self.tr(psT[0:C, j * 128:(j + 1) * 128], kf, self.ident_bf, (qk, "cb"), (("ps", bt),))
            pk = self.ps[0:C, bk, 0:W]
            RM = L["RM"][0:C, 0:W]
            self.tt(RM, pk, L["W2"][0:C, 0:W], ALU.mult, (("ps", bk), "w2"), ("rm",))
            Pc = L["P0"][0:C, 0:W]
            PTc = L["PT0"][0:C, 0:W]
            self.cp(Pc, RM, ("rm",), ("p0",), eng="pool")
            self.tt(PTc, pk, L["E2"][0:C, 0:W], ALU.mult, (("ps", bk), "e2"), ("pt0",))
            self.tt(t3("RM", C), t3("RM", C), bc_h(mk["ID"]), ALU.add, ("rm", "cm"), ("rm",))
            RB = L["RB"][0:C, 0:W]
            self.cp(RB, RM, ("rm",), ("rb",), eng="pool")
            if need_o:
                self.tt(L["QKM"][0:C, 0:W], self.ps[0:C, bq, 0:W], L["E1"][0:C, 0:W], ALU.mult,
                        (("ps", bq), "e1"), ("qkm",))
            KD = L["KDEC"][0:C, 0:HB * 128].rearrange("p (h d) -> p h d", h=HB)
            self.tt(KD, psT[0:C, 0:HB * 128].rearrange("p (h d) -> p h d", h=HB),
                    DCOL[:, h0:h0 + HB].unsqueeze(2).to_broadcast([C, HB, 128]), ALU.mult,
                    (("ps", bt), "dcol"), ("kdec",))
            cur = 0
            for lv in range(nlev):
                Pn = L["P%d" % (1 - cur)][0:C, 0:W]
                PTn = L["PT%d" % (1 - cur)][0:C, 0:W]
                pkey, ptkey = "p%d" % cur, "pt%d" % cur
                pnkey, ptnkey = "p%d" % (1 - cur), "pt%d" % (1 - cur)
                ba = self.bank()
                bb = self.bank()
                for j in range(HB):
                    sl = slice(j * C, (j + 1) * C)
                    if lv < nlev - 1:
                        self.mm(self.ps[0:C, ba, sl], PTc[:, sl], Pc[:, sl], True, True, (pkey, ptkey), (("ps", ba),))
                    self.mm(self.ps[0:C, bb, sl], Pc[:, sl], PTc[:, sl], True, True, (pkey, ptkey), (("ps", bb),))
                if lv < nlev - 1:
                    self.act(Pn, self.ps[0:C, ba, 0:W], AF.Copy, (("ps", ba),), (pnkey,))
                self.cp(PTn, self.ps[0:C, bb, 0:W], (("ps", bb),), (ptnkey,))
                bc = self.bank()
                for j in range(HB):
                    sl = slice(j * C, (j + 1) * C)
                    self.mm(self.ps[0:C, bc, sl], PTn[:, sl], RB[:, sl], True, True, (ptnkey, "rb"), (("ps", bc),))
                self.tt(RM, RM, self.ps[0:C, bc, 0:W], ALU.add, ("rm", ("ps", bc)), ("rm",))
                if lv < nlev - 1:
                    self.cp(RB, RM, ("rm",), ("rb",), eng="pool")
                Pc, PTc = Pn, PTn
                cur = 1 - cur
            self.tt(t3("TBT", C), t3("RM", C), bc_t(TOK[:, h0:h0 + HB]), ALU.mult, ("rm", "tok"), ("tbt",))
            for j in range(HB):
                h = h0 + j
                sl = slice(j * C, (j + 1) * C)
                KGh = L["KG"][:, sl]
                vf = kq[:, 32 + h, :]
                r = self.ring("chain", 4)
                DT = L["DT"][:, r, 0:C]
                DD = L["DD"][0:C, r, :]
                UU = L["UU"][0:C, r, :]
                OT = L["OT"][:, r, 0:C]
                if sample:
                    sr = h % 2
                    SSv = self.v_ss[sr]
                    SSB = L["SB"][:, 0:16, :]
                    self.dma("sp", SSv, self.sdelta[:, h].rearrange("s d v -> d s v"), ("ss", sr), (), (("ss", sr),))
                    self.cp(SSB, SSv, (("ss", sr),), ("ssb",), eng="pool")
                    bK = self.bank()
                    for sg in range(nseg):
                        self.mm(self.ps[:, bK, sg * LS:(sg + 1) * LS], SSB[:, sg, :], KGh[:, sg * LS:(sg + 1) * LS],
                                True, True, ("ssb", "kg"), (("ps", bK),))
                else:
                    SBh = L["SB"][:, h, :]
                    bK = self.bank()
                    self.mm(self.ps[:, bK, 0:C], SBh, KGh, True, True, (("sb", h), "kg"), (("ps", bK),))
                self.tt(DT, vf, self.ps[:, bK, 0:C], ALU.subtract, (qk, ("ps", bK)), (("dt", r),))
                bT = self.bank()
                pT = self.ps[:, bT, :].bitcast(BF16)
                self.tr(pT[0:C, 0:128], DT, self.ident_bf, (("dt", r), "cb"), (("ps", bT),))
                self.act(DD, pT[0:C, 0:128], AF.Copy, (("ps", bT),), (("dd", r),))
                bU = self.bank()
                self.mm(self.ps[0:C, bU, 0:128], L["TBT"][0:C, sl], DD, True, True, ("tbt", ("dd", r)), (("ps", bU),))
                self.act(UU, self.ps[0:C, bU, 0:128], AF.Copy, (("ps", bU),), (("uu", r),))
                if need_o:
                    bO = self.bank()
                    if sample:
                        for sg in range(nseg):
                            self.mm(self.ps[:, bO, sg * LS:(sg + 1) * LS], SSB[:, sg, :],
                                    L["QG"][:, j * C + sg * LS:j * C + (sg + 1) * LS], True, True,
                                    ("ssb", "qg"), (("ps", bO),))
                    else:
                        self.mm(self.ps[:, bO, 0:C], SBh, L["QG"][:, sl], True, True, (("sb", h), "qg"), (("ps", bO),))
                    self.mm(self.ps[:, bO, 128:128 + C], UU, L["QKM"][0:C, sl], True, True,
                            (("uu", r), "qkm"), (("ps", bO),))
                    self.act(OT, self.ps[:, bO, 0:C], AF.Copy, (("ps", bO),), (("ot", r),))
                    self.tt(self.OF[:, h, otok0:otok0 + C], OT, self.ps[:, bO, 128:128 + C], ALU.add,
                            (("ot", r), ("ps", bO)), ())
                KDh = L["KDEC"][0:C, j * 128:(j + 1) * 128]
                if sample:
                    UBD = L["UBD"]
                    self.tt(UBD, UU.unsqueeze(1).to_broadcast([128, 16, 128]),
                            mk["SEG"].unsqueeze(2).to_broadcast([128, 16, 128]), ALU.mult,
                            (("uu", r), "cm"), ("ubd",))
                    gl = GLR3[:, j, :]
                    self.tt(SSv, SSv, gl.unsqueeze(2).to_broadcast([128, 16, 128]), ALU.mult,
                            (("ss", sr), "glr"), (("ss", sr),))
                    for q4 in range(4):
                        bS = self.bank()
                        self.mm(self.ps[:, bS, :], KDh, UBD[:, 4 * q4:4 * q4 + 4, :].rearrange("p a b -> p (a b)"),
                                True, True, ("kdec", "ubd"), (("ps", bS),))
                        sv = SSv[:, 4 * q4:4 * q4 + 4, :].rearrange("p a b -> p (a b)")
                        self.tt(sv, sv, self.ps[:, bS, :], ALU.add, (("ss", sr), ("ps", bS)), (("ss", sr),))
                    self.dma("sp", self.nd_s[:, h].rearrange("s d v -> d s v"), SSv, ("ss", sr), (("ss", sr),), ())
                else:
                    bS = self.bank()
                    self.mm(self.ps[:, bS, 0:128], KDh, UU, True, True, ("kdec", ("uu", r)), (("ps", bS),))
                    Sh = L["S"][:, h, :]
                    self.stt(Sh, Sh, GLR[:, j:j + 1], self.ps[:, bS, 0:128], ALU.mult, ALU.add,
                             (("s", h), "glr", ("ps", bS)), (("s", h),))
                    self.cp(SBh, Sh, (("s", h),), (("sb", h),), eng="pool")

    def p_prep(self, c, hb, pb, ob, qkvc, qk, need_o, mk):
        C, HB, W, nlev = 64, 8, 512, 5
        L = self.L
        ones_f = self.cm(CM_ONE, 128)
        ident_f = self.cm(CM_ID, 128)
        tok0 = c * 64
        cp_ = c % 2

        def T(name):
            return L["%s_%d" % (name, pb)]

        def K(name):
            return (name, pb)

        def t3(name, rows=128):
            return T(name)[0:rows, 0:W].rearrange("p (h t) -> p h t", h=HB)

        def bc_h(m):
            return m.unsqueeze(1).to_broadcast([C, HB, C])

        def bc_t(col, n=C):
            return col.unsqueeze(2).to_broadcast([C, HB, n])
        TOK = L["TOK_%d" % cp_][0:C]
        NBTOK = L["NBTOK_%d" % cp_][0:C]
        COLS = L["COLS_%d" % cp_][0:C]
        DCOL = L["DCOL_%d" % cp_][0:C]
        kt, kn, kc_, kd = ("tok", cp_), ("nbtok", cp_), ("cols", cp_), ("dcol", cp_)
        if hb == 0:
            b = self.bank()
            self.tr(self.ps[0:C, b, 0:16], self.BETA[0:16, tok0:tok0 + C], ident_f[0:16, 0:16], ("cm",), (("ps", b),))
            self.tr(self.ps[0:C, b, 16:32], self.GG[0:16, tok0:tok0 + C], ident_f[0:16, 0:16], ("cm",), (("ps", b),))
            self.act(TOK, self.ps[0:C, b, 0:32], AF.Copy, (("ps", b),), (kt,))
            self.ts(NBTOK, TOK[:, 0:16], -1.0, None, ALU.mult, None, (kt,), (kn,), eng="pool")
            b = self.bank()
            self.mm(self.ps[0:C, b, 0:16], mk["TRI"], TOK[:, 16:32], True, True, (kt, "cm"), (("ps", b),))
            self.mm(self.ps[0:C, b, 16:32], mk["SAME"], TOK[:, 16:32], True, True, (kt, "cm"), (("ps", b),))
            self.act(COLS, self.ps[0:C, b, 0:32], AF.Copy, (("ps", b),), (kc_,))
            self.tt(DCOL, COLS[:, 16:32], COLS[:, 0:16], ALU.subtract, (kc_,), (kd,), eng="pool")
            self.act(DCOL, DCOL, AF.Exp, (kd,), (kd,))
            yield
        h0 = hb * HB
        gt = TOK[:, 16 + h0:16 + h0 + HB]
        kq = qkvc.rearrange("p (c t) -> p c t", c=48)
        bk = self.bank()
        bq = self.bank() if need_o else None
        bt = self.bank()
        psT = self.ps[:, bt, :].bitcast(BF16)
        for j in range(HB):
            kf = kq[:, 16 + h0 + j, :]
            qf = kq[:, h0 + j, :]
            self.mm(self.ps[0:C, bk, j * C:(j + 1) * C], kf, kf, True, True, (qk,), (("ps", bk),))
            if need_o:
                self.mm(self.ps[0:C, bq, j * C:(j + 1) * C], kf, qf, True, True, (qk,), (("ps", bq),))
            self.tr(psT[0:C, j * 128:(j + 1) * 128], kf, self.ident_bf, (qk, "cb"), (("ps", bt),))
        KK = T("KK")[0:C, 0:W]
        self.act(KK, self.ps[0:C, bk, 0:W], AF.Copy, (("ps", bk),), (K("kk"),))
        if need_o:
            QK = T("QK")[0:C, 0:W]
            self.act(QK, self.ps[0:C, bq, 0:W], AF.Copy, (("ps", bq),), (K("qk"),))
        RHS = t3("W2", C)
        self.tt(RHS, bc_h(mk["TRI"]), bc_t(gt), ALU.mult, (kt, "cm", K("w2")), (K("w2"),))
        b1 = self.bank()
        self.mm(self.ps[:, b1, 0:W], ones_f[0:C, :], T("W2")[0:C, 0:W], True, True, (K("w2"), "cm"), (("ps", b1),))
        self.act(T("GROW")[:, 0:W], self.ps[:, b1, 0:W], AF.Copy, (("ps", b1),), (K("grow"),))
        RHSb = t3("E2", C)
        self.tt(RHSb, bc_h(mk["ID"]), bc_t(NBTOK[:, h0:h0 + HB]), ALU.mult, (kn, "cm", K("e2")), (K("e2"),), eng="pool")
        b2 = self.bank()
        self.mm(self.ps[:, b2, 0:W], ones_f[0:C, :], T("E2")[0:C, 0:W], True, True, (K("e2"), "cm"), (("ps", b2),))
        MB = t3("NBROW", C)
        self.tt(MB, self.ps[0:C, b2, 0:W].rearrange("p (h t) -> p h t", h=HB), bc_h(mk["US"]), ALU.mult,
                (("ps", b2), "cm"), (K("mb"),))
        b3 = self.bank()
        self.mm(self.ps[:, b3, 0:HB], ones_f[0:C, :], gt, True, True, (kt, "cm"), (("ps", b3),))
        GLR = L["GLR%d" % ob][:, 0:HB]
        self.act(GLR, self.ps[:, b3, 0:HB], AF.Exp, (("ps", b3),), (("glr", ob),))
        KD = L["KDEC%d" % ob][0:C, 0:HB * 128].rearrange("p (h d) -> p h d", h=HB)
        self.tt(KD, psT[0:C, 0:HB * 128].rearrange("p (h d) -> p h d", h=HB),
                DCOL[:, h0:h0 + HB].unsqueeze(2).to_broadcast([C, HB, 128]), ALU.mult,
                (("ps", bt), kd), (("kdec", ob),))
        yield
        GROWc = t3("GROW", C)
        gcol = COLS[:, h0:h0 + HB]
        E1 = t3("E1", C); W2 = t3("W2", C); E2 = t3("E2", C)
        self.tt(E1, GROWc, bc_t(gcol), ALU.subtract, (K("grow"), kc_), (K("e1"),))
        self.tt(E2, bc_t(gcol), GROWc, ALU.subtract, (K("grow"), kc_, K("e2")), (K("e2"),), eng="pool")
        self.act(T("GAM")[:, 0:W], T("GROW")[:, 0:W], AF.Exp, (K("grow"),), (K("gam"),))
        self.ts(E1, E1, 0.0, None, ALU.min, None, (K("e1"),), (K("e1"),))
        self.ts(E2, E2, 0.0, None, ALU.min, None, (K("e2"),), (K("e2"),), eng="pool")
        yield
        self.act(E1, E1, AF.Exp, (K("e1"),), (K("e1"),))
        self.act(E2, E2, AF.Exp, (K("e2"),), (K("e2"),))
        GAM = t3("GAM")
        self.tt(L["KG%d" % ob][:, 0:W].rearrange("p (h t) -> p h t", h=HB),
                kq[:, 16 + h0:16 + h0 + HB, :], GAM, ALU.mult, (K("gam"), qk), (("kg", ob),))
        if need_o:
            self.tt(L["QG%d" % ob][:, 0:W].rearrange("p (h t) -> p h t", h=HB), kq[:, h0:h0 + HB, :], GAM, ALU.mult,
                    (K("gam"), qk), (("qg", ob),), eng="pool")
        yield
        self.stt(W2, E1, 1.0, MB, ALU.min, ALU.mult, (K("e1"), K("mb")), (K("w2"),))
        self.stt(E2, E2, 1.0, bc_h(mk["LS"]), ALU.min, ALU.mult, (K("e2"), "cm"), (K("e2"),))
        yield
        RM = T("RM")[0:C, 0:W]
        Pc = T("P0")[0:C, 0:W]
        PTc = T("PT0")[0:C, 0:W]
        self.tt(Pc, KK, T("W2")[0:C, 0:W], ALU.mult, (K("kk"), K("w2")), (K("p0"),), eng="pool")
        self.tt(t3("E2", C), t3("E2", C), bc_t(NBTOK[:, h0:h0 + HB]), ALU.mult, (K("e2"), kn), (K("e2"),), eng="pool")
        self.tt(RM, KK, T("W2")[0:C, 0:W], ALU.mult, (K("kk"), K("w2")), (K("rm"),))
        yield
        self.tt(PTc, KK, T("E2")[0:C, 0:W], ALU.mult, (K("kk"), K("e2")), (K("pt0"),))
        self.tt(t3("RM", C), t3("RM", C), bc_h(mk["ID"]), ALU.add, (K("rm"), "cm"), (K("rm"),), eng="pool")
        RB = T("RB")[0:C, 0:W]
        if need_o:
            self.stt(E1, E1, 1.0, bc_h(mk["UI"]), ALU.min, ALU.mult, (K("e1"), "cm"), (K("e1"),))
            self.tt(L["QKM%d" % ob][0:C, 0:W], QK, T("E1")[0:C, 0:W], ALU.mult, (K("qk"), K("e1")), (("qkm", ob),))
        yield "HALF"
        self.act(RB, RM, AF.Copy, (K("rm"),), (K("rb"),))
        cur = 0

        def squares(lv, Pc, PTc, cur):
            Pn = T("P%d" % (1 - cur))[0:C, 0:W]
            PTn = T("PT%d" % (1 - cur))[0:C, 0:W]
            pkey, ptkey = K("p%d" % cur), K("pt%d" % cur)
            ba = self.bank()
            bb = self.bank()
            for j in range(HB):
                sl = slice(j * C, (j + 1) * C)
                if lv < nlev - 1:
                    self.mm(self.ps[0:C, ba, sl], PTc[:, sl], Pc[:, sl], True, True, (pkey, ptkey), (("ps", ba),))
                self.mm(self.ps[0:C, bb, sl], Pc[:, sl], PTc[:, sl], True, True, (pkey, ptkey), (("ps", bb),))
            if lv < nlev - 1:
                self.act(Pn, self.ps[0:C, ba, 0:W], AF.Copy, (("ps", ba),), (K("p%d" % (1 - cur)),))
            self.act(PTn, self.ps[0:C, bb, 0:W], AF.Copy, (("ps", bb),), (K("pt%d" % (1 - cur)),))
            return Pn, PTn
        Pn, PTn = squares(0, Pc, PTc, cur)
        for lv in range(nlev):
            ptnkey = K("pt%d" % (1 - cur))
            Pc, PTc = Pn, PTn
            cur = 1 - cur
            yield
            bc = self.bank()
            for j in range(HB):
                sl = slice(j * C, (j + 1) * C)
                self.mm(self.ps[0:C, bc, sl], PTc[:, sl], RB[:, sl], True, True, (ptnkey, K("rb")), (("ps", bc),))
            if lv + 1 < nlev:
                Pn, PTn = squares(lv + 1, Pc, PTc, cur)
            self.tt(RM, RM, self.ps[0:C, bc, 0:W], ALU.add, (K("rm"), ("ps", bc)), (K("rm"),))
            if lv < nlev - 1:
                yield
                self.act(RB, RM, AF.Copy, (K("rm"),), (K("rb"),))
        yield
        self.tt(L["TBT%d" % pb][0:C, 0:W].rearrange("p (h t) -> p h t", h=HB), t3("RM", C),
                bc_t(TOK[:, h0:h0 + HB]), ALU.mult, (K("rm"), kt), (("tbt", pb),))

    def p_chain(self, c, hb, pb, ob, qkvc, qk, need_o):
        C, HB, W = 64, 8, 512
        L = self.L
        h0 = hb * HB
        kq = qkvc.rearrange("p (c t) -> p c t", c=48)
        SBk = [("sb", h0 + j) for j in range(HB)]
        Sk = [("s", h0 + j) for j in range(HB)]
        KG = L["KG%d" % ob]
        bK = self.bank()
        for j in range(HB):
            self.mm(self.ps[:, bK, j * C:(j + 1) * C], L["SB"][:, h0 + j, :], KG[:, j * C:(j + 1) * C], True, True,
                    (("sb", h0 + j), ("kg", ob)), (("ps", bK),))
        if need_o:
            bO1 = self.bank()
            for j in range(HB):
                self.mm(self.ps[:, bO1, j * C:(j + 1) * C], L["SB"][:, h0 + j, :], L["QG%d" % ob][:, j * C:(j + 1) * C],
                        True, True, (("sb", h0 + j), ("qg", ob)), (("ps", bO1),))
        DT = L["DT"].rearrange("p a b -> p (a b)")
        self.tt(DT.rearrange("p (h t) -> p h t", h=HB), kq[:, 32 + h0:32 + h0 + HB, :],
                self.ps[:, bK, 0:W].rearrange("p (h t) -> p h t", h=HB), ALU.subtract, (qk, ("ps", bK)), ("dt",))
        if need_o:
            OT = L["OTB"]
            self.act(OT, self.ps[:, bO1, 0:W], AF.Copy, (("ps", bO1),), ("ot",))
        yield
        bT = self.bank()
        pT = self.ps[:, bT, :].bitcast(BF16)
        for j in range(HB):
            self.tr(pT[0:C, j * 128:(j + 1) * 128], DT[:, j * C:(j + 1) * C], self.ident_bf, ("dt", "cb"), (("ps", bT),))
        DD = L["DDB"][0:C, :]
        self.act(DD, pT[0:C, 0:1024], AF.Copy, (("ps", bT),), ("dd",))
        yield
        TBT = L["TBT%d" % pb]
        bU = [self.bank(), self.bank()]
        for j in range(HB):
            self.mm(self.ps[0:C, bU[j // 4], (j % 4) * 128:(j % 4 + 1) * 128], TBT[0:C, j * C:(j + 1) * C],
                    DD[:, j * 128:(j + 1) * 128], True, True, (("tbt", pb), "dd"), (("ps", bU[j // 4]),))
        UU = L["UUB"][0:C, :]
        self.act(UU[:, 0:512], self.ps[0:C, bU[0], :], AF.Copy, (("ps", bU[0]),), ("uu0",))
        self.cp(UU[:, 512:1024], self.ps[0:C, bU[1], :], (("ps", bU[1]),), ("uu1",))
        yield
        KD = L["KDEC%d" % ob]
        bS = [self.bank(), self.bank()]
        for j in range(HB):
            self.mm(self.ps[:, bS[j // 4], (j % 4) * 128:(j % 4 + 1) * 128], KD[0:C, j * 128:(j + 1) * 128],
                    UU[:, j * 128:(j + 1) * 128], True, True, (("kdec", ob), "uu%d" % (j // 4)), (("ps", bS[j // 4]),))
        if need_o:
            bO2 = self.bank()
            for j in range(HB):
                self.mm(self.ps[:, bO2, j * C:(j + 1) * C], UU[:, j * 128:(j + 1) * 128],
                        L["QKM%d" % ob][0:C, j * C:(j + 1) * C], True, True,
                        ("uu%d" % (j // 4), ("qkm", ob)), (("ps", bO2),))
        S8 = L["S"][:, h0:h0 + HB, :]
        GLR = L["GLR%d" % ob][:, 0:HB]
        self.tt(S8, S8, GLR.unsqueeze(2).to_broadcast([128, HB, 128]), ALU.mult, Sk + [("glr", ob)], Sk, eng="pool")
        for q in range(2):
            s4 = L["S"][:, h0 + 4 * q:h0 + 4 * q + 4, :].rearrange("p a b -> p (a b)")
            self.tt(s4, s4, self.ps[:, bS[q], :], ALU.add, Sk[4 * q:4 * q + 4] + [("ps", bS[q])], Sk[4 * q:4 * q + 4])
        self.cp(L["SB"][:, h0:h0 + HB, :], S8, Sk, SBk, eng="pool")
        if need_o:
            otok0 = c * 64 - M0
            self.tt(self.OF[:, h0:h0 + HB, otok0:otok0 + C], OT.rearrange("p (h t) -> p h t", h=HB),
                    self.ps[:, bO2, 0:W].rearrange("p (h t) -> p h t", h=HB), ALU.add, ("ot", ("ps", bO2)), ())
        yield

    def delta_prompt_pipelined(self, mk):
        L = self.L
        self.rot = list(range(8))
        units = [(c, hb) for c in range(NTP // 64) for hb in range(2)]
        qslot = {}

        def step(g):
            try:
                return next(g) or True
            except StopIteration:
                return False
        preps = {}
        info = {}

        def start_prep(i):
            c, hb = units[i]
            if hb == 0:
                s = self.ring("qkvc", 2)
                qslot[c] = s
                self.dma("sp", L["QKVC"][:, s, :].rearrange("p (c t) -> p c t", c=48),
                         self.qkvF[:, :, c * 64:(c + 1) * 64].rearrange("c p t -> p c t"), ("qkvc", s), (), (("qkvc", s),))
            s = qslot[c]
            qc = L["QKVC"][:, s, :]
            need_o = c * 64 >= M0
            info[i] = (c, hb, i % 2, i % 3, qc, ("qkvc", s), need_o)
            preps[i] = self.p_prep(c, hb, i % 2, i % 3, qc, ("qkvc", s), need_o, mk)
        n = len(units)
        start_prep(0)
        while step(preps[0]) != "HALF":
            pass
        chain = None
        for i in range(n):
            if i + 1 < n:
                start_prep(i + 1)
            a_live = i + 1 < n
            b_live = True
            c_live = chain is not None
            while a_live or b_live or c_live:
                if b_live:
                    b_live = bool(step(preps[i]))
                if a_live:
                    if step(preps[i + 1]) == "HALF":
                        a_live = False
                if c_live:
                    c_live = bool(step(chain))
            c, hb, pb, ob, qc, qk, need_o = info[i]
            chain = self.p_chain(c, hb, pb, ob, qc, qk, need_o)
        while step(chain):
            pass

    def delta(self):
        self.delta_layout()
        L = self.L
        self.barrier()
        self.rot = list(range(8))
        self.memset(L["S"], 0.0, [("s", h) for h in range(NH)])
        self.memset(L["SB"], 0.0, [("sb", h) for h in range(NH)], eng="pool")
        mk = dict(TRI=self.cm(CM_TRI, 64, 64), UI=self.cm(CM_UI, 64, 64), US=self.cm(CM_US, 64, 64),
                  LS=self.cm(CM_LS, 64, 64), SAME=self.cm(CM_ONE, 64, 64), ID=self.cm(CM_ID, 64, 64),
                  SEG=self.cm(CM_ONE, 1, 64))
        self.delta_prompt_pipelined(mk)
        self.dma("sp", self.nd_p.rearrange("h d v -> d h v"), L["S"], ("sfin",), [("s", h) for h in range(NH)], ())
        self.barrier()
        self.rot = list(range(8))
        mk8 = dict(TRI=self.cm(CM_TRI8, 128), UI=self.cm(CM_UI8, 128), US=self.cm(CM_US8, 128),
                   LS=self.cm(CM_LS8, 128), SAME=self.cm(CM_SAME8, 128), ID=self.cm(CM_ID, 128),
                   SEG=self.cm(CM_SEG, 16))
        self.v_ss = [L["S"][:, 0:16, :], L["S2"]]
        qc = L["QKVC"].rearrange("p a b -> p (a b)")
        self.dma("sp", qc.rearrange("p (c t) -> p c t", c=48),
                 self.qkvF[:, :, NTP:NT].rearrange("c p t -> p c t"), ("qkvc", 0), (), (("qkvc", 0),))
        self.delta_block(128, 4, 16, NTP, qc, True, NMAIN, mk8, 2, sample=True, qk=("qkvc", 0))

    def mixer_tail(self):
        NU = NM + 32
        U0 = M0 - 32
        o = PH_OFF
        RSTD = self.v(o, [128, NU]); o += NU * 4
        SQ = self.v(o, [128, 2, 512], BF16); o += 2048
        MU = self.v(o, [128, NM]); o += NM * 4
        RS = self.v(o, [128, NM]); o += NM * 4
        assert o <= PH_OFF + 2 * NT * 4
        o = PH_OFF + 2 * NT * 4
        OF = self.OF; o += NH * NM * 2
        UN2 = self.v(o, [128, KC, NU], BF16); o += KC * NU * 2
        CB = self.v(o, [128, KC, NM], BF16); o += KC * NM * 2
        XT = self.v(o, [128, 4, 512]); o += 8192
        GPB = self.v(o, [128, 1056], BF16); o += 1056 * 2
        GP30 = self.v(o, [128, 32]); o += 128
        GSX = self.v(o, [128, 608]); o += 608 * 4
        DG31 = self.v(o, [128, 31, 128], BF16); o += 31 * 128 * 2
        CO = self.v(o, [128, NM]); o += NM * 4
        assert o <= ARENA_BYTES, o
        ident = self.cm(CM_ID, 128)
        win = self.w_in.rearrange("(kc p) n -> p kc n", p=128)
        self.barrier()
        self.rot = list(range(8))
        self.norm_load(self.h1, U0, NT, CV_G["mix_pre"], UN2, RSTD, XT, SQ)
        self.barrier()
        tlm = _tiles(0, NM)
        for zb in range(8):
            s = self.wslot()
            wv = self.wview(s, [128, KC, 256])
            self.wload(s, wv, win[:, :, O_Z + zb * 256:O_Z + (zb + 1) * 256])
            for cc in range(2):
                h = zb * 2 + cc
                SS = MU if cc == 0 else RS
                zbanks = []
                for (a, n) in tlm:
                    ok = ("of", h, a)
                    q = self.ring("sq", 2)
                    self.act(SQ[:, q, :n], OF[:, h, a:a + n], AF.Square, (ok,), (("sq", q),))
                    b1 = self.bank()
                    self.mm(self.ps[:, b1, :n], self.ones_bf, SQ[:, q, :n], True, True, (("sq", q), "cb"), (("ps", b1),))
                    b = self.bank()
                    for kc in range(KC):
                        self.mm(self.ps[:, b, :n], wv[:, kc, cc * 128:(cc + 1) * 128], UN2[:, kc, 32 + a:32 + a + n],
                                kc == 0, kc == KC - 1, (("w", s),), (("ps", b),))
                    zbanks.append(b)
                    self.act(SS[:, a:a + n], self.ps[:, b1, :n], AF.Copy, (("ps", b1),), (("ss", cc, a),))
                    r2 = self.ring("xt", 4)
                    self.act(XT[:, r2, :n], self.ps[:, b, :n], AF.Silu, (("ps", b),), (("xt", r2),))
                    self.tt(OF[:, h, a:a + n], OF[:, h, a:a + n], XT[:, r2, :n], ALU.mult, (ok, ("xt", r2)), (ok,))
                sk = [("ss", cc, a) for (a, n) in tlm]
                self.rstd(SS[:, 0:NM], SS[:, 0:NM], 1.0 / 128, sk, sk)
                self.stt(OF[:, h, :], OF[:, h, :], self.cv(CV_ON), SS[:, 0:NM], ALU.mult, ALU.mult,
                         sk + [("of", h, a) for (a, n) in tlm] + ["cv"], [("of", h, a) for (a, n) in tlm])
        self.barrier()
        self.rot = list(range(8))
        SUMB = [(2, 0), (3, 0), (4, 0)]
        SSQB = [(5, 0), (6, 0), (7, 0)]
        GS = GSX.rearrange("p (s j) -> p s j", j=38)
        tlu = _tiles(0, NU)
        for c in range(KC):
            s = self.wslot()
            wv = self.wview(s, [128, KC, 256])
            self.wload(s, wv[:, :, 0:128], win[:, :, O_GLU + c * 128:O_GLU + (c + 1) * 128])
            self.wload(s, wv[:, :, 128:256], win[:, :, O_GLU + 2048 + c * 128:O_GLU + 2048 + (c + 1) * 128])
            for q4 in range(4):
                xs = self.ring("xt", 4)
                self.dma("sp", XT[0:120, xs, 0:128], self.sglu[q4 * 120:(q4 + 1) * 120, c * 128:(c + 1) * 128],
                         ("xt", xs), (), (("xt", xs),))
                b = self.bank()
                self.tr(self.ps[:, b, 0:120], XT[0:120, xs, 0:128], ident[0:120, 0:120], (("xt", xs), "cm"), (("ps", b),))
                self.act(GS[:, 4 * q4:4 * q4 + 4, 0:30], self.ps[:, b, 0:120].rearrange("p (s j) -> p s j", j=30),
                         AF.Copy, (("ps", b),), ("glx",))
            for (a, n) in tlu:
                ba = self.bank()
                for kc in range(KC):
                    self.mm(self.ps[:, ba, :n], wv[:, kc, 0:128], UN2[:, kc, a:a + n], kc == 0, kc == KC - 1,
                            (("w", s),), (("ps", ba),))
                bb = self.bank()
                for kc in range(KC):
                    self.mm(self.ps[:, bb, :n], wv[:, kc, 128:256], UN2[:, kc, a:a + n], kc == 0, kc == KC - 1,
                            (("w", s),), (("ps", bb),))
                r = self.ring("xt", 4)
                self.act(XT[:, r, :n], self.ps[:, bb, :n], AF.Sigmoid, (("ps", bb),), (("xt", r),))
                lo = max(a, 2)
                hi = min(a + n, 1056)
                if hi > lo:
                    self.tt(GPB[:, lo - 2:hi - 2], self.ps[:, ba, lo - a:hi - a], XT[:, r, lo - a:hi - a], ALU.mult,
                            (("ps", ba), ("xt", r)), ("glx",))
                if a <= 1026 < a + n:
                    self.tt(GP30[:, 0:30], self.ps[:, ba, 1026 - a:1056 - a], XT[:, r, 1026 - a:1056 - a], ALU.mult,
                            (("ps", ba), ("xt", r)), ("gp30",))
                if a + n > 1056:
                    o0 = 1056 - a
                    self.tt(GS[:, :, 30:38], self.ps[:, ba, o0:o0 + 128].rearrange("p (s j) -> p s j", j=8),
                            XT[:, r, o0:o0 + 128].rearrange("p (s j) -> p s j", j=8), ALU.mult,
                            (("ps", ba), ("xt", r)), ("glx",))
            COs = CO[:, NMAIN:NM].rearrange("p (s j) -> p s j", j=8)
            bcol = self.cv(CV_G["b_dw"] + c)
            self.tt(DG31, self.ident_bf.unsqueeze(1).to_broadcast([128, 31, 128]),
                    self.cv(CV_DW + c * 31, 31).unsqueeze(2).to_broadcast([128, 31, 128]), ALU.mult,
                    ("cb", "cv"), ("dg31",))
            for t0_ in (0, 512):
                b = self.bank()
                for j in range(31):
                    self.mm(self.ps[:, b, :], DG31[:, j, :], GPB[:, t0_ + j:t0_ + j + 512], j == 0, j == 30,
                            ("dg31", "glx"), (("ps", b),))
                self.act(CO[:, t0_:t0_ + 512], self.ps[:, b, :], AF.Identity, (("ps", b), "cv"), ("co",), bias=bcol, scale=1.0)
            for j in range(31):
                wcol = self.cv(CV_DW + c * 31 + j)
                if j == 0:
                    self.ts(COs, GS[:, :, 0:8], wcol, bcol, ALU.mult, ALU.add, ("glx", "cv"), ("co",))
                else:
                    self.stt(COs, GS[:, :, j:j + 8], wcol, COs, ALU.mult, ALU.add, ("glx", "co", "cv"), ("co",))
            b = self.bank()
            self.tr(self.ps[0:30, b, 0:128], GP30[:, 0:30], ident, ("gp30", "cm"), (("ps", b),))
            k = self.ring("xt", 4)
            self.act(XT[0:30, k, 0:128], self.ps[0:30, b, 0:128], AF.Copy, (("ps", b),), (("xt", k),))
            self.dma("act", self.ng_p[:, c * 128:(c + 1) * 128], XT[0:30, k, 0:128], ("xt", k), (("xt", k),), ())
            for q4 in range(4):
                k = self.ring("xt", 4)
                self.cp(XT[:, k, 0:120].rearrange("p (s j) -> p s j", j=30), GS[:, 4 * q4:4 * q4 + 4, 8:38],
                        ("glx",), (("xt", k),), eng="pool")
                b = self.bank()
                self.tr(self.ps[0:120, b, 0:128], XT[:, k, 0:120], ident, (("xt", k), "cm"), (("ps", b),))
                self.act(XT[0:120, k, 128:256], self.ps[0:120, b, 0:128], AF.Copy, (("ps", b),), (("xt", k),))
                self.dma("act", self.ng_s[q4 * 120:(q4 + 1) * 120, c * 128:(c + 1) * 128], XT[0:120, k, 128:256],
                         ("xt", k), (("xt", k),), ())
            self.dma("sp", self.cT[c], CO, ("co",), ("co",), ())
        self.barrier()
        self.rot = [0, 1]
        for c in range(KC):
            for ti, (a, n) in enumerate(tlm):
                r = self.ring("xt", 4)
                self.dma("sp" if (c + ti) % 2 == 0 else "act", XT[:, r, :n], self.cT[c, :, a:a + n], ("xt", r), (), (("xt", r),))
                COt = XT[:, r, 0:512]
                a0 = a
                a = 0
                q = self.ring("sq", 2)
                self.act(SQ[:, q, :n], COt[:, a:a + n], AF.Square, (("xt", r),), (("sq", q),))
                self.mm(self.ps[:, SSQB[ti][0], SSQB[ti][1]:SSQB[ti][1] + n], self.ones_bf, SQ[:, q, :n], c == 0, c == KC - 1,
                        (("sq", q), "cb"), (("ps", SSQB[ti][0]),))
                q = self.ring("sq", 2)
                self.cp(SQ[:, q, :n], COt[:, a:a + n], (("xt", r),), (("sq", q),))
                a = a0
                self.mm(self.ps[:, SUMB[ti][0], SUMB[ti][1]:SUMB[ti][1] + n], self.ones_bf, SQ[:, q, :n], c == 0, c == KC - 1,
                        (("sq", q), "cb"), (("ps", SUMB[ti][0]),))
        for ti, (a, n) in enumerate(tlm):
            sb_, so_ = SUMB[ti]
            qb_, qo_ = SSQB[ti]
            self.act(MU[:, a:a + n], self.ps[:, sb_, so_:so_ + n], AF.Copy, (("ps", sb_),), (("mu", a),), scale=1.0 / D)
            self.tt(RS[:, a:a + n], MU[:, a:a + n], MU[:, a:a + n], ALU.mult, (("mu", a),), (("rs", a),))
            self.stt(RS[:, a:a + n], self.ps[:, qb_, qo_:qo_ + n], 1.0 / D, RS[:, a:a + n], ALU.mult, ALU.subtract,
                     (("ps", qb_), ("rs", a)), (("rs", a),))
            self.ts(RS[:, a:a + n], RS[:, a:a + n], 0.0, None, ALU.max, None, (("rs", a),), (("rs", a),))
            self.rstd(RS[:, a:a + n], RS[:, a:a + n], 1.0, (("rs", a),), (("rs", a),))
        self.barrier()
        self.rot = list(range(8))
        for c in range(KC):
            for (a, n) in tlm:
                r = self.ring("xt", 4)
                self.dma("sp", XT[:, r, :n], self.cT[c, :, a:a + n], ("xt", r), (), (("xt", r),))
                self.tt(XT[:, r, :n], XT[:, r, :n], MU[:, a:a + n], ALU.subtract, (("xt", r),), (("xt", r),))
                self.tt(XT[:, r, :n], XT[:, r, :n], RS[:, a:a + n], ALU.mult, (("xt", r),), (("xt", r),))
                self.ts(XT[:, r, :n], XT[:, r, :n], self.cv(CV_G["ln_g"] + c), self.cv(CV_G["ln_b"] + c),
                        ALU.mult, ALU.add, (("xt", r), "cv"), (("xt", r),))
                self.act(CB[:, c, a:a + n], XT[:, r, :n], AF.Silu, (("xt", r),), ())
        self.barrier()
        wba = self.w_ba.rearrange("(kc p) n -> p kc n", p=128)
        wbb = self.w_bb.rearrange("(kc p) n -> p kc n", p=128)
        LW = self.v(187392, [128, 2, KC, 256], BF16)
        bring = [0]

        def bslot():
            i = bring[0] % 5
            bring[0] += 1
            if i < 3:
                return self.wview(i, [128, KC, 256]), ("w", i)
            return LW[:, i - 3], ("wl", i - 3)

        def bload(dst, key, src):
            self.dma("pool", dst, src, key, (), (key,))
        for oc in range(KC):
            w1, s1 = bslot()
            bload(w1[:, :, 0:128], s1, wba[:, :, oc * 128:(oc + 1) * 128])
            bload(w1[:, :, 128:256], s1, wbb[:, :, oc * 128:(oc + 1) * 128])
            w2, s2 = bslot()
            bload(w2[:, :, 0:128], s2, win[:, :, O_GATE + oc * 128:O_GATE + (oc + 1) * 128])
            bload(w2[:, :, 128:256], s2, win[:, :, O_GATE + 2048 + oc * 128:O_GATE + 2048 + (oc + 1) * 128])
            for (a, n) in tlm:
                bs = []
                for (wv_, s_, col, src, off) in ((w1, s1, 0, OF, 0), (w1, s1, 128, CB, 0), (w2, s2, 0, UN2, 32), (w2, s2, 128, UN2, 32)):
                    b = self.bank()
                    for kc in range(KC):
                        self.mm(self.ps[:, b, :n], wv_[:, kc, col:col + 128], src[:, kc, off + a:off + a + n],
                                kc == 0, kc == KC - 1, (s_,), (("ps", b),))
                    bs.append(b)
                r1 = self.ring("xt", 4)
                r2 = self.ring("xt", 4)
                self.act(XT[:, r1, :n], self.ps[:, bs[2], :n], AF.Sigmoid, (("ps", bs[2]),), (("xt", r1),))
                self.act(XT[:, r2, :n], self.ps[:, bs[3], :n], AF.Sigmoid, (("ps", bs[3]),), (("xt", r2),))
                self.tt(XT[:, r1, :n], XT[:, r1, :n], self.ps[:, bs[0], :n], ALU.mult, (("xt", r1), ("ps", bs[0])), (("xt", r1),))
                self.tt(XT[:, r2, :n], XT[:, r2, :n], self.ps[:, bs[1], :n], ALU.mult, (("xt", r2), ("ps", bs[1])), (("xt", r2),))
                q = self.ring("sq", 2)
                self.tt(SQ[:, q, :n], XT[:, r1, :n], XT[:, r2, :n], ALU.add, (("xt", r1), ("xt", r2)), (("sq", q),))
                self.dma("sp", self.mgT[oc, :, a:a + n], SQ[:, q, :n], ("sq", q), (("sq", q),), ())

    def proj_post(self, kind):
        o = PH_OFF
        XN = self.v(o, [128, KC, NM], BF16); o += KC * NM * 2
        PB = self.v(o, [128, 2, NM], BF16); o += 2 * NM * 2
        RSTD = self.v(o, [128, NM]); o += NM * 4
        XT = self.v(o, [128, 4, 512]); o += 8192
        SQ = self.v(o, [128, 2, 512], BF16); o += 2048
        FT = self.v(o, [128, 2, 512]); o += 4096
        tlm = _tiles(0, NM)
        self.barrier()
        self.rot = list(range(8))
        if kind == "out":
            for kc in range(KC):
                self.dma("sp", XN[:, kc, :], self.mgT[kc], ("xnl", kc % 4), (), ())
            W = self.w_out.rearrange("(kc p) n -> p kc n", p=128)
            nk = KC
        else:
            self.norm_load(self.h3, M0, NT, CV_G["ple_pre"], XN, RSTD, XT, SQ)
            for kc in range(2):
                self.dma("pool", PB[:, kc, :], self.pT[kc], ("pbl", kc), (), ())
            W = self.w_pg.rearrange("(kc p) n -> p kc n", p=128)
            WP = self.w_pp.rearrange("(kc p) n -> p kc n", p=128)
            nk = KC
        self.barrier()
        ssq = [5, 6, 7]
        self.rot = list(range(5))
        for oc in range(KC):
            s = self.wslot()
            wv = self.wview(s, [128, KC + 2, 128])
            self.wload(s, wv[:, 0:KC, :], W[:, :, oc * 128:(oc + 1) * 128])
            if kind == "ple":
                self.wload(s, wv[:, KC:KC + 2, :], WP[:, :, oc * 128:(oc + 1) * 128])
            for ti, (a, n) in enumerate(tlm):
                b = self.bank()
                for kc in range(nk):
                    self.mm(self.ps[:, b, :n], wv[:, kc, :], XN[:, kc, a:a + n], kc == 0, kc == nk - 1,
                            (("w", s),), (("ps", b),))
                k = self.ring("ft", 2)
                if kind == "out":
                    self.act(FT[:, k, :n], self.ps[:, b, :n], AF.Copy, (("ps", b),), (("ft", k),))
                else:
                    b2 = self.bank()
                    for kc in range(2):
                        self.mm(self.ps[:, b2, :n], wv[:, KC + kc, :], PB[:, kc, a:a + n], kc == 0, kc == 1,
                                (("w", s),), (("ps", b2),))
                    self.act(FT[:, k, :n], self.ps[:, b, :n], AF.Sigmoid, (("ps", b),), (("ft", k),))
                    self.tt(FT[:, k, :n], FT[:, k, :n], self.ps[:, b2, :n], ALU.mult, (("ft", k), ("ps", b2)), (("ft", k),))
                self.dma("sp", self.fT[oc, :, M0 + a:M0 + a + n], FT[:, k, :n], ("ft", k), (("ft", k),), ("fscr",))
                q = self.ring("sq", 2)
                self.tt(SQ[:, q, :n], FT[:, k, :n], FT[:, k, :n], ALU.mult, (("ft", k),), (("sq", q),))
                self.mm(self.ps[:, ssq[ti], :n], self.ones_bf, SQ[:, q, :n], oc == 0, oc == KC - 1,
                        (("sq", q), "cb"), (("ps", ssq[ti]),))
        if kind == "out":
            self.post_residual(self.fT, self.h1, self.h2, M0, NT, CV_G["mix_post"], ssq, RSTD, XT)
        else:
            self.post_residual(self.fT, self.h3, self.h4, M0, NT, CV_G["ple_post"], ssq, RSTD, XT)
        self.rot = list(range(8))

    def write_y(self):
        self.barrier()
        self.rot = list(range(8))
        HIN = self.v(PH_OFF, [128, 2, 4, 128])
        YT = self.v(PH_OFF + 4096, [128, 2, D])
        ident = self.cm(CM_ID, 128)
        for tb in range(NM // 128):
            yk = self.ring("yt", 2)
            for g in range(4):
                k = self.ring("hin", 2)
                self.dma("sp", HIN[:, k], self.h4[g * 4:(g + 1) * 4, :, M0 + tb * 128:M0 + (tb + 1) * 128]
                         .rearrange("c p t -> p c t"), ("hin", k), (), (("hin", k),))
                b = self.bank()
                for q in range(4):
                    self.tr(self.ps[:, b, q * 128:(q + 1) * 128], HIN[:, k, q, :], ident, (("hin", k), "cm"), (("ps", b),))
                self.act(YT[:, yk, g * 512:(g + 1) * 512], self.ps[:, b, :], AF.Copy, (("ps", b),), (("yt", yk),))
            self.dma("act", self.y[tb * 128:(tb + 1) * 128, :], YT[:, yk, :], ("yt", yk), (("yt", yk),), ())

    def dbg_dump_bg(self):
        self.barrier()
        self.dma("sp", self.dbg_bg[0], self.BETA, ("dbg", 0), (), ())
        self.dma("sp", self.dbg_bg[1], self.GG, ("dbg", 0), (), ())

    def build(self):
        self.load_consts()
        self.transpose_in(self.xin, self.xT, NT, KC)
        if self.stop_after == "xT":
            return self.finish()
        self.ffn(self.xT, self.h1, 0, NPRE, self.w_gu1, self.w_dn1, CV_G["ffn1_pre"], CV_H1)
        self.ffn(self.xT, self.h1, M0, NT, self.w_gu1, self.w_dn1, CV_G["ffn1_pre"], CV_H1)
        if self.stop_after == "ffn1":
            return self.finish()
        self.mixer_qkv()
        if self.stop_after == "qkv":
            if self.debug:
                self.dbg_dump_bg()
            return self.finish()
        self.delta()
        if self.stop_after == "delta":
            if self.debug:
                self.barrier()
                self.dma("sp", self.dbg_of.rearrange("h p t -> p h t"), self.OF, ("dbg", 1), (), ())
            return self.finish()
        self.mixer_tail()
        self.proj_post("out")
        if self.stop_after == "mix":
            return self.finish()
        self.ffn(self.h2, self.h3, M0, NT, self.w_gu2, self.w_dn2, CV_G["ffn2_pre"], CV_H2)
        self.transpose_in(self.pin, self.pT, NM, 2)
        self.proj_post("ple")
        self.write_y()
        return self.finish()

    def finish(self):
        self.pg.emit()
        return self.nc


def _masks():
    m = np.zeros((128, NCM), np.float32)
    i = np.arange(128)
    m[:, CM_ID:CM_ID + 128] = np.eye(128, dtype=np.float32)
    m[:, CM_ONE:CM_ONE + 128] = 1.0
    j = np.arange(64)
    le = (j[:, None] <= j[None, :]).astype(np.float32)
    lt = (j[:, None] < j[None, :]).astype(np.float32)
    m[:64, CM_TRI:CM_TRI + 64] = le
    m[:64, CM_UI:CM_UI + 64] = le
    m[:64, CM_US:CM_US + 64] = lt
    m[:64, CM_LS:CM_LS + 64] = lt.T
    same = ((i[:, None] // LS) == (i[None, :] // LS)).astype(np.float32)
    le8 = (i[:, None] <= i[None, :]).astype(np.float32) * same
    lt8 = (i[:, None] < i[None, :]).astype(np.float32) * same
    m[:, CM_TRI8:CM_TRI8 + 128] = le8
    m[:, CM_UI8:CM_UI8 + 128] = le8
    m[:, CM_US8:CM_US8 + 128] = lt8
    m[:, CM_LS8:CM_LS8 + 128] = lt8.T
    m[:, CM_SAME8:CM_SAME8 + 128] = same
    m[:, CM_SEG:CM_SEG + 16] = (i[:, None] // LS == np.arange(16)[None, :]).astype(np.float32)
    return m


def _cvec(inp):
    c = np.zeros((128, NCV), np.float32)

    def fm(vec):
        return np.ascontiguousarray(np.asarray(vec, np.float32).reshape(-1, 128).T)
    for n, col in CV_G.items():
        key = {"b_dw": "b_dw_conv"}.get(n, n)
        c[:, col:col + 16] = fm(inp[key][0])
    wsc = np.asarray(inp["w_short_conv"][0], np.float32)
    c[:, CV_SC:CV_SC + 192] = wsc.reshape(4, 48, 128).transpose(2, 1, 0).reshape(128, 192)
    wdw = np.asarray(inp["w_dw_conv"][0], np.float32)
    c[:, CV_DW:CV_DW + 496] = wdw.reshape(31, 16, 128).transpose(2, 1, 0).reshape(128, 496)
    c[:, CV_ON] = np.asarray(inp["o_norm"][0], np.float32)
    c[:16, CV_AL] = np.asarray(inp["a_log"][0], np.float32)
    c[:16, CV_DT] = np.asarray(inp["dt_bias"][0], np.float32)
    return c


def make_in_maps(inp, cores=range(8)):
    xp = np.asarray(inp["x_prompt"], np.float32)
    xs = np.asarray(inp["x_sample"], np.float32)
    pp = np.asarray(inp["p_prompt"], np.float32)[0]
    psm = np.asarray(inp["p_sample"], np.float32)[0]
    sd = np.asarray(inp["state_delta"], np.float32)[0]
    sq = np.asarray(inp["state_qkv_conv"], np.float32)[0]
    sg = np.asarray(inp["state_glu_conv"], np.float32)[0]
    shared = {
        "cvec": _cvec(inp), "cmask": _masks(),
        "w_gu1": np.asarray(inp["ffn1_w_gu"][0]), "w_dn1": np.asarray(inp["ffn1_w_down"][0]),
        "w_in": np.asarray(inp["w_in"][0]), "w_ba": np.asarray(inp["w_branch_a"][0]),
        "w_bb": np.asarray(inp["w_branch_b"][0]), "w_out": np.asarray(inp["w_out"][0]),
        "w_gu2": np.asarray(inp["ffn2_w_gu"][0]), "w_dn2": np.asarray(inp["ffn2_w_down"][0]),
        "w_pg": np.asarray(inp["w_ple_gate"][0]), "w_pp": np.asarray(inp["w_ple_proj"][0]),
    }
    maps = []
    for c in cores:
        b, half = c // 2, c % 2
        main = xp[b, half * NMAIN:(half + 1) * NMAIN]
        pre = xp[b, 0:NPRE] if half == 1 else np.zeros((NPRE, D), np.float32)
        sl = slice(c * NSEQ, (c + 1) * NSEQ)
        m = dict(shared)
        m["xin"] = np.concatenate([pre, main, xs[sl].reshape(NS, D)], 0)
        m["pin"] = np.concatenate([pp[b, half * NMAIN:(half + 1) * NMAIN], psm[sl].reshape(NS, PLE)], 0)
        m["sdelta"] = np.ascontiguousarray(sd[sl])
        m["sqkv"] = np.ascontiguousarray(sq[sl].reshape(NSEQ * 3, QKV))
        m["sglu"] = np.ascontiguousarray(sg[sl].reshape(NSEQ * 30, D))
        maps.append(m)
    return maps


def kernel(**inputs):
    nc = Builder().build()
    maps = make_in_maps(inputs)
    res = run_bass_kernel_spmd(nc, maps, core_ids=list(range(8)))
    R = res.results
    yp = np.zeros((4, 2048, D), np.float32)
    ys = np.zeros((128, LS, D), np.float32)
    ndp = np.zeros((1, 4, NH, 128, 128), np.float32)
    nqp = np.zeros((1, 4, 3, QKV), np.float32)
    ngp = np.zeros((1, 4, 30, D), np.float32)
    nds = np.zeros((1, 128, NH, 128, 128), np.float32)
    nqs = np.zeros((1, 128, 3, QKV), np.float32)
    ngs = np.zeros((1, 128, 30, D), np.float32)
    for c in range(8):
        b, half = c // 2, c % 2
        r = R[c]
        yp[b, half * NMAIN:(half + 1) * NMAIN] = r["y"][:NMAIN]
        ys[c * NSEQ:(c + 1) * NSEQ] = r["y"][NMAIN:].reshape(NSEQ, LS, D)
        if half == 1:
            ndp[0, b] = r["nd_p"]
            nqp[0, b] = r["nq_p"]
            ngp[0, b] = r["ng_p"]
        nds[0, c * NSEQ:(c + 1) * NSEQ] = r["nd_s"]
        nqs[0, c * NSEQ:(c + 1) * NSEQ] = r["nq_s"].reshape(NSEQ, 3, QKV)
        ngs[0, c * NSEQ:(c + 1) * NSEQ] = r["ng_s"].reshape(NSEQ, 30, D)
    return (yp, ys, ndp, nqp, ngp, nds, nqs, ngs)
```

```python
import numpy as np
import concourse.bass as bass
import concourse.mybir as mybir
from concourse.bass_utils import run_bass_kernel_spmd

F32 = mybir.dt.float32
BF16 = mybir.dt.bfloat16
AF = mybir.ActivationFunctionType
ALU = mybir.AluOpType

ENGS = ("pe", "act", "dve", "pool", "sp")
EPOCH = 30000


class _Op:
    __slots__ = ("eng", "fn", "reads", "writes", "dma", "deps", "sig", "idx", "n", "bar")

    def __init__(self, eng, fn, reads, writes, dma):
        self.eng = eng
        self.fn = fn
        self.reads = reads
        self.writes = writes
        self.dma = dma
        self.deps = None
        self.sig = False
        self.idx = None
        self.n = 0
        self.bar = False


class Prog:
    def __init__(self, nc):
        self.nc = nc
        self.ops = []
        self.streams = {e: [] for e in ENGS}

    def add(self, eng, fn, reads=(), writes=(), dma=None):
        op = _Op(eng, fn, tuple(reads), tuple(writes), dma)
        op.n = len(self.ops)
        self.ops.append(op)
        self.streams[eng].append(op)
        return op

    def barrier(self, fn):
        op = self.add("sp", fn, dma=("bar",))
        op.bar = True
        return op

    def _analyze(self):
        last_w = {}
        readers = {}
        last_eng = {}
        last_dma = {}
        cur_bar = None
        need_bar = set()
        for op in self.ops:
            deps = {}
            if op.bar:
                for d in last_eng.values():
                    deps[d.n] = d
                for d in last_dma.values():
                    deps[d.n] = d
                last_w = {}
                readers = {}
                cur_bar = op
                need_bar = set(ENGS)
            else:
                if cur_bar is not None and op.eng in need_bar:
                    deps[cur_bar.n] = cur_bar
                    need_bar.discard(op.eng)
                for k in op.reads:
                    w = last_w.get(k)
                    if w is not None:
                        deps[w.n] = w
                for k in op.writes:
                    w = last_w.get(k)
                    if w is not None:
                        deps[w.n] = w
                    for r in readers.get(k, ()):
                        deps[r.n] = r
                for k in op.reads:
                    readers.setdefault(k, []).append(op)
                for k in op.writes:
                    last_w[k] = op
                    readers[k] = []
            if op.dma is not None:
                last_dma[op.dma] = op
            else:
                last_eng[op.eng] = op
            deps.pop(op.n, None)
            dl = []
            for d in deps.values():
                if d.dma is None and op.dma is None and d.eng == "pe" and op.eng == "pe":
                    continue
                dl.append(d)
            op.deps = dl
            for d in dl:
                d.sig = True
        cnt = {e: 0 for e in ENGS}
        dcnt = {}
        for op in self.ops:
            if op.dma is not None:
                dcnt[op.dma] = dcnt.get(op.dma, 0) + 16
                op.idx = ("d", op.dma, dcnt[op.dma])
            elif op.sig:
                c = cnt[op.eng]
                cnt[op.eng] += 1
                op.idx = ("e", (op.eng, c // EPOCH), c % EPOCH + 1)
        self.dma_final = dcnt

    def emit(self):
        nc = self.nc
        self._analyze()
        sems = {}

        def sem(kind, key):
            k = (kind, key)
            if k not in sems:
                sems[k] = nc.alloc_semaphore(name="s%d" % len(sems))
            return sems[k]

        waits = {}
        for e in ENGS:
            seen = {}
            for op in self.streams[e]:
                wl = {}
                for d in op.deps:
                    kind, key, val = d.idx
                    k = (kind, key)
                    if seen.get(k, 0) >= val:
                        continue
                    if wl.get(k, 0) < val:
                        wl[k] = val
                for k, v in wl.items():
                    seen[k] = v
                waits[op.n] = [(sem(*k), v) for k, v in wl.items()]
        final_waits = [(sem("d", k), v) for k, v in self.dma_final.items()]
        self.nsem = len(sems)

        def run_stream(engname, eng):
            for op in self.streams[engname]:
                for s, v in waits[op.n]:
                    eng.wait_ge(s, v)
                ins = op.fn(eng)
                if op.idx is not None:
                    kind, key, val = op.idx
                    ins.then_inc(sem(kind, key), 16 if kind == "d" else 1)
            if engname == "sp":
                for s, v in final_waits:
                    eng.wait_ge(s, v)

        with nc.Block() as block:
            @block.tensor
            def _(e):
                run_stream("pe", e)

            @block.scalar
            def _(e):
                run_stream("act", e)

            @block.vector
            def _(e):
                run_stream("dve", e)

            @block.gpsimd
            def _(e):
                run_stream("pool", e)

            @block.sync
            def _(e):
                run_stream("sp", e)


D = 2048
KC = 16
DFF = 5632
JC = 44
NH = 16
QKV = 6144
O_Z = QKV
O_BETA = O_Z + 2048
O_A = O_BETA + NH
O_GLU = O_A + NH
O_GATE = O_GLU + 4096
IN_DIM = O_GATE + 4096
PLE = 256
EPS = 1e-6
NPRE = 1024
NMAIN = 1024
NTP = NPRE + NMAIN
NSEQ = 16
LS = 8
NS = NSEQ * LS
NT = NTP + NS
NM = NMAIN + NS
M0 = NPRE

ARENA_BYTES = 206000
WR_OFF = 16384
WSLOT = 11264
NWS = 3
PH_OFF = WR_OFF + NWS * WSLOT

CV_G = {n: 16 * i for i, n in enumerate(
    ["ffn1_pre", "ffn1_post", "mix_pre", "mix_post", "ffn2_pre", "ffn2_post", "ple_pre", "ple_post",
     "b_dw", "ln_g", "ln_b"])}
CV_SC = 176
CV_DW = CV_SC + 192
CV_ON = CV_DW + 496
CV_AL = CV_ON + 1
CV_DT = CV_AL + 1
CV_H1 = CV_DT + 1
CV_H2 = CV_H1 + 16
CV_NA = CV_H2 + 16
NCV = CV_NA + 1
CM_ID = 0
CM_ONE = 128
CM_TRI = 256
CM_UI = 320
CM_US = 384
CM_LS = 448
CM_TRI8 = 512
CM_UI8 = 640
CM_US8 = 768
CM_LS8 = 896
CM_SAME8 = 1024
CM_SEG = 1152
NCM = CM_SEG + 16


def _tiles(t0, t1, n=512):
    return [(a, min(n, t1 - a)) for a in range(t0, t1, n)]


class Builder:
    def __init__(self, debug=False, stop_after=None):
        self.debug = debug
        self.stop_after = stop_after
        nc = bass.Bass("TRN2", target_bir_lowering=False)
        self.nc = nc
        self.pg = Prog(nc)
        self.arena = nc.alloc_sbuf_tensor("arena", [128, ARENA_BYTES // 4], F32)
        self.ps = nc.alloc_psum_tensor("ps", [128, 8, 512], F32)
        self.rot = list(range(8))
        self.rot_i = 0
        self.ws_i = 0
        self.ring_i = {}
        self._decl()

    def _in(self, name, shape, dt=F32):
        return self.nc.dram_tensor(name, list(shape), dt, kind="ExternalInput").ap()

    def _out(self, name, shape, dt=F32):
        return self.nc.dram_tensor(name, list(shape), dt, kind="ExternalOutput").ap()

    def _scr(self, name, shape, dt=F32):
        kind = "ExternalOutput" if self.debug else "Internal"
        return self.nc.dram_tensor(name, list(shape), dt, kind=kind).ap()

    def _decl(self):
        self.xin = self._in("xin", [NT, D])
        self.pin = self._in("pin", [NM, PLE])
        self.sdelta = self._in("sdelta", [NSEQ, NH, 128, 128])
        self.sqkv = self._in("sqkv", [NSEQ * 3, QKV])
        self.sglu = self._in("sglu", [NSEQ * 30, D])
        self.cvec = self._in("cvec", [128, NCV])
        self.cmask = self._in("cmask", [128, NCM])
        self.w_gu1 = self._in("w_gu1", [D, 2 * DFF])
        self.w_dn1 = self._in("w_dn1", [DFF, D])
        self.w_in = self._in("w_in", [D, IN_DIM])
        self.w_ba = self._in("w_ba", [D, D])
        self.w_bb = self._in("w_bb", [D, D])
        self.w_out = self._in("w_out", [D, D])
        self.w_gu2 = self._in("w_gu2", [D, 2 * DFF])
        self.w_dn2 = self._in("w_dn2", [DFF, D])
        self.w_pg = self._in("w_pg", [D, D])
        self.w_pp = self._in("w_pp", [PLE, D])
        self.y = self._out("y", [NM, D])
        self.nd_p = self._out("nd_p", [NH, 128, 128])
        self.nq_p = self._out("nq_p", [3, QKV])
        self.ng_p = self._out("ng_p", [30, D])
        self.nd_s = self._out("nd_s", [NSEQ, NH, 128, 128])
        self.nq_s = self._out("nq_s", [NSEQ * 3, QKV])
        self.ng_s = self._out("ng_s", [NSEQ * 30, D])
        self.xT = self._scr("xT", [KC, 128, NT])
        self.h1 = self._scr("h1", [KC, 128, NT])
        self.fT = self._scr("fT", [KC, 128, NT])
        self.qkvF = self._scr("qkvF", [48, 128, NT], BF16)
        self.cT = self._scr("cT", [KC, 128, NM])
        self.mgT = self._scr("mgT", [KC, 128, NM], BF16)
        self.h2 = self._scr("h2", [KC, 128, NT])
        self.h3 = self._scr("h3", [KC, 128, NT])
        self.h4 = self._scr("h4", [KC, 128, NT])
        self.pT = self._scr("pT", [2, 128, NM])
        if self.debug:
            self.dbg_bg = self._scr("dbg_bg", [2, 16, NT])
            self.dbg_of = self._scr("dbg_of", [NH, 128, NM], BF16)
        self.bar_a = self.nc.dram_tensor("bar_a", [1, 16], F32, kind="Internal").ap()
        self.bar_b = self.nc.dram_tensor("bar_b", [1, 16], F32, kind="Internal").ap()

    def v(self, off, shape, dt=F32):
        esz = 4 if dt == F32 else 2
        n = 1
        for s in shape[1:]:
            n *= s
        nb = n * esz
        assert off % 4 == 0 and nb % 4 == 0, (off, nb)
        assert off + nb <= ARENA_BYTES, (off, nb)
        a = self.arena[0:shape[0], off // 4:(off + nb) // 4]
        if dt != F32:
            a = a.bitcast(dt)
        if len(shape) == 3:
            a = a.rearrange("p (a b) -> p a b", a=shape[1])
        elif len(shape) == 4:
            a = a.rearrange("p (a b c) -> p a b c", a=shape[1], b=shape[2])
        return a

    def cv(self, col, n=1, rows=128):
        return self.arena[0:rows, col:col + n]

    def cm(self, col, n, rows=128, dt=F32):
        return self.arena[0:rows, 1024 + col:1024 + col + n]

    def bank(self):
        b = self.rot[self.rot_i % len(self.rot)]
        self.rot_i += 1
        return b

    def ring(self, name, n):
        i = self.ring_i.get(name, 0)
        self.ring_i[name] = i + 1
        return i % n

    def mm(self, out, lhsT, rhs, start, stop, reads, writes):
        return self.pg.add("pe", lambda e: e.matmul(out, lhsT, rhs, start=start, stop=stop), reads, writes)

    def tr(self, out, in_, ident, reads, writes):
        return self.pg.add("pe", lambda e: e.transpose(out, in_, ident), reads, writes)

    def act(self, out, in_, func, reads, writes, bias=None, scale=None):
        kw = {}
        if bias is not None:
            kw["bias"] = bias
        if scale is not None:
            kw["scale"] = scale
        return self.pg.add("act", lambda e: e.activation(out=out, in_=in_, func=func, **kw), reads, writes)

    def tt(self, out, a, b, op, reads, writes, eng="dve"):
        return self.pg.add(eng, lambda e: e.tensor_tensor(out, a, b, op), reads, writes)

    def ts(self, out, a, s1, s2, op0, op1, reads, writes, eng="dve"):
        if op1 is None:
            return self.pg.add(eng, lambda e: e.tensor_single_scalar(out, a, s1, op0), reads, writes)
        return self.pg.add(eng, lambda e: e.tensor_scalar(out, a, s1, s2, op0, op1), reads, writes)

    def stt(self, out, in0, scalar, in1, op0, op1, reads, writes, eng="dve"):
        return self.pg.add(eng, lambda e: e.scalar_tensor_tensor(out, in0, scalar, in1, op0, op1), reads, writes)

    def cp(self, out, in_, reads, writes, eng="dve"):
        return self.pg.add(eng, lambda e: e.tensor_copy(out, in_), reads, writes)

    def rstd(self, out, in_, scale, reads, writes):
        self.act(out, in_, AF.Ln, reads, writes, bias=EPS, scale=scale)
        return self.act(out, out, AF.Exp, writes, writes, scale=-0.5)

    def recip(self, out, in_, reads, writes):
        return self.pg.add("dve", lambda e: e.reciprocal(out, in_), reads, writes)

    def memset(self, ap, val, writes, eng="dve"):
        return self.pg.add(eng, lambda e: e.memset(ap, val), (), writes)

    def dma(self, eng, out, in_, key, reads, writes):
        return self.pg.add(eng, lambda e: e.dma_start(out=out, in_=in_), reads, writes, dma=key)

    def barrier(self):
        a, b = self.bar_a, self.bar_b
        self.bar_a, self.bar_b = b, a
        self.pg.barrier(lambda e: e.dma_start(out=b, in_=a))
        self.ring_i = {}

    def wslot(self):
        s = self.ws_i % NWS
        self.ws_i += 1
        return s

    def wview(self, s, shape):
        return self.v(WR_OFF + s * WSLOT, shape, BF16)

    def wload(self, s, dst, src):
        return self.dma("pool", dst, src, ("w", s), (), (("w", s),))

    def load_consts(self):
        self.dma("sp", self.arena[:, 0:NCV], self.cvec, ("c", 0), (), ("cv",))
        self.dma("sp", self.arena[:, 1024:1024 + NCM], self.cmask, ("c", 1), (), ("cm",))
        self.ident_bf = self.v(12288, [128, 128], BF16)
        self.ones_bf = self.v(12288 + 256, [128, 128], BF16)
        self.cp(self.ident_bf, self.cm(CM_ID, 128), ("cm",), ("cb",))
        self.cp(self.ones_bf, self.cm(CM_ONE, 128), ("cm",), ("cb",))
        self.ts(self.cv(CV_H1, 16), self.cv(CV_G["ffn1_post"], 16), 0.5, None, ALU.mult, None, ("cv",), ("cv2",))
        self.ts(self.cv(CV_H2, 16), self.cv(CV_G["ffn2_post"], 16), 0.5, None, ALU.mult, None, ("cv",), ("cv2",))
        self.act(self.cv(CV_NA, 1, 16), self.cv(CV_AL, 1, 16), AF.Exp, ("cv",), ("cv3",))
        self.ts(self.cv(CV_NA, 1, 16), self.cv(CV_NA, 1, 16), -1.0, None, ALU.mult, None, ("cv3",), ("cv3",))

    def transpose_in(self, src_tok, dst_fm, ntok, nfc, tok_off=0):
        self.barrier()
        XIN = self.v(PH_OFF, [128, 2, nfc * 128])
        XST = self.v(PH_OFF + 2 * nfc * 512, [128, 2, 4, 128])
        ident = self.cm(CM_ID, 128)
        for tb in range(ntok // 128):
            s = self.ring("xin", 2)
            self.dma("sp", XIN[:, s, :], src_tok[tb * 128:(tb + 1) * 128, :], ("xin", s), (), (("xin", s),))
            gsz = min(4, nfc)
            for g in range(nfc // gsz):
                b = self.bank()
                for q in range(gsz):
                    fc = g * gsz + q
                    self.tr(self.ps[:, b, q * 128:(q + 1) * 128], XIN[:, s, fc * 128:(fc + 1) * 128], ident,
                            (("xin", s), "cm"), (("ps", b),))
                k = self.ring("xst", 2)
                self.act(XST[:, k, 0:gsz].rearrange("p a b -> p (a b)"), self.ps[:, b, 0:gsz * 128], AF.Copy,
                         (("ps", b),), (("xst", k),))
                self.dma("act", dst_fm[g * gsz:(g + 1) * gsz, :, tok_off + tb * 128: tok_off + (tb + 1) * 128]
                         .rearrange("c p t -> p c t"), XST[:, k, 0:gsz], ("xst", k), (("xst", k),), ())

    def norm_load(self, src, t0, t1, gcol, XN, RSTD, XT, SQ, pre_ssq=None):
        G = t1 - t0
        for ti, (a, n) in enumerate(_tiles(0, G)):
            if pre_ssq is not None:
                b = pre_ssq[ti]
                self.rstd(RSTD[:, a:a + n], self.ps[:, b, :n], 1.0 / D, (("ps", b),), (("rstd", a),))
                continue
            b = self.bank()
            for fc in range(KC):
                s = self.ring("xt", 4)
                self.dma("sp" if fc % 2 == 0 else "act", XT[:, s, :n], src[fc, :, t0 + a:t0 + a + n], ("xt", s), (), (("xt", s),))
                q = self.ring("sq", 2)
                self.act(SQ[:, q, :n], XT[:, s, :n], AF.Square, (("xt", s),), (("sq", q),))
                self.mm(self.ps[:, b, :n], self.ones_bf, SQ[:, q, :n], fc == 0, fc == KC - 1,
                        (("sq", q), "cb"), (("ps", b),))
            self.rstd(RSTD[:, a:a + n], self.ps[:, b, :n], 1.0 / D, (("ps", b),), (("rstd", a),))
        for (a, n) in _tiles(0, G):
            for fc in range(KC):
                s = self.ring("xt", 4)
                self.dma("sp" if fc % 2 == 0 else "act", XT[:, s, :n], src[fc, :, t0 + a:t0 + a + n], ("xt", s), (), (("xt", s),))
                self.stt(XN[:, fc, a:a + n], XT[:, s, :n], self.cv(gcol + fc), RSTD[:, a:a + n], ALU.mult, ALU.mult,
                         (("xt", s), ("rstd", a), "cv"), (("xn", fc, a),))

    def post_residual(self, fsrc, rsrc, dst, t0, t1, gcol, ssq_banks, RSTD, XT, SQ=None, nxt_ssq=False, yout=None, YST=None):
        G = t1 - t0
        tl = _tiles(0, G)
        self.barrier()
        for ti, (a, n) in enumerate(tl):
            b = ssq_banks[ti]
            self.rstd(RSTD[:, a:a + n], self.ps[:, b, :n], 1.0 / D, (("ps", b),), (("rstd", a),))
        its = [(ti, a, n, fc) for ti, (a, n) in enumerate(tl) for fc in range(KC)]

        def load(i):
            ti, a, n, fc = its[i]
            s = self.ring("xt", 4)
            s2 = self.ring("xt", 4)
            self.dma("sp", XT[:, s, :n], fsrc[fc, :, t0 + a:t0 + a + n], ("xt", s), ("fscr",), (("xt", s),))
            self.dma("act", XT[:, s2, :n], rsrc[fc, :, t0 + a:t0 + a + n], ("xt", s2), (), (("xt", s2),))
            return s, s2
        nxt = load(0)
        for i, (ti, a, n, fc) in enumerate(its):
            s, s2 = nxt
            self.tt(XT[:, s, :n], XT[:, s, :n], RSTD[:, a:a + n], ALU.mult, (("xt", s), ("rstd", a)), (("xt", s),))
            self.stt(XT[:, s, :n], XT[:, s, :n], self.cv(gcol + fc), XT[:, s2, :n], ALU.mult, ALU.add,
                     (("xt", s), ("xt", s2), "cv", "cv2"), (("xt", s),))
            if i + 1 < len(its):
                nxt = load(i + 1)
            if nxt_ssq:
                q = self.ring("sq", 2)
                self.act(SQ[:, q, :n], XT[:, s, :n], AF.Square, (("xt", s),), (("sq", q),))
                self.mm(self.ps[:, ssq_banks[ti], :n], self.ones_bf, SQ[:, q, :n], fc == 0, fc == KC - 1,
                        (("sq", q), "cb"), (("ps", ssq_banks[ti]),))
            if yout is None:
                self.dma("sp", dst[fc, :, t0 + a:t0 + a + n], XT[:, s, :n], ("xt", s), (("xt", s),), ())
            else:
                nq = n // 128
                b = self.bank()
                for q4 in range(nq):
                    self.tr(self.ps[:, b, q4 * 128:(q4 + 1) * 128], XT[:, s, q4 * 128:(q4 + 1) * 128], self.cm(CM_ID, 128),
                            (("xt", s), "cm"), (("ps", b),))
                k = self.ring("yst", 2)
                self.act(YST[:, k, :n], self.ps[:, b, :n], AF.Copy, (("ps", b),), (("yst", k),))
                self.dma("sp", yout[a:a + n, fc * 128:(fc + 1) * 128].rearrange("(q p) f -> p q f", p=128),
                         YST[:, k, :n].rearrange("p (q f) -> p q f", f=128), ("yst", k), (("yst", k),), ())

    def ffn(self, src, dst, t0, t1, w_gu, w_dn, g_pre, g_post_half, pre_ssq=None, nxt_ssq=False):
        G = t1 - t0
        XN = self.v(PH_OFF, [128, KC, G], BF16)
        ACTB = self.v(PH_OFF + 36864, [128, JC, G], BF16)
        MO = PH_OFF + 36864 + 101376
        RSTD = self.v(MO, [128, 1152])
        XT = self.v(MO + 4608, [128, 4, 512])
        SQ = self.v(MO + 4608 + 8192, [128, 2, 512], BF16)
        FT = self.v(PH_OFF, [128, 2, 512])
        tl = _tiles(0, G)
        self.barrier()
        self.rot = list(range(8))
        self.norm_load(src, t0, t1, g_pre, XN, RSTD, XT, SQ, pre_ssq=pre_ssq)
        self.barrier()
        wgu = w_gu.rearrange("(kc p) n -> p kc n", p=128)
        for j in range(JC):
            s = self.wslot()
            wv = self.wview(s, [128, KC, 256])
            self.wload(s, wv[:, :, 0:128], wgu[:, :, j * 128:(j + 1) * 128])
            self.wload(s, wv[:, :, 128:256], wgu[:, :, DFF + j * 128:DFF + (j + 1) * 128])
            for ti, (a, n) in enumerate(tl):
                bg = self.bank()
                for kc in range(KC):
                    self.mm(self.ps[:, bg, :n], wv[:, kc, 0:128], XN[:, kc, a:a + n], kc == 0, kc == KC - 1,
                            (("w", s),), (("ps", bg),))
                bu = self.bank()
                for kc in range(KC):
                    self.mm(self.ps[:, bu, :n], wv[:, kc, 128:256], XN[:, kc, a:a + n], kc == 0, kc == KC - 1,
                            (("w", s),), (("ps", bu),))
                q = self.ring("sq", 2)
                self.act(SQ[:, q, :n], self.ps[:, bg, :n], AF.Silu, (("ps", bg),), (("sq", q),))
                self.tt(ACTB[:, j, a:a + n], SQ[:, q, :n], self.ps[:, bu, :n], ALU.mult,
                        (("sq", q), ("ps", bu)), ())
        self.barrier()
        nt = len(tl)
        ssq = list(range(8 - nt, 8))
        self.rot = list(range(8 - nt))
        wdn = w_dn.rearrange("(kc p) n -> p kc n", p=128)
        for oc in range(KC):
            s = self.wslot()
            wv = self.wview(s, [128, JC, 128])
            self.wload(s, wv[:, 0:22, :], wdn[:, 0:22, oc * 128:(oc + 1) * 128])
            self.wload(s, wv[:, 22:44, :], wdn[:, 22:44, oc * 128:(oc + 1) * 128])
            for ti, (a, n) in enumerate(tl):
                b = self.bank()
                for kc in range(JC):
                    self.mm(self.ps[:, b, :n], wv[:, kc, :], ACTB[:, kc, a:a + n], kc == 0, kc == JC - 1,
                            (("w", s),), (("ps", b),))
                k = self.ring("ft", 2)
                self.act(FT[:, k, :n], self.ps[:, b, :n], AF.Copy, (("ps", b),), (("ft", k),))
                self.dma("act", self.fT[oc, :, t0 + a:t0 + a + n], FT[:, k, :n], ("ft", k), (("ft", k),), ("fscr",))
                q = self.ring("sq", 2)
                self.tt(SQ[:, q, :n], FT[:, k, :n], FT[:, k, :n], ALU.mult, (("ft", k),), (("sq", q),))
                self.mm(self.ps[:, ssq[ti], :n], self.ones_bf, SQ[:, q, :n], oc == 0, oc == KC - 1,
                        (("sq", q), "cb"), (("ps", ssq[ti]),))
        if nxt_ssq:
            self.rot = list(range(8 - nt))
        self.post_residual(self.fT, src, dst, t0, t1, g_post_half, ssq, RSTD, XT, SQ=SQ, nxt_ssq=nxt_ssq)
        self.rot = list(range(8))

    def mix_layout(self):
        o = PH_OFF
        self.BETA = self.v(o, [16, NT]); o += NT * 4
        self.GG = self.v(o, [16, NT]); o += NT * 4
        self.UN = self.v(o, [128, KC, NT], BF16); o += KC * NT * 2
        self.mix_free = o

    def mixer_qkv(self):
        self.mix_layout()
        o = self.mix_free
        RAW = self.v(o, [128, 2, 180]); o += 2 * 180 * 4
        RAWB = self.v(o, [128, 2, 2052], BF16); o += 2 * 2052 * 2
        DG = self.v(o, [128, 2, 4, 128], BF16); o += 2 * 4 * 128 * 2
        CVO = self.v(o, [128, 2, NT]); o += 2 * NT * 4
        QO = self.v(o, [128, 2, NT], BF16); o += 2 * NT * 2
        RSTD = self.v(o, [128, NT]); o += NT * 4
        XT = self.v(o, [128, 4, 512]); o += 8192
        SQ = self.v(o, [128, 2, 512], BF16); o += 2048
        ST = self.v(o, [64, 2, 256]); o += 2048
        CT = self.v(o, [128, 2, 48]); o += 384
        SSQ1 = self.v(o, [128, NT]); o += NT * 4
        SSQ = [RSTD, SSQ1]
        UN = self.UN
        ident = self.cm(CM_ID, 128)
        self.barrier()
        self.rot = list(range(8))
        self.norm_load(self.h1, 0, NT, CV_G["mix_pre"], UN, RSTD, XT, SQ)
        self.barrier()
        win = self.w_in.rearrange("(kc p) n -> p kc n", p=128)
        tl = _tiles(0, NT)
        tlp = [(a, n) for (a, n) in tl if a < NTP]
        allk = lambda nm, cc_: [(nm, cc_, a_) for (a_, _n) in tl]

        def emit_l2(cc_, a_, n_, q_):
            b3 = self.bank()
            self.mm(self.ps[:, b3, :n_], self.ones_bf, SQ[:, q_, :n_], True, True, (("sq", q_), "cb"), (("ps", b3),))
            self.act(SSQ[cc_][:, a_:a_ + n_], self.ps[:, b3, :n_], AF.Copy, (("ps", b3),), (("ssq", cc_, a_),))

        def emit_conv(c_, cc_, a_, n_):
            b2 = self.bank()
            for j in range(4):
                self.mm(self.ps[:, b2, :n_], DG[:, cc_, j, :], RAWB[:, cc_, a_ + j:a_ + j + n_], j == 0, j == 3,
                        (("dg", cc_), ("rawb", cc_, a_), ("rawb", cc_, a_ - 512)), (("ps", b2),))
            self.act(CVO[:, cc_, a_:a_ + n_], self.ps[:, b2, :n_], AF.Silu, (("ps", b2),), (("cvo", cc_, a_),))
            if c_ < 32:
                q_ = self.ring("sq", 2)
                self.act(SQ[:, q_, :n_], CVO[:, cc_, a_:a_ + n_], AF.Square, (("cvo", cc_, a_),), (("sq", q_),))
                return (cc_, a_, n_, q_)
            self.cp(QO[:, cc_, a_:a_ + n_], CVO[:, cc_, a_:a_ + n_], (("cvo", cc_, a_),), (("qo", cc_, a_),))
            return None

        def epilogue(c_, cc_, pend, pend_l2):
            RS = RAW[:, cc_, 0:176].rearrange("p (s j) -> p s j", j=11)
            if pend_l2 is not None:
                emit_l2(*pend_l2)
            if pend is not None:
                p2 = emit_conv(c_, *pend)
                if p2 is not None:
                    emit_l2(*p2)
            CS = CVO[:, cc_, NTP:NT].rearrange("p (s j) -> p s j", j=8)
            for j in range(4):
                wcol = self.cv(CV_SC + c_ * 4 + j)
                if j == 0:
                    self.ts(CS, RS[:, :, 0:8], wcol, None, ALU.mult, None, (("raw", cc_), "cv"), (("cvo", cc_, NTP),))
                else:
                    self.stt(CS, RS[:, :, j:j + 8], wcol, CS, ALU.mult, ALU.add,
                             (("raw", cc_), ("cvo", cc_, NTP), "cv"), (("cvo", cc_, NTP),))
            self.act(CVO[:, cc_, NTP:NT], CVO[:, cc_, NTP:NT], AF.Silu, (("cvo", cc_, NTP),), (("cvo", cc_, NTP),))
            if c_ < 32:
                q = self.ring("sq", 2)
                self.act(SQ[:, q, :NS], CVO[:, cc_, NTP:NT], AF.Square, (("cvo", cc_, NTP),), (("sq", q),))
                emit_l2(cc_, NTP, NS, q)
                keys = allk("ssq", cc_)
                self.rstd(SSQ[cc_][:, 0:NT], SSQ[cc_][:, 0:NT], 1.0, keys, keys)
                if c_ < 16:
                    self.stt(QO[:, cc_, :], CVO[:, cc_, :], float(128 ** -0.5), SSQ[cc_][:, 0:NT], ALU.mult, ALU.mult,
                             keys + allk("cvo", cc_), allk("qo", cc_))
                else:
                    self.tt(QO[:, cc_, :], CVO[:, cc_, :], SSQ[cc_][:, 0:NT], ALU.mult,
                            keys + allk("cvo", cc_), allk("qo", cc_))
            else:
                self.cp(QO[:, cc_, NTP:NT], CVO[:, cc_, NTP:NT], (("cvo", cc_, NTP),), (("qo", cc_, NTP),))
            self.dma("sp", self.qkvF[c_], QO[:, cc_, :], ("qo", cc_), allk("qo", cc_), ())
            b = self.bank()
            self.tr(self.ps[0:3, b, 0:128], RAW[:, cc_, 176:179], ident, (("raw", cc_), "cm"), (("ps", b),))
            kc_ = self.ring("ct", 2)
            self.cp(CT[:, kc_, :].rearrange("p (s j) -> p s j", j=3), RS[:, :, 8:11], (("raw", cc_),), (("ct", kc_),), eng="pool")
            self.tr(self.ps[0:48, b, 128:256], CT[:, kc_, :], ident, (("ct", kc_), "cm"), (("ps", b),))
            k = self.ring("st", 2)
            self.act(ST[0:3, k, 0:128], self.ps[0:3, b, 0:128], AF.Copy, (("ps", b),), (("st", k),))
            self.act(ST[0:48, k, 128:256], self.ps[0:48, b, 128:256], AF.Copy, (("ps", b),), (("st", k),))
            self.dma("sp", self.nq_p[:, c_ * 128:(c_ + 1) * 128], ST[0:3, k, 0:128], ("st", k), (("st", k),), ())
            self.dma("sp", self.nq_s[:, c_ * 128:(c_ + 1) * 128], ST[0:48, k, 128:256], ("st", k), (("st", k),), ())
        deferred = None
        for bi in range(24):
            s = self.wslot()
            wv = self.wview(s, [128, KC, 256])
            self.wload(s, wv, win[:, :, bi * 256:(bi + 1) * 256])
            xs = self.ring("xt", 4)
            self.dma("sp", XT[0:48, xs, 0:256], self.sqkv[:, bi * 256:(bi + 1) * 256], ("xt", xs), (), (("xt", xs),))
            for cc in range(2):
                c = bi * 2 + cc
                RS = RAW[:, cc, 0:176].rearrange("p (s j) -> p s j", j=11)
                self.memset(RAWB[:, cc, 0:3], 0.0, (("rawb", cc, -512),), eng="pool")
                b = self.bank()
                self.tr(self.ps[:, b, 0:48], XT[0:48, xs, cc * 128:(cc + 1) * 128], ident[0:48, 0:48],
                        (("xt", xs), "cm"), (("ps", b),))
                self.act(RS[:, :, 0:3], self.ps[:, b, 0:48].rearrange("p (s j) -> p s j", j=3), AF.Copy,
                         (("ps", b),), (("raw", cc),))
                for j in range(4):
                    self.ts(DG[:, cc, j, :], self.ident_bf, self.cv(CV_SC + c * 4 + j), None, ALU.mult, None,
                            ("cb", "cv"), (("dg", cc),))
                pend = None
                pend_l2 = None
                for ti, (a, n) in enumerate(tl):
                    b = self.bank()
                    for kc in range(KC):
                        self.mm(self.ps[:, b, :n], wv[:, kc, cc * 128:(cc + 1) * 128], UN[:, kc, a:a + n],
                                kc == 0, kc == KC - 1, (("w", s),), (("ps", b),))
                    if a < NTP:
                        self.act(RAWB[:, cc, 3 + a:3 + a + n], self.ps[:, b, :n], AF.Copy, (("ps", b),), (("rawb", cc, a),))
                        if a + n == NTP:
                            self.act(RAW[:, cc, 176:179], self.ps[:, b, n - 3:n], AF.Copy, (("ps", b),), (("raw", cc),))
                    else:
                        self.act(RS[:, :, 3:11], self.ps[:, b, 0:128].rearrange("p (s j) -> p s j", j=8), AF.Copy,
                                 (("ps", b),), (("raw", cc),))
                    if ti == 1 and deferred is not None:
                        epilogue(*deferred)
                        deferred = None
                    if pend_l2 is not None:
                        emit_l2(*pend_l2)
                        pend_l2 = None
                    if pend is not None:
                        pend_l2 = emit_conv(c, *pend)
                    pend = (cc, a, n) if a < NTP else None
                deferred = (c, cc, pend, pend_l2)
        epilogue(*deferred)
        s = self.wslot()
        wv = self.wview(s, [128, KC, 32])
        self.wload(s, wv, win[:, :, O_BETA:O_BETA + 32])
        for (a, n) in tl:
            bb = self.bank()
            for kc in range(KC):
                self.mm(self.ps[0:16, bb, :n], wv[:, kc, 0:16], UN[:, kc, a:a + n], kc == 0, kc == KC - 1,
                        (("w", s),), (("ps", bb),))
            ba = self.bank()
            for kc in range(KC):
                self.mm(self.ps[0:16, ba, :n], wv[:, kc, 16:32], UN[:, kc, a:a + n], kc == 0, kc == KC - 1,
                        (("w", s),), (("ps", ba),))
            self.act(self.BETA[:, a:a + n], self.ps[0:16, bb, :n], AF.Sigmoid, (("ps", bb),), (("bg", a),))
            r = self.ring("xt", 4)
            self.act(XT[0:16, r, :n], self.ps[0:16, ba, :n], AF.Exp, (("ps", ba), "cv"), (("xt", r),),
                     bias=self.cv(CV_DT, 1, 16), scale=1.0)
            self.act(XT[0:16, r, :n], XT[0:16, r, :n], AF.Ln, (("xt", r),), (("xt", r),), bias=1.0, scale=1.0)
            self.ts(self.GG[:, a:a + n], XT[0:16, r, :n], self.cv(CV_NA, 1, 16), None, ALU.mult, None,
                    (("xt", r), "cv3"), (("bg", a),))

    def delta_layout(self):
        o = PH_OFF + 2 * NT * 4
        self.OF = self.v(o, [128, NH, NM], BF16); o += NH * NM * 2
        self.dl_free = o
        L = {}

        def al(name, shape, dt=F32):
            nonlocal o
            L[name] = self.v(o, shape, dt)
            n = 1
            for s in shape[1:]:
                n *= s
            o += ((n * (4 if dt == F32 else 2) + 31) // 32) * 32
        al("S", [128, NH, 128]); al("SB", [128, NH, 128], BF16)
        al("QKVC", [128, 2, 48 * 64], BF16)
        for nm in ("GROW", "NBROW", "GAM", "E1", "W2", "E2", "RM"):
            al(nm, [128, 512])
        self.alias_off = o
        for pb_ in range(2):
            for nm in ("GROW", "NBROW", "GAM", "E1", "W2", "E2", "RM"):
                if pb_ == 0:
                    L["%s_0" % nm] = L[nm]
                else:
                    al("%s_1" % nm, [128, 512])
        for nm in ("P0", "P1", "PT0", "PT1", "RB", "TBT0", "QKM0", "KG0", "QG0", "TBT1", "QKM1", "KG1", "QG1"):
            al(nm, [128, 512], BF16)
        al("KDEC0", [128, 1024], BF16); al("KDEC1", [128, 1024], BF16)
        al("DDB", [128, 1024], BF16); al("UUB", [128, 1024], BF16); al("OTB", [128, 512])
        al("TOK", [128, 32]); al("NBTOK", [128, 16]); al("COLS", [128, 32]); al("DCOL", [128, 16])
        al("TOK_1", [128, 32]); al("NBTOK_1", [128, 16]); al("COLS_1", [128, 32]); al("DCOL_1", [128, 16])
        for nm in ("TOK", "NBTOK", "COLS", "DCOL"):
            L[nm + "_0"] = L[nm]
        for nm in ("P0", "P1", "PT0", "PT1", "RB"):
            L[nm + "_0"] = L[nm]
            al(nm + "_1", [128, 512], BF16)
        for pb_ in range(2):
            al("KK_%d" % pb_, [128, 512], BF16); al("QK_%d" % pb_, [128, 512], BF16)
        al("GLR0", [128, 128]); al("GLR1", [128, 128]); al("RHS2", [128, 128])
        for nm in ("QKM2", "KG2", "QG2"):
            al(nm, [128, 512], BF16)
        al("KDEC2", [128, 1024], BF16); al("GLR2", [128, 128])
        for nm in ("TBT", "QKM", "KG", "QG", "KDEC", "GLR"):
            L[nm] = L[nm + "0"]
        al("DT", [128, 4, 128], BF16); al("DD", [128, 4, 128], BF16); al("UU", [128, 4, 128], BF16)
        al("OT", [128, 4, 128])
        o_save = o
        o = self.alias_off
        al("S2", [128, 16, 128])
        al("UBD", [128, 16, 128], BF16)
        assert o <= o_save
        o = o_save
        L["RHS"] = L["RM_1"]
        self.L = L
        print("delta layout end", o)
        assert o <= ARENA_BYTES, o

    def delta_block(self, C, HB, nseg, tok0, qkvc, need_o, otok0, mk, nlev, sample=False, qk="qkvc"):
        L = self.L
        ones_f = self.cm(CM_ONE, 128)
        ident_f = self.cm(CM_ID, 128)
        W = C * HB

        def t3(name, rows=128, dt=None):
            return L[name][0:rows, 0:W].rearrange("p (h t) -> p h t", h=HB)
        b = self.bank()
        self.tr(self.ps[0:C, b, 0:16], self.BETA[0:16, tok0:tok0 + C], ident_f[0:16, 0:16], ("cm",), (("ps", b),))
        self.tr(self.ps[0:C, b, 16:32], self.GG[0:16, tok0:tok0 + C], ident_f[0:16, 0:16], ("cm",), (("ps", b),))
        TOK = L["TOK"][0:C]
        self.act(TOK, self.ps[0:C, b, 0:32], AF.Copy, (("ps", b),), ("tok",))
        NBTOK = L["NBTOK"][0:C]
        self.ts(NBTOK, TOK[:, 0:16], -1.0, None, ALU.mult, None, ("tok",), ("nbtok",))
        b = self.bank()
        self.mm(self.ps[0:C, b, 0:16], mk["TRI"], TOK[:, 16:32], True, True, ("tok", "cm"), (("ps", b),))
        self.mm(self.ps[0:C, b, 16:32], mk["SAME"], TOK[:, 16:32], True, True, ("tok", "cm"), (("ps", b),))
        COLS = L["COLS"][0:C]
        self.act(COLS, self.ps[0:C, b, 0:32], AF.Copy, (("ps", b),), ("cols",))
        DCOL = L["DCOL"][0:C]
        self.tt(DCOL, COLS[:, 16:32], COLS[:, 0:16], ALU.subtract, ("cols",), ("dcol",))
        self.act(DCOL, DCOL, AF.Exp, ("dcol",), ("dcol",))
        for hb in range(NH // HB):
            h0 = hb * HB
            gt = TOK[:, 16 + h0:16 + h0 + HB]
            RHS = t3("RHS", C)

            def bc_h(m):
                return m.unsqueeze(1).to_broadcast([C, HB, C])

            def bc_t(col, n=C):
                return col.unsqueeze(2).to_broadcast([C, HB, n])
            self.tt(RHS, bc_h(mk["TRI"]), bc_t(gt), ALU.mult, ("tok", "cm"), ("rhs",))
            b = self.bank()
            self.mm(self.ps[:, b, 0:W], ones_f[0:C, :], L["RHS"][0:C, 0:W], True, True, ("rhs", "cm"), (("ps", b),))
            self.act(L["GROW"][:, 0:W], self.ps[:, b, 0:W], AF.Copy, (("ps", b),), ("grow",))
            self.tt(RHS, bc_h(mk["ID"]), bc_t(NBTOK[:, h0:h0 + HB]), ALU.mult, ("nbtok", "cm", "rhs"), ("rhs",))
            b = self.bank()
            self.mm(self.ps[:, b, 0:W], ones_f[0:C, :], L["RHS"][0:C, 0:W], True, True, ("rhs", "cm"), (("ps", b),))
            self.act(L["NBROW"][:, 0:W], self.ps[:, b, 0:W], AF.Copy, (("ps", b),), ("nbrow",))
            R2 = L["RHS2"][0:C, 0:HB * nseg].rearrange("p (h s) -> p h s", h=HB)
            self.tt(R2, mk["SEG"].unsqueeze(1).to_broadcast([C, HB, nseg]), bc_t(gt, nseg), ALU.mult,
                    ("tok", "cm"), ("rhs2",))
            b = self.bank()
            self.mm(self.ps[:, b, 0:HB * nseg], ones_f[0:C, :], L["RHS2"][0:C, 0:HB * nseg], True, True,
                    ("rhs2", "cm"), (("ps", b),))
            GLR = L["GLR"][:, 0:HB * nseg]
            self.act(GLR, self.ps[:, b, 0:HB * nseg], AF.Exp, (("ps", b),), ("glr",))
            GLR3 = GLR.rearrange("p (h s) -> p h s", h=HB)
            self.act(L["GAM"][:, 0:W], L["GROW"][:, 0:W], AF.Exp, ("grow",), ("gam",))
            GAM = t3("GAM")
            kq = qkvc.rearrange("p (c t) -> p c t", c=48)
            self.tt(t3("KG"), kq[:, 16 + h0:16 + h0 + HB, :], GAM, ALU.mult, ("gam", qk), ("kg",))
            if need_o:
                self.tt(t3("QG"), kq[:, h0:h0 + HB, :], GAM, ALU.mult, ("gam", qk), ("qg",), eng="pool")
            GROWc = t3("GROW", C)
            gcol = COLS[:, h0:h0 + HB]
            E1 = t3("E1", C); W2 = t3("W2", C); E2 = t3("E2", C)
            self.tt(E1, GROWc, bc_t(gcol), ALU.subtract, ("grow", "cols"), ("e1",))
            self.ts(E1, E1, 0.0, None, ALU.min, None, ("e1",), ("e1",))
            self.act(E1, E1, AF.Exp, ("e1",), ("e1",))
            self.tt(W2, E1, bc_h(mk["US"]), ALU.mult, ("e1", "cm"), ("w2",))
            self.tt(W2, W2, t3("NBROW", C), ALU.mult, ("w2", "nbrow"), ("w2",))
            self.tt(E1, E1, bc_h(mk["UI"]), ALU.mult, ("e1", "cm"), ("e1",))
            self.tt(E2, GROWc, bc_t(gcol), ALU.subtract, ("grow", "cols"), ("e2",))
            self.ts(E2, E2, -1.0, 0.0, ALU.mult, ALU.min, ("e2",), ("e2",))
            self.act(E2, E2, AF.Exp, ("e2",), ("e2",))
            self.tt(E2, E2, bc_h(mk["LS"]), ALU.mult, ("e2", "cm"), ("e2",))
            self.tt(E2, E2, bc_t(NBTOK[:, h0:h0 + HB]), ALU.mult, ("e2", "nbtok"), ("e2",))
            bk = self.bank()
            bq = self.bank()
            bt = self.bank()
            psT = self.ps[:, bt, :].bitcast(BF16)
            for j in range(HB):
                kf = kq[:, 16 + h0 + j, :]
                qf = kq[:, h0 + j, :]
                self.mm(self.ps[0:C, bk, j * C:(j + 1) * C], kf, kf, True, True, (qk,), (("ps", bk),))
                if need_o:
                    self.mm(self.ps[0:C, bq, j * C:(j + 1) * C], kf, qf, True, True, (qk,), (("ps", bq),))
                self.tr(psT[0:C, j * 128:(j + 1) * 128], kf, self.ident_bf, (qk, "cb"), (("ps", bt),))
            pk = self.ps[0:C, bk, 0:W]
            RM = L["RM"][0:C, 0:W]
            self.tt(RM, pk, L["W2"][0:C, 0:W], ALU.mult, (("ps", bk), "w2"), ("rm",))
            Pc = L["P0"][0:C, 0:W]
            PTc = L["PT0"][0:C, 0:W]
            self.cp(Pc, RM, ("rm",), ("p0",), eng="pool")
            self.tt(PTc, pk, L["E2"][0:C, 0:W], ALU.mult, (("ps", bk), "e2"), ("pt0",))
            self.tt(t3("RM", C), t3("RM", C), bc_h(mk["ID"]), ALU.add, ("rm", "cm"), ("rm",))
            RB = L["RB"][0:C, 0:W]
            self.cp(RB, RM, ("rm",), ("rb",), eng="pool")
            if need_o:
                self.tt(L["QKM"][0:C, 0:W], self.ps[0:C, bq, 0:W], L["E1"][0:C, 0:W], ALU.mult,
                        (("ps", bq), "e1"), ("qkm",))
            KD = L["KDEC"][0:C, 0:HB * 128].rearrange("p (h d) -> p h d", h=HB)
            self.tt(KD, psT[0:C, 0:HB * 128].rearrange("p (h d) -> p h d", h=HB),
                    DCOL[:, h0:h0 + HB].unsqueeze(2).to_broadcast([C, HB, 128]), ALU.mult,
                    (("ps", bt), "dcol"), ("kdec",))
            cur = 0
            for lv in range(nlev):
                Pn = L["P%d" % (1 - cur)][0:C, 0:W]
                PTn = L["PT%d" % (1 - cur)][0:C, 0:W]
                pkey, ptkey = "p%d" % cur, "pt%d" % cur
                pnkey, ptnkey = "p%d" % (1 - cur), "pt%d" % (1 - cur)
                ba = self.bank()
                bb = self.bank()
                for j in range(HB):
                    sl = slice(j * C, (j + 1) * C)
                    if lv < nlev - 1:
                        self.mm(self.ps[0:C, ba, sl], PTc[:, sl], Pc[:, sl], True, True, (pkey, ptkey), (("ps", ba),))
                    self.mm(self.ps[0:C, bb, sl], Pc[:, sl], PTc[:, sl], True, True, (pkey, ptkey), (("ps", bb),))
                if lv < nlev - 1:
                    self.act(Pn, self.ps[0:C, ba, 0:W], AF.Copy, (("ps", ba),), (pnkey,))
                self.cp(PTn, self.ps[0:C, bb, 0:W], (("ps", bb),), (ptnkey,))
                bc = self.bank()
                for j in range(HB):
                    sl = slice(j * C, (j + 1) * C)
                    self.mm(self.ps[0:C, bc, sl], PTn[:, sl], RB[:, sl], True, True, (ptnkey, "rb"), (("ps", bc),))
                self.tt(RM, RM, self.ps[0:C, bc, 0:W], ALU.add, ("rm", ("ps", bc)), ("rm",))
                if lv < nlev - 1:
                    self.cp(RB, RM, ("rm",), ("rb",), eng="pool")
                Pc, PTc = Pn, PTn
                cur = 1 - cur
            self.tt(t3("TBT", C), t3("RM", C), bc_t(TOK[:, h0:h0 + HB]), ALU.mult, ("rm", "tok"), ("tbt",))
            for j in range(HB):
                h = h0 + j
                sl = slice(j * C, (j + 1) * C)
                KGh = L["KG"][:, sl]
                vf = kq[:, 32 + h, :]
                r = self.ring("chain", 4)
                DT = L["DT"][:, r, 0:C]
                DD = L["DD"][0:C, r, :]
                UU = L["UU"][0:C, r, :]
                OT = L["OT"][:, r, 0:C]
                if sample:
                    sr = h % 2
                    SSv = self.v_ss[sr]
                    SSB = L["SB"][:, 0:16, :]
                    self.dma("sp", SSv, self.sdelta[:, h].rearrange("s d v -> d s v"), ("ss", sr), (), (("ss", sr),))
                    self.cp(SSB, SSv, (("ss", sr),), ("ssb",), eng="pool")
                    bK = self.bank()
                    for sg in range(nseg):
                        self.mm(self.ps[:, bK, sg * LS:(sg + 1) * LS], SSB[:, sg, :], KGh[:, sg * LS:(sg + 1) * LS],
                                True, True, ("ssb", "kg"), (("ps", bK),))
                else:
                    SBh = L["SB"][:, h, :]
                    bK = self.bank()
                    self.mm(self.ps[:, bK, 0:C], SBh, KGh, True, True, (("sb", h), "kg"), (("ps", bK),))
                self.tt(DT, vf, self.ps[:, bK, 0:C], ALU.subtract, (qk, ("ps", bK)), (("dt", r),))
                bT = self.bank()
                pT = self.ps[:, bT, :].bitcast(BF16)
                self.tr(pT[0:C, 0:128], DT, self.ident_bf, (("dt", r), "cb"), (("ps", bT),))
                self.act(DD, pT[0:C, 0:128], AF.Copy, (("ps", bT),), (("dd", r),))
                bU = self.bank()
                self.mm(self.ps[0:C, bU, 0:128], L["TBT"][0:C, sl], DD, True, True, ("tbt", ("dd", r)), (("ps", bU),))
                self.act(UU, self.ps[0:C, bU, 0:128], AF.Copy, (("ps", bU),), (("uu", r),))
                if need_o:
                    bO = self.bank()
                    if sample:
                        for sg in range(nseg):
                            self.mm(self.ps[:, bO, sg * LS:(sg + 1) * LS], SSB[:, sg, :],
                                    L["QG"][:, j * C + sg * LS:j * C + (sg + 1) * LS], True, True,
                                    ("ssb", "qg"), (("ps", bO),))
                    else:
                        self.mm(self.ps[:, bO, 0:C], SBh, L["QG"][:, sl], True, True, (("sb", h), "qg"), (("ps", bO),))
                    self.mm(self.ps[:, bO, 128:128 + C], UU, L["QKM"][0:C, sl], True, True,
                            (("uu", r), "qkm"), (("ps", bO),))
                    self.act(OT, self.ps[:, bO, 0:C], AF.Copy, (("ps", bO),), (("ot", r),))
                    self.tt(self.OF[:, h, otok0:otok0 + C], OT, self.ps[:, bO, 128:128 + C], ALU.add,
                            (("ot", r), ("ps", bO)), ())
                KDh = L["KDEC"][0:C, j * 128:(j + 1) * 128]
                if sample:
                    UBD = L["UBD"]
                    self.tt(UBD, UU.unsqueeze(1).to_broadcast([128, 16, 128]),
                            mk["SEG"].unsqueeze(2).to_broadcast([128, 16, 128]), ALU.mult,
                            (("uu", r), "cm"), ("ubd",))
                    gl = GLR3[:, j, :]
                    self.tt(SSv, SSv, gl.unsqueeze(2).to_broadcast([128, 16, 128]), ALU.mult,
                            (("ss", sr), "glr"), (("ss", sr),))
                    for q4 in range(4):
                        bS = self.bank()
                        self.mm(self.ps[:, bS, :], KDh, UBD[:, 4 * q4:4 * q4 + 4, :].rearrange("p a b -> p (a b)"),
                                True, True, ("kdec", "ubd"), (("ps", bS),))
                        sv = SSv[:, 4 * q4:4 * q4 + 4, :].rearrange("p a b -> p (a b)")
                        self.tt(sv, sv, self.ps[:, bS, :], ALU.add, (("ss", sr), ("ps", bS)), (("ss", sr),))
                    self.dma("sp", self.nd_s[:, h].rearrange("s d v -> d s v"), SSv, ("ss", sr), (("ss", sr),), ())
                else:
                    bS = self.bank()
                    self.mm(self.ps[:, bS, 0:128], KDh, UU, True, True, ("kdec", ("uu", r)), (("ps", bS),))
                    Sh = L["S"][:, h, :]
                    self.stt(Sh, Sh, GLR[:, j:j + 1], self.ps[:, bS, 0:128], ALU.mult, ALU.add,
                             (("s", h), "glr", ("ps", bS)), (("s", h),))
                    self.cp(SBh, Sh, (("s", h),), (("sb", h),), eng="pool")

    def p_prep(self, c, hb, pb, ob, qkvc, qk, need_o, mk):
        C, HB, W, nlev = 64, 8, 512, 5
        L = self.L
        ones_f = self.cm(CM_ONE, 128)
        ident_f = self.cm(CM_ID, 128)
        tok0 = c * 64
        cp_ = c % 2

        def T(name):
            return L["%s_%d" % (name, pb)]

        def K(name):
            return (name, pb)

        def t3(name, rows=128):
            return T(name)[0:rows, 0:W].rearrange("p (h t) -> p h t", h=HB)

        def bc_h(m):
            return m.unsqueeze(1).to_broadcast([C, HB, C])

        def bc_t(col, n=C):
            return col.unsqueeze(2).to_broadcast([C, HB, n])
        TOK = L["TOK_%d" % cp_][0:C]
        NBTOK = L["NBTOK_%d" % cp_][0:C]
        COLS = L["COLS_%d" % cp_][0:C]
        DCOL = L["DCOL_%d" % cp_][0:C]
        kt, kn, kc_, kd = ("tok", cp_), ("nbtok", cp_), ("cols", cp_), ("dcol", cp_)
        if hb == 0:
            b = self.bank()
            self.tr(self.ps[0:C, b, 0:16], self.BETA[0:16, tok0:tok0 + C], ident_f[0:16, 0:16], ("cm",), (("ps", b),))
            self.tr(self.ps[0:C, b, 16:32], self.GG[0:16, tok0:tok0 + C], ident_f[0:16, 0:16], ("cm",), (("ps", b),))
            self.act(TOK, self.ps[0:C, b, 0:32], AF.Copy, (("ps", b),), (kt,))
            self.ts(NBTOK, TOK[:, 0:16], -1.0, None, ALU.mult, None, (kt,), (kn,), eng="pool")
            b = self.bank()
            self.mm(self.ps[0:C, b, 0:16], mk["TRI"], TOK[:, 16:32], True, True, (kt, "cm"), (("ps", b),))
            self.mm(self.ps[0:C, b, 16:32], mk["SAME"], TOK[:, 16:32], True, True, (kt, "cm"), (("ps", b),))
            self.act(COLS, self.ps[0:C, b, 0:32], AF.Copy, (("ps", b),), (kc_,))
            self.tt(DCOL, COLS[:, 16:32], COLS[:, 0:16], ALU.subtract, (kc_,), (kd,), eng="pool")
            self.act(DCOL, DCOL, AF.Exp, (kd,), (kd,))
            yield
        h0 = hb * HB
        gt = TOK[:, 16 + h0:16 + h0 + HB]
        kq = qkvc.rearrange("p (c t) -> p c t", c=48)
        bk = self.bank()
        bq = self.bank() if need_o else None
        bt = self.bank()
        psT = self.ps[:, bt, :].bitcast(BF16)
        for j in range(HB):
            kf = kq[:, 16 + h0 + j, :]
            qf = kq[:, h0 + j, :]
            self.mm(self.ps[0:C, bk, j * C:(j + 1) * C], kf, kf, True, True, (qk,), (("ps", bk),))
            if need_o:
                self.mm(self.ps[0:C, bq, j * C:(j + 1) * C], kf, qf, True, True, (qk,), (("ps", bq),))
            self.tr(psT[0:C, j * 128:(j + 1) * 128], kf, self.ident_bf, (qk, "cb"), (("ps", bt),))
        KK = T("KK")[0:C, 0:W]
        self.act(KK, self.ps[0:C, bk, 0:W], AF.Copy, (("ps", bk),), (K("kk"),))
        if need_o:
            QK = T("QK")[0:C, 0:W]
            self.act(QK, self.ps[0:C, bq, 0:W], AF.Copy, (("ps", bq),), (K("qk"),))
        RHS = t3("W2", C)
        self.tt(RHS, bc_h(mk["TRI"]), bc_t(gt), ALU.mult, (kt, "cm", K("w2")), (K("w2"),))
        b1 = self.bank()
        self.mm(self.ps[:, b1, 0:W], ones_f[0:C, :], T("W2")[0:C, 0:W], True, True, (K("w2"), "cm"), (("ps", b1),))
        self.act(T("GROW")[:, 0:W], self.ps[:, b1, 0:W], AF.Copy, (("ps", b1),), (K("grow"),))
        RHSb = t3("E2", C)
        self.tt(RHSb, bc_h(mk["ID"]), bc_t(NBTOK[:, h0:h0 + HB]), ALU.mult, (kn, "cm", K("e2")), (K("e2"),), eng="pool")
        b2 = self.bank()
        self.mm(self.ps[:, b2, 0:W], ones_f[0:C, :], T("E2")[0:C, 0:W], True, True, (K("e2"), "cm"), (("ps", b2),))
        MB = t3("NBROW", C)
        self.tt(MB, self.ps[0:C, b2, 0:W].rearrange("p (h t) -> p h t", h=HB), bc_h(mk["US"]), ALU.mult,
                (("ps", b2), "cm"), (K("mb"),))
        b3 = self.bank()
        self.mm(self.ps[:, b3, 0:HB], ones_f[0:C, :], gt, True, True, (kt, "cm"), (("ps", b3),))
        GLR = L["GLR%d" % ob][:, 0:HB]
        self.act(GLR, self.ps[:, b3, 0:HB], AF.Exp, (("ps", b3),), (("glr", ob),))
        KD = L["KDEC%d" % ob][0:C, 0:HB * 128].rearrange("p (h d) -> p h d", h=HB)
        self.tt(KD, psT[0:C, 0:HB * 128].rearrange("p (h d) -> p h d", h=HB),
                DCOL[:, h0:h0 + HB].unsqueeze(2).to_broadcast([C, HB, 128]), ALU.mult,
                (("ps", bt), kd), (("kdec", ob),))
        yield
        GROWc = t3("GROW", C)
        gcol = COLS[:, h0:h0 + HB]
        E1 = t3("E1", C); W2 = t3("W2", C); E2 = t3("E2", C)
        self.tt(E1, GROWc, bc_t(gcol), ALU.subtract, (K("grow"), kc_), (K("e1"),))
        self.tt(E2, bc_t(gcol), GROWc, ALU.subtract, (K("grow"), kc_, K("e2")), (K("e2"),), eng="pool")
        self.act(T("GAM")[:, 0:W], T("GROW")[:, 0:W], AF.Exp, (K("grow"),), (K("gam"),))
        self.ts(E1, E1, 0.0, None, ALU.min, None, (K("e1"),), (K("e1"),))
        self.ts(E2, E2, 0.0, None, ALU.min, None, (K("e2"),), (K("e2"),), eng="pool")
        yield
        self.act(E1, E1, AF.Exp, (K("e1"),), (K("e1"),))
        self.act(E2, E2, AF.Exp, (K("e2"),), (K("e2"),))
        GAM = t3("GAM")
        self.tt(L["KG%d" % ob][:, 0:W].rearrange("p (h t) -> p h t", h=HB),
                kq[:, 16 + h0:16 + h0 + HB, :], GAM, ALU.mult, (K("gam"), qk), (("kg", ob),))
        if need_o:
            self.tt(L["QG%d" % ob][:, 0:W].rearrange("p (h t) -> p h t", h=HB), kq[:, h0:h0 + HB, :], GAM, ALU.mult,
                    (K("gam"), qk), (("qg", ob),), eng="pool")
        yield
        self.stt(W2, E1, 1.0, MB, ALU.min, ALU.mult, (K("e1"), K("mb")), (K("w2"),))
        self.stt(E2, E2, 1.0, bc_h(mk["LS"]), ALU.min, ALU.mult, (K("e2"), "cm"), (K("e2"),))
        yield
        RM = T("RM")[0:C, 0:W]
        Pc = T("P0")[0:C, 0:W]
        PTc = T("PT0")[0:C, 0:W]
        self.tt(Pc, KK, T("W2")[0:C, 0:W], ALU.mult, (K("kk"), K("w2")), (K("p0"),), eng="pool")
        self.tt(t3("E2", C), t3("E2", C), bc_t(NBTOK[:, h0:h0 + HB]), ALU.mult, (K("e2"), kn), (K("e2"),), eng="pool")
        self.tt(RM, KK, T("W2")[0:C, 0:W], ALU.mult, (K("kk"), K("w2")), (K("rm"),))
        yield
        self.tt(PTc, KK, T("E2")[0:C, 0:W], ALU.mult, (K("kk"), K("e2")), (K("pt0"),))
        self.tt(t3("RM", C), t3("RM", C), bc_h(mk["ID"]), ALU.add, (K("rm"), "cm"), (K("rm"),), eng="pool")
        RB = T("RB")[0:C, 0:W]
        if need_o:
            self.stt(E1, E1, 1.0, bc_h(mk["UI"]), ALU.min, ALU.mult, (K("e1"), "cm"), (K("e1"),))
            self.tt(L["QKM%d" % ob][0:C, 0:W], QK, T("E1")[0:C, 0:W], ALU.mult, (K("qk"), K("e1")), (("qkm", ob),))
        yield "HALF"
        self.act(RB, RM, AF.Copy, (K("rm"),), (K("rb"),))
        cur = 0

        def squares(lv, Pc, PTc, cur):
            Pn = T("P%d" % (1 - cur))[0:C, 0:W]
            PTn = T("PT%d" % (1 - cur))[0:C, 0:W]
            pkey, ptkey = K("p%d" % cur), K("pt%d" % cur)
            ba = self.bank()
            bb = self.bank()
            for j in range(HB):
                sl = slice(j * C, (j + 1) * C)
                if lv < nlev - 1:
                    self.mm(self.ps[0:C, ba, sl], PTc[:, sl], Pc[:, sl], True, True, (pkey, ptkey), (("ps", ba),))
                self.mm(self.ps[0:C, bb, sl], Pc[:, sl], PTc[:, sl], True, True, (pkey, ptkey), (("ps", bb),))
            if lv < nlev - 1:
                self.act(Pn, self.ps[0:C, ba, 0:W], AF.Copy, (("ps", ba),), (K("p%d" % (1 - cur)),))
            self.act(PTn, self.ps[0:C, bb, 0:W], AF.Copy, (("ps", bb),), (K("pt%d" % (1 - cur)),))
            return Pn, PTn
        Pn, PTn = squares(0, Pc, PTc, cur)
        for lv in range(nlev):
            ptnkey = K("pt%d" % (1 - cur))
            Pc, PTc = Pn, PTn
            cur = 1 - cur
            yield
            bc = self.bank()
            for j in range(HB):
                sl = slice(j * C, (j + 1) * C)
                self.mm(self.ps[0:C, bc, sl], PTc[:, sl], RB[:, sl], True, True, (ptnkey, K("rb")), (("ps", bc),))
            if lv + 1 < nlev:
                Pn, PTn = squares(lv + 1, Pc, PTc, cur)
            self.tt(RM, RM, self.ps[0:C, bc, 0:W], ALU.add, (K("rm"), ("ps", bc)), (K("rm"),))
            if lv < nlev - 1:
                yield
                self.act(RB, RM, AF.Copy, (K("rm"),), (K("rb"),))
        yield
        self.tt(L["TBT%d" % pb][0:C, 0:W].rearrange("p (h t) -> p h t", h=HB), t3("RM", C),
                bc_t(TOK[:, h0:h0 + HB]), ALU.mult, (K("rm"), kt), (("tbt", pb),))

    def p_chain(self, c, hb, pb, ob, qkvc, qk, need_o):
        C, HB, W = 64, 8, 512
        L = self.L
        h0 = hb * HB
        kq = qkvc.rearrange("p (c t) -> p c t", c=48)
        SBk = [("sb", h0 + j) for j in range(HB)]
        Sk = [("s", h0 + j) for j in range(HB)]
        KG = L["KG%d" % ob]
        bK = self.bank()
        for j in range(HB):
            self.mm(self.ps[:, bK, j * C:(j + 1) * C], L["SB"][:, h0 + j, :], KG[:, j * C:(j + 1) * C], True, True,
                    (("sb", h0 + j), ("kg", ob)), (("ps", bK),))
        if need_o:
            bO1 = self.bank()
            for j in range(HB):
                self.mm(self.ps[:, bO1, j * C:(j + 1) * C], L["SB"][:, h0 + j, :], L["QG%d" % ob][:, j * C:(j + 1) * C],
                        True, True, (("sb", h0 + j), ("qg", ob)), (("ps", bO1),))
        DT = L["DT"].rearrange("p a b -> p (a b)")
        self.tt(DT.rearrange("p (h t) -> p h t", h=HB), kq[:, 32 + h0:32 + h0 + HB, :],
                self.ps[:, bK, 0:W].rearrange("p (h t) -> p h t", h=HB), ALU.subtract, (qk, ("ps", bK)), ("dt",))
        if need_o:
            OT = L["OTB"]
            self.act(OT, self.ps[:, bO1, 0:W], AF.Copy, (("ps", bO1),), ("ot",))
        yield
        bT = self.bank()
        pT = self.ps[:, bT, :].bitcast(BF16)
        for j in range(HB):
            self.tr(pT[0:C, j * 128:(j + 1) * 128], DT[:, j * C:(j + 1) * C], self.ident_bf, ("dt", "cb"), (("ps", bT),))
        DD = L["DDB"][0:C, :]
        self.act(DD, pT[0:C, 0:1024], AF.Copy, (("ps", bT),), ("dd",))
        yield
        TBT = L["TBT%d" % pb]
        bU = [self.bank(), self.bank()]
        for j in range(HB):
            self.mm(self.ps[0:C, bU[j // 4], (j % 4) * 128:(j % 4 + 1) * 128], TBT[0:C, j * C:(j + 1) * C],
                    DD[:, j * 128:(j + 1) * 128], True, True, (("tbt", pb), "dd"), (("ps", bU[j // 4]),))
        UU = L["UUB"][0:C, :]
        self.act(UU[:, 0:512], self.ps[0:C, bU[0], :], AF.Copy, (("ps", bU[0]),), ("uu0",))
        self.cp(UU[:, 512:1024], self.ps[0:C, bU[1], :], (("ps", bU[1]),), ("uu1",))
        yield
        KD = L["KDEC%d" % ob]
        bS = [self.bank(), self.bank()]
        for j in range(HB):
            self.mm(self.ps[:, bS[j // 4], (j % 4) * 128:(j % 4 + 1) * 128], KD[0:C, j * 128:(j + 1) * 128],
                    UU[:, j * 128:(j + 1) * 128], True, True, (("kdec", ob), "uu%d" % (j // 4)), (("ps", bS[j // 4]),))
        if need_o:
            bO2 = self.bank()
            for j in range(HB):
                self.mm(self.ps[:, bO2, j * C:(j + 1) * C], UU[:, j * 128:(j + 1) * 128],
                        L["QKM%d" % ob][0:C, j * C:(j + 1) * C], True, True,
                        ("uu%d" % (j // 4), ("qkm", ob)), (("ps", bO2),))
        S8 = L["S"][:, h0:h0 + HB, :]
        GLR = L["GLR%d" % ob][:, 0:HB]
        self.tt(S8, S8, GLR.unsqueeze(2).to_broadcast([128, HB, 128]), ALU.mult, Sk + [("glr", ob)], Sk, eng="pool")
        for q in range(2):
            s4 = L["S"][:, h0 + 4 * q:h0 + 4 * q + 4, :].rearrange("p a b -> p (a b)")
            self.tt(s4, s4, self.ps[:, bS[q], :], ALU.add, Sk[4 * q:4 * q + 4] + [("ps", bS[q])], Sk[4 * q:4 * q + 4])
        self.cp(L["SB"][:, h0:h0 + HB, :], S8, Sk, SBk, eng="pool")
        if need_o:
            otok0 = c * 64 - M0
            self.tt(self.OF[:, h0:h0 + HB, otok0:otok0 + C], OT.rearrange("p (h t) -> p h t", h=HB),
                    self.ps[:, bO2, 0:W].rearrange("p (h t) -> p h t", h=HB), ALU.add, ("ot", ("ps", bO2)), ())
        yield

    def delta_prompt_pipelined(self, mk):
        L = self.L
        self.rot = list(range(8))
        units = [(c, hb) for c in range(NTP // 64) for hb in range(2)]
        qslot = {}

        def step(g):
            try:
                return next(g) or True
            except StopIteration:
                return False
        preps = {}
        info = {}

        def start_prep(i):
            c, hb = units[i]
            if hb == 0:
                s = self.ring("qkvc", 2)
                qslot[c] = s
                self.dma("sp", L["QKVC"][:, s, :].rearrange("p (c t) -> p c t", c=48),
                         self.qkvF[:, :, c * 64:(c + 1) * 64].rearrange("c p t -> p c t"), ("qkvc", s), (), (("qkvc", s),))
            s = qslot[c]
            qc = L["QKVC"][:, s, :]
            need_o = c * 64 >= M0
            info[i] = (c, hb, i % 2, i % 3, qc, ("qkvc", s), need_o)
            preps[i] = self.p_prep(c, hb, i % 2, i % 3, qc, ("qkvc", s), need_o, mk)
        n = len(units)
        start_prep(0)
        while step(preps[0]) != "HALF":
            pass
        chain = None
        for i in range(n):
            if i + 1 < n:
                start_prep(i + 1)
            a_live = i + 1 < n
            b_live = True
            c_live = chain is not None
            while a_live or b_live or c_live:
                if b_live:
                    b_live = bool(step(preps[i]))
                if a_live:
                    if step(preps[i + 1]) == "HALF":
                        a_live = False
                if c_live:
                    c_live = bool(step(chain))
            c, hb, pb, ob, qc, qk, need_o = info[i]
            chain = self.p_chain(c, hb, pb, ob, qc, qk, need_o)
        while step(chain):
            pass

    def delta(self):
        self.delta_layout()
        L = self.L
        self.barrier()
        self.rot = list(range(8))
        self.memset(L["S"], 0.0, [("s", h) for h in range(NH)])
        self.memset(L["SB"], 0.0, [("sb", h) for h in range(NH)], eng="pool")
        mk = dict(TRI=self.cm(CM_TRI, 64, 64), UI=self.cm(CM_UI, 64, 64), US=self.cm(CM_US, 64, 64),
                  LS=self.cm(CM_LS, 64, 64), SAME=self.cm(CM_ONE, 64, 64), ID=self.cm(CM_ID, 64, 64),
                  SEG=self.cm(CM_ONE, 1, 64))
        self.delta_prompt_pipelined(mk)
        self.dma("sp", self.nd_p.rearrange("h d v -> d h v"), L["S"], ("sfin",), [("s", h) for h in range(NH)], ())
        self.barrier()
        self.rot = list(range(8))
        mk8 = dict(TRI=self.cm(CM_TRI8, 128), UI=self.cm(CM_UI8, 128), US=self.cm(CM_US8, 128),
                   LS=self.cm(CM_LS8, 128), SAME=self.cm(CM_SAME8, 128), ID=self.cm(CM_ID, 128),
                   SEG=self.cm(CM_SEG, 16))
        self.v_ss = [L["S"][:, 0:16, :], L["S2"]]
        qc = L["QKVC"].rearrange("p a b -> p (a b)")
        self.dma("sp", qc.rearrange("p (c t) -> p c t", c=48),
                 self.qkvF[:, :, NTP:NT].rearrange("c p t -> p c t"), ("qkvc", 0), (), (("qkvc", 0),))
        self.delta_block(128, 4, 16, NTP, qc, True, NMAIN, mk8, 2, sample=True, qk=("qkvc", 0))

    def mixer_tail(self):
        NU = NM + 32
        U0 = M0 - 32
        o = PH_OFF
        RSTD = self.v(o, [128, NU]); o += NU * 4
        SQ = self.v(o, [128, 2, 512], BF16); o += 2048
        MU = self.v(o, [128, NM]); o += NM * 4
        RS = self.v(o, [128, NM]); o += NM * 4
        assert o <= PH_OFF + 2 * NT * 4
        o = PH_OFF + 2 * NT * 4
        OF = self.OF; o += NH * NM * 2
        UN2 = self.v(o, [128, KC, NU], BF16); o += KC * NU * 2
        CB = self.v(o, [128, KC, NM], BF16); o += KC * NM * 2
        XT = self.v(o, [128, 4, 512]); o += 8192
        GPB = self.v(o, [128, 1056], BF16); o += 1056 * 2
        GP30 = self.v(o, [128, 32]); o += 128
        GSX = self.v(o, [128, 608]); o += 608 * 4
        DG31 = self.v(o, [128, 31, 128], BF16); o += 31 * 128 * 2
        CO = self.v(o, [128, NM]); o += NM * 4
        assert o <= ARENA_BYTES, o
        ident = self.cm(CM_ID, 128)
        win = self.w_in.rearrange("(kc p) n -> p kc n", p=128)
        self.barrier()
        self.rot = list(range(8))
        self.norm_load(self.h1, U0, NT, CV_G["mix_pre"], UN2, RSTD, XT, SQ)
        self.barrier()
        tlm = _tiles(0, NM)
        for zb in range(8):
            s = self.wslot()
            wv = self.wview(s, [128, KC, 256])
            self.wload(s, wv, win[:, :, O_Z + zb * 256:O_Z + (zb + 1) * 256])
            for cc in range(2):
                h = zb * 2 + cc
                SS = MU if cc == 0 else RS
                zbanks = []
                for (a, n) in tlm:
                    ok = ("of", h, a)
                    q = self.ring("sq", 2)
                    self.act(SQ[:, q, :n], OF[:, h, a:a + n], AF.Square, (ok,), (("sq", q),))
                    b1 = self.bank()
                    self.mm(self.ps[:, b1, :n], self.ones_bf, SQ[:, q, :n], True, True, (("sq", q), "cb"), (("ps", b1),))
                    b = self.bank()
                    for kc in range(KC):
                        self.mm(self.ps[:, b, :n], wv[:, kc, cc * 128:(cc + 1) * 128], UN2[:, kc, 32 + a:32 + a + n],
                                kc == 0, kc == KC - 1, (("w", s),), (("ps", b),))
                    zbanks.append(b)
                    self.act(SS[:, a:a + n], self.ps[:, b1, :n], AF.Copy, (("ps", b1),), (("ss", cc, a),))
                    r2 = self.ring("xt", 4)
                    self.act(XT[:, r2, :n], self.ps[:, b, :n], AF.Silu, (("ps", b),), (("xt", r2),))
                    self.tt(OF[:, h, a:a + n], OF[:, h, a:a + n], XT[:, r2, :n], ALU.mult, (ok, ("xt", r2)), (ok,))
                sk = [("ss", cc, a) for (a, n) in tlm]
                self.rstd(SS[:, 0:NM], SS[:, 0:NM], 1.0 / 128, sk, sk)
                self.stt(OF[:, h, :], OF[:, h, :], self.cv(CV_ON), SS[:, 0:NM], ALU.mult, ALU.mult,
                         sk + [("of", h, a) for (a, n) in tlm] + ["cv"], [("of", h, a) for (a, n) in tlm])
        self.barrier()
        self.rot = list(range(8))
        SUMB = [(2, 0), (3, 0), (4, 0)]
        SSQB = [(5, 0), (6, 0), (7, 0)]
        GS = GSX.rearrange("p (s j) -> p s j", j=38)
        tlu = _tiles(0, NU)
        for c in range(KC):
            s = self.wslot()
            wv = self.wview(s, [128, KC, 256])
            self.wload(s, wv[:, :, 0:128], win[:, :, O_GLU + c * 128:O_GLU + (c + 1) * 128])
            self.wload(s, wv[:, :, 128:256], win[:, :, O_GLU + 2048 + c * 128:O_GLU + 2048 + (c + 1) * 128])
            for q4 in range(4):
                xs = self.ring("xt", 4)
                self.dma("sp", XT[0:120, xs, 0:128], self.sglu[q4 * 120:(q4 + 1) * 120, c * 128:(c + 1) * 128],
                         ("xt", xs), (), (("xt", xs),))
                b = self.bank()
                self.tr(self.ps[:, b, 0:120], XT[0:120, xs, 0:128], ident[0:120, 0:120], (("xt", xs), "cm"), (("ps", b),))
                self.act(GS[:, 4 * q4:4 * q4 + 4, 0:30], self.ps[:, b, 0:120].rearrange("p (s j) -> p s j", j=30),
                         AF.Copy, (("ps", b),), ("glx",))
            for (a, n) in tlu:
                ba = self.bank()
                for kc in range(KC):
                    self.mm(self.ps[:, ba, :n], wv[:, kc, 0:128], UN2[:, kc, a:a + n], kc == 0, kc == KC - 1,
                            (("w", s),), (("ps", ba),))
                bb = self.bank()
                for kc in range(KC):
                    self.mm(self.ps[:, bb, :n], wv[:, kc, 128:256], UN2[:, kc, a:a + n], kc == 0, kc == KC - 1,
                            (("w", s),), (("ps", bb),))
                r = self.ring("xt", 4)
                self.act(XT[:, r, :n], self.ps[:, bb, :n], AF.Sigmoid, (("ps", bb),), (("xt", r),))
                lo = max(a, 2)
                hi = min(a + n, 1056)
                if hi > lo:
                    self.tt(GPB[:, lo - 2:hi - 2], self.ps[:, ba, lo - a:hi - a], XT[:, r, lo - a:hi - a], ALU.mult,
                            (("ps", ba), ("xt", r)), ("glx",))
                if a <= 1026 < a + n:
                    self.tt(GP30[:, 0:30], self.ps[:, ba, 1026 - a:1056 - a], XT[:, r, 1026 - a:1056 - a], ALU.mult,
                            (("ps", ba), ("xt", r)), ("gp30",))
                if a + n > 1056:
                    o0 = 1056 - a
                    self.tt(GS[:, :, 30:38], self.ps[:, ba, o0:o0 + 128].rearrange("p (s j) -> p s j", j=8),
                            XT[:, r, o0:o0 + 128].rearrange("p (s j) -> p s j", j=8), ALU.mult,
                            (("ps", ba), ("xt", r)), ("glx",))
            COs = CO[:, NMAIN:NM].rearrange("p (s j) -> p s j", j=8)
            bcol = self.cv(CV_G["b_dw"] + c)
            self.tt(DG31, self.ident_bf.unsqueeze(1).to_broadcast([128, 31, 128]),
                    self.cv(CV_DW + c * 31, 31).unsqueeze(2).to_broadcast([128, 31, 128]), ALU.mult,
                    ("cb", "cv"), ("dg31",))
            for t0_ in (0, 512):
                b = self.bank()
                for j in range(31):
                    self.mm(self.ps[:, b, :], DG31[:, j, :], GPB[:, t0_ + j:t0_ + j + 512], j == 0, j == 30,
                            ("dg31", "glx"), (("ps", b),))
                self.act(CO[:, t0_:t0_ + 512], self.ps[:, b, :], AF.Identity, (("ps", b), "cv"), ("co",), bias=bcol, scale=1.0)
            for j in range(31):
                wcol = self.cv(CV_DW + c * 31 + j)
                if j == 0:
                    self.ts(COs, GS[:, :, 0:8], wcol, bcol, ALU.mult, ALU.add, ("glx", "cv"), ("co",))
                else:
                    self.stt(COs, GS[:, :, j:j + 8], wcol, COs, ALU.mult, ALU.add, ("glx", "co", "cv"), ("co",))
            b = self.bank()
            self.tr(self.ps[0:30, b, 0:128], GP30[:, 0:30], ident, ("gp30", "cm"), (("ps", b),))
            k = self.ring("xt", 4)
            self.act(XT[0:30, k, 0:128], self.ps[0:30, b, 0:128], AF.Copy, (("ps", b),), (("xt", k),))
            self.dma("act", self.ng_p[:, c * 128:(c + 1) * 128], XT[0:30, k, 0:128], ("xt", k), (("xt", k),), ())
            for q4 in range(4):
                k = self.ring("xt", 4)
                self.cp(XT[:, k, 0:120].rearrange("p (s j) -> p s j", j=30), GS[:, 4 * q4:4 * q4 + 4, 8:38],
                        ("glx",), (("xt", k),), eng="pool")
                b = self.bank()
                self.tr(self.ps[0:120, b, 0:128], XT[:, k, 0:120], ident, (("xt", k), "cm"), (("ps", b),))
                self.act(XT[0:120, k, 128:256], self.ps[0:120, b, 0:128], AF.Copy, (("ps", b),), (("xt", k),))
                self.dma("act", self.ng_s[q4 * 120:(q4 + 1) * 120, c * 128:(c + 1) * 128], XT[0:120, k, 128:256],
                         ("xt", k), (("xt", k),), ())
            self.dma("sp", self.cT[c], CO, ("co",), ("co",), ())
        self.barrier()
        self.rot = [0, 1]
        for c in range(KC):
            for ti, (a, n) in enumerate(tlm):
                r = self.ring("xt", 4)
                self.dma("sp" if (c + ti) % 2 == 0 else "act", XT[:, r, :n], self.cT[c, :, a:a + n], ("xt", r), (), (("xt", r),))
                COt = XT[:, r, 0:512]
                a0 = a
                a = 0
                q = self.ring("sq", 2)
                self.act(SQ[:, q, :n], COt[:, a:a + n], AF.Square, (("xt", r),), (("sq", q),))
                self.mm(self.ps[:, SSQB[ti][0], SSQB[ti][1]:SSQB[ti][1] + n], self.ones_bf, SQ[:, q, :n], c == 0, c == KC - 1,
                        (("sq", q), "cb"), (("ps", SSQB[ti][0]),))
                q = self.ring("sq", 2)
                self.cp(SQ[:, q, :n], COt[:, a:a + n], (("xt", r),), (("sq", q),))
                a = a0
                self.mm(self.ps[:, SUMB[ti][0], SUMB[ti][1]:SUMB[ti][1] + n], self.ones_bf, SQ[:, q, :n], c == 0, c == KC - 1,
                        (("sq", q), "cb"), (("ps", SUMB[ti][0]),))
        for ti, (a, n) in enumerate(tlm):
            sb_, so_ = SUMB[ti]
            qb_, qo_ = SSQB[ti]
            self.act(MU[:, a:a + n], self.ps[:, sb_, so_:so_ + n], AF.Copy, (("ps", sb_),), (("mu", a),), scale=1.0 / D)
            self.tt(RS[:, a:a + n], MU[:, a:a + n], MU[:, a:a + n], ALU.mult, (("mu", a),), (("rs", a),))
            self.stt(RS[:, a:a + n], self.ps[:, qb_, qo_:qo_ + n], 1.0 / D, RS[:, a:a + n], ALU.mult, ALU.subtract,
                     (("ps", qb_), ("rs", a)), (("rs", a),))
            self.ts(RS[:, a:a + n], RS[:, a:a + n], 0.0, None, ALU.max, None, (("rs", a),), (("rs", a),))
            self.rstd(RS[:, a:a + n], RS[:, a:a + n], 1.0, (("rs", a),), (("rs", a),))
        self.barrier()
        self.rot = list(range(8))
        for c in range(KC):
            for (a, n) in tlm:
                r = self.ring("xt", 4)
                self.dma("sp", XT[:, r, :n], self.cT[c, :, a:a + n], ("xt", r), (), (("xt", r),))
                self.tt(XT[:, r, :n], XT[:, r, :n], MU[:, a:a + n], ALU.subtract, (("xt", r),), (("xt", r),))
                self.tt(XT[:, r, :n], XT[:, r, :n], RS[:, a:a + n], ALU.mult, (("xt", r),), (("xt", r),))
                self.ts(XT[:, r, :n], XT[:, r, :n], self.cv(CV_G["ln_g"] + c), self.cv(CV_G["ln_b"] + c),
                        ALU.mult, ALU.add, (("xt", r), "cv"), (("xt", r),))
                self.act(CB[:, c, a:a + n], XT[:, r, :n], AF.Silu, (("xt", r),), ())
        self.barrier()
        wba = self.w_ba.rearrange("(kc p) n -> p kc n", p=128)
        wbb = self.w_bb.rearrange("(kc p) n -> p kc n", p=128)
        LW = self.v(187392, [128, 2, KC, 256], BF16)
        bring = [0]

        def bslot():
            i = bring[0] % 5
            bring[0] += 1
            if i < 3:
                return self.wview(i, [128, KC, 256]), ("w", i)
            return LW[:, i - 3], ("wl", i - 3)

        def bload(dst, key, src):
            self.dma("pool", dst, src, key, (), (key,))
        for oc in range(KC):
            w1, s1 = bslot()
            bload(w1[:, :, 0:128], s1, wba[:, :, oc * 128:(oc + 1) * 128])
            bload(w1[:, :, 128:256], s1, wbb[:, :, oc * 128:(oc + 1) * 128])
            w2, s2 = bslot()
            bload(w2[:, :, 0:128], s2, win[:, :, O_GATE + oc * 128:O_GATE + (oc + 1) * 128])
            bload(w2[:, :, 128:256], s2, win[:, :, O_GATE + 2048 + oc * 128:O_GATE + 2048 + (oc + 1) * 128])
            for (a, n) in tlm:
                bs = []
                for (wv_, s_, col, src, off) in ((w1, s1, 0, OF, 0), (w1, s1, 128, CB, 0), (w2, s2, 0, UN2, 32), (w2, s2, 128, UN2, 32)):
                    b = self.bank()
                    for kc in range(KC):
                        self.mm(self.ps[:, b, :n], wv_[:, kc, col:col + 128], src[:, kc, off + a:off + a + n],
                                kc == 0, kc == KC - 1, (s_,), (("ps", b),))
                    bs.append(b)
                r1 = self.ring("xt", 4)
                r2 = self.ring("xt", 4)
                self.act(XT[:, r1, :n], self.ps[:, bs[2], :n], AF.Sigmoid, (("ps", bs[2]),), (("xt", r1),))
                self.act(XT[:, r2, :n], self.ps[:, bs[3], :n], AF.Sigmoid, (("ps", bs[3]),), (("xt", r2),))
                self.tt(XT[:, r1, :n], XT[:, r1, :n], self.ps[:, bs[0], :n], ALU.mult, (("xt", r1), ("ps", bs[0])), (("xt", r1),))
                self.tt(XT[:, r2, :n], XT[:, r2, :n], self.ps[:, bs[1], :n], ALU.mult, (("xt", r2), ("ps", bs[1])), (("xt", r2),))
                q = self.ring("sq", 2)
                self.tt(SQ[:, q, :n], XT[:, r1, :n], XT[:, r2, :n], ALU.add, (("xt", r1), ("xt", r2)), (("sq", q),))
                self.dma("sp", self.mgT[oc, :, a:a + n], SQ[:, q, :n], ("sq", q), (("sq", q),), ())

    def proj_post(self, kind):
        o = PH_OFF
        XN = self.v(o, [128, KC, NM], BF16); o += KC * NM * 2
        PB = self.v(o, [128, 2, NM], BF16); o += 2 * NM * 2
        RSTD = self.v(o, [128, NM]); o += NM * 4
        XT = self.v(o, [128, 4, 512]); o += 8192
        SQ = self.v(o, [128, 2, 512], BF16); o += 2048
        FT = self.v(o, [128, 2, 512]); o += 4096
        YST = self.v(o, [128, 2, 512]); o += 4096
        tlm = _tiles(0, NM)
        self.barrier()
        self.rot = list(range(5))
        if kind == "out":
            for kc in range(KC):
                self.dma("sp", XN[:, kc, :], self.mgT[kc], ("xnl", kc % 4), (), ())
            W = self.w_out.rearrange("(kc p) n -> p kc n", p=128)
            nk = KC
        else:
            self.norm_load(self.h3, M0, NT, CV_G["ple_pre"], XN, RSTD, XT, SQ, pre_ssq=[5, 6, 7])
            for kc in range(2):
                self.dma("pool", PB[:, kc, :], self.pT[kc], ("pbl", kc), (), ())
            W = self.w_pg.rearrange("(kc p) n -> p kc n", p=128)
            WP = self.w_pp.rearrange("(kc p) n -> p kc n", p=128)
            nk = KC
        self.barrier()
        ssq = [5, 6, 7]
        self.rot = list(range(5))
        for oc in range(KC):
            s = self.wslot()
            wv = self.wview(s, [128, KC + 2, 128])
            self.wload(s, wv[:, 0:KC, :], W[:, :, oc * 128:(oc + 1) * 128])
            if kind == "ple":
                self.wload(s, wv[:, KC:KC + 2, :], WP[:, :, oc * 128:(oc + 1) * 128])
            for ti, (a, n) in enumerate(tlm):
                b = self.bank()
                for kc in range(nk):
                    self.mm(self.ps[:, b, :n], wv[:, kc, :], XN[:, kc, a:a + n], kc == 0, kc == nk - 1,
                            (("w", s),), (("ps", b),))
                k = self.ring("ft", 2)
                if kind == "out":
                    self.act(FT[:, k, :n], self.ps[:, b, :n], AF.Copy, (("ps", b),), (("ft", k),))
                else:
                    b2 = self.bank()
                    for kc in range(2):
                        self.mm(self.ps[:, b2, :n], wv[:, KC + kc, :], PB[:, kc, a:a + n], kc == 0, kc == 1,
                                (("w", s),), (("ps", b2),))
                    self.act(FT[:, k, :n], self.ps[:, b, :n], AF.Sigmoid, (("ps", b),), (("ft", k),))
                    self.tt(FT[:, k, :n], FT[:, k, :n], self.ps[:, b2, :n], ALU.mult, (("ft", k), ("ps", b2)), (("ft", k),))
                self.dma("sp", self.fT[oc, :, M0 + a:M0 + a + n], FT[:, k, :n], ("ft", k), (("ft", k),), ("fscr",))
                q = self.ring("sq", 2)
                self.tt(SQ[:, q, :n], FT[:, k, :n], FT[:, k, :n], ALU.mult, (("ft", k),), (("sq", q),))
                self.mm(self.ps[:, ssq[ti], :n], self.ones_bf, SQ[:, q, :n], oc == 0, oc == KC - 1,
                        (("sq", q), "cb"), (("ps", ssq[ti]),))
        if kind == "out":
            self.post_residual(self.fT, self.h1, self.h2, M0, NT, CV_G["mix_post"], ssq, RSTD, XT, SQ=SQ, nxt_ssq=True)
        else:
            self.post_residual(self.fT, self.h3, self.h4, M0, NT, CV_G["ple_post"], ssq, RSTD, XT, yout=self.y, YST=YST)
        self.rot = list(range(8))

    def write_y(self):
        self.barrier()
        self.rot = list(range(8))
        HIN = self.v(PH_OFF, [128, 2, 4, 128])
        YT = self.v(PH_OFF + 4096, [128, 2, D])
        ident = self.cm(CM_ID, 128)
        for tb in range(NM // 128):
            yk = self.ring("yt", 2)
            for g in range(4):
                k = self.ring("hin", 2)
                self.dma("sp", HIN[:, k], self.h4[g * 4:(g + 1) * 4, :, M0 + tb * 128:M0 + (tb + 1) * 128]
                         .rearrange("c p t -> p c t"), ("hin", k), (), (("hin", k),))
                b = self.bank()
                for q in range(4):
                    self.tr(self.ps[:, b, q * 128:(q + 1) * 128], HIN[:, k, q, :], ident, (("hin", k), "cm"), (("ps", b),))
                self.act(YT[:, yk, g * 512:(g + 1) * 512], self.ps[:, b, :], AF.Copy, (("ps", b),), (("yt", yk),))
            self.dma("act", self.y[tb * 128:(tb + 1) * 128, :], YT[:, yk, :], ("yt", yk), (("yt", yk),), ())

    def dbg_dump_bg(self):
        self.barrier()
        self.dma("sp", self.dbg_bg[0], self.BETA, ("dbg", 0), (), ())
        self.dma("sp", self.dbg_bg[1], self.GG, ("dbg", 0), (), ())

    def build(self):
        self.load_consts()
        self.transpose_in(self.xin, self.xT, NT, KC)
        if self.stop_after == "xT":
            return self.finish()
        self.ffn(self.xT, self.h1, 0, NPRE, self.w_gu1, self.w_dn1, CV_G["ffn1_pre"], CV_H1)
        self.ffn(self.xT, self.h1, M0, NT, self.w_gu1, self.w_dn1, CV_G["ffn1_pre"], CV_H1)
        if self.stop_after == "ffn1":
            return self.finish()
        self.mixer_qkv()
        if self.stop_after == "qkv":
            if self.debug:
                self.dbg_dump_bg()
            return self.finish()
        self.delta()
        if self.stop_after == "delta":
            if self.debug:
                self.barrier()
                self.dma("sp", self.dbg_of.rearrange("h p t -> p h t"), self.OF, ("dbg", 1), (), ())
            return self.finish()
        self.mixer_tail()
        self.transpose_in(self.pin, self.pT, NM, 2)
        self.proj_post("out")
        if self.stop_after == "mix":
            return self.finish()
        self.ffn(self.h2, self.h3, M0, NT, self.w_gu2, self.w_dn2, CV_G["ffn2_pre"], CV_H2, pre_ssq=[5, 6, 7], nxt_ssq=True)
        self.proj_post("ple")
        return self.finish()

    def finish(self):
        self.pg.emit()
        return self.nc


def _masks():
    m = np.zeros((128, NCM), np.float32)
    i = np.arange(128)
    m[:, CM_ID:CM_ID + 128] = np.eye(128, dtype=np.float32)
    m[:, CM_ONE:CM_ONE + 128] = 1.0
    j = np.arange(64)
    le = (j[:, None] <= j[None, :]).astype(np.float32)
    lt = (j[:, None] < j[None, :]).astype(np.float32)
    m[:64, CM_TRI:CM_TRI + 64] = le
    m[:64, CM_UI:CM_UI + 64] = le
    m[:64, CM_US:CM_US + 64] = lt
    m[:64, CM_LS:CM_LS + 64] = lt.T
    same = ((i[:, None] // LS) == (i[None, :] // LS)).astype(np.float32)
    le8 = (i[:, None] <= i[None, :]).astype(np.float32) * same
    lt8 = (i[:, None] < i[None, :]).astype(np.float32) * same
    m[:, CM_TRI8:CM_TRI8 + 128] = le8
    m[:, CM_UI8:CM_UI8 + 128] = le8
    m[:, CM_US8:CM_US8 + 128] = lt8
    m[:, CM_LS8:CM_LS8 + 128] = lt8.T
    m[:, CM_SAME8:CM_SAME8 + 128] = same
    m[:, CM_SEG:CM_SEG + 16] = (i[:, None] // LS == np.arange(16)[None, :]).astype(np.float32)
    return m


def _cvec(inp):
    c = np.zeros((128, NCV), np.float32)

    def fm(vec):
        return np.ascontiguousarray(np.asarray(vec, np.float32).reshape(-1, 128).T)
    for n, col in CV_G.items():
        key = {"b_dw": "b_dw_conv"}.get(n, n)
        c[:, col:col + 16] = fm(inp[key][0])
    wsc = np.asarray(inp["w_short_conv"][0], np.float32)
    c[:, CV_SC:CV_SC + 192] = wsc.reshape(4, 48, 128).transpose(2, 1, 0).reshape(128, 192)
    wdw = np.asarray(inp["w_dw_conv"][0], np.float32)
    c[:, CV_DW:CV_DW + 496] = wdw.reshape(31, 16, 128).transpose(2, 1, 0).reshape(128, 496)
    c[:, CV_ON] = np.asarray(inp["o_norm"][0], np.float32)
    c[:16, CV_AL] = np.asarray(inp["a_log"][0], np.float32)
    c[:16, CV_DT] = np.asarray(inp["dt_bias"][0], np.float32)
    return c


def make_in_maps(inp, cores=range(8)):
    xp = np.asarray(inp["x_prompt"], np.float32)
    xs = np.asarray(inp["x_sample"], np.float32)
    pp = np.asarray(inp["p_prompt"], np.float32)[0]
    psm = np.asarray(inp["p_sample"], np.float32)[0]
    sd = np.asarray(inp["state_delta"], np.float32)[0]
    sq = np.asarray(inp["state_qkv_conv"], np.float32)[0]
    sg = np.asarray(inp["state_glu_conv"], np.float32)[0]
    shared = {
        "cvec": _cvec(inp), "cmask": _masks(),
        "w_gu1": np.asarray(inp["ffn1_w_gu"][0]), "w_dn1": np.asarray(inp["ffn1_w_down"][0]),
        "w_in": np.asarray(inp["w_in"][0]), "w_ba": np.asarray(inp["w_branch_a"][0]),
        "w_bb": np.asarray(inp["w_branch_b"][0]), "w_out": np.asarray(inp["w_out"][0]),
        "w_gu2": np.asarray(inp["ffn2_w_gu"][0]), "w_dn2": np.asarray(inp["ffn2_w_down"][0]),
        "w_pg": np.asarray(inp["w_ple_gate"][0]), "w_pp": np.asarray(inp["w_ple_proj"][0]),
    }
    maps = []
    for c in cores:
        b, half = c // 2, c % 2
        main = xp[b, half * NMAIN:(half + 1) * NMAIN]
        pre = xp[b, 0:NPRE] if half == 1 else np.zeros((NPRE, D), np.float32)
        sl = slice(c * NSEQ, (c + 1) * NSEQ)
        m = dict(shared)
        m["xin"] = np.concatenate([pre, main, xs[sl].reshape(NS, D)], 0)
        m["pin"] = np.concatenate([pp[b, half * NMAIN:(half + 1) * NMAIN], psm[sl].reshape(NS, PLE)], 0)
        m["sdelta"] = np.ascontiguousarray(sd[sl])
        m["sqkv"] = np.ascontiguousarray(sq[sl].reshape(NSEQ * 3, QKV))
        m["sglu"] = np.ascontiguousarray(sg[sl].reshape(NSEQ * 30, D))
        maps.append(m)
    return maps


def kernel(**inputs):
    nc = Builder().build()
    maps = make_in_maps(inputs)
    res = run_bass_kernel_spmd(nc, maps, core_ids=list(range(8)))
    R = res.results
    yp = np.zeros((4, 2048, D), np.float32)
    ys = np.zeros((128, LS, D), np.float32)
    ndp = np.zeros((1, 4, NH, 128, 128), np.float32)
    nqp = np.zeros((1, 4, 3, QKV), np.float32)
    ngp = np.zeros((1, 4, 30, D), np.float32)
    nds = np.zeros((1, 128, NH, 128, 128), np.float32)
    nqs = np.zeros((1, 128, 3, QKV), np.float32)
    ngs = np.zeros((1, 128, 30, D), np.float32)
    for c in range(8):
        b, half = c // 2, c % 2
        r = R[c]
        yp[b, half * NMAIN:(half + 1) * NMAIN] = r["y"][:NMAIN]
        ys[c * NSEQ:(c + 1) * NSEQ] = r["y"][NMAIN:].reshape(NSEQ, LS, D)
        if half == 1:
            ndp[0, b] = r["nd_p"]
            nqp[0, b] = r["nq_p"]
            ngp[0, b] = r["ng_p"]
        nds[0, c * NSEQ:(c + 1) * NSEQ] = r["nd_s"]
        nqs[0, c * NSEQ:(c + 1) * NSEQ] = r["nq_s"].reshape(NSEQ, 3, QKV)
        ngs[0, c * NSEQ:(c + 1) * NSEQ] = r["ng_s"].reshape(NSEQ, 30, D)
    return (yp, ys, ndp, nqp, ngp, nds, nqs, ngs)
```

```python
import numpy as np
import concourse.bass as bass
import concourse.mybir as mybir
from concourse.bass_utils import run_bass_kernel_spmd

F32 = mybir.dt.float32
BF16 = mybir.dt.bfloat16
AF = mybir.ActivationFunctionType
ALU = mybir.AluOpType

ENGS = ("pe", "act", "dve", "pool", "sp")
EPOCH = 30000


class _Op:
    __slots__ = ("eng", "fn", "reads", "writes", "dma", "deps", "sig", "idx", "n", "bar")

    def __init__(self, eng, fn, reads, writes, dma):
        self.eng = eng
        self.fn = fn
        self.reads = reads
        self.writes = writes
        self.dma = dma
        self.deps = None
        self.sig = False
        self.idx = None
        self.n = 0
        self.bar = False


class Prog:
    def __init__(self, nc):
        self.nc = nc
        self.ops = []
        self.streams = {e: [] for e in ENGS}

    def add(self, eng, fn, reads=(), writes=(), dma=None):
        op = _Op(eng, fn, tuple(reads), tuple(writes), dma)
        op.n = len(self.ops)
        self.ops.append(op)
        self.streams[eng].append(op)
        return op

    def barrier(self, fn):
        op = self.add("sp", fn, dma=("bar",))
        op.bar = True
        return op

    def _analyze(self):
        last_w = {}
        readers = {}
        last_eng = {}
        last_dma = {}
        cur_bar = None
        need_bar = set()
        for op in self.ops:
            deps = {}
            if op.bar:
                for d in last_eng.values():
                    deps[d.n] = d
                for d in last_dma.values():
                    deps[d.n] = d
                last_w = {}
                readers = {}
                cur_bar = op
                need_bar = set(ENGS)
            else:
                if cur_bar is not None and op.eng in need_bar:
                    deps[cur_bar.n] = cur_bar
                    need_bar.discard(op.eng)
                for k in op.reads:
                    w = last_w.get(k)
                    if w is not None:
                        deps[w.n] = w
                for k in op.writes:
                    w = last_w.get(k)
                    if w is not None:
                        deps[w.n] = w
                    for r in readers.get(k, ()):
                        deps[r.n] = r
                for k in op.reads:
                    readers.setdefault(k, []).append(op)
                for k in op.writes:
                    last_w[k] = op
                    readers[k] = []
            if op.dma is not None:
                last_dma[op.dma] = op
            else:
                last_eng[op.eng] = op
            deps.pop(op.n, None)
            dl = []
            for d in deps.values():
                if d.dma is None and op.dma is None and d.eng == "pe" and op.eng == "pe":
                    continue
                dl.append(d)
            op.deps = dl
            for d in dl:
                d.sig = True
        cnt = {e: 0 for e in ENGS}
        dcnt = {}
        for op in self.ops:
            if op.dma is not None:
                dcnt[op.dma] = dcnt.get(op.dma, 0) + 16
                op.idx = ("d", op.dma, dcnt[op.dma])
            elif op.sig:
                c = cnt[op.eng]
                cnt[op.eng] += 1
                op.idx = ("e", (op.eng, c // EPOCH), c % EPOCH + 1)
        self.dma_final = dcnt

    def emit(self):
        nc = self.nc
        self._analyze()
        sems = {}

        def sem(kind, key):
            k = (kind, key)
            if k not in sems:
                sems[k] = nc.alloc_semaphore(name="s%d" % len(sems))
            return sems[k]

        waits = {}
        for e in ENGS:
            seen = {}
            for op in self.streams[e]:
                wl = {}
                for d in op.deps:
                    kind, key, val = d.idx
                    k = (kind, key)
                    if seen.get(k, 0) >= val:
                        continue
                    if wl.get(k, 0) < val:
                        wl[k] = val
                for k, v in wl.items():
                    seen[k] = v
                waits[op.n] = [(sem(*k), v) for k, v in wl.items()]
        final_waits = [(sem("d", k), v) for k, v in self.dma_final.items()]
        self.nsem = len(sems)

        def run_stream(engname, eng):
            for op in self.streams[engname]:
                for s, v in waits[op.n]:
                    eng.wait_ge(s, v)
                ins = op.fn(eng)
                if op.idx is not None:
                    kind, key, val = op.idx
                    ins.then_inc(sem(kind, key), 16 if kind == "d" else 1)
            if engname == "sp":
                for s, v in final_waits:
                    eng.wait_ge(s, v)

        with nc.Block() as block:
            @block.tensor
            def _(e):
                run_stream("pe", e)

            @block.scalar
            def _(e):
                run_stream("act", e)

            @block.vector
            def _(e):
                run_stream("dve", e)

            @block.gpsimd
            def _(e):
                run_stream("pool", e)

            @block.sync
            def _(e):
                run_stream("sp", e)


D = 2048
KC = 16
DFF = 5632
JC = 44
NH = 16
QKV = 6144
O_Z = QKV
O_BETA = O_Z + 2048
O_A = O_BETA + NH
O_GLU = O_A + NH
O_GATE = O_GLU + 4096
IN_DIM = O_GATE + 4096
PLE = 256
EPS = 1e-6
NPRE = 1024
NMAIN = 1024
NTP = NPRE + NMAIN
NSEQ = 16
LS = 8
NS = NSEQ * LS
NT = NTP + NS
NM = NMAIN + NS
M0 = NPRE

ARENA_BYTES = 206000
WR_OFF = 16384
WSLOT = 11264
NWS = 3
PH_OFF = WR_OFF + NWS * WSLOT

CV_G = {n: 16 * i for i, n in enumerate(
    ["ffn1_pre", "ffn1_post", "mix_pre", "mix_post", "ffn2_pre", "ffn2_post", "ple_pre", "ple_post",
     "b_dw", "ln_g", "ln_b"])}
CV_SC = 176
CV_DW = CV_SC + 192
CV_ON = CV_DW + 496
CV_AL = CV_ON + 1
CV_DT = CV_AL + 1
CV_H1 = CV_DT + 1
CV_H2 = CV_H1 + 16
CV_NA = CV_H2 + 16
NCV = CV_NA + 1
CM_ID = 0
CM_ONE = 128
CM_TRI = 256
CM_UI = 320
CM_US = 384
CM_LS = 448
CM_TRI8 = 512
CM_UI8 = 640
CM_US8 = 768
CM_LS8 = 896
CM_SAME8 = 1024
CM_SEG = 1152
NCM = CM_SEG + 16


def _tiles(t0, t1, n=512):
    return [(a, min(n, t1 - a)) for a in range(t0, t1, n)]


class Builder:
    def __init__(self, debug=False, stop_after=None):
        self.debug = debug
        self.stop_after = stop_after
        nc = bass.Bass("TRN2", target_bir_lowering=False)
        self.nc = nc
        self.pg = Prog(nc)
        self.arena = nc.alloc_sbuf_tensor("arena", [128, ARENA_BYTES // 4], F32)
        self.ps = nc.alloc_psum_tensor("ps", [128, 8, 512], F32)
        self.rot = list(range(8))
        self.rot_i = 0
        self.ws_i = 0
        self.ring_i = {}
        self._decl()

    def _in(self, name, shape, dt=F32):
        return self.nc.dram_tensor(name, list(shape), dt, kind="ExternalInput").ap()

    def _out(self, name, shape, dt=F32):
        return self.nc.dram_tensor(name, list(shape), dt, kind="ExternalOutput").ap()

    def _scr(self, name, shape, dt=F32):
        kind = "ExternalOutput" if self.debug else "Internal"
        return self.nc.dram_tensor(name, list(shape), dt, kind=kind).ap()

    def _decl(self):
        self.xin = self._in("xin", [NT, D])
        self.pin = self._in("pin", [NM, PLE])
        self.sdelta = self._in("sdelta", [NSEQ, NH, 128, 128])
        self.sqkv = self._in("sqkv", [NSEQ * 3, QKV])
        self.sglu = self._in("sglu", [NSEQ * 30, D])
        self.cvec = self._in("cvec", [128, NCV])
        self.cmask = self._in("cmask", [128, NCM])
        self.w_gu1 = self._in("w_gu1", [D, 2 * DFF])
        self.w_dn1 = self._in("w_dn1", [DFF, D])
        self.w_in = self._in("w_in", [D, IN_DIM])
        self.w_ba = self._in("w_ba", [D, D])
        self.w_bb = self._in("w_bb", [D, D])
        self.w_out = self._in("w_out", [D, D])
        self.w_gu2 = self._in("w_gu2", [D, 2 * DFF])
        self.w_dn2 = self._in("w_dn2", [DFF, D])
        self.w_pg = self._in("w_pg", [D, D])
        self.w_pp = self._in("w_pp", [PLE, D])
        self.y = self._out("y", [NM, D])
        self.nd_p = self._out("nd_p", [NH, 128, 128])
        self.nq_p = self._out("nq_p", [3, QKV])
        self.ng_p = self._out("ng_p", [30, D])
        self.nd_s = self._out("nd_s", [NSEQ, NH, 128, 128])
        self.nq_s = self._out("nq_s", [NSEQ * 3, QKV])
        self.ng_s = self._out("ng_s", [NSEQ * 30, D])
        self.xT = self._scr("xT", [KC, 128, NT])
        self.h1 = self._scr("h1", [KC, 128, NT])
        self.fT = self._scr("fT", [KC, 128, NT])
        self.qkvF = self._scr("qkvF", [48, 128, NT], BF16)
        self.cT = self._scr("cT", [KC, 128, NM])
        self.mgT = self._scr("mgT", [KC, 128, NM], BF16)
        self.h2 = self._scr("h2", [KC, 128, NT])
        self.h3 = self._scr("h3", [KC, 128, NT])
        self.h4 = self._scr("h4", [KC, 128, NT])
        self.pT = self._scr("pT", [2, 128, NM])
        if self.debug:
            self.dbg_bg = self._scr("dbg_bg", [2, 16, NT])
            self.dbg_of = self._scr("dbg_of", [NH, 128, NM], BF16)
        self.bar_a = self.nc.dram_tensor("bar_a", [1, 16], F32, kind="Internal").ap()
        self.bar_b = self.nc.dram_tensor("bar_b", [1, 16], F32, kind="Internal").ap()

    def v(self, off, shape, dt=F32):
        esz = 4 if dt == F32 else 2
        n = 1
        for s in shape[1:]:
            n *= s
        nb = n * esz
        assert off % 4 == 0 and nb % 4 == 0, (off, nb)
        assert off + nb <= ARENA_BYTES, (off, nb)
        a = self.arena[0:shape[0], off // 4:(off + nb) // 4]
        if dt != F32:
            a = a.bitcast(dt)
        if len(shape) == 3:
            a = a.rearrange("p (a b) -> p a b", a=shape[1])
        elif len(shape) == 4:
            a = a.rearrange("p (a b c) -> p a b c", a=shape[1], b=shape[2])
        return a

    def cv(self, col, n=1, rows=128):
        return self.arena[0:rows, col:col + n]

    def cm(self, col, n, rows=128, dt=F32):
        return self.arena[0:rows, 1024 + col:1024 + col + n]

    def bank(self):
        b = self.rot[self.rot_i % len(self.rot)]
        self.rot_i += 1
        return b

    def ring(self, name, n):
        i = self.ring_i.get(name, 0)
        self.ring_i[name] = i + 1
        return i % n

    def mm(self, out, lhsT, rhs, start, stop, reads, writes):
        return self.pg.add("pe", lambda e: e.matmul(out, lhsT, rhs, start=start, stop=stop), reads, writes)

    def tr(self, out, in_, ident, reads, writes):
        return self.pg.add("pe", lambda e: e.transpose(out, in_, ident), reads, writes)

    def act(self, out, in_, func, reads, writes, bias=None, scale=None):
        kw = {}
        if bias is not None:
            kw["bias"] = bias
        if scale is not None:
            kw["scale"] = scale
        return self.pg.add("act", lambda e: e.activation(out=out, in_=in_, func=func, **kw), reads, writes)

    def tt(self, out, a, b, op, reads, writes, eng="dve"):
        return self.pg.add(eng, lambda e: e.tensor_tensor(out, a, b, op), reads, writes)

    def ts(self, out, a, s1, s2, op0, op1, reads, writes, eng="dve"):
        if op1 is None:
            return self.pg.add(eng, lambda e: e.tensor_single_scalar(out, a, s1, op0), reads, writes)
        return self.pg.add(eng, lambda e: e.tensor_scalar(out, a, s1, s2, op0, op1), reads, writes)

    def stt(self, out, in0, scalar, in1, op0, op1, reads, writes, eng="dve"):
        return self.pg.add(eng, lambda e: e.scalar_tensor_tensor(out, in0, scalar, in1, op0, op1), reads, writes)

    def cp(self, out, in_, reads, writes, eng="dve"):
        return self.pg.add(eng, lambda e: e.tensor_copy(out, in_), reads, writes)

    def rstd(self, out, in_, scale, reads, writes):
        self.act(out, in_, AF.Ln, reads, writes, bias=EPS, scale=scale)
        return self.act(out, out, AF.Exp, writes, writes, scale=-0.5)

    def recip(self, out, in_, reads, writes):
        return self.pg.add("dve", lambda e: e.reciprocal(out, in_), reads, writes)

    def memset(self, ap, val, writes, eng="dve"):
        return self.pg.add(eng, lambda e: e.memset(ap, val), (), writes)

    def dma(self, eng, out, in_, key, reads, writes):
        return self.pg.add(eng, lambda e: e.dma_start(out=out, in_=in_), reads, writes, dma=key)

    def barrier(self):
        a, b = self.bar_a, self.bar_b
        self.bar_a, self.bar_b = b, a
        self.pg.barrier(lambda e: e.dma_start(out=b, in_=a))
        self.ring_i = {}

    def wslot(self):
        s = self.ws_i % NWS
        self.ws_i += 1
        return s

    def wview(self, s, shape):
        return self.v(WR_OFF + s * WSLOT, shape, BF16)

    def wload(self, s, dst, src):
        return self.dma("pool", dst, src, ("w", s), (), (("w", s),))

    def load_consts(self):
        self.dma("sp", self.arena[:, 0:NCV], self.cvec, ("c", 0), (), ("cv",))
        self.dma("sp", self.arena[:, 1024:1024 + NCM], self.cmask, ("c", 1), (), ("cm",))
        self.ident_bf = self.v(12288, [128, 128], BF16)
        self.ones_bf = self.v(12288 + 256, [128, 128], BF16)
        self.cp(self.ident_bf, self.cm(CM_ID, 128), ("cm",), ("cb",))
        self.cp(self.ones_bf, self.cm(CM_ONE, 128), ("cm",), ("cb",))
        self.ts(self.cv(CV_H1, 16), self.cv(CV_G["ffn1_post"], 16), 0.5, None, ALU.mult, None, ("cv",), ("cv2",))
        self.ts(self.cv(CV_H2, 16), self.cv(CV_G["ffn2_post"], 16), 0.5, None, ALU.mult, None, ("cv",), ("cv2",))
        self.act(self.cv(CV_NA, 1, 16), self.cv(CV_AL, 1, 16), AF.Exp, ("cv",), ("cv3",))
        self.ts(self.cv(CV_NA, 1, 16), self.cv(CV_NA, 1, 16), -1.0, None, ALU.mult, None, ("cv3",), ("cv3",))

    def transpose_in(self, src_tok, dst_fm, ntok, nfc, tok_off=0):
        self.barrier()
        XIN = self.v(PH_OFF, [128, 2, nfc * 128])
        XST = self.v(PH_OFF + 2 * nfc * 512, [128, 2, 4, 128])
        ident = self.cm(CM_ID, 128)
        for tb in range(ntok // 128):
            s = self.ring("xin", 2)
            self.dma("sp", XIN[:, s, :], src_tok[tb * 128:(tb + 1) * 128, :], ("xin", s), (), (("xin", s),))
            gsz = min(4, nfc)
            for g in range(nfc // gsz):
                b = self.bank()
                for q in range(gsz):
                    fc = g * gsz + q
                    self.tr(self.ps[:, b, q * 128:(q + 1) * 128], XIN[:, s, fc * 128:(fc + 1) * 128], ident,
                            (("xin", s), "cm"), (("ps", b),))
                k = self.ring("xst", 2)
                self.act(XST[:, k, 0:gsz].rearrange("p a b -> p (a b)"), self.ps[:, b, 0:gsz * 128], AF.Copy,
                         (("ps", b),), (("xst", k),))
                self.dma("act", dst_fm[g * gsz:(g + 1) * gsz, :, tok_off + tb * 128: tok_off + (tb + 1) * 128]
                         .rearrange("c p t -> p c t"), XST[:, k, 0:gsz], ("xst", k), (("xst", k),), ())

    def norm_load(self, src, t0, t1, gcol, XN, RSTD, XT, SQ, pre_ssq=None):
        G = t1 - t0
        for ti, (a, n) in enumerate(_tiles(0, G)):
            if pre_ssq is not None:
                b = pre_ssq[ti]
                self.rstd(RSTD[:, a:a + n], self.ps[:, b, :n], 1.0 / D, (("ps", b),), (("rstd", a),))
                continue
            b = self.bank()
            for fc in range(KC):
                s = self.ring("xt", 4)
                self.dma("sp" if fc % 2 == 0 else "act", XT[:, s, :n], src[fc, :, t0 + a:t0 + a + n], ("xt", s), (), (("xt", s),))
                q = self.ring("sq", 2)
                self.act(SQ[:, q, :n], XT[:, s, :n], AF.Square, (("xt", s),), (("sq", q),))
                self.mm(self.ps[:, b, :n], self.ones_bf, SQ[:, q, :n], fc == 0, fc == KC - 1,
                        (("sq", q), "cb"), (("ps", b),))
            self.rstd(RSTD[:, a:a + n], self.ps[:, b, :n], 1.0 / D, (("ps", b),), (("rstd", a),))
        for (a, n) in _tiles(0, G):
            for fc in range(KC):
                s = self.ring("xt", 4)
                self.dma("sp" if fc % 2 == 0 else "act", XT[:, s, :n], src[fc, :, t0 + a:t0 + a + n], ("xt", s), (), (("xt", s),))
                self.stt(XN[:, fc, a:a + n], XT[:, s, :n], self.cv(gcol + fc), RSTD[:, a:a + n], ALU.mult, ALU.mult,
                         (("xt", s), ("rstd", a), "cv"), (("xn", fc, a),))

    def post_residual(self, fsrc, rsrc, dst, t0, t1, gcol, ssq_banks, RSTD, XT, SQ=None, nxt_ssq=False, yout=None, YST=None):
        G = t1 - t0
        tl = _tiles(0, G)
        self.barrier()
        for ti, (a, n) in enumerate(tl):
            b = ssq_banks[ti]
            self.rstd(RSTD[:, a:a + n], self.ps[:, b, :n], 1.0 / D, (("ps", b),), (("rstd", a),))
        its = [(ti, a, n, fc) for ti, (a, n) in enumerate(tl) for fc in range(KC)]

        def load(i):
            ti, a, n, fc = its[i]
            s = self.ring("xt", 4)
            s2 = self.ring("xt", 4)
            self.dma("sp", XT[:, s, :n], fsrc[fc, :, t0 + a:t0 + a + n], ("xt", s), ("fscr",), (("xt", s),))
            self.dma("act", XT[:, s2, :n], rsrc[fc, :, t0 + a:t0 + a + n], ("xt", s2), (), (("xt", s2),))
            return s, s2
        nxt = load(0)
        for i, (ti, a, n, fc) in enumerate(its):
            s, s2 = nxt
            self.tt(XT[:, s, :n], XT[:, s, :n], RSTD[:, a:a + n], ALU.mult, (("xt", s), ("rstd", a)), (("xt", s),))
            self.stt(XT[:, s, :n], XT[:, s, :n], self.cv(gcol + fc), XT[:, s2, :n], ALU.mult, ALU.add,
                     (("xt", s), ("xt", s2), "cv", "cv2"), (("xt", s),))
            if i + 1 < len(its):
                nxt = load(i + 1)
            if nxt_ssq:
                q = self.ring("sq", 2)
                self.act(SQ[:, q, :n], XT[:, s, :n], AF.Square, (("xt", s),), (("sq", q),))
                self.mm(self.ps[:, ssq_banks[ti], :n], self.ones_bf, SQ[:, q, :n], fc == 0, fc == KC - 1,
                        (("sq", q), "cb"), (("ps", ssq_banks[ti]),))
            if yout is None:
                self.dma("sp", dst[fc, :, t0 + a:t0 + a + n], XT[:, s, :n], ("xt", s), (("xt", s),), ())
            else:
                nq = n // 128
                b = self.bank()
                for q4 in range(nq):
                    self.tr(self.ps[:, b, q4 * 128:(q4 + 1) * 128], XT[:, s, q4 * 128:(q4 + 1) * 128], self.cm(CM_ID, 128),
                            (("xt", s), "cm"), (("ps", b),))
                k = self.ring("yst", 2)
                self.act(YST[:, k, :n], self.ps[:, b, :n], AF.Copy, (("ps", b),), (("yst", k),))
                self.dma("sp", yout[a:a + n, fc * 128:(fc + 1) * 128].rearrange("(q p) f -> p q f", p=128),
                         YST[:, k, :n].rearrange("p (q f) -> p q f", f=128), ("yst", k), (("yst", k),), ())

    def ffn(self, src, dst, t0, t1, w_gu, w_dn, g_pre, g_post_half, pre_ssq=None, nxt_ssq=False):
        G = t1 - t0
        XN = self.v(PH_OFF, [128, KC, G], BF16)
        ACTB = self.v(PH_OFF + 36864, [128, JC, G], BF16)
        MO = PH_OFF + 36864 + 101376
        RSTD = self.v(MO, [128, 1152])
        XT = self.v(MO + 4608, [128, 4, 512])
        SQ = self.v(MO + 4608 + 8192, [128, 2, 512], BF16)
        FT = self.v(PH_OFF, [128, 2, 512])
        tl = _tiles(0, G)
        self.barrier()
        self.rot = list(range(8))
        self.norm_load(src, t0, t1, g_pre, XN, RSTD, XT, SQ, pre_ssq=pre_ssq)
        self.barrier()
        wgu = w_gu.rearrange("(kc p) n -> p kc n", p=128)
        for j in range(JC):
            s = self.wslot()
            wv = self.wview(s, [128, KC, 256])
            self.wload(s, wv[:, :, 0:128], wgu[:, :, j * 128:(j + 1) * 128])
            self.wload(s, wv[:, :, 128:256], wgu[:, :, DFF + j * 128:DFF + (j + 1) * 128])
            for ti, (a, n) in enumerate(tl):
                bg = self.bank()
                for kc in range(KC):
                    self.mm(self.ps[:, bg, :n], wv[:, kc, 0:128], XN[:, kc, a:a + n], kc == 0, kc == KC - 1,
                            (("w", s),), (("ps", bg),))
                bu = self.bank()
                for kc in range(KC):
                    self.mm(self.ps[:, bu, :n], wv[:, kc, 128:256], XN[:, kc, a:a + n], kc == 0, kc == KC - 1,
                            (("w", s),), (("ps", bu),))
                q = self.ring("sq", 2)
                self.act(SQ[:, q, :n], self.ps[:, bg, :n], AF.Silu, (("ps", bg),), (("sq", q),))
                self.tt(ACTB[:, j, a:a + n], SQ[:, q, :n], self.ps[:, bu, :n], ALU.mult,
                        (("sq", q), ("ps", bu)), ())
        self.barrier()
        nt = len(tl)
        ssq = list(range(8 - nt, 8))
        self.rot = list(range(8 - nt))
        wdn = w_dn.rearrange("(kc p) n -> p kc n", p=128)
        for oc in range(KC):
            s = self.wslot()
            wv = self.wview(s, [128, JC, 128])
            self.wload(s, wv[:, 0:22, :], wdn[:, 0:22, oc * 128:(oc + 1) * 128])
            self.wload(s, wv[:, 22:44, :], wdn[:, 22:44, oc * 128:(oc + 1) * 128])
            for ti, (a, n) in enumerate(tl):
                b = self.bank()
                for kc in range(JC):
                    self.mm(self.ps[:, b, :n], wv[:, kc, :], ACTB[:, kc, a:a + n], kc == 0, kc == JC - 1,
                            (("w", s),), (("ps", b),))
                k = self.ring("ft", 2)
                self.act(FT[:, k, :n], self.ps[:, b, :n], AF.Copy, (("ps", b),), (("ft", k),))
                self.dma("act", self.fT[oc, :, t0 + a:t0 + a + n], FT[:, k, :n], ("ft", k), (("ft", k),), ("fscr",))
                q = self.ring("sq", 2)
                self.tt(SQ[:, q, :n], FT[:, k, :n], FT[:, k, :n], ALU.mult, (("ft", k),), (("sq", q),))
                self.mm(self.ps[:, ssq[ti], :n], self.ones_bf, SQ[:, q, :n], oc == 0, oc == KC - 1,
                        (("sq", q), "cb"), (("ps", ssq[ti]),))
        if nxt_ssq:
            self.rot = list(range(8 - nt))
        self.post_residual(self.fT, src, dst, t0, t1, g_post_half, ssq, RSTD, XT, SQ=SQ, nxt_ssq=nxt_ssq)
        self.rot = list(range(8))

    def mix_layout(self):
        o = PH_OFF
        self.BETA = self.v(o, [16, NT]); o += NT * 4
        self.GG = self.v(o, [16, NT]); o += NT * 4
        self.UN = self.v(o, [128, KC, NT], BF16); o += KC * NT * 2
        self.mix_free = o

    def mixer_qkv(self):
        self.mix_layout()
        o = self.mix_free
        RAW = self.v(o, [128, 2, 180]); o += 2 * 180 * 4
        RAWB = self.v(o, [128, 2, 2052], BF16); o += 2 * 2052 * 2
        DG = self.v(o, [128, 2, 4, 128], BF16); o += 2 * 4 * 128 * 2
        CVO = self.v(o, [128, 2, NT]); o += 2 * NT * 4
        QO = self.v(o, [128, 2, NT], BF16); o += 2 * NT * 2
        RSTD = self.v(o, [128, NT]); o += NT * 4
        XT = self.v(o, [128, 4, 512]); o += 8192
        SQ = self.v(o, [128, 2, 512], BF16); o += 2048
        ST = self.v(o, [64, 2, 256]); o += 2048
        CT = self.v(o, [128, 2, 48]); o += 384
        SSQ1 = self.v(o, [128, NT]); o += NT * 4
        SSQ = [RSTD, SSQ1]
        UN = self.UN
        ident = self.cm(CM_ID, 128)
        self.barrier()
        self.rot = list(range(8))
        self.norm_load(self.h1, 0, NT, CV_G["mix_pre"], UN, RSTD, XT, SQ)
        self.barrier()
        win = self.w_in.rearrange("(kc p) n -> p kc n", p=128)
        tl = _tiles(0, NT)
        tlp = [(a, n) for (a, n) in tl if a < NTP]
        allk = lambda nm, cc_: [(nm, cc_, a_) for (a_, _n) in tl]

        def emit_l2(cc_, a_, n_, q_):
            b3 = self.bank()
            self.mm(self.ps[:, b3, :n_], self.ones_bf, SQ[:, q_, :n_], True, True, (("sq", q_), "cb"), (("ps", b3),))
            self.act(SSQ[cc_][:, a_:a_ + n_], self.ps[:, b3, :n_], AF.Copy, (("ps", b3),), (("ssq", cc_, a_),))

        def emit_conv(c_, cc_, a_, n_):
            b2 = self.bank()
            for j in range(4):
                self.mm(self.ps[:, b2, :n_], DG[:, cc_, j, :], RAWB[:, cc_, a_ + j:a_ + j + n_], j == 0, j == 3,
                        (("dg", cc_), ("rawb", cc_, a_), ("rawb", cc_, a_ - 512)), (("ps", b2),))
            self.act(CVO[:, cc_, a_:a_ + n_], self.ps[:, b2, :n_], AF.Silu, (("ps", b2),), (("cvo", cc_, a_),))
            if c_ < 32:
                q_ = self.ring("sq", 2)
                self.act(SQ[:, q_, :n_], CVO[:, cc_, a_:a_ + n_], AF.Square, (("cvo", cc_, a_),), (("sq", q_),))
                return (cc_, a_, n_, q_)
            self.cp(QO[:, cc_, a_:a_ + n_], CVO[:, cc_, a_:a_ + n_], (("cvo", cc_, a_),), (("qo", cc_, a_),))
            return None

        def epilogue(c_, cc_, pend, pend_l2):
            RS = RAW[:, cc_, 0:176].rearrange("p (s j) -> p s j", j=11)
            if pend_l2 is not None:
                emit_l2(*pend_l2)
            if pend is not None:
                p2 = emit_conv(c_, *pend)
                if p2 is not None:
                    emit_l2(*p2)
            CS = CVO[:, cc_, NTP:NT].rearrange("p (s j) -> p s j", j=8)
            for j in range(4):
                wcol = self.cv(CV_SC + c_ * 4 + j)
                if j == 0:
                    self.ts(CS, RS[:, :, 0:8], wcol, None, ALU.mult, None, (("raw", cc_), "cv"), (("cvo", cc_, NTP),))
                else:
                    self.stt(CS, RS[:, :, j:j + 8], wcol, CS, ALU.mult, ALU.add,
                             (("raw", cc_), ("cvo", cc_, NTP), "cv"), (("cvo", cc_, NTP),))
            self.act(CVO[:, cc_, NTP:NT], CVO[:, cc_, NTP:NT], AF.Silu, (("cvo", cc_, NTP),), (("cvo", cc_, NTP),))
            if c_ < 32:
                q = self.ring("sq", 2)
                self.act(SQ[:, q, :NS], CVO[:, cc_, NTP:NT], AF.Square, (("cvo", cc_, NTP),), (("sq", q),))
                emit_l2(cc_, NTP, NS, q)
                keys = allk("ssq", cc_)
                self.rstd(SSQ[cc_][:, 0:NT], SSQ[cc_][:, 0:NT], 1.0, keys, keys)
                if c_ < 16:
                    self.stt(QO[:, cc_, :], CVO[:, cc_, :], float(128 ** -0.5), SSQ[cc_][:, 0:NT], ALU.mult, ALU.mult,
                             keys + allk("cvo", cc_), allk("qo", cc_))
                else:
                    self.tt(QO[:, cc_, :], CVO[:, cc_, :], SSQ[cc_][:, 0:NT], ALU.mult,
                            keys + allk("cvo", cc_), allk("qo", cc_))
            else:
                self.cp(QO[:, cc_, NTP:NT], CVO[:, cc_, NTP:NT], (("cvo", cc_, NTP),), (("qo", cc_, NTP),))
            self.dma("sp", self.qkvF[c_], QO[:, cc_, :], ("qo", cc_), allk("qo", cc_), ())
            b = self.bank()
            self.tr(self.ps[0:3, b, 0:128], RAW[:, cc_, 176:179], ident, (("raw", cc_), "cm"), (("ps", b),))
            kc_ = self.ring("ct", 2)
            self.cp(CT[:, kc_, :].rearrange("p (s j) -> p s j", j=3), RS[:, :, 8:11], (("raw", cc_),), (("ct", kc_),), eng="pool")
            self.tr(self.ps[0:48, b, 128:256], CT[:, kc_, :], ident, (("ct", kc_), "cm"), (("ps", b),))
            k = self.ring("st", 2)
            self.act(ST[0:3, k, 0:128], self.ps[0:3, b, 0:128], AF.Copy, (("ps", b),), (("st", k),))
            self.act(ST[0:48, k, 128:256], self.ps[0:48, b, 128:256], AF.Copy, (("ps", b),), (("st", k),))
            self.dma("sp", self.nq_p[:, c_ * 128:(c_ + 1) * 128], ST[0:3, k, 0:128], ("st", k), (("st", k),), ())
            self.dma("sp", self.nq_s[:, c_ * 128:(c_ + 1) * 128], ST[0:48, k, 128:256], ("st", k), (("st", k),), ())
        deferred = None
        for bi in range(24):
            s = self.wslot()
            wv = self.wview(s, [128, KC, 256])
            self.wload(s, wv, win[:, :, bi * 256:(bi + 1) * 256])
            xs = self.ring("xt", 4)
            self.dma("sp", XT[0:48, xs, 0:256], self.sqkv[:, bi * 256:(bi + 1) * 256], ("xt", xs), (), (("xt", xs),))
            for cc in range(2):
                c = bi * 2 + cc
                RS = RAW[:, cc, 0:176].rearrange("p (s j) -> p s j", j=11)
                self.memset(RAWB[:, cc, 0:3], 0.0, (("rawb", cc, -512),), eng="pool")
                b = self.bank()
                self.tr(self.ps[:, b, 0:48], XT[0:48, xs, cc * 128:(cc + 1) * 128], ident[0:48, 0:48],
                        (("xt", xs), "cm"), (("ps", b),))
                self.act(RS[:, :, 0:3], self.ps[:, b, 0:48].rearrange("p (s j) -> p s j", j=3), AF.Copy,
                         (("ps", b),), (("raw", cc),))
                for j in range(4):
                    self.ts(DG[:, cc, j, :], self.ident_bf, self.cv(CV_SC + c * 4 + j), None, ALU.mult, None,
                            ("cb", "cv"), (("dg", cc),))
                pend = None
                pend_l2 = None
                for ti, (a, n) in enumerate(tl):
                    b = self.bank()
                    for kc in range(KC):
                        self.mm(self.ps[:, b, :n], wv[:, kc, cc * 128:(cc + 1) * 128], UN[:, kc, a:a + n],
                                kc == 0, kc == KC - 1, (("w", s),), (("ps", b),))
                    if a < NTP:
                        self.act(RAWB[:, cc, 3 + a:3 + a + n], self.ps[:, b, :n], AF.Copy, (("ps", b),), (("rawb", cc, a),))
                        if a + n == NTP:
                            self.act(RAW[:, cc, 176:179], self.ps[:, b, n - 3:n], AF.Copy, (("ps", b),), (("raw", cc),))
                    else:
                        self.act(RS[:, :, 3:11], self.ps[:, b, 0:128].rearrange("p (s j) -> p s j", j=8), AF.Copy,
                                 (("ps", b),), (("raw", cc),))
                    if ti == 1 and deferred is not None:
                        epilogue(*deferred)
                        deferred = None
                    if pend_l2 is not None:
                        emit_l2(*pend_l2)
                        pend_l2 = None
                    if pend is not None:
                        pend_l2 = emit_conv(c, *pend)
                    pend = (cc, a, n) if a < NTP else None
                deferred = (c, cc, pend, pend_l2)
        epilogue(*deferred)
        s = self.wslot()
        wv = self.wview(s, [128, KC, 32])
        self.wload(s, wv, win[:, :, O_BETA:O_BETA + 32])
        for (a, n) in tl:
            bb = self.bank()
            for kc in range(KC):
                self.mm(self.ps[0:16, bb, :n], wv[:, kc, 0:16], UN[:, kc, a:a + n], kc == 0, kc == KC - 1,
                        (("w", s),), (("ps", bb),))
            ba = self.bank()
            for kc in range(KC):
                self.mm(self.ps[0:16, ba, :n], wv[:, kc, 16:32], UN[:, kc, a:a + n], kc == 0, kc == KC - 1,
                        (("w", s),), (("ps", ba),))
            self.act(self.BETA[:, a:a + n], self.ps[0:16, bb, :n], AF.Sigmoid, (("ps", bb),), (("bg", a),))
            r = self.ring("xt", 4)
            self.act(XT[0:16, r, :n], self.ps[0:16, ba, :n], AF.Exp, (("ps", ba), "cv"), (("xt", r),),
                     bias=self.cv(CV_DT, 1, 16), scale=1.0)
            self.act(XT[0:16, r, :n], XT[0:16, r, :n], AF.Ln, (("xt", r),), (("xt", r),), bias=1.0, scale=1.0)
            self.ts(self.GG[:, a:a + n], XT[0:16, r, :n], self.cv(CV_NA, 1, 16), None, ALU.mult, None,
                    (("xt", r), "cv3"), (("bg", a),))

    def delta_layout(self):
        o = PH_OFF + 2 * NT * 4
        self.OF = self.v(o, [128, NH, NM], BF16); o += NH * NM * 2
        self.dl_free = o
        L = {}

        def al(name, shape, dt=F32):
            nonlocal o
            L[name] = self.v(o, shape, dt)
            n = 1
            for s in shape[1:]:
                n *= s
            o += ((n * (4 if dt == F32 else 2) + 31) // 32) * 32
        al("S", [128, NH, 128]); al("SB", [128, NH, 128], BF16)
        al("QKVC", [128, 2, 48 * 64], BF16)
        for nm in ("GROW", "NBROW", "GAM", "E1", "W2", "E2", "RM"):
            al(nm, [128, 512])
        self.alias_off = o
        for pb_ in range(2):
            for nm in ("GROW", "NBROW", "GAM", "E1", "W2", "E2", "RM"):
                if pb_ == 0:
                    L["%s_0" % nm] = L[nm]
                else:
                    al("%s_1" % nm, [128, 512])
        for nm in ("P0", "P1", "PT0", "PT1", "RB", "TBT0", "QKM0", "KG0", "QG0", "TBT1", "QKM1", "KG1", "QG1"):
            al(nm, [128, 512], BF16)
        al("KDEC0", [128, 1024], BF16); al("KDEC1", [128, 1024], BF16)
        al("DDB", [128, 1024], BF16); al("UUB", [128, 1024], BF16); al("OTB", [128, 512])
        al("TOK", [128, 32]); al("NBTOK", [128, 16]); al("COLS", [128, 32]); al("DCOL", [128, 16])
        al("TOK_1", [128, 32]); al("NBTOK_1", [128, 16]); al("COLS_1", [128, 32]); al("DCOL_1", [128, 16])
        for nm in ("TOK", "NBTOK", "COLS", "DCOL"):
            L[nm + "_0"] = L[nm]
        for nm in ("P0", "P1", "PT0", "PT1", "RB"):
            L[nm + "_0"] = L[nm]
            al(nm + "_1", [128, 512], BF16)
        for pb_ in range(2):
            al("KK_%d" % pb_, [128, 512], BF16); al("QK_%d" % pb_, [128, 512], BF16)
        al("GLR0", [128, 128]); al("GLR1", [128, 128]); al("RHS2", [128, 128])
        for nm in ("QKM2", "KG2", "QG2"):
            al(nm, [128, 512], BF16)
        al("KDEC2", [128, 1024], BF16); al("GLR2", [128, 128])
        for nm in ("TBT", "QKM", "KG", "QG", "KDEC", "GLR"):
            L[nm] = L[nm + "0"]
        al("DT", [128, 4, 128], BF16); al("DD", [128, 4, 128], BF16); al("UU", [128, 4, 128], BF16)
        al("OT", [128, 4, 128])
        o_save = o
        o = self.alias_off
        al("S2", [128, 16, 128])
        al("UBD", [128, 16, 128], BF16)
        assert o <= o_save
        o = o_save
        L["RHS"] = L["RM_1"]
        self.L = L
        print("delta layout end", o)
        assert o <= ARENA_BYTES, o

    def delta_block(self, C, HB, nseg, tok0, qkvc, need_o, otok0, mk, nlev, sample=False, qk="qkvc"):
        L = self.L
        ones_f = self.cm(CM_ONE, 128)
        ident_f = self.cm(CM_ID, 128)
        W = C * HB

        def t3(name, rows=128, dt=None):
            return L[name][0:rows, 0:W].rearrange("p (h t) -> p h t", h=HB)
        b = self.bank()
        self.tr(self.ps[0:C, b, 0:16], self.BETA[0:16, tok0:tok0 + C], ident_f[0:16, 0:16], ("cm",), (("ps", b),))
        self.tr(self.ps[0:C, b, 16:32], self.GG[0:16, tok0:tok0 + C], ident_f[0:16, 0:16], ("cm",), (("ps", b),))
        TOK = L["TOK"][0:C]
        self.act(TOK, self.ps[0:C, b, 0:32], AF.Copy, (("ps", b),), ("tok",))
        NBTOK = L["NBTOK"][0:C]
        self.ts(NBTOK, TOK[:, 0:16], -1.0, None, ALU.mult, None, ("tok",), ("nbtok",))
        b = self.bank()
        self.mm(self.ps[0:C, b, 0:16], mk["TRI"], TOK[:, 16:32], True, True, ("tok", "cm"), (("ps", b),))
        self.mm(self.ps[0:C, b, 16:32], mk["SAME"], TOK[:, 16:32], True, True, ("tok", "cm"), (("ps", b),))
        COLS = L["COLS"][0:C]
        self.act(COLS, self.ps[0:C, b, 0:32], AF.Copy, (("ps", b),), ("cols",))
        DCOL = L["DCOL"][0:C]
        self.tt(DCOL, COLS[:, 16:32], COLS[:, 0:16], ALU.subtract, ("cols",), ("dcol",))
        self.act(DCOL, DCOL, AF.Exp, ("dcol",), ("dcol",))
        for hb in range(NH // HB):
            h0 = hb * HB
            gt = TOK[:, 16 + h0:16 + h0 + HB]
            RHS = t3("RHS", C)

            def bc_h(m):
                return m.unsqueeze(1).to_broadcast([C, HB, C])

            def bc_t(col, n=C):
                return col.unsqueeze(2).to_broadcast([C, HB, n])
            self.tt(RHS, bc_h(mk["TRI"]), bc_t(gt), ALU.mult, ("tok", "cm"), ("rhs",))
            b = self.bank()
            self.mm(self.ps[:, b, 0:W], ones_f[0:C, :], L["RHS"][0:C, 0:W], True, True, ("rhs", "cm"), (("ps", b),))
            self.act(L["GROW"][:, 0:W], self.ps[:, b, 0:W], AF.Copy, (("ps", b),), ("grow",))
            self.tt(RHS, bc_h(mk["ID"]), bc_t(NBTOK[:, h0:h0 + HB]), ALU.mult, ("nbtok", "cm", "rhs"), ("rhs",))
            b = self.bank()
            self.mm(self.ps[:, b, 0:W], ones_f[0:C, :], L["RHS"][0:C, 0:W], True, True, ("rhs", "cm"), (("ps", b),))
            self.act(L["NBROW"][:, 0:W], self.ps[:, b, 0:W], AF.Copy, (("ps", b),), ("nbrow",))
            R2 = L["RHS2"][0:C, 0:HB * nseg].rearrange("p (h s) -> p h s", h=HB)
            self.tt(R2, mk["SEG"].unsqueeze(1).to_broadcast([C, HB, nseg]), bc_t(gt, nseg), ALU.mult,
                    ("tok", "cm"), ("rhs2",))
            b = self.bank()
            self.mm(self.ps[:, b, 0:HB * nseg], ones_f[0:C, :], L["RHS2"][0:C, 0:HB * nseg], True, True,
                    ("rhs2", "cm"), (("ps", b),))
            GLR = L["GLR"][:, 0:HB * nseg]
            self.act(GLR, self.ps[:, b, 0:HB * nseg], AF.Exp, (("ps", b),), ("glr",))
            GLR3 = GLR.rearrange("p (h s) -> p h s", h=HB)
            self.act(L["GAM"][:, 0:W], L["GROW"][:, 0:W], AF.Exp, ("grow",), ("gam",))
            GAM = t3("GAM")
            kq = qkvc.rearrange("p (c t) -> p c t", c=48)
            self.tt(t3("KG"), kq[:, 16 + h0:16 + h0 + HB, :], GAM, ALU.mult, ("gam", qk), ("kg",))
            if need_o:
                self.tt(t3("QG"), kq[:, h0:h0 + HB, :], GAM, ALU.mult, ("gam", qk), ("qg",), eng="pool")
            GROWc = t3("GROW", C)
            gcol = COLS[:, h0:h0 + HB]
            E1 = t3("E1", C); W2 = t3("W2", C); E2 = t3("E2", C)
            self.tt(E1, GROWc, bc_t(gcol), ALU.subtract, ("grow", "cols"), ("e1",))
            self.ts(E1, E1, 0.0, None, ALU.min, None, ("e1",), ("e1",))
            self.act(E1, E1, AF.Exp, ("e1",), ("e1",))
            self.tt(W2, E1, bc_h(mk["US"]), ALU.mult, ("e1", "cm"), ("w2",))
            self.tt(W2, W2, t3("NBROW", C), ALU.mult, ("w2", "nbrow"), ("w2",))
            self.tt(E1, E1, bc_h(mk["UI"]), ALU.mult, ("e1", "cm"), ("e1",))
            self.tt(E2, GROWc, bc_t(gcol), ALU.subtract, ("grow", "cols"), ("e2",))
            self.ts(E2, E2, -1.0, 0.0, ALU.mult, ALU.min, ("e2",), ("e2",))
            self.act(E2, E2, AF.Exp, ("e2",), ("e2",))
            self.tt(E2, E2, bc_h(mk["LS"]), ALU.mult, ("e2", "cm"), ("e2",))
            self.tt(E2, E2, bc_t(NBTOK[:, h0:h0 + HB]), ALU.mult, ("e2", "nbtok"), ("e2",))
            bk = self.bank()
            bq = self.bank()
            bt = self.bank()
            psT = self.ps[:, bt, :].bitcast(BF16)
            for j in range(HB):
                kf = kq[:, 16 + h0 + j, :]
                qf = kq[:, h0 + j, :]
                self.mm(self.ps[0:C, bk, j * C:(j + 1) * C], kf, kf, True, True, (qk,), (("ps", bk),))
                if need_o:
                    self.mm(self.ps[0:C, bq, j * C:(j + 1) * C], kf, qf, True, True, (qk,), (("ps", bq),))
                self.tr(psT[0:C, j * 128:(j + 1) * 128], kf, self.ident_bf, (qk, "cb"), (("ps", bt),))
            pk = self.ps[0:C, bk, 0:W]
            RM = L["RM"][0:C, 0:W]
            self.tt(RM, pk, L["W2"][0:C, 0:W], ALU.mult, (("ps", bk), "w2"), ("rm",))
            Pc = L["P0"][0:C, 0:W]
            PTc = L["PT0"][0:C, 0:W]
            self.cp(Pc, RM, ("rm",), ("p0",), eng="pool")
            self.tt(PTc, pk, L["E2"][0:C, 0:W], ALU.mult, (("ps", bk), "e2"), ("pt0",))
            self.tt(t3("RM", C), t3("RM", C), bc_h(mk["ID"]), ALU.add, ("rm", "cm"), ("rm",))
            RB = L["RB"][0:C, 0:W]
            self.cp(RB, RM, ("rm",), ("rb",), eng="pool")
            if need_o:
                self.tt(L["QKM"][0:C, 0:W], self.ps[0:C, bq, 0:W], L["E1"][0:C, 0:W], ALU.mult,
                        (("ps", bq), "e1"), ("qkm",))
            KD = L["KDEC"][0:C, 0:HB * 128].rearrange("p (h d) -> p h d", h=HB)
            self.tt(KD, psT[0:C, 0:HB * 128].rearrange("p (h d) -> p h d", h=HB),
                    DCOL[:, h0:h0 + HB].unsqueeze(2).to_broadcast([C, HB, 128]), ALU.mult,
                    (("ps", bt), "dcol"), ("kdec",))
            cur = 0
            for lv in range(nlev):
                Pn = L["P%d" % (1 - cur)][0:C, 0:W]
                PTn = L["PT%d" % (1 - cur)][0:C, 0:W]
                pkey, ptkey = "p%d" % cur, "pt%d" % cur
                pnkey, ptnkey = "p%d" % (1 - cur), "pt%d" % (1 - cur)
                ba = self.bank()
                bb = self.bank()
                for j in range(HB):
                    sl = slice(j * C, (j + 1) * C)
                    if lv < nlev - 1:
                        self.mm(self.ps[0:C, ba, sl], PTc[:, sl], Pc[:, sl], True, True, (pkey, ptkey), (("ps", ba),))
                    self.mm(self.ps[0:C, bb, sl], Pc[:, sl], PTc[:, sl], True, True, (pkey, ptkey), (("ps", bb),))
                if lv < nlev - 1:
                    self.act(Pn, self.ps[0:C, ba, 0:W], AF.Copy, (("ps", ba),), (pnkey,))
                self.cp(PTn, self.ps[0:C, bb, 0:W], (("ps", bb),), (ptnkey,))
                bc = self.bank()
                for j in range(HB):
                    sl = slice(j * C, (j + 1) * C)
                    self.mm(self.ps[0:C, bc, sl], PTn[:, sl], RB[:, sl], True, True, (ptnkey, "rb"), (("ps", bc),))
                self.tt(RM, RM, self.ps[0:C, bc, 0:W], ALU.add, ("rm", ("ps", bc)), ("rm",))
                if lv < nlev - 1:
                    self.cp(RB, RM, ("rm",), ("rb",), eng="pool")
                Pc, PTc = Pn, PTn
                cur = 1 - cur
            self.tt(t3("TBT", C), t3("RM", C), bc_t(TOK[:, h0:h0 + HB]), ALU.mult, ("rm", "tok"), ("tbt",))
            for j in range(HB):
                h = h0 + j
                sl = slice(j * C, (j + 1) * C)
                KGh = L["KG"][:, sl]
                vf = kq[:, 32 + h, :]
                r = self.ring("chain", 4)
                DT = L["DT"][:, r, 0:C]
                DD = L["DD"][0:C, r, :]
                UU = L["UU"][0:C, r, :]
                OT = L["OT"][:, r, 0:C]
                if sample:
                    sr = h % 2
                    SSv = self.v_ss[sr]
                    SSB = L["SB"][:, 0:16, :]
                    self.dma("sp", SSv, self.sdelta[:, h].rearrange("s d v -> d s v"), ("ss", sr), (), (("ss", sr),))
                    self.cp(SSB, SSv, (("ss", sr),), ("ssb",), eng="pool")
                    bK = self.bank()
                    for sg in range(nseg):
                        self.mm(self.ps[:, bK, sg * LS:(sg + 1) * LS], SSB[:, sg, :], KGh[:, sg * LS:(sg + 1) * LS],
                                True, True, ("ssb", "kg"), (("ps", bK),))
                else:
                    SBh = L["SB"][:, h, :]
                    bK = self.bank()
                    self.mm(self.ps[:, bK, 0:C], SBh, KGh, True, True, (("sb", h), "kg"), (("ps", bK),))
                self.tt(DT, vf, self.ps[:, bK, 0:C], ALU.subtract, (qk, ("ps", bK)), (("dt", r),))
                bT = self.bank()
                pT = self.ps[:, bT, :].bitcast(BF16)
                self.tr(pT[0:C, 0:128], DT, self.ident_bf, (("dt", r), "cb"), (("ps", bT),))
                self.act(DD, pT[0:C, 0:128], AF.Copy, (("ps", bT),), (("dd", r),))
                bU = self.bank()
                self.mm(self.ps[0:C, bU, 0:128], L["TBT"][0:C, sl], DD, True, True, ("tbt", ("dd", r)), (("ps", bU),))
                self.act(UU, self.ps[0:C, bU, 0:128], AF.Copy, (("ps", bU),), (("uu", r),))
                if need_o:
                    bO = self.bank()
                    if sample:
                        for sg in range(nseg):
                            self.mm(self.ps[:, bO, sg * LS:(sg + 1) * LS], SSB[:, sg, :],
                                    L["QG"][:, j * C + sg * LS:j * C + (sg + 1) * LS], True, True,
                                    ("ssb", "qg"), (("ps", bO),))
                    else:
                        self.mm(self.ps[:, bO, 0:C], SBh, L["QG"][:, sl], True, True, (("sb", h), "qg"), (("ps", bO),))
                    self.mm(self.ps[:, bO, 128:128 + C], UU, L["QKM"][0:C, sl], True, True,
                            (("uu", r), "qkm"), (("ps", bO),))
                    self.act(OT, self.ps[:, bO, 0:C], AF.Copy, (("ps", bO),), (("ot", r),))
                    self.tt(self.OF[:, h, otok0:otok0 + C], OT, self.ps[:, bO, 128:128 + C], ALU.add,
                            (("ot", r), ("ps", bO)), ())
                KDh = L["KDEC"][0:C, j * 128:(j + 1) * 128]
                if sample:
                    UBD = L["UBD"]
                    self.tt(UBD, UU.unsqueeze(1).to_broadcast([128, 16, 128]),
                            mk["SEG"].unsqueeze(2).to_broadcast([128, 16, 128]), ALU.mult,
                            (("uu", r), "cm"), ("ubd",))
                    gl = GLR3[:, j, :]
                    self.tt(SSv, SSv, gl.unsqueeze(2).to_broadcast([128, 16, 128]), ALU.mult,
                            (("ss", sr), "glr"), (("ss", sr),))
                    for q4 in range(4):
                        bS = self.bank()
                        self.mm(self.ps[:, bS, :], KDh, UBD[:, 4 * q4:4 * q4 + 4, :].rearrange("p a b -> p (a b)"),
                                True, True, ("kdec", "ubd"), (("ps", bS),))
                        sv = SSv[:, 4 * q4:4 * q4 + 4, :].rearrange("p a b -> p (a b)")
                        self.tt(sv, sv, self.ps[:, bS, :], ALU.add, (("ss", sr), ("ps", bS)), (("ss", sr),))
                    self.dma("sp", self.nd_s[:, h].rearrange("s d v -> d s v"), SSv, ("ss", sr), (("ss", sr),), ())
                else:
                    bS = self.bank()
                    self.mm(self.ps[:, bS, 0:128], KDh, UU, True, True, ("kdec", ("uu", r)), (("ps", bS),))
                    Sh = L["S"][:, h, :]
                    self.stt(Sh, Sh, GLR[:, j:j + 1], self.ps[:, bS, 0:128], ALU.mult, ALU.add,
                             (("s", h), "glr", ("ps", bS)), (("s", h),))
                    self.cp(SBh, Sh, (("s", h),), (("sb", h),), eng="pool")

    def p_prep(self, c, hb, pb, ob, qkvc, qk, need_o, mk):
        C, HB, W, nlev = 64, 8, 512, 5
        L = self.L
        ones_f = self.cm(CM_ONE, 128)
        ident_f = self.cm(CM_ID, 128)
        tok0 = c * 64
        cp_ = c % 2

        def T(name):
            return L["%s_%d" % (name, pb)]

        def K(name):
            return (name, pb)

        def t3(name, rows=128):
            return T(name)[0:rows, 0:W].rearrange("p (h t) -> p h t", h=HB)

        def bc_h(m):
            return m.unsqueeze(1).to_broadcast([C, HB, C])

        def bc_t(col, n=C):
            return col.unsqueeze(2).to_broadcast([C, HB, n])
        TOK = L["TOK_%d" % cp_][0:C]
        NBTOK = L["NBTOK_%d" % cp_][0:C]
        COLS = L["COLS_%d" % cp_][0:C]
        DCOL = L["DCOL_%d" % cp_][0:C]
        kt, kn, kc_, kd = ("tok", cp_), ("nbtok", cp_), ("cols", cp_), ("dcol", cp_)
        if hb == 0:
            b = self.bank()
            self.tr(self.ps[0:C, b, 0:16], self.BETA[0:16, tok0:tok0 + C], ident_f[0:16, 0:16], ("cm",), (("ps", b),))
            self.tr(self.ps[0:C, b, 16:32], self.GG[0:16, tok0:tok0 + C], ident_f[0:16, 0:16], ("cm",), (("ps", b),))
            self.act(TOK, self.ps[0:C, b, 0:32], AF.Copy, (("ps", b),), (kt,))
            self.ts(NBTOK, TOK[:, 0:16], -1.0, None, ALU.mult, None, (kt,), (kn,), eng="pool")
            b = self.bank()
            self.mm(self.ps[0:C, b, 0:16], mk["TRI"], TOK[:, 16:32], True, True, (kt, "cm"), (("ps", b),))
            self.mm(self.ps[0:C, b, 16:32], mk["SAME"], TOK[:, 16:32], True, True, (kt, "cm"), (("ps", b),))
            self.act(COLS, self.ps[0:C, b, 0:32], AF.Copy, (("ps", b),), (kc_,))
            self.tt(DCOL, COLS[:, 16:32], COLS[:, 0:16], ALU.subtract, (kc_,), (kd,), eng="pool")
            self.act(DCOL, DCOL, AF.Exp, (kd,), (kd,))
            yield
        h0 = hb * HB
        gt = TOK[:, 16 + h0:16 + h0 + HB]
        kq = qkvc.rearrange("p (c t) -> p c t", c=48)
        bk = self.bank()
        bq = self.bank() if need_o else None
        bt = self.bank()
        psT = self.ps[:, bt, :].bitcast(BF16)
        for j in range(HB):
            kf = kq[:, 16 + h0 + j, :]
            qf = kq[:, h0 + j, :]
            self.mm(self.ps[0:C, bk, j * C:(j + 1) * C], kf, kf, True, True, (qk,), (("ps", bk),))
            if need_o:
                self.mm(self.ps[0:C, bq, j * C:(j + 1) * C], kf, qf, True, True, (qk,), (("ps", bq),))
            self.tr(psT[0:C, j * 128:(j + 1) * 128], kf, self.ident_bf, (qk, "cb"), (("ps", bt),))
        KK = T("KK")[0:C, 0:W]
        self.act(KK, self.ps[0:C, bk, 0:W], AF.Copy, (("ps", bk),), (K("kk"),))
        if need_o:
            QK = T("QK")[0:C, 0:W]
            self.act(QK, self.ps[0:C, bq, 0:W], AF.Copy, (("ps", bq),), (K("qk"),))
        RHS = t3("W2", C)
        self.tt(RHS, bc_h(mk["TRI"]), bc_t(gt), ALU.mult, (kt, "cm", K("w2")), (K("w2"),))
        b1 = self.bank()
        self.mm(self.ps[:, b1, 0:W], ones_f[0:C, :], T("W2")[0:C, 0:W], True, True, (K("w2"), "cm"), (("ps", b1),))
        self.act(T("GROW")[:, 0:W], self.ps[:, b1, 0:W], AF.Copy, (("ps", b1),), (K("grow"),))
        RHSb = t3("E2", C)
        self.tt(RHSb, bc_h(mk["ID"]), bc_t(NBTOK[:, h0:h0 + HB]), ALU.mult, (kn, "cm", K("e2")), (K("e2"),), eng="pool")
        b2 = self.bank()
        self.mm(self.ps[:, b2, 0:W], ones_f[0:C, :], T("E2")[0:C, 0:W], True, True, (K("e2"), "cm"), (("ps", b2),))
        MB = t3("NBROW", C)
        self.tt(MB, self.ps[0:C, b2, 0:W].rearrange("p (h t) -> p h t", h=HB), bc_h(mk["US"]), ALU.mult,
                (("ps", b2), "cm"), (K("mb"),))
        b3 = self.bank()
        self.mm(self.ps[:, b3, 0:HB], ones_f[0:C, :], gt, True, True, (kt, "cm"), (("ps", b3),))
        GLR = L["GLR%d" % ob][:, 0:HB]
        self.act(GLR, self.ps[:, b3, 0:HB], AF.Exp, (("ps", b3),), (("glr", ob),))
        KD = L["KDEC%d" % ob][0:C, 0:HB * 128].rearrange("p (h d) -> p h d", h=HB)
        self.tt(KD, psT[0:C, 0:HB * 128].rearrange("p (h d) -> p h d", h=HB),
                DCOL[:, h0:h0 + HB].unsqueeze(2).to_broadcast([C, HB, 128]), ALU.mult,
                (("ps", bt), kd), (("kdec", ob),))
        yield
        GROWc = t3("GROW", C)
        gcol = COLS[:, h0:h0 + HB]
        E1 = t3("E1", C); W2 = t3("W2", C); E2 = t3("E2", C)
        self.tt(E1, GROWc, bc_t(gcol), ALU.subtract, (K("grow"), kc_), (K("e1"),))
        self.tt(E2, bc_t(gcol), GROWc, ALU.subtract, (K("grow"), kc_, K("e2")), (K("e2"),))
        self.act(T("GAM")[:, 0:W], T("GROW")[:, 0:W], AF.Exp, (K("grow"),), (K("gam"),))
        self.ts(E1, E1, 0.0, None, ALU.min, None, (K("e1"),), (K("e1"),))
        self.ts(E2, E2, 0.0, None, ALU.min, None, (K("e2"),), (K("e2"),))
        yield
        self.act(E1, E1, AF.Exp, (K("e1"),), (K("e1"),))
        self.act(E2, E2, AF.Exp, (K("e2"),), (K("e2"),))
        GAM = t3("GAM")
        self.tt(L["KG%d" % ob][:, 0:W].rearrange("p (h t) -> p h t", h=HB),
                kq[:, 16 + h0:16 + h0 + HB, :], GAM, ALU.mult, (K("gam"), qk), (("kg", ob),))
        if need_o:
            self.tt(L["QG%d" % ob][:, 0:W].rearrange("p (h t) -> p h t", h=HB), kq[:, h0:h0 + HB, :], GAM, ALU.mult,
                    (K("gam"), qk), (("qg", ob),), eng="pool")
        yield
        self.stt(W2, E1, 1.0, MB, ALU.min, ALU.mult, (K("e1"), K("mb")), (K("w2"),))
        self.stt(E2, E2, 1.0, bc_h(mk["LS"]), ALU.min, ALU.mult, (K("e2"), "cm"), (K("e2"),))
        yield
        RM = T("RM")[0:C, 0:W]
        Pc = T("P0")[0:C, 0:W]
        PTc = T("PT0")[0:C, 0:W]
        self.tt(Pc, KK, T("W2")[0:C, 0:W], ALU.mult, (K("kk"), K("w2")), (K("p0"),), eng="pool")
        self.tt(t3("E2", C), t3("E2", C), bc_t(NBTOK[:, h0:h0 + HB]), ALU.mult, (K("e2"), kn), (K("e2"),), eng="pool")
        self.tt(RM, KK, T("W2")[0:C, 0:W], ALU.mult, (K("kk"), K("w2")), (K("rm"),))
        yield
        self.tt(PTc, KK, T("E2")[0:C, 0:W], ALU.mult, (K("kk"), K("e2")), (K("pt0"),))
        self.tt(t3("RM", C), t3("RM", C), bc_h(mk["ID"]), ALU.add, (K("rm"), "cm"), (K("rm"),), eng="pool")
        RB = T("RB")[0:C, 0:W]
        if need_o:
            self.stt(E1, E1, 1.0, bc_h(mk["UI"]), ALU.min, ALU.mult, (K("e1"), "cm"), (K("e1"),))
            self.tt(L["QKM%d" % ob][0:C, 0:W], QK, T("E1")[0:C, 0:W], ALU.mult, (K("qk"), K("e1")), (("qkm", ob),))
        yield "HALF"
        self.act(RB, RM, AF.Copy, (K("rm"),), (K("rb"),))
        cur = 0

        def squares(lv, Pc, PTc, cur):
            Pn = T("P%d" % (1 - cur))[0:C, 0:W]
            PTn = T("PT%d" % (1 - cur))[0:C, 0:W]
            pkey, ptkey = K("p%d" % cur), K("pt%d" % cur)
            ba = self.bank()
            bb = self.bank()
            for j in range(HB):
                sl = slice(j * C, (j + 1) * C)
                if lv < nlev - 1:
                    self.mm(self.ps[0:C, ba, sl], PTc[:, sl], Pc[:, sl], True, True, (pkey, ptkey), (("ps", ba),))
                self.mm(self.ps[0:C, bb, sl], Pc[:, sl], PTc[:, sl], True, True, (pkey, ptkey), (("ps", bb),))
            if lv < nlev - 1:
                self.act(Pn, self.ps[0:C, ba, 0:W], AF.Copy, (("ps", ba),), (K("p%d" % (1 - cur)),))
            self.act(PTn, self.ps[0:C, bb, 0:W], AF.Copy, (("ps", bb),), (K("pt%d" % (1 - cur)),))
            return Pn, PTn
        Pn, PTn = squares(0, Pc, PTc, cur)
        for lv in range(nlev):
            ptnkey = K("pt%d" % (1 - cur))
            Pc, PTc = Pn, PTn
            cur = 1 - cur
            yield
            bc = self.bank()
            for j in range(HB):
                sl = slice(j * C, (j + 1) * C)
                self.mm(self.ps[0:C, bc, sl], PTc[:, sl], RB[:, sl], True, True, (ptnkey, K("rb")), (("ps", bc),))
            if lv + 1 < nlev:
                Pn, PTn = squares(lv + 1, Pc, PTc, cur)
            self.tt(RM, RM, self.ps[0:C, bc, 0:W], ALU.add, (K("rm"), ("ps", bc)), (K("rm"),))
            if lv < nlev - 1:
                yield
                self.act(RB, RM, AF.Copy, (K("rm"),), (K("rb"),))
        yield
        self.tt(L["TBT%d" % pb][0:C, 0:W].rearrange("p (h t) -> p h t", h=HB), t3("RM", C),
                bc_t(TOK[:, h0:h0 + HB]), ALU.mult, (K("rm"), kt), (("tbt", pb),))

    def p_chain(self, c, hb, pb, ob, qkvc, qk, need_o):
        C, HB, W = 64, 8, 512
        L = self.L
        h0 = hb * HB
        kq = qkvc.rearrange("p (c t) -> p c t", c=48)
        SBk = [("sb", h0 + j) for j in range(HB)]
        Sk = [("s", h0 + j) for j in range(HB)]
        KG = L["KG%d" % ob]
        bK = self.bank()
        for j in range(HB):
            self.mm(self.ps[:, bK, j * C:(j + 1) * C], L["SB"][:, h0 + j, :], KG[:, j * C:(j + 1) * C], True, True,
                    (("sb", h0 + j), ("kg", ob)), (("ps", bK),))
        if need_o:
            bO1 = self.bank()
            for j in range(HB):
                self.mm(self.ps[:, bO1, j * C:(j + 1) * C], L["SB"][:, h0 + j, :], L["QG%d" % ob][:, j * C:(j + 1) * C],
                        True, True, (("sb", h0 + j), ("qg", ob)), (("ps", bO1),))
        DT = L["DT"].rearrange("p a b -> p (a b)")
        self.tt(DT.rearrange("p (h t) -> p h t", h=HB), kq[:, 32 + h0:32 + h0 + HB, :],
                self.ps[:, bK, 0:W].rearrange("p (h t) -> p h t", h=HB), ALU.subtract, (qk, ("ps", bK)), ("dt",))
        if need_o:
            OT = L["OTB"]
            self.act(OT, self.ps[:, bO1, 0:W], AF.Copy, (("ps", bO1),), ("ot",))
        yield
        bT = self.bank()
        pT = self.ps[:, bT, :].bitcast(BF16)
        for j in range(HB):
            self.tr(pT[0:C, j * 128:(j + 1) * 128], DT[:, j * C:(j + 1) * C], self.ident_bf, ("dt", "cb"), (("ps", bT),))
        DD = L["DDB"][0:C, :]
        self.act(DD, pT[0:C, 0:1024], AF.Copy, (("ps", bT),), ("dd",))
        yield
        TBT = L["TBT%d" % pb]
        bU = [self.bank(), self.bank()]
        for j in range(HB):
            self.mm(self.ps[0:C, bU[j // 4], (j % 4) * 128:(j % 4 + 1) * 128], TBT[0:C, j * C:(j + 1) * C],
                    DD[:, j * 128:(j + 1) * 128], True, True, (("tbt", pb), "dd"), (("ps", bU[j // 4]),))
        UU = L["UUB"][0:C, :]
        self.act(UU[:, 0:512], self.ps[0:C, bU[0], :], AF.Copy, (("ps", bU[0]),), ("uu0",))
        self.cp(UU[:, 512:1024], self.ps[0:C, bU[1], :], (("ps", bU[1]),), ("uu1",))
        yield
        KD = L["KDEC%d" % ob]
        bS = [self.bank(), self.bank()]
        for j in range(HB):
            self.mm(self.ps[:, bS[j // 4], (j % 4) * 128:(j % 4 + 1) * 128], KD[0:C, j * 128:(j + 1) * 128],
                    UU[:, j * 128:(j + 1) * 128], True, True, (("kdec", ob), "uu%d" % (j // 4)), (("ps", bS[j // 4]),))
        if need_o:
            bO2 = self.bank()
            for j in range(HB):
                self.mm(self.ps[:, bO2, j * C:(j + 1) * C], UU[:, j * 128:(j + 1) * 128],
                        L["QKM%d" % ob][0:C, j * C:(j + 1) * C], True, True,
                        ("uu%d" % (j // 4), ("qkm", ob)), (("ps", bO2),))
        S8 = L["S"][:, h0:h0 + HB, :]
        GLR = L["GLR%d" % ob][:, 0:HB]
        self.tt(S8, S8, GLR.unsqueeze(2).to_broadcast([128, HB, 128]), ALU.mult, Sk + [("glr", ob)], Sk, eng="pool")
        for q in range(2):
            s4 = L["S"][:, h0 + 4 * q:h0 + 4 * q + 4, :].rearrange("p a b -> p (a b)")
            self.tt(s4, s4, self.ps[:, bS[q], :], ALU.add, Sk[4 * q:4 * q + 4] + [("ps", bS[q])], Sk[4 * q:4 * q + 4])
        self.act(L["SB"][:, h0:h0 + HB, :], S8, AF.Copy, Sk, SBk)
        if need_o:
            otok0 = c * 64 - M0
            self.tt(self.OF[:, h0:h0 + HB, otok0:otok0 + C], OT.rearrange("p (h t) -> p h t", h=HB),
                    self.ps[:, bO2, 0:W].rearrange("p (h t) -> p h t", h=HB), ALU.add, ("ot", ("ps", bO2)), ())
        yield

    def delta_prompt_pipelined(self, mk):
        L = self.L
        self.rot = list(range(8))
        units = [(c, hb) for c in range(NTP // 64) for hb in range(2)]
        qslot = {}

        def step(g):
            try:
                return next(g) or True
            except StopIteration:
                return False
        preps = {}
        info = {}

        def start_prep(i):
            c, hb = units[i]
            if hb == 0:
                s = self.ring("qkvc", 2)
                qslot[c] = s
                self.dma("sp", L["QKVC"][:, s, :].rearrange("p (c t) -> p c t", c=48),
                         self.qkvF[:, :, c * 64:(c + 1) * 64].rearrange("c p t -> p c t"), ("qkvc", s), (), (("qkvc", s),))
            s = qslot[c]
            qc = L["QKVC"][:, s, :]
            need_o = c * 64 >= M0
            info[i] = (c, hb, i % 2, i % 3, qc, ("qkvc", s), need_o)
            preps[i] = self.p_prep(c, hb, i % 2, i % 3, qc, ("qkvc", s), need_o, mk)
        n = len(units)
        start_prep(0)
        while step(preps[0]) != "HALF":
            pass
        chain = None
        for i in range(n):
            if i + 1 < n:
                start_prep(i + 1)
            a_live = i + 1 < n
            b_live = True
            c_live = chain is not None
            while a_live or b_live or c_live:
                if b_live:
                    b_live = bool(step(preps[i]))
                if a_live:
                    if step(preps[i + 1]) == "HALF":
                        a_live = False
                if c_live:
                    c_live = bool(step(chain))
            c, hb, pb, ob, qc, qk, need_o = info[i]
            chain = self.p_chain(c, hb, pb, ob, qc, qk, need_o)
        while step(chain):
            pass

    def delta(self):
        self.delta_layout()
        L = self.L
        self.barrier()
        self.rot = list(range(8))
        self.memset(L["S"], 0.0, [("s", h) for h in range(NH)])
        self.memset(L["SB"], 0.0, [("sb", h) for h in range(NH)], eng="pool")
        mk = dict(TRI=self.cm(CM_TRI, 64, 64), UI=self.cm(CM_UI, 64, 64), US=self.cm(CM_US, 64, 64),
                  LS=self.cm(CM_LS, 64, 64), SAME=self.cm(CM_ONE, 64, 64), ID=self.cm(CM_ID, 64, 64),
                  SEG=self.cm(CM_ONE, 1, 64))
        self.delta_prompt_pipelined(mk)
        self.dma("sp", self.nd_p.rearrange("h d v -> d h v"), L["S"], ("sfin",), [("s", h) for h in range(NH)], ())
        self.barrier()
        self.rot = list(range(8))
        mk8 = dict(TRI=self.cm(CM_TRI8, 128), UI=self.cm(CM_UI8, 128), US=self.cm(CM_US8, 128),
                   LS=self.cm(CM_LS8, 128), SAME=self.cm(CM_SAME8, 128), ID=self.cm(CM_ID, 128),
                   SEG=self.cm(CM_SEG, 16))
        self.v_ss = [L["S"][:, 0:16, :], L["S2"]]
        qc = L["QKVC"].rearrange("p a b -> p (a b)")
        self.dma("sp", qc.rearrange("p (c t) -> p c t", c=48),
                 self.qkvF[:, :, NTP:NT].rearrange("c p t -> p c t"), ("qkvc", 0), (), (("qkvc", 0),))
        self.delta_block(128, 4, 16, NTP, qc, True, NMAIN, mk8, 2, sample=True, qk=("qkvc", 0))

    def mixer_tail(self):
        NU = NM + 32
        U0 = M0 - 32
        o = PH_OFF
        RSTD = self.v(o, [128, NU]); o += NU * 4
        SQ = self.v(o, [128, 2, 512], BF16); o += 2048
        MU = self.v(o, [128, NM]); o += NM * 4
        RS = self.v(o, [128, NM]); o += NM * 4
        assert o <= PH_OFF + 2 * NT * 4
        o = PH_OFF + 2 * NT * 4
        OF = self.OF; o += NH * NM * 2
        UN2 = self.v(o, [128, KC, NU], BF16); o += KC * NU * 2
        CB = self.v(o, [128, KC, NM], BF16); o += KC * NM * 2
        XT = self.v(o, [128, 4, 512]); o += 8192
        GPB = self.v(o, [128, 1056], BF16); o += 1056 * 2
        GP30 = self.v(o, [128, 32]); o += 128
        GSX = self.v(o, [128, 608]); o += 608 * 4
        DG31 = self.v(o, [128, 31, 128], BF16); o += 31 * 128 * 2
        CO = self.v(o, [128, NM]); o += NM * 4
        assert o <= ARENA_BYTES, o
        ident = self.cm(CM_ID, 128)
        win = self.w_in.rearrange("(kc p) n -> p kc n", p=128)
        self.barrier()
        self.rot = list(range(8))
        self.norm_load(self.h1, U0, NT, CV_G["mix_pre"], UN2, RSTD, XT, SQ)
        self.barrier()
        tlm = _tiles(0, NM)
        for zb in range(8):
            s = self.wslot()
            wv = self.wview(s, [128, KC, 256])
            self.wload(s, wv, win[:, :, O_Z + zb * 256:O_Z + (zb + 1) * 256])
            for cc in range(2):
                h = zb * 2 + cc
                SS = MU if cc == 0 else RS
                zbanks = []
                for (a, n) in tlm:
                    ok = ("of", h, a)
                    q = self.ring("sq", 2)
                    self.act(SQ[:, q, :n], OF[:, h, a:a + n], AF.Square, (ok,), (("sq", q),))
                    b1 = self.bank()
                    self.mm(self.ps[:, b1, :n], self.ones_bf, SQ[:, q, :n], True, True, (("sq", q), "cb"), (("ps", b1),))
                    b = self.bank()
                    for kc in range(KC):
                        self.mm(self.ps[:, b, :n], wv[:, kc, cc * 128:(cc + 1) * 128], UN2[:, kc, 32 + a:32 + a + n],
                                kc == 0, kc == KC - 1, (("w", s),), (("ps", b),))
                    zbanks.append(b)
                    self.act(SS[:, a:a + n], self.ps[:, b1, :n], AF.Copy, (("ps", b1),), (("ss", cc, a),))
                    r2 = self.ring("xt", 4)
                    self.act(XT[:, r2, :n], self.ps[:, b, :n], AF.Silu, (("ps", b),), (("xt", r2),))
                    self.tt(OF[:, h, a:a + n], OF[:, h, a:a + n], XT[:, r2, :n], ALU.mult, (ok, ("xt", r2)), (ok,))
                sk = [("ss", cc, a) for (a, n) in tlm]
                self.rstd(SS[:, 0:NM], SS[:, 0:NM], 1.0 / 128, sk, sk)
                self.stt(OF[:, h, :], OF[:, h, :], self.cv(CV_ON), SS[:, 0:NM], ALU.mult, ALU.mult,
                         sk + [("of", h, a) for (a, n) in tlm] + ["cv"], [("of", h, a) for (a, n) in tlm])
        self.barrier()
        self.rot = list(range(8))
        SUMB = [(2, 0), (3, 0), (4, 0)]
        SSQB = [(5, 0), (6, 0), (7, 0)]
        GS = GSX.rearrange("p (s j) -> p s j", j=38)
        tlu = _tiles(0, NU)
        def sgt_load(c_):
            for q4 in range(4):
                self.dma("sp", MU[0:120, (c_ % 2) * 512 + q4 * 128:(c_ % 2) * 512 + (q4 + 1) * 128],
                         self.sglu[q4 * 120:(q4 + 1) * 120, c_ * 128:(c_ + 1) * 128],
                         ("sgt", c_ % 2), (), (("sgt", c_ % 2),))
        sgt_load(0)
        for c in range(KC):
            s = self.wslot()
            wv = self.wview(s, [128, KC, 256])
            self.wload(s, wv[:, :, 0:128], win[:, :, O_GLU + c * 128:O_GLU + (c + 1) * 128])
            self.wload(s, wv[:, :, 128:256], win[:, :, O_GLU + 2048 + c * 128:O_GLU + 2048 + (c + 1) * 128])
            if c + 1 < KC:
                sgt_load(c + 1)
            for q4 in range(4):
                sg_ = MU[0:120, (c % 2) * 512 + q4 * 128:(c % 2) * 512 + (q4 + 1) * 128]
                b = self.bank()
                self.tr(self.ps[:, b, 0:120], sg_, ident[0:120, 0:120], (("sgt", c % 2), "cm"), (("ps", b),))
                self.act(GS[:, 4 * q4:4 * q4 + 4, 0:30], self.ps[:, b, 0:120].rearrange("p (s j) -> p s j", j=30),
                         AF.Copy, (("ps", b),), ("glx",))
            for (a, n) in tlu:
                ba = self.bank()
                for kc in range(KC):
                    self.mm(self.ps[:, ba, :n], wv[:, kc, 0:128], UN2[:, kc, a:a + n], kc == 0, kc == KC - 1,
                            (("w", s),), (("ps", ba),))
                bb = self.bank()
                for kc in range(KC):
                    self.mm(self.ps[:, bb, :n], wv[:, kc, 128:256], UN2[:, kc, a:a + n], kc == 0, kc == KC - 1,
                            (("w", s),), (("ps", bb),))
                r = self.ring("xt", 4)
                self.act(XT[:, r, :n], self.ps[:, bb, :n], AF.Sigmoid, (("ps", bb),), (("xt", r),))
                lo = max(a, 2)
                hi = min(a + n, 1056)
                if hi > lo:
                    self.tt(GPB[:, lo - 2:hi - 2], self.ps[:, ba, lo - a:hi - a], XT[:, r, lo - a:hi - a], ALU.mult,
                            (("ps", ba), ("xt", r)), ("glx",))
                if a <= 1026 < a + n:
                    self.tt(GP30[:, 0:30], self.ps[:, ba, 1026 - a:1056 - a], XT[:, r, 1026 - a:1056 - a], ALU.mult,
                            (("ps", ba), ("xt", r)), ("gp30",))
                if a + n > 1056:
                    o0 = 1056 - a
                    self.tt(GS[:, :, 30:38], self.ps[:, ba, o0:o0 + 128].rearrange("p (s j) -> p s j", j=8),
                            XT[:, r, o0:o0 + 128].rearrange("p (s j) -> p s j", j=8), ALU.mult,
                            (("ps", ba), ("xt", r)), ("glx",))
            COs = CO[:, NMAIN:NM].rearrange("p (s j) -> p s j", j=8)
            bcol = self.cv(CV_G["b_dw"] + c)
            self.tt(DG31, self.ident_bf.unsqueeze(1).to_broadcast([128, 31, 128]),
                    self.cv(CV_DW + c * 31, 31).unsqueeze(2).to_broadcast([128, 31, 128]), ALU.mult,
                    ("cb", "cv"), ("dg31",))
            for t0_ in (0, 512):
                b = self.bank()
                for j in range(31):
                    self.mm(self.ps[:, b, :], DG31[:, j, :], GPB[:, t0_ + j:t0_ + j + 512], j == 0, j == 30,
                            ("dg31", "glx"), (("ps", b),))
                self.act(CO[:, t0_:t0_ + 512], self.ps[:, b, :], AF.Identity, (("ps", b), "cv"), ("co",), bias=bcol, scale=1.0)
            for j in range(31):
                wcol = self.cv(CV_DW + c * 31 + j)
                if j == 0:
                    self.ts(COs, GS[:, :, 0:8], wcol, bcol, ALU.mult, ALU.add, ("glx", "cv"), ("co",))
                else:
                    self.stt(COs, GS[:, :, j:j + 8], wcol, COs, ALU.mult, ALU.add, ("glx", "co", "cv"), ("co",))
            b = self.bank()
            self.tr(self.ps[0:30, b, 0:128], GP30[:, 0:30], ident, ("gp30", "cm"), (("ps", b),))
            k = self.ring("xt", 4)
            self.act(XT[0:30, k, 0:128], self.ps[0:30, b, 0:128], AF.Copy, (("ps", b),), (("xt", k),))
            self.dma("act", self.ng_p[:, c * 128:(c + 1) * 128], XT[0:30, k, 0:128], ("xt", k), (("xt", k),), ())
            for q4 in range(4):
                k = self.ring("xt", 4)
                self.cp(XT[:, k, 0:120].rearrange("p (s j) -> p s j", j=30), GS[:, 4 * q4:4 * q4 + 4, 8:38],
                        ("glx",), (("xt", k),), eng="pool")
                b = self.bank()
                self.tr(self.ps[0:120, b, 0:128], XT[:, k, 0:120], ident, (("xt", k), "cm"), (("ps", b),))
                self.act(XT[0:120, k, 128:256], self.ps[0:120, b, 0:128], AF.Copy, (("ps", b),), (("xt", k),))
                self.dma("act", self.ng_s[q4 * 120:(q4 + 1) * 120, c * 128:(c + 1) * 128], XT[0:120, k, 128:256],
                         ("xt", k), (("xt", k),), ())
            self.dma("sp", self.cT[c], CO, ("co",), ("co",), ())
        self.barrier()
        self.rot = [0, 1]
        for c in range(KC):
            for ti, (a, n) in enumerate(tlm):
                r = self.ring("xt", 4)
                self.dma("sp" if (c + ti) % 2 == 0 else "act", XT[:, r, :n], self.cT[c, :, a:a + n], ("xt", r), (), (("xt", r),))
                COt = XT[:, r, 0:512]
                a0 = a
                a = 0
                q = self.ring("sq", 2)
                self.act(SQ[:, q, :n], COt[:, a:a + n], AF.Square, (("xt", r),), (("sq", q),))
                self.mm(self.ps[:, SSQB[ti][0], SSQB[ti][1]:SSQB[ti][1] + n], self.ones_bf, SQ[:, q, :n], c == 0, c == KC - 1,
                        (("sq", q), "cb"), (("ps", SSQB[ti][0]),))
                q = self.ring("sq", 2)
                self.cp(SQ[:, q, :n], COt[:, a:a + n], (("xt", r),), (("sq", q),))
                a = a0
                self.mm(self.ps[:, SUMB[ti][0], SUMB[ti][1]:SUMB[ti][1] + n], self.ones_bf, SQ[:, q, :n], c == 0, c == KC - 1,
                        (("sq", q), "cb"), (("ps", SUMB[ti][0]),))
        for ti, (a, n) in enumerate(tlm):
            sb_, so_ = SUMB[ti]
            qb_, qo_ = SSQB[ti]
            self.act(MU[:, a:a + n], self.ps[:, sb_, so_:so_ + n], AF.Copy, (("ps", sb_),), (("mu", a),), scale=1.0 / D)
            self.tt(RS[:, a:a + n], MU[:, a:a + n], MU[:, a:a + n], ALU.mult, (("mu", a),), (("rs", a),))
            self.stt(RS[:, a:a + n], self.ps[:, qb_, qo_:qo_ + n], 1.0 / D, RS[:, a:a + n], ALU.mult, ALU.subtract,
                     (("ps", qb_), ("rs", a)), (("rs", a),))
            self.ts(RS[:, a:a + n], RS[:, a:a + n], 0.0, None, ALU.max, None, (("rs", a),), (("rs", a),))
            self.rstd(RS[:, a:a + n], RS[:, a:a + n], 1.0, (("rs", a),), (("rs", a),))
        self.barrier()
        self.rot = list(range(8))
        for c in range(KC):
            for (a, n) in tlm:
                r = self.ring("xt", 4)
                self.dma("sp", XT[:, r, :n], self.cT[c, :, a:a + n], ("xt", r), (), (("xt", r),))
                self.tt(XT[:, r, :n], XT[:, r, :n], MU[:, a:a + n], ALU.subtract, (("xt", r),), (("xt", r),))
                self.tt(XT[:, r, :n], XT[:, r, :n], RS[:, a:a + n], ALU.mult, (("xt", r),), (("xt", r),))
                self.ts(XT[:, r, :n], XT[:, r, :n], self.cv(CV_G["ln_g"] + c), self.cv(CV_G["ln_b"] + c),
                        ALU.mult, ALU.add, (("xt", r), "cv"), (("xt", r),))
                self.act(CB[:, c, a:a + n], XT[:, r, :n], AF.Silu, (("xt", r),), ())
        self.barrier()
        wba = self.w_ba.rearrange("(kc p) n -> p kc n", p=128)
        wbb = self.w_bb.rearrange("(kc p) n -> p kc n", p=128)
        LW = self.v(187392, [128, 2, KC, 256], BF16)
        bring = [0]

        def bslot():
            i = bring[0] % 5
            bring[0] += 1
            if i < 3:
                return self.wview(i, [128, KC, 256]), ("w", i)
            return LW[:, i - 3], ("wl", i - 3)

        def bload(dst, key, src):
            self.dma("pool", dst, src, key, (), (key,))
        for oc in range(KC):
            w1, s1 = bslot()
            bload(w1[:, :, 0:128], s1, wba[:, :, oc * 128:(oc + 1) * 128])
            bload(w1[:, :, 128:256], s1, wbb[:, :, oc * 128:(oc + 1) * 128])
            w2, s2 = bslot()
            bload(w2[:, :, 0:128], s2, win[:, :, O_GATE + oc * 128:O_GATE + (oc + 1) * 128])
            bload(w2[:, :, 128:256], s2, win[:, :, O_GATE + 2048 + oc * 128:O_GATE + 2048 + (oc + 1) * 128])
            for (a, n) in tlm:
                bs = []
                for (wv_, s_, col, src, off) in ((w1, s1, 0, OF, 0), (w1, s1, 128, CB, 0), (w2, s2, 0, UN2, 32), (w2, s2, 128, UN2, 32)):
                    b = self.bank()
                    for kc in range(KC):
                        self.mm(self.ps[:, b, :n], wv_[:, kc, col:col + 128], src[:, kc, off + a:off + a + n],
                                kc == 0, kc == KC - 1, (s_,), (("ps", b),))
                    bs.append(b)
                r1 = self.ring("xt", 4)
                r2 = self.ring("xt", 4)
                self.act(XT[:, r1, :n], self.ps[:, bs[2], :n], AF.Sigmoid, (("ps", bs[2]),), (("xt", r1),))
                self.act(XT[:, r2, :n], self.ps[:, bs[3], :n], AF.Sigmoid, (("ps", bs[3]),), (("xt", r2),))
                self.tt(XT[:, r1, :n], XT[:, r1, :n], self.ps[:, bs[0], :n], ALU.mult, (("xt", r1), ("ps", bs[0])), (("xt", r1),))
                self.tt(XT[:, r2, :n], XT[:, r2, :n], self.ps[:, bs[1], :n], ALU.mult, (("xt", r2), ("ps", bs[1])), (("xt", r2),))
                q = self.ring("sq", 2)
                self.tt(SQ[:, q, :n], XT[:, r1, :n], XT[:, r2, :n], ALU.add, (("xt", r1), ("xt", r2)), (("sq", q),))
                self.dma("sp", self.mgT[oc, :, a:a + n], SQ[:, q, :n], ("sq", q), (("sq", q),), ())

    def proj_post(self, kind):
        o = PH_OFF
        XN = self.v(o, [128, KC, NM], BF16); o += KC * NM * 2
        PB = self.v(o, [128, 2, NM], BF16); o += 2 * NM * 2
        RSTD = self.v(o, [128, NM]); o += NM * 4
        XT = self.v(o, [128, 4, 512]); o += 8192
        SQ = self.v(o, [128, 2, 512], BF16); o += 2048
        FT = self.v(o, [128, 2, 512]); o += 4096
        YST = self.v(o, [128, 2, 512]); o += 4096
        tlm = _tiles(0, NM)
        self.barrier()
        self.rot = list(range(5))
        if kind == "out":
            for kc in range(KC):
                self.dma("sp", XN[:, kc, :], self.mgT[kc], ("xnl", kc % 4), (), ())
            W = self.w_out.rearrange("(kc p) n -> p kc n", p=128)
            nk = KC
        else:
            self.norm_load(self.h3, M0, NT, CV_G["ple_pre"], XN, RSTD, XT, SQ, pre_ssq=[5, 6, 7])
            for kc in range(2):
                self.dma("pool", PB[:, kc, :], self.pT[kc], ("pbl", kc), (), ())
            W = self.w_pg.rearrange("(kc p) n -> p kc n", p=128)
            WP = self.w_pp.rearrange("(kc p) n -> p kc n", p=128)
            nk = KC
        self.barrier()
        ssq = [5, 6, 7]
        self.rot = list(range(5))
        for oc in range(KC):
            s = self.wslot()
            wv = self.wview(s, [128, KC + 2, 128])
            self.wload(s, wv[:, 0:KC, :], W[:, :, oc * 128:(oc + 1) * 128])
            if kind == "ple":
                self.wload(s, wv[:, KC:KC + 2, :], WP[:, :, oc * 128:(oc + 1) * 128])
            for ti, (a, n) in enumerate(tlm):
                b = self.bank()
                for kc in range(nk):
                    self.mm(self.ps[:, b, :n], wv[:, kc, :], XN[:, kc, a:a + n], kc == 0, kc == nk - 1,
                            (("w", s),), (("ps", b),))
                k = self.ring("ft", 2)
                if kind == "out":
                    self.act(FT[:, k, :n], self.ps[:, b, :n], AF.Copy, (("ps", b),), (("ft", k),))
                else:
                    b2 = self.bank()
                    for kc in range(2):
                        self.mm(self.ps[:, b2, :n], wv[:, KC + kc, :], PB[:, kc, a:a + n], kc == 0, kc == 1,
                                (("w", s),), (("ps", b2),))
                    self.act(FT[:, k, :n], self.ps[:, b, :n], AF.Sigmoid, (("ps", b),), (("ft", k),))
                    self.tt(FT[:, k, :n], FT[:, k, :n], self.ps[:, b2, :n], ALU.mult, (("ft", k), ("ps", b2)), (("ft", k),))
                self.dma("sp", self.fT[oc, :, M0 + a:M0 + a + n], FT[:, k, :n], ("ft", k), (("ft", k),), ("fscr",))
                q = self.ring("sq", 2)
                self.tt(SQ[:, q, :n], FT[:, k, :n], FT[:, k, :n], ALU.mult, (("ft", k),), (("sq", q),))
                self.mm(self.ps[:, ssq[ti], :n], self.ones_bf, SQ[:, q, :n], oc == 0, oc == KC - 1,
                        (("sq", q), "cb"), (("ps", ssq[ti]),))
        if kind == "out":
            self.post_residual(self.fT, self.h1, self.h2, M0, NT, CV_G["mix_post"], ssq, RSTD, XT, SQ=SQ, nxt_ssq=True)
        else:
            self.post_residual(self.fT, self.h3, self.h4, M0, NT, CV_G["ple_post"], ssq, RSTD, XT, yout=self.y, YST=YST)
        self.rot = list(range(8))

    def write_y(self):
        self.barrier()
        self.rot = list(range(8))
        HIN = self.v(PH_OFF, [128, 2, 4, 128])
        YT = self.v(PH_OFF + 4096, [128, 2, D])
        ident = self.cm(CM_ID, 128)
        for tb in range(NM // 128):
            yk = self.ring("yt", 2)
            for g in range(4):
                k = self.ring("hin", 2)
                self.dma("sp", HIN[:, k], self.h4[g * 4:(g + 1) * 4, :, M0 + tb * 128:M0 + (tb + 1) * 128]
                         .rearrange("c p t -> p c t"), ("hin", k), (), (("hin", k),))
                b = self.bank()
                for q in range(4):
                    self.tr(self.ps[:, b, q * 128:(q + 1) * 128], HIN[:, k, q, :], ident, (("hin", k), "cm"), (("ps", b),))
                self.act(YT[:, yk, g * 512:(g + 1) * 512], self.ps[:, b, :], AF.Copy, (("ps", b),), (("yt", yk),))
            self.dma("act", self.y[tb * 128:(tb + 1) * 128, :], YT[:, yk, :], ("yt", yk), (("yt", yk),), ())

    def dbg_dump_bg(self):
        self.barrier()
        self.dma("sp", self.dbg_bg[0], self.BETA, ("dbg", 0), (), ())
        self.dma("sp", self.dbg_bg[1], self.GG, ("dbg", 0), (), ())

    def build(self):
        self.load_consts()
        self.transpose_in(self.xin, self.xT, NT, KC)
        if self.stop_after == "xT":
            return self.finish()
        self.ffn(self.xT, self.h1, 0, NPRE, self.w_gu1, self.w_dn1, CV_G["ffn1_pre"], CV_H1)
        self.ffn(self.xT, self.h1, M0, NT, self.w_gu1, self.w_dn1, CV_G["ffn1_pre"], CV_H1)
        if self.stop_after == "ffn1":
            return self.finish()
        self.mixer_qkv()
        if self.stop_after == "qkv":
            if self.debug:
                self.dbg_dump_bg()
            return self.finish()
        self.delta()
        if self.stop_after == "delta":
            if self.debug:
                self.barrier()
                self.dma("sp", self.dbg_of.rearrange("h p t -> p h t"), self.OF, ("dbg", 1), (), ())
            return self.finish()
        self.mixer_tail()
        self.transpose_in(self.pin, self.pT, NM, 2)
        self.proj_post("out")
        if self.stop_after == "mix":
            return self.finish()
        self.ffn(self.h2, self.h3, M0, NT, self.w_gu2, self.w_dn2, CV_G["ffn2_pre"], CV_H2, pre_ssq=[5, 6, 7], nxt_ssq=True)
        self.proj_post("ple")
        return self.finish()

    def finish(self):
        self.pg.emit()
        return self.nc


def _masks():
    m = np.zeros((128, NCM), np.float32)
    i = np.arange(128)
    m[:, CM_ID:CM_ID + 128] = np.eye(128, dtype=np.float32)
    m[:, CM_ONE:CM_ONE + 128] = 1.0
    j = np.arange(64)
    le = (j[:, None] <= j[None, :]).astype(np.float32)
    lt = (j[:, None] < j[None, :]).astype(np.float32)
    m[:64, CM_TRI:CM_TRI + 64] = le
    m[:64, CM_UI:CM_UI + 64] = le
    m[:64, CM_US:CM_US + 64] = lt
    m[:64, CM_LS:CM_LS + 64] = lt.T
    same = ((i[:, None] // LS) == (i[None, :] // LS)).astype(np.float32)
    le8 = (i[:, None] <= i[None, :]).astype(np.float32) * same
    lt8 = (i[:, None] < i[None, :]).astype(np.float32) * same
    m[:, CM_TRI8:CM_TRI8 + 128] = le8
    m[:, CM_UI8:CM_UI8 + 128] = le8
    m[:, CM_US8:CM_US8 + 128] = lt8
    m[:, CM_LS8:CM_LS8 + 128] = lt8.T
    m[:, CM_SAME8:CM_SAME8 + 128] = same
    m[:, CM_SEG:CM_SEG + 16] = (i[:, None] // LS == np.arange(16)[None, :]).astype(np.float32)
    return m


def _cvec(inp):
    c = np.zeros((128, NCV), np.float32)

    def fm(vec):
        return np.ascontiguousarray(np.asarray(vec, np.float32).reshape(-1, 128).T)
    for n, col in CV_G.items():
        key = {"b_dw": "b_dw_conv"}.get(n, n)
        c[:, col:col + 16] = fm(inp[key][0])
    wsc = np.asarray(inp["w_short_conv"][0], np.float32)
    c[:, CV_SC:CV_SC + 192] = wsc.reshape(4, 48, 128).transpose(2, 1, 0).reshape(128, 192)
    wdw = np.asarray(inp["w_dw_conv"][0], np.float32)
    c[:, CV_DW:CV_DW + 496] = wdw.reshape(31, 16, 128).transpose(2, 1, 0).reshape(128, 496)
    c[:, CV_ON] = np.asarray(inp["o_norm"][0], np.float32)
    c[:16, CV_AL] = np.asarray(inp["a_log"][0], np.float32)
    c[:16, CV_DT] = np.asarray(inp["dt_bias"][0], np.float32)
    return c


def make_in_maps(inp, cores=range(8)):
    xp = np.asarray(inp["x_prompt"], np.float32)
    xs = np.asarray(inp["x_sample"], np.float32)
    pp = np.asarray(inp["p_prompt"], np.float32)[0]
    psm = np.asarray(inp["p_sample"], np.float32)[0]
    sd = np.asarray(inp["state_delta"], np.float32)[0]
    sq = np.asarray(inp["state_qkv_conv"], np.float32)[0]
    sg = np.asarray(inp["state_glu_conv"], np.float32)[0]
    shared = {
        "cvec": _cvec(inp), "cmask": _masks(),
        "w_gu1": np.asarray(inp["ffn1_w_gu"][0]), "w_dn1": np.asarray(inp["ffn1_w_down"][0]),
        "w_in": np.asarray(inp["w_in"][0]), "w_ba": np.asarray(inp["w_branch_a"][0]),
        "w_bb": np.asarray(inp["w_branch_b"][0]), "w_out": np.asarray(inp["w_out"][0]),
        "w_gu2": np.asarray(inp["ffn2_w_gu"][0]), "w_dn2": np.asarray(inp["ffn2_w_down"][0]),
        "w_pg": np.asarray(inp["w_ple_gate"][0]), "w_pp": np.asarray(inp["w_ple_proj"][0]),
    }
    maps = []
    for c in cores:
        b, half = c // 2, c % 2
        main = xp[b, half * NMAIN:(half + 1) * NMAIN]
        pre = xp[b, 0:NPRE] if half == 1 else np.zeros((NPRE, D), np.float32)
        sl = slice(c * NSEQ, (c + 1) * NSEQ)
        m = dict(shared)
        m["xin"] = np.concatenate([pre, main, xs[sl].reshape(NS, D)], 0)
        m["pin"] = np.concatenate([pp[b, half * NMAIN:(half + 1) * NMAIN], psm[sl].reshape(NS, PLE)], 0)
        m["sdelta"] = np.ascontiguousarray(sd[sl])
        m["sqkv"] = np.ascontiguousarray(sq[sl].reshape(NSEQ * 3, QKV))
        m["sglu"] = np.ascontiguousarray(sg[sl].reshape(NSEQ * 30, D))
        maps.append(m)
    return maps


def kernel(**inputs):
    nc = Builder().build()
    maps = make_in_maps(inputs)
    res = run_bass_kernel_spmd(nc, maps, core_ids=list(range(8)))
    R = res.results
    yp = np.zeros((4, 2048, D), np.float32)
    ys = np.zeros((128, LS, D), np.float32)
    ndp = np.zeros((1, 4, NH, 128, 128), np.float32)
    nqp = np.zeros((1, 4, 3, QKV), np.float32)
    ngp = np.zeros((1, 4, 30, D), np.float32)
    nds = np.zeros((1, 128, NH, 128, 128), np.float32)
    nqs = np.zeros((1, 128, 3, QKV), np.float32)
    ngs = np.zeros((1, 128, 30, D), np.float32)
    for c in range(8):
        b, half = c // 2, c % 2
        r = R[c]
        yp[b, half * NMAIN:(half + 1) * NMAIN] = r["y"][:NMAIN]
        ys[c * NSEQ:(c + 1) * NSEQ] = r["y"][NMAIN:].reshape(NSEQ, LS, D)
        if half == 1:
            ndp[0, b] = r["nd_p"]
            nqp[0, b] = r["nq_p"]
            ngp[0, b] = r["ng_p"]
        nds[0, c * NSEQ:(c + 1) * NSEQ] = r["nd_s"]
        nqs[0, c * NSEQ:(c + 1) * NSEQ] = r["nq_s"].reshape(NSEQ, 3, QKV)
        ngs[0, c * NSEQ:(c + 1) * NSEQ] = r["ng_s"].reshape(NSEQ, 30, D)
    return (yp, ys, ndp, nqp, ngp, nds, nqs, ngs)
```

```python
import numpy as np
import concourse.bass as bass
import concourse.mybir as mybir
from concourse.bass_utils import run_bass_kernel_spmd

F32 = mybir.dt.float32
BF16 = mybir.dt.bfloat16
AF = mybir.ActivationFunctionType
ALU = mybir.AluOpType

ENGS = ("pe", "act", "dve", "pool", "sp")
EPOCH = 30000


class _Op:
    __slots__ = ("eng", "fn", "reads", "writes", "dma", "deps", "sig", "idx", "n", "bar")

    def __init__(self, eng, fn, reads, writes, dma):
        self.eng = eng
        self.fn = fn
        self.reads = reads
        self.writes = writes
        self.dma = dma
        self.deps = None
        self.sig = False
        self.idx = None
        self.n = 0
        self.bar = False


class Prog:
    def __init__(self, nc):
        self.nc = nc
        self.ops = []
        self.streams = {e: [] for e in ENGS}

    def add(self, eng, fn, reads=(), writes=(), dma=None):
        op = _Op(eng, fn, tuple(reads), tuple(writes), dma)
        op.n = len(self.ops)
        self.ops.append(op)
        self.streams[eng].append(op)
        return op

    def barrier(self, fn):
        op = self.add("sp", fn, dma=("bar",))
        op.bar = True
        return op

    def _analyze(self):
        last_w = {}
        readers = {}
        last_eng = {}
        last_dma = {}
        cur_bar = None
        need_bar = set()
        for op in self.ops:
            deps = {}
            if op.bar:
                for d in last_eng.values():
                    deps[d.n] = d
                for d in last_dma.values():
                    deps[d.n] = d
                last_w = {}
                readers = {}
                cur_bar = op
                need_bar = set(ENGS)
            else:
                if cur_bar is not None and op.eng in need_bar:
                    deps[cur_bar.n] = cur_bar
                    need_bar.discard(op.eng)
                for k in op.reads:
                    w = last_w.get(k)
                    if w is not None:
                        deps[w.n] = w
                for k in op.writes:
                    w = last_w.get(k)
                    if w is not None:
                        deps[w.n] = w
                    for r in readers.get(k, ()):
                        deps[r.n] = r
                for k in op.reads:
                    readers.setdefault(k, []).append(op)
                for k in op.writes:
                    last_w[k] = op
                    readers[k] = []
            if op.dma is not None:
                last_dma[op.dma] = op
            else:
                last_eng[op.eng] = op
            deps.pop(op.n, None)
            dl = []
            for d in deps.values():
                if d.dma is None and op.dma is None and d.eng == "pe" and op.eng == "pe":
                    continue
                dl.append(d)
            op.deps = dl
            for d in dl:
                d.sig = True
        cnt = {e: 0 for e in ENGS}
        dcnt = {}
        for op in self.ops:
            if op.dma is not None:
                dcnt[op.dma] = dcnt.get(op.dma, 0) + 16
                op.idx = ("d", op.dma, dcnt[op.dma])
            elif op.sig:
                c = cnt[op.eng]
                cnt[op.eng] += 1
                op.idx = ("e", (op.eng, c // EPOCH), c % EPOCH + 1)
        self.dma_final = dcnt

    def emit(self):
        nc = self.nc
        self._analyze()
        sems = {}

        def sem(kind, key):
            k = (kind, key)
            if k not in sems:
                sems[k] = nc.alloc_semaphore(name="s%d" % len(sems))
            return sems[k]

        waits = {}
        for e in ENGS:
            seen = {}
            for op in self.streams[e]:
                wl = {}
                for d in op.deps:
                    kind, key, val = d.idx
                    k = (kind, key)
                    if seen.get(k, 0) >= val:
                        continue
                    if wl.get(k, 0) < val:
                        wl[k] = val
                for k, v in wl.items():
                    seen[k] = v
                waits[op.n] = [(sem(*k), v) for k, v in wl.items()]
        final_waits = [(sem("d", k), v) for k, v in self.dma_final.items()]
        self.nsem = len(sems)

        def run_stream(engname, eng):
            for op in self.streams[engname]:
                for s, v in waits[op.n]:
                    eng.wait_ge(s, v)
                ins = op.fn(eng)
                if op.idx is not None:
                    kind, key, val = op.idx
                    ins.then_inc(sem(kind, key), 16 if kind == "d" else 1)
            if engname == "sp":
                for s, v in final_waits:
                    eng.wait_ge(s, v)

        with nc.Block() as block:
            @block.tensor
            def _(e):
                run_stream("pe", e)

            @block.scalar
            def _(e):
                run_stream("act", e)

            @block.vector
            def _(e):
                run_stream("dve", e)

            @block.gpsimd
            def _(e):
                run_stream("pool", e)

            @block.sync
            def _(e):
                run_stream("sp", e)


D = 2048
KC = 16
DFF = 5632
JC = 44
NH = 16
QKV = 6144
O_Z = QKV
O_BETA = O_Z + 2048
O_A = O_BETA + NH
O_GLU = O_A + NH
O_GATE = O_GLU + 4096
IN_DIM = O_GATE + 4096
PLE = 256
EPS = 1e-6
NPRE = 1024
NMAIN = 1024
NTP = NPRE + NMAIN
NSEQ = 16
LS = 8
NS = NSEQ * LS
NT = NTP + NS
NM = NMAIN + NS
M0 = NPRE

ARENA_BYTES = 206000
WR_OFF = 16384
WSLOT = 11264
NWS = 3
PH_OFF = WR_OFF + NWS * WSLOT

CV_G = {n: 16 * i for i, n in enumerate(
    ["ffn1_pre", "ffn1_post", "mix_pre", "mix_post", "ffn2_pre", "ffn2_post", "ple_pre", "ple_post",
     "b_dw", "ln_g", "ln_b"])}
CV_SC = 176
CV_DW = CV_SC + 192
CV_ON = CV_DW + 496
CV_AL = CV_ON + 1
CV_DT = CV_AL + 1
CV_H1 = CV_DT + 1
CV_H2 = CV_H1 + 16
CV_NA = CV_H2 + 16
NCV = CV_NA + 1
CM_ID = 0
CM_ONE = 128
CM_TRI = 256
CM_UI = 320
CM_US = 384
CM_LS = 448
CM_TRI8 = 512
CM_UI8 = 640
CM_US8 = 768
CM_LS8 = 896
CM_SAME8 = 1024
CM_SEG = 1152
NCM = CM_SEG + 16


def _tiles(t0, t1, n=512):
    return [(a, min(n, t1 - a)) for a in range(t0, t1, n)]


class Builder:
    def __init__(self, debug=False, stop_after=None):
        self.debug = debug
        self.stop_after = stop_after
        nc = bass.Bass("TRN2", target_bir_lowering=False)
        self.nc = nc
        self.pg = Prog(nc)
        self.arena = nc.alloc_sbuf_tensor("arena", [128, ARENA_BYTES // 4], F32)
        self.ps = nc.alloc_psum_tensor("ps", [128, 8, 512], F32)
        self.rot = list(range(8))
        self.rot_i = 0
        self.ws_i = 0
        self.ring_i = {}
        self._decl()

    def _in(self, name, shape, dt=F32):
        return self.nc.dram_tensor(name, list(shape), dt, kind="ExternalInput").ap()

    def _out(self, name, shape, dt=F32):
        return self.nc.dram_tensor(name, list(shape), dt, kind="ExternalOutput").ap()

    def _scr(self, name, shape, dt=F32):
        kind = "ExternalOutput" if self.debug else "Internal"
        return self.nc.dram_tensor(name, list(shape), dt, kind=kind).ap()

    def _decl(self):
        self.xin = self._in("xin", [NT, D])
        self.pin = self._in("pin", [NM, PLE])
        self.sdelta = self._in("sdelta", [NSEQ, NH, 128, 128])
        self.sqkv = self._in("sqkv", [NSEQ * 3, QKV])
        self.sglu = self._in("sglu", [NSEQ * 30, D])
        self.cvec = self._in("cvec", [128, NCV])
        self.cmask = self._in("cmask", [128, NCM])
        self.w_gu1 = self._in("w_gu1", [D, 2 * DFF])
        self.w_dn1 = self._in("w_dn1", [DFF, D])
        self.w_in = self._in("w_in", [D, IN_DIM])
        self.w_ba = self._in("w_ba", [D, D])
        self.w_bb = self._in("w_bb", [D, D])
        self.w_out = self._in("w_out", [D, D])
        self.w_gu2 = self._in("w_gu2", [D, 2 * DFF])
        self.w_dn2 = self._in("w_dn2", [DFF, D])
        self.w_pg = self._in("w_pg", [D, D])
        self.w_pp = self._in("w_pp", [PLE, D])
        self.y = self._out("y", [NM, D])
        self.nd_p = self._out("nd_p", [NH, 128, 128])
        self.nq_p = self._out("nq_p", [3, QKV])
        self.ng_p = self._out("ng_p", [30, D])
        self.nd_s = self._out("nd_s", [NSEQ, NH, 128, 128])
        self.nq_s = self._out("nq_s", [NSEQ * 3, QKV])
        self.ng_s = self._out("ng_s", [NSEQ * 30, D])
        self.xT = self._scr("xT", [KC, 128, NT])
        self.h1 = self._scr("h1", [KC, 128, NT])
        self.fT = self._scr("fT", [KC, 128, NT])
        self.qkvF = self._scr("qkvF", [48, 128, NT], BF16)
        self.cT = self._scr("cT", [KC, 128, NM])
        self.mgT = self._scr("mgT", [KC, 128, NM], BF16)
        self.h2 = self._scr("h2", [KC, 128, NT])
        self.h3 = self._scr("h3", [KC, 128, NT])
        self.h4 = self._scr("h4", [KC, 128, NT])
        self.pT = self._scr("pT", [2, 128, NM])
        if self.debug:
            self.dbg_bg = self._scr("dbg_bg", [2, 16, NT])
            self.dbg_of = self._scr("dbg_of", [NH, 128, NM], BF16)
        self.bar_a = self.nc.dram_tensor("bar_a", [1, 16], F32, kind="Internal").ap()
        self.bar_b = self.nc.dram_tensor("bar_b", [1, 16], F32, kind="Internal").ap()

    def v(self, off, shape, dt=F32):
        esz = 4 if dt == F32 else 2
        n = 1
        for s in shape[1:]:
            n *= s
        nb = n * esz
        assert off % 4 == 0 and nb % 4 == 0, (off, nb)
        assert off + nb <= ARENA_BYTES, (off, nb)
        a = self.arena[0:shape[0], off // 4:(off + nb) // 4]
        if dt != F32:
            a = a.bitcast(dt)
        if len(shape) == 3:
            a = a.rearrange("p (a b) -> p a b", a=shape[1])
        elif len(shape) == 4:
            a = a.rearrange("p (a b c) -> p a b c", a=shape[1], b=shape[2])
        return a

    def cv(self, col, n=1, rows=128):
        return self.arena[0:rows, col:col + n]

    def cm(self, col, n, rows=128, dt=F32):
        return self.arena[0:rows, 1024 + col:1024 + col + n]

    def bank(self):
        b = self.rot[self.rot_i % len(self.rot)]
        self.rot_i += 1
        return b

    def ring(self, name, n):
        i = self.ring_i.get(name, 0)
        self.ring_i[name] = i + 1
        return i % n

    def mm(self, out, lhsT, rhs, start, stop, reads, writes):
        return self.pg.add("pe", lambda e: e.matmul(out, lhsT, rhs, start=start, stop=stop), reads, writes)

    def tr(self, out, in_, ident, reads, writes):
        return self.pg.add("pe", lambda e: e.transpose(out, in_, ident), reads, writes)

    def act(self, out, in_, func, reads, writes, bias=None, scale=None):
        kw = {}
        if bias is not None:
            kw["bias"] = bias
        if scale is not None:
            kw["scale"] = scale
        return self.pg.add("act", lambda e: e.activation(out=out, in_=in_, func=func, **kw), reads, writes)

    def tt(self, out, a, b, op, reads, writes, eng="dve"):
        return self.pg.add(eng, lambda e: e.tensor_tensor(out, a, b, op), reads, writes)

    def ts(self, out, a, s1, s2, op0, op1, reads, writes, eng="dve"):
        if op1 is None:
            return self.pg.add(eng, lambda e: e.tensor_single_scalar(out, a, s1, op0), reads, writes)
        return self.pg.add(eng, lambda e: e.tensor_scalar(out, a, s1, s2, op0, op1), reads, writes)

    def stt(self, out, in0, scalar, in1, op0, op1, reads, writes, eng="dve"):
        return self.pg.add(eng, lambda e: e.scalar_tensor_tensor(out, in0, scalar, in1, op0, op1), reads, writes)

    def cp(self, out, in_, reads, writes, eng="dve"):
        return self.pg.add(eng, lambda e: e.tensor_copy(out, in_), reads, writes)

    def rstd(self, out, in_, scale, reads, writes):
        self.act(out, in_, AF.Ln, reads, writes, bias=EPS, scale=scale)
        return self.act(out, out, AF.Exp, writes, writes, scale=-0.5)

    def recip(self, out, in_, reads, writes):
        return self.pg.add("dve", lambda e: e.reciprocal(out, in_), reads, writes)

    def memset(self, ap, val, writes, eng="dve"):
        return self.pg.add(eng, lambda e: e.memset(ap, val), (), writes)

    def dma(self, eng, out, in_, key, reads, writes):
        return self.pg.add(eng, lambda e: e.dma_start(out=out, in_=in_), reads, writes, dma=key)

    def barrier(self):
        a, b = self.bar_a, self.bar_b
        self.bar_a, self.bar_b = b, a
        self.pg.barrier(lambda e: e.dma_start(out=b, in_=a))
        self.ring_i = {}

    def wslot(self):
        s = self.ws_i % NWS
        self.ws_i += 1
        return s

    def wview(self, s, shape):
        return self.v(WR_OFF + s * WSLOT, shape, BF16)

    def wload(self, s, dst, src):
        return self.dma("pool", dst, src, ("w", s), (), (("w", s),))

    def load_consts(self):
        self.dma("sp", self.arena[:, 0:NCV], self.cvec, ("c", 0), (), ("cv",))
        self.dma("sp", self.arena[:, 1024:1024 + NCM], self.cmask, ("c", 1), (), ("cm",))
        self.ident_bf = self.v(12288, [128, 128], BF16)
        self.ones_bf = self.v(12288 + 256, [128, 128], BF16)
        self.cp(self.ident_bf, self.cm(CM_ID, 128), ("cm",), ("cb",))
        self.cp(self.ones_bf, self.cm(CM_ONE, 128), ("cm",), ("cb",))
        self.ts(self.cv(CV_H1, 16), self.cv(CV_G["ffn1_post"], 16), 0.5, None, ALU.mult, None, ("cv",), ("cv2",))
        self.ts(self.cv(CV_H2, 16), self.cv(CV_G["ffn2_post"], 16), 0.5, None, ALU.mult, None, ("cv",), ("cv2",))
        self.act(self.cv(CV_NA, 1, 16), self.cv(CV_AL, 1, 16), AF.Exp, ("cv",), ("cv3",))
        self.ts(self.cv(CV_NA, 1, 16), self.cv(CV_NA, 1, 16), -1.0, None, ALU.mult, None, ("cv3",), ("cv3",))

    def transpose_in(self, src_tok, dst_fm, ntok, nfc, tok_off=0):
        self.barrier()
        XIN = self.v(PH_OFF, [128, 2, nfc * 128])
        XST = self.v(PH_OFF + 2 * nfc * 512, [128, 2, 4, 128])
        ident = self.cm(CM_ID, 128)
        for tb in range(ntok // 128):
            s = self.ring("xin", 2)
            self.dma("sp", XIN[:, s, :], src_tok[tb * 128:(tb + 1) * 128, :], ("xin", s), (), (("xin", s),))
            gsz = min(4, nfc)
            for g in range(nfc // gsz):
                b = self.bank()
                for q in range(gsz):
                    fc = g * gsz + q
                    self.tr(self.ps[:, b, q * 128:(q + 1) * 128], XIN[:, s, fc * 128:(fc + 1) * 128], ident,
                            (("xin", s), "cm"), (("ps", b),))
                k = self.ring("xst", 2)
                self.act(XST[:, k, 0:gsz].rearrange("p a b -> p (a b)"), self.ps[:, b, 0:gsz * 128], AF.Copy,
                         (("ps", b),), (("xst", k),))
                self.dma("act", dst_fm[g * gsz:(g + 1) * gsz, :, tok_off + tb * 128: tok_off + (tb + 1) * 128]
                         .rearrange("c p t -> p c t"), XST[:, k, 0:gsz], ("xst", k), (("xst", k),), ())

    def norm_load(self, src, t0, t1, gcol, XN, RSTD, XT, SQ, pre_ssq=None):
        G = t1 - t0
        for ti, (a, n) in enumerate(_tiles(0, G)):
            if pre_ssq is not None:
                b = pre_ssq[ti]
                self.rstd(RSTD[:, a:a + n], self.ps[:, b, :n], 1.0 / D, (("ps", b),), (("rstd", a),))
                continue
            b = self.bank()
            for fc in range(KC):
                s = self.ring("xt", 4)
                self.dma("sp" if fc % 2 == 0 else "act", XT[:, s, :n], src[fc, :, t0 + a:t0 + a + n], ("xt", s), (), (("xt", s),))
                q = self.ring("sq", 2)
                self.act(SQ[:, q, :n], XT[:, s, :n], AF.Square, (("xt", s),), (("sq", q),))
                self.mm(self.ps[:, b, :n], self.ones_bf, SQ[:, q, :n], fc == 0, fc == KC - 1,
                        (("sq", q), "cb"), (("ps", b),))
            self.rstd(RSTD[:, a:a + n], self.ps[:, b, :n], 1.0 / D, (("ps", b),), (("rstd", a),))
        for (a, n) in _tiles(0, G):
            for fc in range(KC):
                s = self.ring("xt", 4)
                self.dma("sp" if fc % 2 == 0 else "act", XT[:, s, :n], src[fc, :, t0 + a:t0 + a + n], ("xt", s), (), (("xt", s),))
                self.stt(XN[:, fc, a:a + n], XT[:, s, :n], self.cv(gcol + fc), RSTD[:, a:a + n], ALU.mult, ALU.mult,
                         (("xt", s), ("rstd", a), "cv"), (("xn", fc, a),))

    def post_residual(self, fsrc, rsrc, dst, t0, t1, gcol, ssq_banks, RSTD, XT, SQ=None, nxt_ssq=False, yout=None, YST=None):
        G = t1 - t0
        tl = _tiles(0, G)
        self.barrier()
        for ti, (a, n) in enumerate(tl):
            b = ssq_banks[ti]
            self.rstd(RSTD[:, a:a + n], self.ps[:, b, :n], 1.0 / D, (("ps", b),), (("rstd", a),))
        its = [(ti, a, n, fc) for ti, (a, n) in enumerate(tl) for fc in range(KC)]

        def load(i):
            ti, a, n, fc = its[i]
            s = self.ring("xt", 4)
            s2 = self.ring("xt", 4)
            self.dma("sp", XT[:, s, :n], fsrc[fc, :, t0 + a:t0 + a + n], ("xt", s), ("fscr",), (("xt", s),))
            self.dma("act", XT[:, s2, :n], rsrc[fc, :, t0 + a:t0 + a + n], ("xt", s2), (), (("xt", s2),))
            return s, s2
        nxt = load(0)
        for i, (ti, a, n, fc) in enumerate(its):
            s, s2 = nxt
            self.tt(XT[:, s, :n], XT[:, s, :n], RSTD[:, a:a + n], ALU.mult, (("xt", s), ("rstd", a)), (("xt", s),))
            self.stt(XT[:, s, :n], XT[:, s, :n], self.cv(gcol + fc), XT[:, s2, :n], ALU.mult, ALU.add,
                     (("xt", s), ("xt", s2), "cv", "cv2"), (("xt", s),))
            if i + 1 < len(its):
                nxt = load(i + 1)
            if nxt_ssq:
                q = self.ring("sq", 2)
                self.act(SQ[:, q, :n], XT[:, s, :n], AF.Square, (("xt", s),), (("sq", q),))
                self.mm(self.ps[:, ssq_banks[ti], :n], self.ones_bf, SQ[:, q, :n], fc == 0, fc == KC - 1,
                        (("sq", q), "cb"), (("ps", ssq_banks[ti]),))
            if yout is None:
                self.dma("sp", dst[fc, :, t0 + a:t0 + a + n], XT[:, s, :n], ("xt", s), (("xt", s),), ())
            else:
                nq = n // 128
                b = self.bank()
                for q4 in range(nq):
                    self.tr(self.ps[:, b, q4 * 128:(q4 + 1) * 128], XT[:, s, q4 * 128:(q4 + 1) * 128], self.cm(CM_ID, 128),
                            (("xt", s), "cm"), (("ps", b),))
                k = self.ring("yst", 2)
                self.act(YST[:, k, :n], self.ps[:, b, :n], AF.Copy, (("ps", b),), (("yst", k),))
                self.dma("sp", yout[a:a + n, fc * 128:(fc + 1) * 128].rearrange("(q p) f -> p q f", p=128),
                         YST[:, k, :n].rearrange("p (q f) -> p q f", f=128), ("yst", k), (("yst", k),), ())

    def ffn(self, src, dst, t0, t1, w_gu, w_dn, g_pre, g_post_half, pre_ssq=None, nxt_ssq=False):
        G = t1 - t0
        XN = self.v(PH_OFF, [128, KC, G], BF16)
        ACTB = self.v(PH_OFF + 36864, [128, JC, G], BF16)
        MO = PH_OFF + 36864 + 101376
        RSTD = self.v(MO, [128, 1152])
        XT = self.v(MO + 4608, [128, 4, 512])
        SQ = self.v(MO + 4608 + 8192, [128, 2, 512], BF16)
        FT = self.v(PH_OFF, [128, 2, 512])
        tl = _tiles(0, G)
        self.barrier()
        self.rot = list(range(8))
        self.norm_load(src, t0, t1, g_pre, XN, RSTD, XT, SQ, pre_ssq=pre_ssq)
        self.barrier()
        wgu = w_gu.rearrange("(kc p) n -> p kc n", p=128)
        for j in range(JC):
            s = self.wslot()
            wv = self.wview(s, [128, KC, 256])
            self.wload(s, wv[:, :, 0:128], wgu[:, :, j * 128:(j + 1) * 128])
            self.wload(s, wv[:, :, 128:256], wgu[:, :, DFF + j * 128:DFF + (j + 1) * 128])
            for ti, (a, n) in enumerate(tl):
                bg = self.bank()
                for kc in range(KC):
                    self.mm(self.ps[:, bg, :n], wv[:, kc, 0:128], XN[:, kc, a:a + n], kc == 0, kc == KC - 1,
                            (("w", s),), (("ps", bg),))
                bu = self.bank()
                for kc in range(KC):
                    self.mm(self.ps[:, bu, :n], wv[:, kc, 128:256], XN[:, kc, a:a + n], kc == 0, kc == KC - 1,
                            (("w", s),), (("ps", bu),))
                q = self.ring("sq", 2)
                self.act(SQ[:, q, :n], self.ps[:, bg, :n], AF.Silu, (("ps", bg),), (("sq", q),))
                self.tt(ACTB[:, j, a:a + n], SQ[:, q, :n], self.ps[:, bu, :n], ALU.mult,
                        (("sq", q), ("ps", bu)), ())
        self.barrier()
        nt = len(tl)
        ssq = list(range(8 - nt, 8))
        self.rot = list(range(8 - nt))
        wdn = w_dn.rearrange("(kc p) n -> p kc n", p=128)
        for oc in range(KC):
            s = self.wslot()
            wv = self.wview(s, [128, JC, 128])
            self.wload(s, wv[:, 0:22, :], wdn[:, 0:22, oc * 128:(oc + 1) * 128])
            self.wload(s, wv[:, 22:44, :], wdn[:, 22:44, oc * 128:(oc + 1) * 128])
            for ti, (a, n) in enumerate(tl):
                b = self.bank()
                for kc in range(JC):
                    self.mm(self.ps[:, b, :n], wv[:, kc, :], ACTB[:, kc, a:a + n], kc == 0, kc == JC - 1,
                            (("w", s),), (("ps", b),))
                k = self.ring("ft", 2)
                self.act(FT[:, k, :n], self.ps[:, b, :n], AF.Copy, (("ps", b),), (("ft", k),))
                self.dma("act", self.fT[oc, :, t0 + a:t0 + a + n], FT[:, k, :n], ("ft", k), (("ft", k),), ("fscr",))
                q = self.ring("sq", 2)
                self.tt(SQ[:, q, :n], FT[:, k, :n], FT[:, k, :n], ALU.mult, (("ft", k),), (("sq", q),))
                self.mm(self.ps[:, ssq[ti], :n], self.ones_bf, SQ[:, q, :n], oc == 0, oc == KC - 1,
                        (("sq", q), "cb"), (("ps", ssq[ti]),))
        if nxt_ssq:
            self.rot = list(range(8 - nt))
        self.post_residual(self.fT, src, dst, t0, t1, g_post_half, ssq, RSTD, XT, SQ=SQ, nxt_ssq=nxt_ssq)
        self.rot = list(range(8))

    def mix_layout(self):
        o = PH_OFF
        self.BETA = self.v(o, [16, NT]); o += NT * 4
        self.GG = self.v(o, [16, NT]); o += NT * 4
        self.UN = self.v(o, [128, KC, NT], BF16); o += KC * NT * 2
        self.mix_free = o

    def mixer_qkv(self):
        self.mix_layout()
        o = self.mix_free
        RAW = self.v(o, [128, 2, 180]); o += 2 * 180 * 4
        RAWB = self.v(o, [128, 2, 2052], BF16); o += 2 * 2052 * 2
        DG = self.v(o, [128, 2, 4, 128], BF16); o += 2 * 4 * 128 * 2
        CVO = self.v(o, [128, 2, NT]); o += 2 * NT * 4
        QO = self.v(o, [128, 2, NT], BF16); o += 2 * NT * 2
        RSTD = self.v(o, [128, NT]); o += NT * 4
        XT = self.v(o, [128, 4, 512]); o += 8192
        SQ = self.v(o, [128, 2, 512], BF16); o += 2048
        ST = self.v(o, [64, 2, 256]); o += 2048
        CT = self.v(o, [128, 2, 48]); o += 384
        SSQ1 = self.v(o, [128, NT]); o += NT * 4
        SSQ = [RSTD, SSQ1]
        UN = self.UN
        ident = self.cm(CM_ID, 128)
        self.barrier()
        self.rot = list(range(8))
        self.norm_load(self.h1, 0, NT, CV_G["mix_pre"], UN, RSTD, XT, SQ)
        self.barrier()
        win = self.w_in.rearrange("(kc p) n -> p kc n", p=128)
        tl = _tiles(0, NT)
        tlp = [(a, n) for (a, n) in tl if a < NTP]
        allk = lambda nm, cc_: [(nm, cc_, a_) for (a_, _n) in tl]

        def emit_l2(cc_, a_, n_, q_):
            b3 = self.bank()
            self.mm(self.ps[:, b3, :n_], self.ones_bf, SQ[:, q_, :n_], True, True, (("sq", q_), "cb"), (("ps", b3),))
            self.act(SSQ[cc_][:, a_:a_ + n_], self.ps[:, b3, :n_], AF.Copy, (("ps", b3),), (("ssq", cc_, a_),))

        def emit_conv(c_, cc_, a_, n_):
            b2 = self.bank()
            for j in range(4):
                self.mm(self.ps[:, b2, :n_], DG[:, cc_, j, :], RAWB[:, cc_, a_ + j:a_ + j + n_], j == 0, j == 3,
                        (("dg", cc_), ("rawb", cc_, a_), ("rawb", cc_, a_ - 512)), (("ps", b2),))
            self.act(CVO[:, cc_, a_:a_ + n_], self.ps[:, b2, :n_], AF.Silu, (("ps", b2),), (("cvo", cc_, a_),))
            if c_ < 32:
                q_ = self.ring("sq", 2)
                self.act(SQ[:, q_, :n_], CVO[:, cc_, a_:a_ + n_], AF.Square, (("cvo", cc_, a_),), (("sq", q_),))
                return (cc_, a_, n_, q_)
            self.cp(QO[:, cc_, a_:a_ + n_], CVO[:, cc_, a_:a_ + n_], (("cvo", cc_, a_),), (("qo", cc_, a_),))
            return None

        def epilogue(c_, cc_, pend, pend_l2):
            RS = RAW[:, cc_, 0:176].rearrange("p (s j) -> p s j", j=11)
            if pend_l2 is not None:
                emit_l2(*pend_l2)
            if pend is not None:
                p2 = emit_conv(c_, *pend)
                if p2 is not None:
                    emit_l2(*p2)
            CS = CVO[:, cc_, NTP:NT].rearrange("p (s j) -> p s j", j=8)
            for j in range(4):
                wcol = self.cv(CV_SC + c_ * 4 + j)
                if j == 0:
                    self.ts(CS, RS[:, :, 0:8], wcol, None, ALU.mult, None, (("raw", cc_), "cv"), (("cvo", cc_, NTP),))
                else:
                    self.stt(CS, RS[:, :, j:j + 8], wcol, CS, ALU.mult, ALU.add,
                             (("raw", cc_), ("cvo", cc_, NTP), "cv"), (("cvo", cc_, NTP),))
            self.act(CVO[:, cc_, NTP:NT], CVO[:, cc_, NTP:NT], AF.Silu, (("cvo", cc_, NTP),), (("cvo", cc_, NTP),))
            if c_ < 32:
                q = self.ring("sq", 2)
                self.act(SQ[:, q, :NS], CVO[:, cc_, NTP:NT], AF.Square, (("cvo", cc_, NTP),), (("sq", q),))
                emit_l2(cc_, NTP, NS, q)
                keys = allk("ssq", cc_)
                self.rstd(SSQ[cc_][:, 0:NT], SSQ[cc_][:, 0:NT], 1.0, keys, keys)
                if c_ < 16:
                    self.stt(QO[:, cc_, :], CVO[:, cc_, :], float(128 ** -0.5), SSQ[cc_][:, 0:NT], ALU.mult, ALU.mult,
                             keys + allk("cvo", cc_), allk("qo", cc_))
                else:
                    self.tt(QO[:, cc_, :], CVO[:, cc_, :], SSQ[cc_][:, 0:NT], ALU.mult,
                            keys + allk("cvo", cc_), allk("qo", cc_))
            else:
                self.cp(QO[:, cc_, NTP:NT], CVO[:, cc_, NTP:NT], (("cvo", cc_, NTP),), (("qo", cc_, NTP),))
            self.dma("sp", self.qkvF[c_], QO[:, cc_, :], ("qo", cc_), allk("qo", cc_), ())
            b = self.bank()
            self.tr(self.ps[0:3, b, 0:128], RAW[:, cc_, 176:179], ident, (("raw", cc_), "cm"), (("ps", b),))
            kc_ = self.ring("ct", 2)
            self.cp(CT[:, kc_, :].rearrange("p (s j) -> p s j", j=3), RS[:, :, 8:11], (("raw", cc_),), (("ct", kc_),), eng="pool")
            self.tr(self.ps[0:48, b, 128:256], CT[:, kc_, :], ident, (("ct", kc_), "cm"), (("ps", b),))
            k = self.ring("st", 2)
            self.act(ST[0:3, k, 0:128], self.ps[0:3, b, 0:128], AF.Copy, (("ps", b),), (("st", k),))
            self.act(ST[0:48, k, 128:256], self.ps[0:48, b, 128:256], AF.Copy, (("ps", b),), (("st", k),))
            self.dma("sp", self.nq_p[:, c_ * 128:(c_ + 1) * 128], ST[0:3, k, 0:128], ("st", k), (("st", k),), ())
            self.dma("sp", self.nq_s[:, c_ * 128:(c_ + 1) * 128], ST[0:48, k, 128:256], ("st", k), (("st", k),), ())
        deferred = None

        def qkv_wload(bi_):
            s_ = self.wslot()
            wv_ = self.wview(s_, [128, KC, 256])
            self.wload(s_, wv_, win[:, :, bi_ * 256:(bi_ + 1) * 256])
            return s_, wv_
        nxt_w = qkv_wload(0)
        for bi in range(24):
            s, wv = nxt_w
            if bi + 1 < 24:
                nxt_w = qkv_wload(bi + 1)
            xs = self.ring("xt", 4)
            self.dma("sp", XT[0:48, xs, 0:256], self.sqkv[:, bi * 256:(bi + 1) * 256], ("xt", xs), (), (("xt", xs),))
            for cc in range(2):
                c = bi * 2 + cc
                RS = RAW[:, cc, 0:176].rearrange("p (s j) -> p s j", j=11)
                self.memset(RAWB[:, cc, 0:3], 0.0, (("rawb", cc, -512),), eng="pool")
                b = self.bank()
                self.tr(self.ps[:, b, 0:48], XT[0:48, xs, cc * 128:(cc + 1) * 128], ident[0:48, 0:48],
                        (("xt", xs), "cm"), (("ps", b),))
                self.act(RS[:, :, 0:3], self.ps[:, b, 0:48].rearrange("p (s j) -> p s j", j=3), AF.Copy,
                         (("ps", b),), (("raw", cc),))
                for j in range(4):
                    self.ts(DG[:, cc, j, :], self.ident_bf, self.cv(CV_SC + c * 4 + j), None, ALU.mult, None,
                            ("cb", "cv"), (("dg", cc),))
                pend = None
                pend_l2 = None
                for ti, (a, n) in enumerate(tl):
                    b = self.bank()
                    for kc in range(KC):
                        self.mm(self.ps[:, b, :n], wv[:, kc, cc * 128:(cc + 1) * 128], UN[:, kc, a:a + n],
                                kc == 0, kc == KC - 1, (("w", s),), (("ps", b),))
                    if a < NTP:
                        self.act(RAWB[:, cc, 3 + a:3 + a + n], self.ps[:, b, :n], AF.Copy, (("ps", b),), (("rawb", cc, a),))
                        if a + n == NTP:
                            self.act(RAW[:, cc, 176:179], self.ps[:, b, n - 3:n], AF.Copy, (("ps", b),), (("raw", cc),))
                    else:
                        self.act(RS[:, :, 3:11], self.ps[:, b, 0:128].rearrange("p (s j) -> p s j", j=8), AF.Copy,
                                 (("ps", b),), (("raw", cc),))
                    if ti == 1 and deferred is not None:
                        epilogue(*deferred)
                        deferred = None
                    if pend_l2 is not None:
                        emit_l2(*pend_l2)
                        pend_l2 = None
                    if pend is not None:
                        pend_l2 = emit_conv(c, *pend)
                    pend = (cc, a, n) if a < NTP else None
                deferred = (c, cc, pend, pend_l2)
        epilogue(*deferred)
        s = self.wslot()
        wv = self.wview(s, [128, KC, 32])
        self.wload(s, wv, win[:, :, O_BETA:O_BETA + 32])
        for (a, n) in tl:
            bb = self.bank()
            for kc in range(KC):
                self.mm(self.ps[0:16, bb, :n], wv[:, kc, 0:16], UN[:, kc, a:a + n], kc == 0, kc == KC - 1,
                        (("w", s),), (("ps", bb),))
            ba = self.bank()
            for kc in range(KC):
                self.mm(self.ps[0:16, ba, :n], wv[:, kc, 16:32], UN[:, kc, a:a + n], kc == 0, kc == KC - 1,
                        (("w", s),), (("ps", ba),))
            self.act(self.BETA[:, a:a + n], self.ps[0:16, bb, :n], AF.Sigmoid, (("ps", bb),), (("bg", a),))
            r = self.ring("xt", 4)
            self.act(XT[0:16, r, :n], self.ps[0:16, ba, :n], AF.Exp, (("ps", ba), "cv"), (("xt", r),),
                     bias=self.cv(CV_DT, 1, 16), scale=1.0)
            self.act(XT[0:16, r, :n], XT[0:16, r, :n], AF.Ln, (("xt", r),), (("xt", r),), bias=1.0, scale=1.0)
            self.ts(self.GG[:, a:a + n], XT[0:16, r, :n], self.cv(CV_NA, 1, 16), None, ALU.mult, None,
                    (("xt", r), "cv3"), (("bg", a),))

    def delta_layout(self):
        o = PH_OFF + 2 * NT * 4
        self.OF = self.v(o, [128, NH, NM], BF16); o += NH * NM * 2
        self.dl_free = o
        L = {}

        def al(name, shape, dt=F32):
            nonlocal o
            L[name] = self.v(o, shape, dt)
            n = 1
            for s in shape[1:]:
                n *= s
            o += ((n * (4 if dt == F32 else 2) + 31) // 32) * 32
        al("S", [128, NH, 128]); al("SB", [128, NH, 128], BF16)
        al("QKVC", [128, 2, 48 * 64], BF16)
        for nm in ("GROW", "NBROW", "GAM", "E1", "W2", "E2", "RM"):
            al(nm, [128, 512])
        self.alias_off = o
        for pb_ in range(2):
            for nm in ("GROW", "NBROW", "GAM", "E1", "W2", "E2", "RM"):
                if pb_ == 0:
                    L["%s_0" % nm] = L[nm]
                else:
                    al("%s_1" % nm, [128, 512])
        for nm in ("P0", "P1", "PT0", "PT1", "RB", "TBT0", "QKM0", "KG0", "QG0", "TBT1", "QKM1", "KG1", "QG1"):
            al(nm, [128, 512], BF16)
        al("KDEC0", [128, 1024], BF16); al("KDEC1", [128, 1024], BF16)
        al("DDB", [128, 1024], BF16); al("UUB", [128, 1024], BF16); al("OTB", [128, 512])
        al("TOK", [128, 32]); al("NBTOK", [128, 16]); al("COLS", [128, 32]); al("DCOL", [128, 16])
        al("TOK_1", [128, 32]); al("NBTOK_1", [128, 16]); al("COLS_1", [128, 32]); al("DCOL_1", [128, 16])
        for nm in ("TOK", "NBTOK", "COLS", "DCOL"):
            L[nm + "_0"] = L[nm]
        for nm in ("P0", "P1", "PT0", "PT1", "RB"):
            L[nm + "_0"] = L[nm]
            al(nm + "_1", [128, 512], BF16)
        for pb_ in range(2):
            al("KK_%d" % pb_, [128, 512], BF16); al("QK_%d" % pb_, [128, 512], BF16)
        al("GLR0", [128, 128]); al("GLR1", [128, 128]); al("RHS2", [128, 128])
        for nm in ("QKM2", "KG2", "QG2"):
            al(nm, [128, 512], BF16)
        al("KDEC2", [128, 1024], BF16); al("GLR2", [128, 128])
        for nm in ("TBT", "QKM", "KG", "QG", "KDEC", "GLR"):
            L[nm] = L[nm + "0"]
        al("DT", [128, 4, 128], BF16); al("DD", [128, 4, 128], BF16); al("UU", [128, 4, 128], BF16)
        al("OT", [128, 4, 128])
        o_save = o
        o = self.alias_off
        al("S2", [128, 16, 128])
        al("UBD", [128, 16, 128], BF16)
        assert o <= o_save
        o = o_save
        L["RHS"] = L["RM_1"]
        self.L = L
        print("delta layout end", o)
        assert o <= ARENA_BYTES, o

    def delta_block(self, C, HB, nseg, tok0, qkvc, need_o, otok0, mk, nlev, sample=False, qk="qkvc"):
        L = self.L
        ones_f = self.cm(CM_ONE, 128)
        ident_f = self.cm(CM_ID, 128)
        W = C * HB

        def t3(name, rows=128, dt=None):
            return L[name][0:rows, 0:W].rearrange("p (h t) -> p h t", h=HB)
        b = self.bank()
        self.tr(self.ps[0:C, b, 0:16], self.BETA[0:16, tok0:tok0 + C], ident_f[0:16, 0:16], ("cm",), (("ps", b),))
        self.tr(self.ps[0:C, b, 16:32], self.GG[0:16, tok0:tok0 + C], ident_f[0:16, 0:16], ("cm",), (("ps", b),))
        TOK = L["TOK"][0:C]
        self.act(TOK, self.ps[0:C, b, 0:32], AF.Copy, (("ps", b),), ("tok",))
        NBTOK = L["NBTOK"][0:C]
        self.ts(NBTOK, TOK[:, 0:16], -1.0, None, ALU.mult, None, ("tok",), ("nbtok",))
        b = self.bank()
        self.mm(self.ps[0:C, b, 0:16], mk["TRI"], TOK[:, 16:32], True, True, ("tok", "cm"), (("ps", b),))
        self.mm(self.ps[0:C, b, 16:32], mk["SAME"], TOK[:, 16:32], True, True, ("tok", "cm"), (("ps", b),))
        COLS = L["COLS"][0:C]
        self.act(COLS, self.ps[0:C, b, 0:32], AF.Copy, (("ps", b),), ("cols",))
        DCOL = L["DCOL"][0:C]
        self.tt(DCOL, COLS[:, 16:32], COLS[:, 0:16], ALU.subtract, ("cols",), ("dcol",))
        self.act(DCOL, DCOL, AF.Exp, ("dcol",), ("dcol",))
        for hb in range(NH // HB):
            h0 = hb * HB
            gt = TOK[:, 16 + h0:16 + h0 + HB]
            RHS = t3("RHS", C)

            def bc_h(m):
                return m.unsqueeze(1).to_broadcast([C, HB, C])

            def bc_t(col, n=C):
                return col.unsqueeze(2).to_broadcast([C, HB, n])
            self.tt(RHS, bc_h(mk["TRI"]), bc_t(gt), ALU.mult, ("tok", "cm"), ("rhs",))
            b = self.bank()
            self.mm(self.ps[:, b, 0:W], ones_f[0:C, :], L["RHS"][0:C, 0:W], True, True, ("rhs", "cm"), (("ps", b),))
            self.act(L["GROW"][:, 0:W], self.ps[:, b, 0:W], AF.Copy, (("ps", b),), ("grow",))
            self.tt(RHS, bc_h(mk["ID"]), bc_t(NBTOK[:, h0:h0 + HB]), ALU.mult, ("nbtok", "cm", "rhs"), ("rhs",))
            b = self.bank()
            self.mm(self.ps[:, b, 0:W], ones_f[0:C, :], L["RHS"][0:C, 0:W], True, True, ("rhs", "cm"), (("ps", b),))
            self.act(L["NBROW"][:, 0:W], self.ps[:, b, 0:W], AF.Copy, (("ps", b),), ("nbrow",))
            R2 = L["RHS2"][0:C, 0:HB * nseg].rearrange("p (h s) -> p h s", h=HB)
            self.tt(R2, mk["SEG"].unsqueeze(1).to_broadcast([C, HB, nseg]), bc_t(gt, nseg), ALU.mult,
                    ("tok", "cm"), ("rhs2",))
            b = self.bank()
            self.mm(self.ps[:, b, 0:HB * nseg], ones_f[0:C, :], L["RHS2"][0:C, 0:HB * nseg], True, True,
                    ("rhs2", "cm"), (("ps", b),))
            GLR = L["GLR"][:, 0:HB * nseg]
            self.act(GLR, self.ps[:, b, 0:HB * nseg], AF.Exp, (("ps", b),), ("glr",))
            GLR3 = GLR.rearrange("p (h s) -> p h s", h=HB)
            self.act(L["GAM"][:, 0:W], L["GROW"][:, 0:W], AF.Exp, ("grow",), ("gam",))
            GAM = t3("GAM")
            kq = qkvc.rearrange("p (c t) -> p c t", c=48)
            self.tt(t3("KG"), kq[:, 16 + h0:16 + h0 + HB, :], GAM, ALU.mult, ("gam", qk), ("kg",))
            if need_o:
                self.tt(t3("QG"), kq[:, h0:h0 + HB, :], GAM, ALU.mult, ("gam", qk), ("qg",), eng="pool")
            GROWc = t3("GROW", C)
            gcol = COLS[:, h0:h0 + HB]
            E1 = t3("E1", C); W2 = t3("W2", C); E2 = t3("E2", C)
            self.tt(E1, GROWc, bc_t(gcol), ALU.subtract, ("grow", "cols"), ("e1",))
            self.ts(E1, E1, 0.0, None, ALU.min, None, ("e1",), ("e1",))
            self.act(E1, E1, AF.Exp, ("e1",), ("e1",))
            self.tt(W2, E1, bc_h(mk["US"]), ALU.mult, ("e1", "cm"), ("w2",))
            self.tt(W2, W2, t3("NBROW", C), ALU.mult, ("w2", "nbrow"), ("w2",))
            self.tt(E1, E1, bc_h(mk["UI"]), ALU.mult, ("e1", "cm"), ("e1",))
            self.tt(E2, GROWc, bc_t(gcol), ALU.subtract, ("grow", "cols"), ("e2",))
            self.ts(E2, E2, -1.0, 0.0, ALU.mult, ALU.min, ("e2",), ("e2",))
            self.act(E2, E2, AF.Exp, ("e2",), ("e2",))
            self.tt(E2, E2, bc_h(mk["LS"]), ALU.mult, ("e2", "cm"), ("e2",))
            self.tt(E2, E2, bc_t(NBTOK[:, h0:h0 + HB]), ALU.mult, ("e2", "nbtok"), ("e2",))
            bk = self.bank()
            bq = self.bank()
            bt = self.bank()
            psT = self.ps[:, bt, :].bitcast(BF16)
            for j in range(HB):
                kf = kq[:, 16 + h0 + j, :]
                qf = kq[:, h0 + j, :]
                self.mm(self.ps[0:C, bk, j * C:(j + 1) * C], kf, kf, True, True, (qk,), (("ps", bk),))
                if need_o:
                    self.mm(self.ps[0:C, bq, j * C:(j + 1) * C], kf, qf, True, True, (qk,), (("ps", bq),))
                self.tr(psT[0:C, j * 128:(j + 1) * 128], kf, self.ident_bf, (qk, "cb"), (("ps", bt),))
            pk = self.ps[0:C, bk, 0:W]
            RM = L["RM"][0:C, 0:W]
            self.tt(RM, pk, L["W2"][0:C, 0:W], ALU.mult, (("ps", bk), "w2"), ("rm",))
            Pc = L["P0"][0:C, 0:W]
            PTc = L["PT0"][0:C, 0:W]
            self.cp(Pc, RM, ("rm",), ("p0",), eng="pool")
            self.tt(PTc, pk, L["E2"][0:C, 0:W], ALU.mult, (("ps", bk), "e2"), ("pt0",))
            self.tt(t3("RM", C), t3("RM", C), bc_h(mk["ID"]), ALU.add, ("rm", "cm"), ("rm",))
            RB = L["RB"][0:C, 0:W]
            self.cp(RB, RM, ("rm",), ("rb",), eng="pool")
            if need_o:
                self.tt(L["QKM"][0:C, 0:W], self.ps[0:C, bq, 0:W], L["E1"][0:C, 0:W], ALU.mult,
                        (("ps", bq), "e1"), ("qkm",))
            KD = L["KDEC"][0:C, 0:HB * 128].rearrange("p (h d) -> p h d", h=HB)
            self.tt(KD, psT[0:C, 0:HB * 128].rearrange("p (h d) -> p h d", h=HB),
                    DCOL[:, h0:h0 + HB].unsqueeze(2).to_broadcast([C, HB, 128]), ALU.mult,
                    (("ps", bt), "dcol"), ("kdec",))
            cur = 0
            for lv in range(nlev):
                Pn = L["P%d" % (1 - cur)][0:C, 0:W]
                PTn = L["PT%d" % (1 - cur)][0:C, 0:W]
                pkey, ptkey = "p%d" % cur, "pt%d" % cur
                pnkey, ptnkey = "p%d" % (1 - cur), "pt%d" % (1 - cur)
                ba = self.bank()
                bb = self.bank()
                for j in range(HB):
                    sl = slice(j * C, (j + 1) * C)
                    if lv < nlev - 1:
                        self.mm(self.ps[0:C, ba, sl], PTc[:, sl], Pc[:, sl], True, True, (pkey, ptkey), (("ps", ba),))
                    self.mm(self.ps[0:C, bb, sl], Pc[:, sl], PTc[:, sl], True, True, (pkey, ptkey), (("ps", bb),))
                if lv < nlev - 1:
                    self.act(Pn, self.ps[0:C, ba, 0:W], AF.Copy, (("ps", ba),), (pnkey,))
                self.cp(PTn, self.ps[0:C, bb, 0:W], (("ps", bb),), (ptnkey,))
                bc = self.bank()
                for j in range(HB):
                    sl = slice(j * C, (j + 1) * C)
                    self.mm(self.ps[0:C, bc, sl], PTn[:, sl], RB[:, sl], True, True, (ptnkey, "rb"), (("ps", bc),))
                self.tt(RM, RM, self.ps[0:C, bc, 0:W], ALU.add, ("rm", ("ps", bc)), ("rm",))
                if lv < nlev - 1:
                    self.cp(RB, RM, ("rm",), ("rb",), eng="pool")
                Pc, PTc = Pn, PTn
                cur = 1 - cur
            self.tt(t3("TBT", C), t3("RM", C), bc_t(TOK[:, h0:h0 + HB]), ALU.mult, ("rm", "tok"), ("tbt",))
            for j in range(HB):
                h = h0 + j
                sl = slice(j * C, (j + 1) * C)
                KGh = L["KG"][:, sl]
                vf = kq[:, 32 + h, :]
                r = self.ring("chain", 4)
                DT = L["DT"][:, r, 0:C]
                DD = L["DD"][0:C, r, :]
                UU = L["UU"][0:C, r, :]
                OT = L["OT"][:, r, 0:C]
                if sample:
                    sr = h % 2
                    SSv = self.v_ss[sr]
                    SSB = L["SB"][:, 0:16, :]
                    self.dma("sp", SSv, self.sdelta[:, h].rearrange("s d v -> d s v"), ("ss", sr), (), (("ss", sr),))
                    self.cp(SSB, SSv, (("ss", sr),), ("ssb",), eng="pool")
                    bK = self.bank()
                    for sg in range(nseg):
                        self.mm(self.ps[:, bK, sg * LS:(sg + 1) * LS], SSB[:, sg, :], KGh[:, sg * LS:(sg + 1) * LS],
                                True, True, ("ssb", "kg"), (("ps", bK),))
                else:
                    SBh = L["SB"][:, h, :]
                    bK = self.bank()
                    self.mm(self.ps[:, bK, 0:C], SBh, KGh, True, True, (("sb", h), "kg"), (("ps", bK),))
                self.tt(DT, vf, self.ps[:, bK, 0:C], ALU.subtract, (qk, ("ps", bK)), (("dt", r),))
                bT = self.bank()
                pT = self.ps[:, bT, :].bitcast(BF16)
                self.tr(pT[0:C, 0:128], DT, self.ident_bf, (("dt", r), "cb"), (("ps", bT),))
                self.act(DD, pT[0:C, 0:128], AF.Copy, (("ps", bT),), (("dd", r),))
                bU = self.bank()
                self.mm(self.ps[0:C, bU, 0:128], L["TBT"][0:C, sl], DD, True, True, ("tbt", ("dd", r)), (("ps", bU),))
                self.act(UU, self.ps[0:C, bU, 0:128], AF.Copy, (("ps", bU),), (("uu", r),))
                if need_o:
                    bO = self.bank()
                    if sample:
                        for sg in range(nseg):
                            self.mm(self.ps[:, bO, sg * LS:(sg + 1) * LS], SSB[:, sg, :],
                                    L["QG"][:, j * C + sg * LS:j * C + (sg + 1) * LS], True, True,
                                    ("ssb", "qg"), (("ps", bO),))
                    else:
                        self.mm(self.ps[:, bO, 0:C], SBh, L["QG"][:, sl], True, True, (("sb", h), "qg"), (("ps", bO),))
                    self.mm(self.ps[:, bO, 128:128 + C], UU, L["QKM"][0:C, sl], True, True,
                            (("uu", r), "qkm"), (("ps", bO),))
                    self.act(OT, self.ps[:, bO, 0:C], AF.Copy, (("ps", bO),), (("ot", r),))
                    self.tt(self.OF[:, h, otok0:otok0 + C], OT, self.ps[:, bO, 128:128 + C], ALU.add,
                            (("ot", r), ("ps", bO)), ())
                KDh = L["KDEC"][0:C, j * 128:(j + 1) * 128]
                if sample:
                    UBD = L["UBD"]
                    self.tt(UBD, UU.unsqueeze(1).to_broadcast([128, 16, 128]),
                            mk["SEG"].unsqueeze(2).to_broadcast([128, 16, 128]), ALU.mult,
                            (("uu", r), "cm"), ("ubd",))
                    gl = GLR3[:, j, :]
                    self.tt(SSv, SSv, gl.unsqueeze(2).to_broadcast([128, 16, 128]), ALU.mult,
                            (("ss", sr), "glr"), (("ss", sr),))
                    for q4 in range(4):
                        bS = self.bank()
                        self.mm(self.ps[:, bS, :], KDh, UBD[:, 4 * q4:4 * q4 + 4, :].rearrange("p a b -> p (a b)"),
                                True, True, ("kdec", "ubd"), (("ps", bS),))
                        sv = SSv[:, 4 * q4:4 * q4 + 4, :].rearrange("p a b -> p (a b)")
                        self.tt(sv, sv, self.ps[:, bS, :], ALU.add, (("ss", sr), ("ps", bS)), (("ss", sr),))
                    self.dma("sp", self.nd_s[:, h].rearrange("s d v -> d s v"), SSv, ("ss", sr), (("ss", sr),), ())
                else:
                    bS = self.bank()
                    self.mm(self.ps[:, bS, 0:128], KDh, UU, True, True, ("kdec", ("uu", r)), (("ps", bS),))
                    Sh = L["S"][:, h, :]
                    self.stt(Sh, Sh, GLR[:, j:j + 1], self.ps[:, bS, 0:128], ALU.mult, ALU.add,
                             (("s", h), "glr", ("ps", bS)), (("s", h),))
                    self.cp(SBh, Sh, (("s", h),), (("sb", h),), eng="pool")

    def p_prep(self, c, hb, pb, ob, qkvc, qk, need_o, mk):
        C, HB, W, nlev = 64, 8, 512, 5
        L = self.L
        ones_f = self.cm(CM_ONE, 128)
        ident_f = self.cm(CM_ID, 128)
        tok0 = c * 64
        cp_ = c % 2

        def T(name):
            return L["%s_%d" % (name, pb)]

        def K(name):
            return (name, pb)

        def t3(name, rows=128):
            return T(name)[0:rows, 0:W].rearrange("p (h t) -> p h t", h=HB)

        def bc_h(m):
            return m.unsqueeze(1).to_broadcast([C, HB, C])

        def bc_t(col, n=C):
            return col.unsqueeze(2).to_broadcast([C, HB, n])
        TOK = L["TOK_%d" % cp_][0:C]
        NBTOK = L["NBTOK_%d" % cp_][0:C]
        COLS = L["COLS_%d" % cp_][0:C]
        DCOL = L["DCOL_%d" % cp_][0:C]
        kt, kn, kc_, kd = ("tok", cp_), ("nbtok", cp_), ("cols", cp_), ("dcol", cp_)
        if hb == 0:
            b = self.bank()
            self.tr(self.ps[0:C, b, 0:16], self.BETA[0:16, tok0:tok0 + C], ident_f[0:16, 0:16], ("cm",), (("ps", b),))
            self.tr(self.ps[0:C, b, 16:32], self.GG[0:16, tok0:tok0 + C], ident_f[0:16, 0:16], ("cm",), (("ps", b),))
            self.act(TOK, self.ps[0:C, b, 0:32], AF.Copy, (("ps", b),), (kt,))
            self.ts(NBTOK, TOK[:, 0:16], -1.0, None, ALU.mult, None, (kt,), (kn,), eng="pool")
            b = self.bank()
            self.mm(self.ps[0:C, b, 0:16], mk["TRI"], TOK[:, 16:32], True, True, (kt, "cm"), (("ps", b),))
            self.mm(self.ps[0:C, b, 16:32], mk["SAME"], TOK[:, 16:32], True, True, (kt, "cm"), (("ps", b),))
            self.act(COLS, self.ps[0:C, b, 0:32], AF.Copy, (("ps", b),), (kc_,))
            self.tt(DCOL, COLS[:, 16:32], COLS[:, 0:16], ALU.subtract, (kc_,), (kd,), eng="pool")
            self.act(DCOL, DCOL, AF.Exp, (kd,), (kd,))
            yield
        h0 = hb * HB
        gt = TOK[:, 16 + h0:16 + h0 + HB]
        kq = qkvc.rearrange("p (c t) -> p c t", c=48)
        bk = self.bank()
        bq = self.bank() if need_o else None
        bt = self.bank()
        psT = self.ps[:, bt, :].bitcast(BF16)
        for j in range(HB):
            kf = kq[:, 16 + h0 + j, :]
            qf = kq[:, h0 + j, :]
            self.mm(self.ps[0:C, bk, j * C:(j + 1) * C], kf, kf, True, True, (qk,), (("ps", bk),))
            if need_o:
                self.mm(self.ps[0:C, bq, j * C:(j + 1) * C], kf, qf, True, True, (qk,), (("ps", bq),))
            self.tr(psT[0:C, j * 128:(j + 1) * 128], kf, self.ident_bf, (qk, "cb"), (("ps", bt),))
        KK = T("KK")[0:C, 0:W]
        self.act(KK, self.ps[0:C, bk, 0:W], AF.Copy, (("ps", bk),), (K("kk"),))
        if need_o:
            QK = T("QK")[0:C, 0:W]
            self.act(QK, self.ps[0:C, bq, 0:W], AF.Copy, (("ps", bq),), (K("qk"),))
        RHS = t3("W2", C)
        self.tt(RHS, bc_h(mk["TRI"]), bc_t(gt), ALU.mult, (kt, "cm", K("w2")), (K("w2"),))
        b1 = self.bank()
        self.mm(self.ps[:, b1, 0:W], ones_f[0:C, :], T("W2")[0:C, 0:W], True, True, (K("w2"), "cm"), (("ps", b1),))
        self.act(T("GROW")[:, 0:W], self.ps[:, b1, 0:W], AF.Copy, (("ps", b1),), (K("grow"),))
        RHSb = t3("E2", C)
        self.tt(RHSb, bc_h(mk["ID"]), bc_t(NBTOK[:, h0:h0 + HB]), ALU.mult, (kn, "cm", K("e2")), (K("e2"),), eng="pool")
        b2 = self.bank()
        self.mm(self.ps[:, b2, 0:W], ones_f[0:C, :], T("E2")[0:C, 0:W], True, True, (K("e2"), "cm"), (("ps", b2),))
        MB = t3("NBROW", C)
        self.tt(MB, self.ps[0:C, b2, 0:W].rearrange("p (h t) -> p h t", h=HB), bc_h(mk["US"]), ALU.mult,
                (("ps", b2), "cm"), (K("mb"),))
        b3 = self.bank()
        self.mm(self.ps[:, b3, 0:HB], ones_f[0:C, :], gt, True, True, (kt, "cm"), (("ps", b3),))
        GLR = L["GLR%d" % ob][:, 0:HB]
        self.act(GLR, self.ps[:, b3, 0:HB], AF.Exp, (("ps", b3),), (("glr", ob),))
        KD = L["KDEC%d" % ob][0:C, 0:HB * 128].rearrange("p (h d) -> p h d", h=HB)
        self.tt(KD, psT[0:C, 0:HB * 128].rearrange("p (h d) -> p h d", h=HB),
                DCOL[:, h0:h0 + HB].unsqueeze(2).to_broadcast([C, HB, 128]), ALU.mult,
                (("ps", bt), kd), (("kdec", ob),))
        yield
        GROWc = t3("GROW", C)
        gcol = COLS[:, h0:h0 + HB]
        E1 = t3("E1", C); W2 = t3("W2", C); E2 = t3("E2", C)
        self.tt(E1, GROWc, bc_t(gcol), ALU.subtract, (K("grow"), kc_), (K("e1"),))
        self.tt(E2, bc_t(gcol), GROWc, ALU.subtract, (K("grow"), kc_, K("e2")), (K("e2"),))
        self.act(T("GAM")[:, 0:W], T("GROW")[:, 0:W], AF.Exp, (K("grow"),), (K("gam"),))
        self.ts(E1, E1, 0.0, None, ALU.min, None, (K("e1"),), (K("e1"),))
        self.ts(E2, E2, 0.0, None, ALU.min, None, (K("e2"),), (K("e2"),))
        yield
        self.act(E1, E1, AF.Exp, (K("e1"),), (K("e1"),))
        self.act(E2, E2, AF.Exp, (K("e2"),), (K("e2"),))
        GAM = t3("GAM")
        self.tt(L["KG%d" % ob][:, 0:W].rearrange("p (h t) -> p h t", h=HB),
                kq[:, 16 + h0:16 + h0 + HB, :], GAM, ALU.mult, (K("gam"), qk), (("kg", ob),))
        if need_o:
            self.tt(L["QG%d" % ob][:, 0:W].rearrange("p (h t) -> p h t", h=HB), kq[:, h0:h0 + HB, :], GAM, ALU.mult,
                    (K("gam"), qk), (("qg", ob),), eng="pool")
        yield
        self.stt(W2, E1, 1.0, MB, ALU.min, ALU.mult, (K("e1"), K("mb")), (K("w2"),))
        self.stt(E2, E2, 1.0, bc_h(mk["LS"]), ALU.min, ALU.mult, (K("e2"), "cm"), (K("e2"),))
        yield
        RM = T("RM")[0:C, 0:W]
        Pc = T("P0")[0:C, 0:W]
        PTc = T("PT0")[0:C, 0:W]
        self.tt(Pc, KK, T("W2")[0:C, 0:W], ALU.mult, (K("kk"), K("w2")), (K("p0"),), eng="pool")
        self.tt(t3("E2", C), t3("E2", C), bc_t(NBTOK[:, h0:h0 + HB]), ALU.mult, (K("e2"), kn), (K("e2"),), eng="pool")
        self.tt(RM, KK, T("W2")[0:C, 0:W], ALU.mult, (K("kk"), K("w2")), (K("rm"),))
        yield
        self.tt(PTc, KK, T("E2")[0:C, 0:W], ALU.mult, (K("kk"), K("e2")), (K("pt0"),))
        self.tt(t3("RM", C), t3("RM", C), bc_h(mk["ID"]), ALU.add, (K("rm"), "cm"), (K("rm"),), eng="pool")
        RB = T("RB")[0:C, 0:W]
        if need_o:
            self.stt(E1, E1, 1.0, bc_h(mk["UI"]), ALU.min, ALU.mult, (K("e1"), "cm"), (K("e1"),))
            self.tt(L["QKM%d" % ob][0:C, 0:W], QK, T("E1")[0:C, 0:W], ALU.mult, (K("qk"), K("e1")), (("qkm", ob),))
        yield "HALF"
        self.act(RB, RM, AF.Copy, (K("rm"),), (K("rb"),))
        cur = 0

        def squares(lv, Pc, PTc, cur):
            Pn = T("P%d" % (1 - cur))[0:C, 0:W]
            PTn = T("PT%d" % (1 - cur))[0:C, 0:W]
            pkey, ptkey = K("p%d" % cur), K("pt%d" % cur)
            ba = self.bank()
            bb = self.bank()
            for j in range(HB):
                sl = slice(j * C, (j + 1) * C)
                if lv < nlev - 1:
                    self.mm(self.ps[0:C, ba, sl], PTc[:, sl], Pc[:, sl], True, True, (pkey, ptkey), (("ps", ba),))
                self.mm(self.ps[0:C, bb, sl], Pc[:, sl], PTc[:, sl], True, True, (pkey, ptkey), (("ps", bb),))
            if lv < nlev - 1:
                self.act(Pn, self.ps[0:C, ba, 0:W], AF.Copy, (("ps", ba),), (K("p%d" % (1 - cur)),))
            self.act(PTn, self.ps[0:C, bb, 0:W], AF.Copy, (("ps", bb),), (K("pt%d" % (1 - cur)),))
            return Pn, PTn
        Pn, PTn = squares(0, Pc, PTc, cur)
        for lv in range(nlev):
            ptnkey = K("pt%d" % (1 - cur))
            Pc, PTc = Pn, PTn
            cur = 1 - cur
            yield
            bc = self.bank()
            for j in range(HB):
                sl = slice(j * C, (j + 1) * C)
                self.mm(self.ps[0:C, bc, sl], PTc[:, sl], RB[:, sl], True, True, (ptnkey, K("rb")), (("ps", bc),))
            if lv + 1 < nlev:
                Pn, PTn = squares(lv + 1, Pc, PTc, cur)
            self.tt(RM, RM, self.ps[0:C, bc, 0:W], ALU.add, (K("rm"), ("ps", bc)), (K("rm"),))
            if lv < nlev - 1:
                yield
                self.act(RB, RM, AF.Copy, (K("rm"),), (K("rb"),))
        yield
        self.tt(L["TBT%d" % pb][0:C, 0:W].rearrange("p (h t) -> p h t", h=HB), t3("RM", C),
                bc_t(TOK[:, h0:h0 + HB]), ALU.mult, (K("rm"), kt), (("tbt", pb),))

    def p_chain(self, c, hb, pb, ob, qkvc, qk, need_o):
        C, HB, W = 64, 8, 512
        L = self.L
        h0 = hb * HB
        kq = qkvc.rearrange("p (c t) -> p c t", c=48)
        SBk = [("sb", h0 + j) for j in range(HB)]
        Sk = [("s", h0 + j) for j in range(HB)]
        KG = L["KG%d" % ob]
        bK = self.bank()
        for j in range(HB):
            self.mm(self.ps[:, bK, j * C:(j + 1) * C], L["SB"][:, h0 + j, :], KG[:, j * C:(j + 1) * C], True, True,
                    (("sb", h0 + j), ("kg", ob)), (("ps", bK),))
        if need_o:
            bO1 = self.bank()
            for j in range(HB):
                self.mm(self.ps[:, bO1, j * C:(j + 1) * C], L["SB"][:, h0 + j, :], L["QG%d" % ob][:, j * C:(j + 1) * C],
                        True, True, (("sb", h0 + j), ("qg", ob)), (("ps", bO1),))
        DT = L["DT"].rearrange("p a b -> p (a b)")
        self.tt(DT.rearrange("p (h t) -> p h t", h=HB), kq[:, 32 + h0:32 + h0 + HB, :],
                self.ps[:, bK, 0:W].rearrange("p (h t) -> p h t", h=HB), ALU.subtract, (qk, ("ps", bK)), ("dt",))
        if need_o:
            OT = L["OTB"]
            self.act(OT, self.ps[:, bO1, 0:W], AF.Copy, (("ps", bO1),), ("ot",))
        yield
        bT = self.bank()
        pT = self.ps[:, bT, :].bitcast(BF16)
        for j in range(HB):
            self.tr(pT[0:C, j * 128:(j + 1) * 128], DT[:, j * C:(j + 1) * C], self.ident_bf, ("dt", "cb"), (("ps", bT),))
        DD = L["DDB"][0:C, :]
        self.act(DD, pT[0:C, 0:1024], AF.Copy, (("ps", bT),), ("dd",))
        yield
        TBT = L["TBT%d" % pb]
        bU = [self.bank(), self.bank()]
        for j in range(HB):
            self.mm(self.ps[0:C, bU[j // 4], (j % 4) * 128:(j % 4 + 1) * 128], TBT[0:C, j * C:(j + 1) * C],
                    DD[:, j * 128:(j + 1) * 128], True, True, (("tbt", pb), "dd"), (("ps", bU[j // 4]),))
        UU = L["UUB"][0:C, :]
        self.act(UU[:, 0:512], self.ps[0:C, bU[0], :], AF.Copy, (("ps", bU[0]),), ("uu0",))
        self.cp(UU[:, 512:1024], self.ps[0:C, bU[1], :], (("ps", bU[1]),), ("uu1",))
        yield
        KD = L["KDEC%d" % ob]
        bS = [self.bank(), self.bank()]
        for j in range(HB):
            self.mm(self.ps[:, bS[j // 4], (j % 4) * 128:(j % 4 + 1) * 128], KD[0:C, j * 128:(j + 1) * 128],
                    UU[:, j * 128:(j + 1) * 128], True, True, (("kdec", ob), "uu%d" % (j // 4)), (("ps", bS[j // 4]),))
        if need_o:
            bO2 = self.bank()
            for j in range(HB):
                self.mm(self.ps[:, bO2, j * C:(j + 1) * C], UU[:, j * 128:(j + 1) * 128],
                        L["QKM%d" % ob][0:C, j * C:(j + 1) * C], True, True,
                        ("uu%d" % (j // 4), ("qkm", ob)), (("ps", bO2),))
        S8 = L["S"][:, h0:h0 + HB, :]
        GLR = L["GLR%d" % ob][:, 0:HB]
        self.tt(S8, S8, GLR.unsqueeze(2).to_broadcast([128, HB, 128]), ALU.mult, Sk + [("glr", ob)], Sk, eng="pool")
        for q in range(2):
            s4 = L["S"][:, h0 + 4 * q:h0 + 4 * q + 4, :].rearrange("p a b -> p (a b)")
            self.tt(s4, s4, self.ps[:, bS[q], :], ALU.add, Sk[4 * q:4 * q + 4] + [("ps", bS[q])], Sk[4 * q:4 * q + 4])
        self.act(L["SB"][:, h0:h0 + HB, :], S8, AF.Copy, Sk, SBk)
        if need_o:
            otok0 = c * 64 - M0
            self.tt(self.OF[:, h0:h0 + HB, otok0:otok0 + C], OT.rearrange("p (h t) -> p h t", h=HB),
                    self.ps[:, bO2, 0:W].rearrange("p (h t) -> p h t", h=HB), ALU.add, ("ot", ("ps", bO2)), ())
        yield

    def delta_prompt_pipelined(self, mk):
        L = self.L
        self.rot = list(range(8))
        units = [(c, hb) for c in range(NTP // 64) for hb in range(2)]
        qslot = {}

        def step(g):
            try:
                return next(g) or True
            except StopIteration:
                return False
        preps = {}
        info = {}

        def start_prep(i):
            c, hb = units[i]
            if hb == 0:
                s = self.ring("qkvc", 2)
                qslot[c] = s
                self.dma("sp", L["QKVC"][:, s, :].rearrange("p (c t) -> p c t", c=48),
                         self.qkvF[:, :, c * 64:(c + 1) * 64].rearrange("c p t -> p c t"), ("qkvc", s), (), (("qkvc", s),))
            s = qslot[c]
            qc = L["QKVC"][:, s, :]
            need_o = c * 64 >= M0
            info[i] = (c, hb, i % 2, i % 3, qc, ("qkvc", s), need_o)
            preps[i] = self.p_prep(c, hb, i % 2, i % 3, qc, ("qkvc", s), need_o, mk)
        n = len(units)
        start_prep(0)
        while step(preps[0]) != "HALF":
            pass
        chain = None
        for i in range(n):
            if i + 1 < n:
                start_prep(i + 1)
            a_live = i + 1 < n
            b_live = True
            c_live = chain is not None
            while a_live or b_live or c_live:
                if b_live:
                    b_live = bool(step(preps[i]))
                if a_live:
                    if step(preps[i + 1]) == "HALF":
                        a_live = False
                if c_live:
                    c_live = bool(step(chain))
            c, hb, pb, ob, qc, qk, need_o = info[i]
            chain = self.p_chain(c, hb, pb, ob, qc, qk, need_o)
        while step(chain):
            pass

    def delta(self):
        self.delta_layout()
        L = self.L
        self.barrier()
        self.rot = list(range(8))
        self.memset(L["S"], 0.0, [("s", h) for h in range(NH)])
        self.memset(L["SB"], 0.0, [("sb", h) for h in range(NH)], eng="pool")
        mk = dict(TRI=self.cm(CM_TRI, 64, 64), UI=self.cm(CM_UI, 64, 64), US=self.cm(CM_US, 64, 64),
                  LS=self.cm(CM_LS, 64, 64), SAME=self.cm(CM_ONE, 64, 64), ID=self.cm(CM_ID, 64, 64),
                  SEG=self.cm(CM_ONE, 1, 64))
        self.delta_prompt_pipelined(mk)
        self.dma("sp", self.nd_p.rearrange("h d v -> d h v"), L["S"], ("sfin",), [("s", h) for h in range(NH)], ())
        self.barrier()
        self.rot = list(range(8))
        mk8 = dict(TRI=self.cm(CM_TRI8, 128), UI=self.cm(CM_UI8, 128), US=self.cm(CM_US8, 128),
                   LS=self.cm(CM_LS8, 128), SAME=self.cm(CM_SAME8, 128), ID=self.cm(CM_ID, 128),
                   SEG=self.cm(CM_SEG, 16))
        self.v_ss = [L["S"][:, 0:16, :], L["S2"]]
        qc = L["QKVC"].rearrange("p a b -> p (a b)")
        self.dma("sp", qc.rearrange("p (c t) -> p c t", c=48),
                 self.qkvF[:, :, NTP:NT].rearrange("c p t -> p c t"), ("qkvc", 0), (), (("qkvc", 0),))
        self.delta_block(128, 4, 16, NTP, qc, True, NMAIN, mk8, 2, sample=True, qk=("qkvc", 0))

    def mixer_tail(self):
        NU = NM + 32
        U0 = M0 - 32
        o = PH_OFF
        RSTD = self.v(o, [128, NU]); o += NU * 4
        SQ = self.v(o, [128, 2, 512], BF16); o += 2048
        MU = self.v(o, [128, NM]); o += NM * 4
        RS = self.v(o, [128, NM]); o += NM * 4
        assert o <= PH_OFF + 2 * NT * 4
        o = PH_OFF + 2 * NT * 4
        OF = self.OF; o += NH * NM * 2
        UN2 = self.v(o, [128, KC, NU], BF16); o += KC * NU * 2
        CB = self.v(o, [128, KC, NM], BF16); o += KC * NM * 2
        XT = self.v(o, [128, 4, 512]); o += 8192
        GPB = self.v(o, [128, 1056], BF16); o += 1056 * 2
        GP30 = self.v(o, [128, 32]); o += 128
        GSX = self.v(o, [128, 608]); o += 608 * 4
        DG31 = self.v(o, [128, 31, 128], BF16); o += 31 * 128 * 2
        CO = self.v(o, [128, NM]); o += NM * 4
        assert o <= ARENA_BYTES, o
        ident = self.cm(CM_ID, 128)
        win = self.w_in.rearrange("(kc p) n -> p kc n", p=128)
        self.barrier()
        self.rot = list(range(8))
        self.norm_load(self.h1, U0, NT, CV_G["mix_pre"], UN2, RSTD, XT, SQ)
        self.barrier()
        tlm = _tiles(0, NM)
        for zb in range(8):
            s = self.wslot()
            wv = self.wview(s, [128, KC, 256])
            self.wload(s, wv, win[:, :, O_Z + zb * 256:O_Z + (zb + 1) * 256])
            for cc in range(2):
                h = zb * 2 + cc
                SS = MU if cc == 0 else RS
                zbanks = []
                for (a, n) in tlm:
                    ok = ("of", h, a)
                    q = self.ring("sq", 2)
                    self.act(SQ[:, q, :n], OF[:, h, a:a + n], AF.Square, (ok,), (("sq", q),))
                    b1 = self.bank()
                    self.mm(self.ps[:, b1, :n], self.ones_bf, SQ[:, q, :n], True, True, (("sq", q), "cb"), (("ps", b1),))
                    b = self.bank()
                    for kc in range(KC):
                        self.mm(self.ps[:, b, :n], wv[:, kc, cc * 128:(cc + 1) * 128], UN2[:, kc, 32 + a:32 + a + n],
                                kc == 0, kc == KC - 1, (("w", s),), (("ps", b),))
                    zbanks.append(b)
                    self.act(SS[:, a:a + n], self.ps[:, b1, :n], AF.Copy, (("ps", b1),), (("ss", cc, a),))
                    r2 = self.ring("xt", 4)
                    self.act(XT[:, r2, :n], self.ps[:, b, :n], AF.Silu, (("ps", b),), (("xt", r2),))
                    self.tt(OF[:, h, a:a + n], OF[:, h, a:a + n], XT[:, r2, :n], ALU.mult, (ok, ("xt", r2)), (ok,))
                sk = [("ss", cc, a) for (a, n) in tlm]
                self.rstd(SS[:, 0:NM], SS[:, 0:NM], 1.0 / 128, sk, sk)
                self.stt(OF[:, h, :], OF[:, h, :], self.cv(CV_ON), SS[:, 0:NM], ALU.mult, ALU.mult,
                         sk + [("of", h, a) for (a, n) in tlm] + ["cv"], [("of", h, a) for (a, n) in tlm])
        self.barrier()
        self.rot = list(range(8))
        SUMB = [(2, 0), (3, 0), (4, 0)]
        SSQB = [(5, 0), (6, 0), (7, 0)]
        GS = GSX.rearrange("p (s j) -> p s j", j=38)
        tlu = _tiles(0, NU)
        def sgt_load(c_):
            for q4 in range(4):
                self.dma("sp", MU[0:120, (c_ % 2) * 512 + q4 * 128:(c_ % 2) * 512 + (q4 + 1) * 128],
                         self.sglu[q4 * 120:(q4 + 1) * 120, c_ * 128:(c_ + 1) * 128],
                         ("sgt", c_ % 2), (), (("sgt", c_ % 2),))
        sgt_load(0)

        def glu_wload(c_):
            s_ = self.wslot()
            wv_ = self.wview(s_, [128, KC, 256])
            self.wload(s_, wv_[:, :, 0:128], win[:, :, O_GLU + c_ * 128:O_GLU + (c_ + 1) * 128])
            self.wload(s_, wv_[:, :, 128:256], win[:, :, O_GLU + 2048 + c_ * 128:O_GLU + 2048 + (c_ + 1) * 128])
            return s_, wv_
        nxt_w = glu_wload(0)
        for c in range(KC):
            s, wv = nxt_w
            if c + 1 < KC:
                nxt_w = glu_wload(c + 1)
                sgt_load(c + 1)
            for q4 in range(4):
                sg_ = MU[0:120, (c % 2) * 512 + q4 * 128:(c % 2) * 512 + (q4 + 1) * 128]
                b = self.bank()
                self.tr(self.ps[:, b, 0:120], sg_, ident[0:120, 0:120], (("sgt", c % 2), "cm"), (("ps", b),))
                self.act(GS[:, 4 * q4:4 * q4 + 4, 0:30], self.ps[:, b, 0:120].rearrange("p (s j) -> p s j", j=30),
                         AF.Copy, (("ps", b),), ("glx",))
            for (a, n) in tlu:
                ba = self.bank()
                for kc in range(KC):
                    self.mm(self.ps[:, ba, :n], wv[:, kc, 0:128], UN2[:, kc, a:a + n], kc == 0, kc == KC - 1,
                            (("w", s),), (("ps", ba),))
                bb = self.bank()
                for kc in range(KC):
                    self.mm(self.ps[:, bb, :n], wv[:, kc, 128:256], UN2[:, kc, a:a + n], kc == 0, kc == KC - 1,
                            (("w", s),), (("ps", bb),))
                r = self.ring("xt", 4)
                self.act(XT[:, r, :n], self.ps[:, bb, :n], AF.Sigmoid, (("ps", bb),), (("xt", r),))
                lo = max(a, 2)
                hi = min(a + n, 1056)
                if hi > lo:
                    self.tt(GPB[:, lo - 2:hi - 2], self.ps[:, ba, lo - a:hi - a], XT[:, r, lo - a:hi - a], ALU.mult,
                            (("ps", ba), ("xt", r)), ("glx",))
                if a <= 1026 < a + n:
                    self.tt(GP30[:, 0:30], self.ps[:, ba, 1026 - a:1056 - a], XT[:, r, 1026 - a:1056 - a], ALU.mult,
                            (("ps", ba), ("xt", r)), ("gp30",))
                if a + n > 1056:
                    o0 = 1056 - a
                    self.tt(GS[:, :, 30:38], self.ps[:, ba, o0:o0 + 128].rearrange("p (s j) -> p s j", j=8),
                            XT[:, r, o0:o0 + 128].rearrange("p (s j) -> p s j", j=8), ALU.mult,
                            (("ps", ba), ("xt", r)), ("glx",))
            COs = CO[:, NMAIN:NM].rearrange("p (s j) -> p s j", j=8)
            bcol = self.cv(CV_G["b_dw"] + c)
            self.tt(DG31, self.ident_bf.unsqueeze(1).to_broadcast([128, 31, 128]),
                    self.cv(CV_DW + c * 31, 31).unsqueeze(2).to_broadcast([128, 31, 128]), ALU.mult,
                    ("cb", "cv"), ("dg31",))
            for t0_ in (0, 512):
                b = self.bank()
                for j in range(31):
                    self.mm(self.ps[:, b, :], DG31[:, j, :], GPB[:, t0_ + j:t0_ + j + 512], j == 0, j == 30,
                            ("dg31", "glx"), (("ps", b),))
                self.act(CO[:, t0_:t0_ + 512], self.ps[:, b, :], AF.Identity, (("ps", b), "cv"), ("co",), bias=bcol, scale=1.0)
            for j in range(31):
                wcol = self.cv(CV_DW + c * 31 + j)
                if j == 0:
                    self.ts(COs, GS[:, :, 0:8], wcol, bcol, ALU.mult, ALU.add, ("glx", "cv"), ("co",))
                else:
                    self.stt(COs, GS[:, :, j:j + 8], wcol, COs, ALU.mult, ALU.add, ("glx", "co", "cv"), ("co",))
            b = self.bank()
            self.tr(self.ps[0:30, b, 0:128], GP30[:, 0:30], ident, ("gp30", "cm"), (("ps", b),))
            k = self.ring("xt", 4)
            self.act(XT[0:30, k, 0:128], self.ps[0:30, b, 0:128], AF.Copy, (("ps", b),), (("xt", k),))
            self.dma("act", self.ng_p[:, c * 128:(c + 1) * 128], XT[0:30, k, 0:128], ("xt", k), (("xt", k),), ())
            for q4 in range(4):
                k = self.ring("xt", 4)
                self.cp(XT[:, k, 0:120].rearrange("p (s j) -> p s j", j=30), GS[:, 4 * q4:4 * q4 + 4, 8:38],
                        ("glx",), (("xt", k),), eng="pool")
                b = self.bank()
                self.tr(self.ps[0:120, b, 0:128], XT[:, k, 0:120], ident, (("xt", k), "cm"), (("ps", b),))
                self.act(XT[0:120, k, 128:256], self.ps[0:120, b, 0:128], AF.Copy, (("ps", b),), (("xt", k),))
                self.dma("act", self.ng_s[q4 * 120:(q4 + 1) * 120, c * 128:(c + 1) * 128], XT[0:120, k, 128:256],
                         ("xt", k), (("xt", k),), ())
            self.dma("sp", self.cT[c], CO, ("co",), ("co",), ())
        self.barrier()
        self.rot = [0, 1]
        for c in range(KC):
            for ti, (a, n) in enumerate(tlm):
                r = self.ring("xt", 4)
                self.dma("sp" if (c + ti) % 2 == 0 else "act", XT[:, r, :n], self.cT[c, :, a:a + n], ("xt", r), (), (("xt", r),))
                COt = XT[:, r, 0:512]
                a0 = a
                a = 0
                q = self.ring("sq", 2)
                self.act(SQ[:, q, :n], COt[:, a:a + n], AF.Square, (("xt", r),), (("sq", q),))
                self.mm(self.ps[:, SSQB[ti][0], SSQB[ti][1]:SSQB[ti][1] + n], self.ones_bf, SQ[:, q, :n], c == 0, c == KC - 1,
                        (("sq", q), "cb"), (("ps", SSQB[ti][0]),))
                q = self.ring("sq", 2)
                self.cp(SQ[:, q, :n], COt[:, a:a + n], (("xt", r),), (("sq", q),))
                a = a0
                self.mm(self.ps[:, SUMB[ti][0], SUMB[ti][1]:SUMB[ti][1] + n], self.ones_bf, SQ[:, q, :n], c == 0, c == KC - 1,
                        (("sq", q), "cb"), (("ps", SUMB[ti][0]),))
        for ti, (a, n) in enumerate(tlm):
            sb_, so_ = SUMB[ti]
            qb_, qo_ = SSQB[ti]
            self.act(MU[:, a:a + n], self.ps[:, sb_, so_:so_ + n], AF.Copy, (("ps", sb_),), (("mu", a),), scale=1.0 / D)
            self.tt(RS[:, a:a + n], MU[:, a:a + n], MU[:, a:a + n], ALU.mult, (("mu", a),), (("rs", a),))
            self.stt(RS[:, a:a + n], self.ps[:, qb_, qo_:qo_ + n], 1.0 / D, RS[:, a:a + n], ALU.mult, ALU.subtract,
                     (("ps", qb_), ("rs", a)), (("rs", a),))
            self.ts(RS[:, a:a + n], RS[:, a:a + n], 0.0, None, ALU.max, None, (("rs", a),), (("rs", a),))
            self.rstd(RS[:, a:a + n], RS[:, a:a + n], 1.0, (("rs", a),), (("rs", a),))
        self.barrier()
        self.rot = list(range(8))
        for c in range(KC):
            for (a, n) in tlm:
                r = self.ring("xt", 4)
                self.dma("sp", XT[:, r, :n], self.cT[c, :, a:a + n], ("xt", r), (), (("xt", r),))
                self.tt(XT[:, r, :n], XT[:, r, :n], MU[:, a:a + n], ALU.subtract, (("xt", r),), (("xt", r),))
                self.tt(XT[:, r, :n], XT[:, r, :n], RS[:, a:a + n], ALU.mult, (("xt", r),), (("xt", r),))
                self.ts(XT[:, r, :n], XT[:, r, :n], self.cv(CV_G["ln_g"] + c), self.cv(CV_G["ln_b"] + c),
                        ALU.mult, ALU.add, (("xt", r), "cv"), (("xt", r),))
                self.act(CB[:, c, a:a + n], XT[:, r, :n], AF.Silu, (("xt", r),), ())
        self.barrier()
        wba = self.w_ba.rearrange("(kc p) n -> p kc n", p=128)
        wbb = self.w_bb.rearrange("(kc p) n -> p kc n", p=128)
        LW = self.v(187392, [128, 2, KC, 256], BF16)
        bring = [0]

        def bslot():
            i = bring[0] % 5
            bring[0] += 1
            if i < 3:
                return self.wview(i, [128, KC, 256]), ("w", i)
            return LW[:, i - 3], ("wl", i - 3)

        def bload(dst, key, src):
            self.dma("pool", dst, src, key, (), (key,))
        for oc in range(KC):
            w1, s1 = bslot()
            bload(w1[:, :, 0:128], s1, wba[:, :, oc * 128:(oc + 1) * 128])
            bload(w1[:, :, 128:256], s1, wbb[:, :, oc * 128:(oc + 1) * 128])
            w2, s2 = bslot()
            bload(w2[:, :, 0:128], s2, win[:, :, O_GATE + oc * 128:O_GATE + (oc + 1) * 128])
            bload(w2[:, :, 128:256], s2, win[:, :, O_GATE + 2048 + oc * 128:O_GATE + 2048 + (oc + 1) * 128])
            for (a, n) in tlm:
                bs = []
                for (wv_, s_, col, src, off) in ((w1, s1, 0, OF, 0), (w1, s1, 128, CB, 0), (w2, s2, 0, UN2, 32), (w2, s2, 128, UN2, 32)):
                    b = self.bank()
                    for kc in range(KC):
                        self.mm(self.ps[:, b, :n], wv_[:, kc, col:col + 128], src[:, kc, off + a:off + a + n],
                                kc == 0, kc == KC - 1, (s_,), (("ps", b),))
                    bs.append(b)
                r1 = self.ring("xt", 4)
                r2 = self.ring("xt", 4)
                self.act(XT[:, r1, :n], self.ps[:, bs[2], :n], AF.Sigmoid, (("ps", bs[2]),), (("xt", r1),))
                self.act(XT[:, r2, :n], self.ps[:, bs[3], :n], AF.Sigmoid, (("ps", bs[3]),), (("xt", r2),))
                self.tt(XT[:, r1, :n], XT[:, r1, :n], self.ps[:, bs[0], :n], ALU.mult, (("xt", r1), ("ps", bs[0])), (("xt", r1),))
                self.tt(XT[:, r2, :n], XT[:, r2, :n], self.ps[:, bs[1], :n], ALU.mult, (("xt", r2), ("ps", bs[1])), (("xt", r2),))
                q = self.ring("sq", 2)
                self.tt(SQ[:, q, :n], XT[:, r1, :n], XT[:, r2, :n], ALU.add, (("xt", r1), ("xt", r2)), (("sq", q),))
                self.dma("sp", self.mgT[oc, :, a:a + n], SQ[:, q, :n], ("sq", q), (("sq", q),), ())

    def proj_post(self, kind):
        o = PH_OFF
        XN = self.v(o, [128, KC, NM], BF16); o += KC * NM * 2
        PB = self.v(o, [128, 2, NM], BF16); o += 2 * NM * 2
        RSTD = self.v(o, [128, NM]); o += NM * 4
        XT = self.v(o, [128, 4, 512]); o += 8192
        SQ = self.v(o, [128, 2, 512], BF16); o += 2048
        FT = self.v(o, [128, 2, 512]); o += 4096
        YST = self.v(o, [128, 2, 512]); o += 4096
        tlm = _tiles(0, NM)
        self.barrier()
        self.rot = list(range(5))
        if kind == "out":
            for kc in range(KC):
                self.dma("sp", XN[:, kc, :], self.mgT[kc], ("xnl", kc % 4), (), ())
            W = self.w_out.rearrange("(kc p) n -> p kc n", p=128)
            nk = KC
        else:
            self.norm_load(self.h3, M0, NT, CV_G["ple_pre"], XN, RSTD, XT, SQ, pre_ssq=[5, 6, 7])
            for kc in range(2):
                self.dma("pool", PB[:, kc, :], self.pT[kc], ("pbl", kc), (), ())
            W = self.w_pg.rearrange("(kc p) n -> p kc n", p=128)
            WP = self.w_pp.rearrange("(kc p) n -> p kc n", p=128)
            nk = KC
        self.barrier()
        ssq = [5, 6, 7]
        self.rot = list(range(5))
        for oc in range(KC):
            s = self.wslot()
            wv = self.wview(s, [128, KC + 2, 128])
            self.wload(s, wv[:, 0:KC, :], W[:, :, oc * 128:(oc + 1) * 128])
            if kind == "ple":
                self.wload(s, wv[:, KC:KC + 2, :], WP[:, :, oc * 128:(oc + 1) * 128])
            for ti, (a, n) in enumerate(tlm):
                b = self.bank()
                for kc in range(nk):
                    self.mm(self.ps[:, b, :n], wv[:, kc, :], XN[:, kc, a:a + n], kc == 0, kc == nk - 1,
                            (("w", s),), (("ps", b),))
                k = self.ring("ft", 2)
                if kind == "out":
                    self.act(FT[:, k, :n], self.ps[:, b, :n], AF.Copy, (("ps", b),), (("ft", k),))
                else:
                    b2 = self.bank()
                    for kc in range(2):
                        self.mm(self.ps[:, b2, :n], wv[:, KC + kc, :], PB[:, kc, a:a + n], kc == 0, kc == 1,
                                (("w", s),), (("ps", b2),))
                    self.act(FT[:, k, :n], self.ps[:, b, :n], AF.Sigmoid, (("ps", b),), (("ft", k),))
                    self.tt(FT[:, k, :n], FT[:, k, :n], self.ps[:, b2, :n], ALU.mult, (("ft", k), ("ps", b2)), (("ft", k),))
                self.dma("sp", self.fT[oc, :, M0 + a:M0 + a + n], FT[:, k, :n], ("ft", k), (("ft", k),), ("fscr",))
                q = self.ring("sq", 2)
                self.tt(SQ[:, q, :n], FT[:, k, :n], FT[:, k, :n], ALU.mult, (("ft", k),), (("sq", q),))
                self.mm(self.ps[:, ssq[ti], :n], self.ones_bf, SQ[:, q, :n], oc == 0, oc == KC - 1,
                        (("sq", q), "cb"), (("ps", ssq[ti]),))
        if kind == "out":
            self.post_residual(self.fT, self.h1, self.h2, M0, NT, CV_G["mix_post"], ssq, RSTD, XT, SQ=SQ, nxt_ssq=True)
        else:
            self.post_residual(self.fT, self.h3, self.h4, M0, NT, CV_G["ple_post"], ssq, RSTD, XT, yout=self.y, YST=YST)
        self.rot = list(range(8))

    def write_y(self):
        self.barrier()
        self.rot = list(range(8))
        HIN = self.v(PH_OFF, [128, 2, 4, 128])
        YT = self.v(PH_OFF + 4096, [128, 2, D])
        ident = self.cm(CM_ID, 128)
        for tb in range(NM // 128):
            yk = self.ring("yt", 2)
            for g in range(4):
                k = self.ring("hin", 2)
                self.dma("sp", HIN[:, k], self.h4[g * 4:(g + 1) * 4, :, M0 + tb * 128:M0 + (tb + 1) * 128]
                         .rearrange("c p t -> p c t"), ("hin", k), (), (("hin", k),))
                b = self.bank()
                for q in range(4):
                    self.tr(self.ps[:, b, q * 128:(q + 1) * 128], HIN[:, k, q, :], ident, (("hin", k), "cm"), (("ps", b),))
                self.act(YT[:, yk, g * 512:(g + 1) * 512], self.ps[:, b, :], AF.Copy, (("ps", b),), (("yt", yk),))
            self.dma("act", self.y[tb * 128:(tb + 1) * 128, :], YT[:, yk, :], ("yt", yk), (("yt", yk),), ())

    def dbg_dump_bg(self):
        self.barrier()
        self.dma("sp", self.dbg_bg[0], self.BETA, ("dbg", 0), (), ())
        self.dma("sp", self.dbg_bg[1], self.GG, ("dbg", 0), (), ())

    def build(self):
        self.load_consts()
        self.transpose_in(self.xin, self.xT, NT, KC)
        if self.stop_after == "xT":
            return self.finish()
        self.ffn(self.xT, self.h1, 0, NPRE, self.w_gu1, self.w_dn1, CV_G["ffn1_pre"], CV_H1)
        self.ffn(self.xT, self.h1, M0, NT, self.w_gu1, self.w_dn1, CV_G["ffn1_pre"], CV_H1)
        if self.stop_after == "ffn1":
            return self.finish()
        self.mixer_qkv()
        if self.stop_after == "qkv":
            if self.debug:
                self.dbg_dump_bg()
            return self.finish()
        self.delta()
        if self.stop_after == "delta":
            if self.debug:
                self.barrier()
                self.dma("sp", self.dbg_of.rearrange("h p t -> p h t"), self.OF, ("dbg", 1), (), ())
            return self.finish()
        self.mixer_tail()
        self.transpose_in(self.pin, self.pT, NM, 2)
        self.proj_post("out")
        if self.stop_after == "mix":
            return self.finish()
        self.ffn(self.h2, self.h3, M0, NT, self.w_gu2, self.w_dn2, CV_G["ffn2_pre"], CV_H2, pre_ssq=[5, 6, 7], nxt_ssq=True)
        self.proj_post("ple")
        return self.finish()

    def finish(self):
        self.pg.emit()
        return self.nc


def _masks():
    m = np.zeros((128, NCM), np.float32)
    i = np.arange(128)
    m[:, CM_ID:CM_ID + 128] = np.eye(128, dtype=np.float32)
    m[:, CM_ONE:CM_ONE + 128] = 1.0
    j = np.arange(64)
    le = (j[:, None] <= j[None, :]).astype(np.float32)
    lt = (j[:, None] < j[None, :]).astype(np.float32)
    m[:64, CM_TRI:CM_TRI + 64] = le
    m[:64, CM_UI:CM_UI + 64] = le
    m[:64, CM_US:CM_US + 64] = lt
    m[:64, CM_LS:CM_LS + 64] = lt.T
    same = ((i[:, None] // LS) == (i[None, :] // LS)).astype(np.float32)
    le8 = (i[:, None] <= i[None, :]).astype(np.float32) * same
    lt8 = (i[:, None] < i[None, :]).astype(np.float32) * same
    m[:, CM_TRI8:CM_TRI8 + 128] = le8
    m[:, CM_UI8:CM_UI8 + 128] = le8
    m[:, CM_US8:CM_US8 + 128] = lt8
    m[:, CM_LS8:CM_LS8 + 128] = lt8.T
    m[:, CM_SAME8:CM_SAME8 + 128] = same
    m[:, CM_SEG:CM_SEG + 16] = (i[:, None] // LS == np.arange(16)[None, :]).astype(np.float32)
    return m


def _cvec(inp):
    c = np.zeros((128, NCV), np.float32)

    def fm(vec):
        return np.ascontiguousarray(np.asarray(vec, np.float32).reshape(-1, 128).T)
    for n, col in CV_G.items():
        key = {"b_dw": "b_dw_conv"}.get(n, n)
        c[:, col:col + 16] = fm(inp[key][0])
    wsc = np.asarray(inp["w_short_conv"][0], np.float32)
    c[:, CV_SC:CV_SC + 192] = wsc.reshape(4, 48, 128).transpose(2, 1, 0).reshape(128, 192)
    wdw = np.asarray(inp["w_dw_conv"][0], np.float32)
    c[:, CV_DW:CV_DW + 496] = wdw.reshape(31, 16, 128).transpose(2, 1, 0).reshape(128, 496)
    c[:, CV_ON] = np.asarray(inp["o_norm"][0], np.float32)
    c[:16, CV_AL] = np.asarray(inp["a_log"][0], np.float32)
    c[:16, CV_DT] = np.asarray(inp["dt_bias"][0], np.float32)
    return c


def make_in_maps(inp, cores=range(8)):
    xp = np.asarray(inp["x_prompt"], np.float32)
    xs = np.asarray(inp["x_sample"], np.float32)
    pp = np.asarray(inp["p_prompt"], np.float32)[0]
    psm = np.asarray(inp["p_sample"], np.float32)[0]
    sd = np.asarray(inp["state_delta"], np.float32)[0]
    sq = np.asarray(inp["state_qkv_conv"], np.float32)[0]
    sg = np.asarray(inp["state_glu_conv"], np.float32)[0]
    shared = {
        "cvec": _cvec(inp), "cmask": _masks(),
        "w_gu1": np.asarray(inp["ffn1_w_gu"][0]), "w_dn1": np.asarray(inp["ffn1_w_down"][0]),
        "w_in": np.asarray(inp["w_in"][0]), "w_ba": np.asarray(inp["w_branch_a"][0]),
        "w_bb": np.asarray(inp["w_branch_b"][0]), "w_out": np.asarray(inp["w_out"][0]),
        "w_gu2": np.asarray(inp["ffn2_w_gu"][0]), "w_dn2": np.asarray(inp["ffn2_w_down"][0]),
        "w_pg": np.asarray(inp["w_ple_gate"][0]), "w_pp": np.asarray(inp["w_ple_proj"][0]),
    }
    maps = []
    for c in cores:
        b, half = c // 2, c % 2
        main = xp[b, half * NMAIN:(half + 1) * NMAIN]
        pre = xp[b, 0:NPRE] if half == 1 else np.zeros((NPRE, D), np.float32)
        sl = slice(c * NSEQ, (c + 1) * NSEQ)
        m = dict(shared)
        m["xin"] = np.concatenate([pre, main, xs[sl].reshape(NS, D)], 0)
        m["pin"] = np.concatenate([pp[b, half * NMAIN:(half + 1) * NMAIN], psm[sl].reshape(NS, PLE)], 0)
        m["sdelta"] = np.ascontiguousarray(sd[sl])
        m["sqkv"] = np.ascontiguousarray(sq[sl].reshape(NSEQ * 3, QKV))
        m["sglu"] = np.ascontiguousarray(sg[sl].reshape(NSEQ * 30, D))
        maps.append(m)
    return maps


def kernel(**inputs):
    nc = Builder().build()
    maps = make_in_maps(inputs)
    res = run_bass_kernel_spmd(nc, maps, core_ids=list(range(8)))
    R = res.results
    yp = np.zeros((4, 2048, D), np.float32)
    ys = np.zeros((128, LS, D), np.float32)
    ndp = np.zeros((1, 4, NH, 128, 128), np.float32)
    nqp = np.zeros((1, 4, 3, QKV), np.float32)
    ngp = np.zeros((1, 4, 30, D), np.float32)
    nds = np.zeros((1, 128, NH, 128, 128), np.float32)
    nqs = np.zeros((1, 128, 3, QKV), np.float32)
    ngs = np.zeros((1, 128, 30, D), np.float32)
    for c in range(8):
        b, half = c // 2, c % 2
        r = R[c]
        yp[b, half * NMAIN:(half + 1) * NMAIN] = r["y"][:NMAIN]
        ys[c * NSEQ:(c + 1) * NSEQ] = r["y"][NMAIN:].reshape(NSEQ, LS, D)
        if half == 1:
            ndp[0, b] = r["nd_p"]
            nqp[0, b] = r["nq_p"]
            ngp[0, b] = r["ng_p"]
        nds[0, c * NSEQ:(c + 1) * NSEQ] = r["nd_s"]
        nqs[0, c * NSEQ:(c + 1) * NSEQ] = r["nq_s"].reshape(NSEQ, 3, QKV)
        ngs[0, c * NSEQ:(c + 1) * NSEQ] = r["ng_s"].reshape(NSEQ, 30, D)
    return (yp, ys, ndp, nqp, ngp, nds, nqs, ngs)
```

```python
import numpy as np
import concourse.bass as bass
import concourse.mybir as mybir
from concourse.bass_utils import run_bass_kernel_spmd

F32 = mybir.dt.float32
BF16 = mybir.dt.bfloat16
AF = mybir.ActivationFunctionType
ALU = mybir.AluOpType

ENGS = ("pe", "act", "dve", "pool", "sp")
EPOCH = 30000


class _Op:
    __slots__ = ("eng", "fn", "reads", "writes", "dma", "deps", "sig", "idx", "n", "bar")

    def __init__(self, eng, fn, reads, writes, dma):
        self.eng = eng
        self.fn = fn
        self.reads = reads
        self.writes = writes
        self.dma = dma
        self.deps = None
        self.sig = False
        self.idx = None
        self.n = 0
        self.bar = False


class Prog:
    def __init__(self, nc):
        self.nc = nc
        self.ops = []
        self.streams = {e: [] for e in ENGS}

    def add(self, eng, fn, reads=(), writes=(), dma=None):
        op = _Op(eng, fn, tuple(reads), tuple(writes), dma)
        op.n = len(self.ops)
        self.ops.append(op)
        self.streams[eng].append(op)
        return op

    def barrier(self, fn):
        op = self.add("sp", fn, dma=("bar",))
        op.bar = True
        return op

    def _analyze(self):
        last_w = {}
        readers = {}
        last_eng = {}
        last_dma = {}
        cur_bar = None
        need_bar = set()
        for op in self.ops:
            deps = {}
            if op.bar:
                for d in last_eng.values():
                    deps[d.n] = d
                for d in last_dma.values():
                    deps[d.n] = d
                last_w = {}
                readers = {}
                cur_bar = op
                need_bar = set(ENGS)
            else:
                if cur_bar is not None and op.eng in need_bar:
                    deps[cur_bar.n] = cur_bar
                    need_bar.discard(op.eng)
                for k in op.reads:
                    w = last_w.get(k)
                    if w is not None:
                        deps[w.n] = w
                for k in op.writes:
                    w = last_w.get(k)
                    if w is not None:
                        deps[w.n] = w
                    for r in readers.get(k, ()):
                        deps[r.n] = r
                for k in op.reads:
                    readers.setdefault(k, []).append(op)
                for k in op.writes:
                    last_w[k] = op
                    readers[k] = []
            if op.dma is not None:
                last_dma[op.dma] = op
            else:
                last_eng[op.eng] = op
            deps.pop(op.n, None)
            dl = []
            for d in deps.values():
                if d.dma is None and op.dma is None and d.eng == "pe" and op.eng == "pe":
                    continue
                dl.append(d)
            op.deps = dl
            for d in dl:
                d.sig = True
        cnt = {e: 0 for e in ENGS}
        dcnt = {}
        for op in self.ops:
            if op.dma is not None:
                dcnt[op.dma] = dcnt.get(op.dma, 0) + 16
                op.idx = ("d", op.dma, dcnt[op.dma])
            elif op.sig:
                c = cnt[op.eng]
                cnt[op.eng] += 1
                op.idx = ("e", (op.eng, c // EPOCH), c % EPOCH + 1)
        self.dma_final = dcnt

    def emit(self):
        nc = self.nc
        self._analyze()
        sems = {}

        def sem(kind, key):
            k = (kind, key)
            if k not in sems:
                sems[k] = nc.alloc_semaphore(name="s%d" % len(sems))
            return sems[k]

        waits = {}
        for e in ENGS:
            seen = {}
            for op in self.streams[e]:
                wl = {}
                for d in op.deps:
                    kind, key, val = d.idx
                    k = (kind, key)
                    if seen.get(k, 0) >= val:
                        continue
                    if wl.get(k, 0) < val:
                        wl[k] = val
                for k, v in wl.items():
                    seen[k] = v
                waits[op.n] = [(sem(*k), v) for k, v in wl.items()]
        final_waits = [(sem("d", k), v) for k, v in self.dma_final.items()]
        self.nsem = len(sems)

        def run_stream(engname, eng):
            for op in self.streams[engname]:
                for s, v in waits[op.n]:
                    eng.wait_ge(s, v)
                ins = op.fn(eng)
                if op.idx is not None:
                    kind, key, val = op.idx
                    ins.then_inc(sem(kind, key), 16 if kind == "d" else 1)
            if engname == "sp":
                for s, v in final_waits:
                    eng.wait_ge(s, v)

        with nc.Block() as block:
            @block.tensor
            def _(e):
                run_stream("pe", e)

            @block.scalar
            def _(e):
                run_stream("act", e)

            @block.vector
            def _(e):
                run_stream("dve", e)

            @block.gpsimd
            def _(e):
                run_stream("pool", e)

            @block.sync
            def _(e):
                run_stream("sp", e)


D = 2048
KC = 16
DFF = 5632
JC = 44
NH = 16
QKV = 6144
O_Z = QKV
O_BETA = O_Z + 2048
O_A = O_BETA + NH
O_GLU = O_A + NH
O_GATE = O_GLU + 4096
IN_DIM = O_GATE + 4096
PLE = 256
EPS = 1e-6
NPRE = 1024
NMAIN = 1024
NTP = NPRE + NMAIN
NSEQ = 16
LS = 8
NS = NSEQ * LS
NT = NTP + NS
NM = NMAIN + NS
M0 = NPRE

ARENA_BYTES = 206000
WR_OFF = 16384
WSLOT = 11264
NWS = 3
PH_OFF = WR_OFF + NWS * WSLOT

CV_G = {n: 16 * i for i, n in enumerate(
    ["ffn1_pre", "ffn1_post", "mix_pre", "mix_post", "ffn2_pre", "ffn2_post", "ple_pre", "ple_post",
     "b_dw", "ln_g", "ln_b"])}
CV_SC = 176
CV_DW = CV_SC + 192
CV_ON = CV_DW + 496
CV_AL = CV_ON + 1
CV_DT = CV_AL + 1
CV_H1 = CV_DT + 1
CV_H2 = CV_H1 + 16
CV_NA = CV_H2 + 16
NCV = CV_NA + 1
CM_ID = 0
CM_ONE = 128
CM_TRI = 256
CM_UI = 320
CM_US = 384
CM_LS = 448
CM_TRI8 = 512
CM_UI8 = 640
CM_US8 = 768
CM_LS8 = 896
CM_SAME8 = 1024
CM_SEG = 1152
NCM = CM_SEG + 16


def _tiles(t0, t1, n=512):
    return [(a, min(n, t1 - a)) for a in range(t0, t1, n)]


class Builder:
    def __init__(self, debug=False, stop_after=None):
        self.debug = debug
        self.stop_after = stop_after
        nc = bass.Bass("TRN2", target_bir_lowering=False)
        self.nc = nc
        self.pg = Prog(nc)
        self.arena = nc.alloc_sbuf_tensor("arena", [128, ARENA_BYTES // 4], F32)
        self.ps = nc.alloc_psum_tensor("ps", [128, 8, 512], F32)
        self.rot = list(range(8))
        self.rot_i = 0
        self.ws_i = 0
        self.ring_i = {}
        self._decl()

    def _in(self, name, shape, dt=F32):
        return self.nc.dram_tensor(name, list(shape), dt, kind="ExternalInput").ap()

    def _out(self, name, shape, dt=F32):
        return self.nc.dram_tensor(name, list(shape), dt, kind="ExternalOutput").ap()

    def _scr(self, name, shape, dt=F32):
        kind = "ExternalOutput" if self.debug else "Internal"
        return self.nc.dram_tensor(name, list(shape), dt, kind=kind).ap()

    def _decl(self):
        self.xin = self._in("xin", [NT, D])
        self.pin = self._in("pin", [NM, PLE])
        self.sdelta = self._in("sdelta", [NSEQ, NH, 128, 128])
        self.sqkv = self._in("sqkv", [NSEQ * 3, QKV])
        self.sglu = self._in("sglu", [NSEQ * 30, D])
        self.cvec = self._in("cvec", [128, NCV])
        self.cmask = self._in("cmask", [128, NCM])
        self.w_gu1 = self._in("w_gu1", [D, 2 * DFF])
        self.w_dn1 = self._in("w_dn1", [DFF, D])
        self.w_in = self._in("w_in", [D, IN_DIM])
        self.w_ba = self._in("w_ba", [D, D])
        self.w_bb = self._in("w_bb", [D, D])
        self.w_out = self._in("w_out", [D, D])
        self.w_gu2 = self._in("w_gu2", [D, 2 * DFF])
        self.w_dn2 = self._in("w_dn2", [DFF, D])
        self.w_pg = self._in("w_pg", [D, D])
        self.w_pp = self._in("w_pp", [PLE, D])
        self.y = self._out("y", [NM, D])
        self.nd_p = self._out("nd_p", [NH, 128, 128])
        self.nq_p = self._out("nq_p", [3, QKV])
        self.ng_p = self._out("ng_p", [30, D])
        self.nd_s = self._out("nd_s", [NSEQ, NH, 128, 128])
        self.nq_s = self._out("nq_s", [NSEQ * 3, QKV])
        self.ng_s = self._out("ng_s", [NSEQ * 30, D])
        self.xT = self._scr("xT", [KC, 128, NT])
        self.h1 = self._scr("h1", [KC, 128, NT])
        self.fT = self._scr("fT", [KC, 128, NT])
        self.qkvF = self._scr("qkvF", [48, 128, NT], BF16)
        self.cT = self._scr("cT", [KC, 128, NM])
        self.mgT = self._scr("mgT", [KC, 128, NM], BF16)
        self.h2 = self._scr("h2", [KC, 128, NT])
        self.h3 = self._scr("h3", [KC, 128, NT])
        self.h4 = self._scr("h4", [KC, 128, NT])
        self.pT = self._scr("pT", [2, 128, NM])
        if self.debug:
            self.dbg_bg = self._scr("dbg_bg", [2, 16, NT])
            self.dbg_of = self._scr("dbg_of", [NH, 128, NM], BF16)
        self.bar_a = self.nc.dram_tensor("bar_a", [1, 16], F32, kind="Internal").ap()
        self.bar_b = self.nc.dram_tensor("bar_b", [1, 16], F32, kind="Internal").ap()

    def v(self, off, shape, dt=F32):
        esz = 4 if dt == F32 else 2
        n = 1
        for s in shape[1:]:
            n *= s
        nb = n * esz
        assert off % 4 == 0 and nb % 4 == 0, (off, nb)
        assert off + nb <= ARENA_BYTES, (off, nb)
        a = self.arena[0:shape[0], off // 4:(off + nb) // 4]
        if dt != F32:
            a = a.bitcast(dt)
        if len(shape) == 3:
            a = a.rearrange("p (a b) -> p a b", a=shape[1])
        elif len(shape) == 4:
            a = a.rearrange("p (a b c) -> p a b c", a=shape[1], b=shape[2])
        return a

    def cv(self, col, n=1, rows=128):
        return self.arena[0:rows, col:col + n]

    def cm(self, col, n, rows=128, dt=F32):
        return self.arena[0:rows, 1024 + col:1024 + col + n]

    def bank(self):
        b = self.rot[self.rot_i % len(self.rot)]
        self.rot_i += 1
        return b

    def ring(self, name, n):
        i = self.ring_i.get(name, 0)
        self.ring_i[name] = i + 1
        return i % n

    def mm(self, out, lhsT, rhs, start, stop, reads, writes):
        return self.pg.add("pe", lambda e: e.matmul(out, lhsT, rhs, start=start, stop=stop), reads, writes)

    def tr(self, out, in_, ident, reads, writes):
        return self.pg.add("pe", lambda e: e.transpose(out, in_, ident), reads, writes)

    def act(self, out, in_, func, reads, writes, bias=None, scale=None):
        kw = {}
        if bias is not None:
            kw["bias"] = bias
        if scale is not None:
            kw["scale"] = scale
        return self.pg.add("act", lambda e: e.activation(out=out, in_=in_, func=func, **kw), reads, writes)

    def tt(self, out, a, b, op, reads, writes, eng="dve"):
        return self.pg.add(eng, lambda e: e.tensor_tensor(out, a, b, op), reads, writes)

    def ts(self, out, a, s1, s2, op0, op1, reads, writes, eng="dve"):
        if op1 is None:
            return self.pg.add(eng, lambda e: e.tensor_single_scalar(out, a, s1, op0), reads, writes)
        return self.pg.add(eng, lambda e: e.tensor_scalar(out, a, s1, s2, op0, op1), reads, writes)

    def stt(self, out, in0, scalar, in1, op0, op1, reads, writes, eng="dve"):
        return self.pg.add(eng, lambda e: e.scalar_tensor_tensor(out, in0, scalar, in1, op0, op1), reads, writes)

    def cp(self, out, in_, reads, writes, eng="dve"):
        return self.pg.add(eng, lambda e: e.tensor_copy(out, in_), reads, writes)

    def rstd(self, out, in_, scale, reads, writes):
        self.act(out, in_, AF.Ln, reads, writes, bias=EPS, scale=scale)
        return self.act(out, out, AF.Exp, writes, writes, scale=-0.5)

    def recip(self, out, in_, reads, writes):
        return self.pg.add("dve", lambda e: e.reciprocal(out, in_), reads, writes)

    def memset(self, ap, val, writes, eng="dve"):
        return self.pg.add(eng, lambda e: e.memset(ap, val), (), writes)

    def dma(self, eng, out, in_, key, reads, writes):
        return self.pg.add(eng, lambda e: e.dma_start(out=out, in_=in_), reads, writes, dma=key)

    def barrier(self):
        a, b = self.bar_a, self.bar_b
        self.bar_a, self.bar_b = b, a
        self.pg.barrier(lambda e: e.dma_start(out=b, in_=a))
        self.ring_i = {}

    def wslot(self):
        s = self.ws_i % NWS
        self.ws_i += 1
        return s

    def wview(self, s, shape):
        return self.v(WR_OFF + s * WSLOT, shape, BF16)

    def wload(self, s, dst, src):
        return self.dma("pool", dst, src, ("w", s), (), (("w", s),))

    def load_consts(self):
        self.dma("sp", self.arena[:, 0:NCV], self.cvec, ("c", 0), (), ("cv",))
        self.dma("sp", self.arena[:, 1024:1024 + NCM], self.cmask, ("c", 1), (), ("cm",))
        self.ident_bf = self.v(12288, [128, 128], BF16)
        self.ones_bf = self.v(12288 + 256, [128, 128], BF16)
        self.cp(self.ident_bf, self.cm(CM_ID, 128), ("cm",), ("cb",))
        self.cp(self.ones_bf, self.cm(CM_ONE, 128), ("cm",), ("cb",))
        self.ts(self.cv(CV_H1, 16), self.cv(CV_G["ffn1_post"], 16), 0.5, None, ALU.mult, None, ("cv",), ("cv2",))
        self.ts(self.cv(CV_H2, 16), self.cv(CV_G["ffn2_post"], 16), 0.5, None, ALU.mult, None, ("cv",), ("cv2",))
        self.act(self.cv(CV_NA, 1, 16), self.cv(CV_AL, 1, 16), AF.Exp, ("cv",), ("cv3",))
        self.ts(self.cv(CV_NA, 1, 16), self.cv(CV_NA, 1, 16), -1.0, None, ALU.mult, None, ("cv3",), ("cv3",))

    def transpose_in(self, src_tok, dst_fm, ntok, nfc, tok_off=0):
        self.barrier()
        XIN = self.v(PH_OFF, [128, 2, nfc * 128])
        XST = self.v(PH_OFF + 2 * nfc * 512, [128, 2, 4, 128])
        ident = self.cm(CM_ID, 128)
        for tb in range(ntok // 128):
            s = self.ring("xin", 2)
            self.dma("sp", XIN[:, s, :], src_tok[tb * 128:(tb + 1) * 128, :], ("xin", s), (), (("xin", s),))
            gsz = min(4, nfc)
            for g in range(nfc // gsz):
                b = self.bank()
                for q in range(gsz):
                    fc = g * gsz + q
                    self.tr(self.ps[:, b, q * 128:(q + 1) * 128], XIN[:, s, fc * 128:(fc + 1) * 128], ident,
                            (("xin", s), "cm"), (("ps", b),))
                k = self.ring("xst", 2)
                self.act(XST[:, k, 0:gsz].rearrange("p a b -> p (a b)"), self.ps[:, b, 0:gsz * 128], AF.Copy,
                         (("ps", b),), (("xst", k),))
                self.dma("act", dst_fm[g * gsz:(g + 1) * gsz, :, tok_off + tb * 128: tok_off + (tb + 1) * 128]
                         .rearrange("c p t -> p c t"), XST[:, k, 0:gsz], ("xst", k), (("xst", k),), ())

    def norm_load(self, src, t0, t1, gcol, XN, RSTD, XT, SQ, pre_ssq=None):
        G = t1 - t0
        for ti, (a, n) in enumerate(_tiles(0, G)):
            if pre_ssq is not None:
                b = pre_ssq[ti]
                self.rstd(RSTD[:, a:a + n], self.ps[:, b, :n], 1.0 / D, (("ps", b),), (("rstd", a),))
                continue
            b = self.bank()
            for fc in range(KC):
                s = self.ring("xt", 4)
                self.dma("sp" if fc % 2 == 0 else "act", XT[:, s, :n], src[fc, :, t0 + a:t0 + a + n], ("xt", s), (), (("xt", s),))
                q = self.ring("sq", 2)
                self.act(SQ[:, q, :n], XT[:, s, :n], AF.Square, (("xt", s),), (("sq", q),))
                self.mm(self.ps[:, b, :n], self.ones_bf, SQ[:, q, :n], fc == 0, fc == KC - 1,
                        (("sq", q), "cb"), (("ps", b),))
            self.rstd(RSTD[:, a:a + n], self.ps[:, b, :n], 1.0 / D, (("ps", b),), (("rstd", a),))
        for (a, n) in _tiles(0, G):
            for fc in range(KC):
                s = self.ring("xt", 4)
                self.dma("sp" if fc % 2 == 0 else "act", XT[:, s, :n], src[fc, :, t0 + a:t0 + a + n], ("xt", s), (), (("xt", s),))
                self.stt(XN[:, fc, a:a + n], XT[:, s, :n], self.cv(gcol + fc), RSTD[:, a:a + n], ALU.mult, ALU.mult,
                         (("xt", s), ("rstd", a), "cv"), (("xn", fc, a),))

    def post_residual(self, fsrc, rsrc, dst, t0, t1, gcol, ssq_banks, RSTD, XT, SQ=None, nxt_ssq=False, yout=None, YST=None):
        G = t1 - t0
        tl = _tiles(0, G)
        self.barrier()
        for ti, (a, n) in enumerate(tl):
            b = ssq_banks[ti]
            self.rstd(RSTD[:, a:a + n], self.ps[:, b, :n], 1.0 / D, (("ps", b),), (("rstd", a),))
        its = [(ti, a, n, fc) for ti, (a, n) in enumerate(tl) for fc in range(KC)]

        def load(i):
            ti, a, n, fc = its[i]
            s = self.ring("xt", 4)
            s2 = self.ring("xt", 4)
            self.dma("sp", XT[:, s, :n], fsrc[fc, :, t0 + a:t0 + a + n], ("xt", s), ("fscr",), (("xt", s),))
            self.dma("act", XT[:, s2, :n], rsrc[fc, :, t0 + a:t0 + a + n], ("xt", s2), (), (("xt", s2),))
            return s, s2
        nxt = load(0)
        for i, (ti, a, n, fc) in enumerate(its):
            s, s2 = nxt
            self.tt(XT[:, s, :n], XT[:, s, :n], RSTD[:, a:a + n], ALU.mult, (("xt", s), ("rstd", a)), (("xt", s),))
            self.stt(XT[:, s, :n], XT[:, s, :n], self.cv(gcol + fc), XT[:, s2, :n], ALU.mult, ALU.add,
                     (("xt", s), ("xt", s2), "cv", "cv2"), (("xt", s),))
            if i + 1 < len(its):
                nxt = load(i + 1)
            if nxt_ssq:
                q = self.ring("sq", 2)
                self.act(SQ[:, q, :n], XT[:, s, :n], AF.Square, (("xt", s),), (("sq", q),))
                self.mm(self.ps[:, ssq_banks[ti], :n], self.ones_bf, SQ[:, q, :n], fc == 0, fc == KC - 1,
                        (("sq", q), "cb"), (("ps", ssq_banks[ti]),))
            if yout is None:
                self.dma("sp", dst[fc, :, t0 + a:t0 + a + n], XT[:, s, :n], ("xt", s), (("xt", s),), ())
            else:
                nq = n // 128
                b = self.bank()
                for q4 in range(nq):
                    self.tr(self.ps[:, b, q4 * 128:(q4 + 1) * 128], XT[:, s, q4 * 128:(q4 + 1) * 128], self.cm(CM_ID, 128),
                            (("xt", s), "cm"), (("ps", b),))
                k = self.ring("yst", 2)
                self.act(YST[:, k, :n], self.ps[:, b, :n], AF.Copy, (("ps", b),), (("yst", k),))
                self.dma("sp", yout[a:a + n, fc * 128:(fc + 1) * 128].rearrange("(q p) f -> p q f", p=128),
                         YST[:, k, :n].rearrange("p (q f) -> p q f", f=128), ("yst", k), (("yst", k),), ())

    def ffn(self, src, dst, t0, t1, w_gu, w_dn, g_pre, g_post_half, pre_ssq=None, nxt_ssq=False):
        G = t1 - t0
        XN = self.v(PH_OFF, [128, KC, G], BF16)
        ACTB = self.v(PH_OFF + 36864, [128, JC, G], BF16)
        MO = PH_OFF + 36864 + 101376
        RSTD = self.v(MO, [128, 1152])
        XT = self.v(MO + 4608, [128, 4, 512])
        SQ = self.v(MO + 4608 + 8192, [128, 2, 512], BF16)
        FT = self.v(PH_OFF, [128, 2, 512])
        tl = _tiles(0, G)
        self.barrier()
        self.rot = list(range(8))
        self.norm_load(src, t0, t1, g_pre, XN, RSTD, XT, SQ, pre_ssq=pre_ssq)
        self.barrier()
        wgu = w_gu.rearrange("(kc p) n -> p kc n", p=128)
        for j in range(JC):
            s = self.wslot()
            wv = self.wview(s, [128, KC, 256])
            self.wload(s, wv[:, :, 0:128], wgu[:, :, j * 128:(j + 1) * 128])
            self.wload(s, wv[:, :, 128:256], wgu[:, :, DFF + j * 128:DFF + (j + 1) * 128])
            for ti, (a, n) in enumerate(tl):
                bg = self.bank()
                for kc in range(KC):
                    self.mm(self.ps[:, bg, :n], wv[:, kc, 0:128], XN[:, kc, a:a + n], kc == 0, kc == KC - 1,
                            (("w", s),), (("ps", bg),))
                bu = self.bank()
                for kc in range(KC):
                    self.mm(self.ps[:, bu, :n], wv[:, kc, 128:256], XN[:, kc, a:a + n], kc == 0, kc == KC - 1,
                            (("w", s),), (("ps", bu),))
                q = self.ring("sq", 2)
                self.act(SQ[:, q, :n], self.ps[:, bg, :n], AF.Silu, (("ps", bg),), (("sq", q),))
                self.tt(ACTB[:, j, a:a + n], SQ[:, q, :n], self.ps[:, bu, :n], ALU.mult,
                        (("sq", q), ("ps", bu)), ())
        self.barrier()
        nt = len(tl)
        ssq = list(range(8 - nt, 8))
        self.rot = list(range(8 - nt))
        wdn = w_dn.rearrange("(kc p) n -> p kc n", p=128)
        pend_ssq = None
        for oc in range(KC):
            s = self.wslot()
            wv = self.wview(s, [128, JC, 128])
            self.wload(s, wv[:, 0:22, :], wdn[:, 0:22, oc * 128:(oc + 1) * 128])
            self.wload(s, wv[:, 22:44, :], wdn[:, 22:44, oc * 128:(oc + 1) * 128])
            for ti, (a, n) in enumerate(tl):
                b = self.bank()
                for kc in range(JC):
                    self.mm(self.ps[:, b, :n], wv[:, kc, :], ACTB[:, kc, a:a + n], kc == 0, kc == JC - 1,
                            (("w", s),), (("ps", b),))
                k = self.ring("ft", 2)
                self.act(FT[:, k, :n], self.ps[:, b, :n], AF.Copy, (("ps", b),), (("ft", k),))
                self.dma("act", self.fT[oc, :, t0 + a:t0 + a + n], FT[:, k, :n], ("ft", k), (("ft", k),), ("fscr",))
                q = self.ring("sq", 2)
                self.tt(SQ[:, q, :n], FT[:, k, :n], FT[:, k, :n], ALU.mult, (("ft", k),), (("sq", q),))
                if pend_ssq is not None:
                    pend_ssq()
                pend_ssq = (lambda ti=ti, n=n, q=q, oc=oc: self.mm(
                    self.ps[:, ssq[ti], :n], self.ones_bf, SQ[:, q, :n], oc == 0, oc == KC - 1,
                    (("sq", q), "cb"), (("ps", ssq[ti]),)))
        pend_ssq()
        if nxt_ssq:
            self.rot = list(range(8 - nt))
        self.post_residual(self.fT, src, dst, t0, t1, g_post_half, ssq, RSTD, XT, SQ=SQ, nxt_ssq=nxt_ssq)
        self.rot = list(range(8))

    def mix_layout(self):
        o = PH_OFF
        self.BETA = self.v(o, [16, NT]); o += NT * 4
        self.GG = self.v(o, [16, NT]); o += NT * 4
        self.UN = self.v(o, [128, KC, NT], BF16); o += KC * NT * 2
        self.mix_free = o

    def mixer_qkv(self):
        self.mix_layout()
        o = self.mix_free
        RAW = self.v(o, [128, 2, 180]); o += 2 * 180 * 4
        RAWB = self.v(o, [128, 2, 2052], BF16); o += 2 * 2052 * 2
        DG = self.v(o, [128, 2, 4, 128], BF16); o += 2 * 4 * 128 * 2
        CVO = self.v(o, [128, 2, NT]); o += 2 * NT * 4
        QO = self.v(o, [128, 2, NT], BF16); o += 2 * NT * 2
        RSTD = self.v(o, [128, NT]); o += NT * 4
        XT = self.v(o, [128, 4, 512]); o += 8192
        SQ = self.v(o, [128, 2, 512], BF16); o += 2048
        ST = self.v(o, [64, 2, 256]); o += 2048
        CT = self.v(o, [128, 2, 48]); o += 384
        SSQ1 = self.v(o, [128, NT]); o += NT * 4
        SSQ = [RSTD, SSQ1]
        UN = self.UN
        ident = self.cm(CM_ID, 128)
        self.barrier()
        self.rot = list(range(8))
        self.norm_load(self.h1, 0, NT, CV_G["mix_pre"], UN, RSTD, XT, SQ)
        self.barrier()
        win = self.w_in.rearrange("(kc p) n -> p kc n", p=128)
        tl = _tiles(0, NT)
        tlp = [(a, n) for (a, n) in tl if a < NTP]
        allk = lambda nm, cc_: [(nm, cc_, a_) for (a_, _n) in tl]

        def emit_l2(cc_, a_, n_, q_):
            b3 = self.bank()
            self.mm(self.ps[:, b3, :n_], self.ones_bf, SQ[:, q_, :n_], True, True, (("sq", q_), "cb"), (("ps", b3),))
            self.act(SSQ[cc_][:, a_:a_ + n_], self.ps[:, b3, :n_], AF.Copy, (("ps", b3),), (("ssq", cc_, a_),))

        def emit_conv(c_, cc_, a_, n_):
            b2 = self.bank()
            for j in range(4):
                self.mm(self.ps[:, b2, :n_], DG[:, cc_, j, :], RAWB[:, cc_, a_ + j:a_ + j + n_], j == 0, j == 3,
                        (("dg", cc_), ("rawb", cc_, a_), ("rawb", cc_, a_ - 512)), (("ps", b2),))
            self.act(CVO[:, cc_, a_:a_ + n_], self.ps[:, b2, :n_], AF.Silu, (("ps", b2),), (("cvo", cc_, a_),))
            if c_ < 32:
                q_ = self.ring("sq", 2)
                self.act(SQ[:, q_, :n_], CVO[:, cc_, a_:a_ + n_], AF.Square, (("cvo", cc_, a_),), (("sq", q_),))
                return (cc_, a_, n_, q_)
            self.cp(QO[:, cc_, a_:a_ + n_], CVO[:, cc_, a_:a_ + n_], (("cvo", cc_, a_),), (("qo", cc_, a_),))
            return None

        def epilogue(c_, cc_, pend, pend_l2):
            RS = RAW[:, cc_, 0:176].rearrange("p (s j) -> p s j", j=11)
            if pend_l2 is not None:
                emit_l2(*pend_l2)
            if pend is not None:
                p2 = emit_conv(c_, *pend)
                if p2 is not None:
                    emit_l2(*p2)
            CS = CVO[:, cc_, NTP:NT].rearrange("p (s j) -> p s j", j=8)
            for j in range(4):
                wcol = self.cv(CV_SC + c_ * 4 + j)
                if j == 0:
                    self.ts(CS, RS[:, :, 0:8], wcol, None, ALU.mult, None, (("raw", cc_), "cv"), (("cvo", cc_, NTP),))
                else:
                    self.stt(CS, RS[:, :, j:j + 8], wcol, CS, ALU.mult, ALU.add,
                             (("raw", cc_), ("cvo", cc_, NTP), "cv"), (("cvo", cc_, NTP),))
            self.act(CVO[:, cc_, NTP:NT], CVO[:, cc_, NTP:NT], AF.Silu, (("cvo", cc_, NTP),), (("cvo", cc_, NTP),))
            if c_ < 32:
                q = self.ring("sq", 2)
                self.act(SQ[:, q, :NS], CVO[:, cc_, NTP:NT], AF.Square, (("cvo", cc_, NTP),), (("sq", q),))
                emit_l2(cc_, NTP, NS, q)
                keys = allk("ssq", cc_)
                self.rstd(SSQ[cc_][:, 0:NT], SSQ[cc_][:, 0:NT], 1.0, keys, keys)
                if c_ < 16:
                    self.stt(QO[:, cc_, :], CVO[:, cc_, :], float(128 ** -0.5), SSQ[cc_][:, 0:NT], ALU.mult, ALU.mult,
                             keys + allk("cvo", cc_), allk("qo", cc_))
                else:
                    self.tt(QO[:, cc_, :], CVO[:, cc_, :], SSQ[cc_][:, 0:NT], ALU.mult,
                            keys + allk("cvo", cc_), allk("qo", cc_))
            else:
                self.cp(QO[:, cc_, NTP:NT], CVO[:, cc_, NTP:NT], (("cvo", cc_, NTP),), (("qo", cc_, NTP),))
            self.dma("sp", self.qkvF[c_], QO[:, cc_, :], ("qo", cc_), allk("qo", cc_), ())
            b = self.bank()
            self.tr(self.ps[0:3, b, 0:128], RAW[:, cc_, 176:179], ident, (("raw", cc_), "cm"), (("ps", b),))
            kc_ = self.ring("ct", 2)
            self.cp(CT[:, kc_, :].rearrange("p (s j) -> p s j", j=3), RS[:, :, 8:11], (("raw", cc_),), (("ct", kc_),), eng="pool")
            self.tr(self.ps[0:48, b, 128:256], CT[:, kc_, :], ident, (("ct", kc_), "cm"), (("ps", b),))
            k = self.ring("st", 2)
            self.act(ST[0:3, k, 0:128], self.ps[0:3, b, 0:128], AF.Copy, (("ps", b),), (("st", k),))
            self.act(ST[0:48, k, 128:256], self.ps[0:48, b, 128:256], AF.Copy, (("ps", b),), (("st", k),))
            self.dma("sp", self.nq_p[:, c_ * 128:(c_ + 1) * 128], ST[0:3, k, 0:128], ("st", k), (("st", k),), ())
            self.dma("sp", self.nq_s[:, c_ * 128:(c_ + 1) * 128], ST[0:48, k, 128:256], ("st", k), (("st", k),), ())
        deferred = None

        def qkv_wload(bi_):
            s_ = self.wslot()
            wv_ = self.wview(s_, [128, KC, 256])
            self.wload(s_, wv_, win[:, :, bi_ * 256:(bi_ + 1) * 256])
            return s_, wv_
        nxt_w = qkv_wload(0)
        for bi in range(24):
            s, wv = nxt_w
            if bi + 1 < 24:
                nxt_w = qkv_wload(bi + 1)
            xs = self.ring("xt", 4)
            self.dma("sp", XT[0:48, xs, 0:256], self.sqkv[:, bi * 256:(bi + 1) * 256], ("xt", xs), (), (("xt", xs),))
            for cc in range(2):
                c = bi * 2 + cc
                RS = RAW[:, cc, 0:176].rearrange("p (s j) -> p s j", j=11)
                self.memset(RAWB[:, cc, 0:3], 0.0, (("rawb", cc, -512),), eng="pool")
                b = self.bank()
                self.tr(self.ps[:, b, 0:48], XT[0:48, xs, cc * 128:(cc + 1) * 128], ident[0:48, 0:48],
                        (("xt", xs), "cm"), (("ps", b),))
                self.act(RS[:, :, 0:3], self.ps[:, b, 0:48].rearrange("p (s j) -> p s j", j=3), AF.Copy,
                         (("ps", b),), (("raw", cc),))
                for j in range(4):
                    self.ts(DG[:, cc, j, :], self.ident_bf, self.cv(CV_SC + c * 4 + j), None, ALU.mult, None,
                            ("cb", "cv"), (("dg", cc),))
                pend = None
                pend_l2 = None
                for ti, (a, n) in enumerate(tl):
                    b = self.bank()
                    for kc in range(KC):
                        self.mm(self.ps[:, b, :n], wv[:, kc, cc * 128:(cc + 1) * 128], UN[:, kc, a:a + n],
                                kc == 0, kc == KC - 1, (("w", s),), (("ps", b),))
                    if a < NTP:
                        self.act(RAWB[:, cc, 3 + a:3 + a + n], self.ps[:, b, :n], AF.Copy, (("ps", b),), (("rawb", cc, a),))
                        if a + n == NTP:
                            self.act(RAW[:, cc, 176:179], self.ps[:, b, n - 3:n], AF.Copy, (("ps", b),), (("raw", cc),))
                    else:
                        self.act(RS[:, :, 3:11], self.ps[:, b, 0:128].rearrange("p (s j) -> p s j", j=8), AF.Copy,
                                 (("ps", b),), (("raw", cc),))
                    if ti == 1 and deferred is not None:
                        epilogue(*deferred)
                        deferred = None
                    if pend_l2 is not None:
                        emit_l2(*pend_l2)
                        pend_l2 = None
                    if pend is not None:
                        pend_l2 = emit_conv(c, *pend)
                    pend = (cc, a, n) if a < NTP else None
                deferred = (c, cc, pend, pend_l2)
        epilogue(*deferred)
        s = self.wslot()
        wv = self.wview(s, [128, KC, 32])
        self.wload(s, wv, win[:, :, O_BETA:O_BETA + 32])
        for (a, n) in tl:
            bb = self.bank()
            for kc in range(KC):
                self.mm(self.ps[0:16, bb, :n], wv[:, kc, 0:16], UN[:, kc, a:a + n], kc == 0, kc == KC - 1,
                        (("w", s),), (("ps", bb),))
            ba = self.bank()
            for kc in range(KC):
                self.mm(self.ps[0:16, ba, :n], wv[:, kc, 16:32], UN[:, kc, a:a + n], kc == 0, kc == KC - 1,
                        (("w", s),), (("ps", ba),))
            self.act(self.BETA[:, a:a + n], self.ps[0:16, bb, :n], AF.Sigmoid, (("ps", bb),), (("bg", a),))
            r = self.ring("xt", 4)
            self.act(XT[0:16, r, :n], self.ps[0:16, ba, :n], AF.Exp, (("ps", ba), "cv"), (("xt", r),),
                     bias=self.cv(CV_DT, 1, 16), scale=1.0)
            self.act(XT[0:16, r, :n], XT[0:16, r, :n], AF.Ln, (("xt", r),), (("xt", r),), bias=1.0, scale=1.0)
            self.ts(self.GG[:, a:a + n], XT[0:16, r, :n], self.cv(CV_NA, 1, 16), None, ALU.mult, None,
                    (("xt", r), "cv3"), (("bg", a),))

    def delta_layout(self):
        o = PH_OFF + 2 * NT * 4
        self.OF = self.v(o, [128, NH, NM], BF16); o += NH * NM * 2
        self.dl_free = o
        L = {}

        def al(name, shape, dt=F32):
            nonlocal o
            L[name] = self.v(o, shape, dt)
            n = 1
            for s in shape[1:]:
                n *= s
            o += ((n * (4 if dt == F32 else 2) + 31) // 32) * 32
        al("S", [128, NH, 128]); al("SB", [128, NH, 128], BF16)
        al("QKVC", [128, 2, 48 * 64], BF16)
        for nm in ("GROW", "NBROW", "GAM", "E1", "W2", "E2", "RM"):
            al(nm, [128, 512])
        self.alias_off = o
        for pb_ in range(2):
            for nm in ("GROW", "NBROW", "GAM", "E1", "W2", "E2", "RM"):
                if pb_ == 0:
                    L["%s_0" % nm] = L[nm]
                else:
                    al("%s_1" % nm, [128, 512])
        for nm in ("P0", "P1", "PT0", "PT1", "RB", "TBT0", "QKM0", "KG0", "QG0", "TBT1", "QKM1", "KG1", "QG1"):
            al(nm, [128, 512], BF16)
        al("KDEC0", [128, 1024], BF16); al("KDEC1", [128, 1024], BF16)
        al("DDB", [128, 1024], BF16); al("UUB", [128, 1024], BF16); al("OTB", [128, 512])
        al("TOK", [128, 32]); al("NBTOK", [128, 16]); al("COLS", [128, 32]); al("DCOL", [128, 16])
        al("TOK_1", [128, 32]); al("NBTOK_1", [128, 16]); al("COLS_1", [128, 32]); al("DCOL_1", [128, 16])
        for nm in ("TOK", "NBTOK", "COLS", "DCOL"):
            L[nm + "_0"] = L[nm]
        for nm in ("P0", "P1", "PT0", "PT1", "RB"):
            L[nm + "_0"] = L[nm]
            al(nm + "_1", [128, 512], BF16)
        for pb_ in range(2):
            al("KK_%d" % pb_, [128, 512], BF16); al("QK_%d" % pb_, [128, 512], BF16)
        al("GLR0", [128, 128]); al("GLR1", [128, 128]); al("RHS2", [128, 128])
        for nm in ("QKM2", "KG2", "QG2"):
            al(nm, [128, 512], BF16)
        al("KDEC2", [128, 1024], BF16); al("GLR2", [128, 128])
        for nm in ("TBT", "QKM", "KG", "QG", "KDEC", "GLR"):
            L[nm] = L[nm + "0"]
        al("DT", [128, 4, 128], BF16); al("DD", [128, 4, 128], BF16); al("UU", [128, 4, 128], BF16)
        al("OT", [128, 4, 128])
        o_save = o
        o = self.alias_off
        al("S2", [128, 16, 128])
        al("UBD", [128, 16, 128], BF16)
        assert o <= o_save
        o = o_save
        L["RHS"] = L["RM_1"]
        self.L = L
        print("delta layout end", o)
        assert o <= ARENA_BYTES, o

    def delta_block(self, C, HB, nseg, tok0, qkvc, need_o, otok0, mk, nlev, sample=False, qk="qkvc"):
        L = self.L
        ones_f = self.cm(CM_ONE, 128)
        ident_f = self.cm(CM_ID, 128)
        W = C * HB

        def t3(name, rows=128, dt=None):
            return L[name][0:rows, 0:W].rearrange("p (h t) -> p h t", h=HB)
        b = self.bank()
        self.tr(self.ps[0:C, b, 0:16], self.BETA[0:16, tok0:tok0 + C], ident_f[0:16, 0:16], ("cm",), (("ps", b),))
        self.tr(self.ps[0:C, b, 16:32], self.GG[0:16, tok0:tok0 + C], ident_f[0:16, 0:16], ("cm",), (("ps", b),))
        TOK = L["TOK"][0:C]
        self.act(TOK, self.ps[0:C, b, 0:32], AF.Copy, (("ps", b),), ("tok",))
        NBTOK = L["NBTOK"][0:C]
        self.ts(NBTOK, TOK[:, 0:16], -1.0, None, ALU.mult, None, ("tok",), ("nbtok",))
        b = self.bank()
        self.mm(self.ps[0:C, b, 0:16], mk["TRI"], TOK[:, 16:32], True, True, ("tok", "cm"), (("ps", b),))
        self.mm(self.ps[0:C, b, 16:32], mk["SAME"], TOK[:, 16:32], True, True, ("tok", "cm"), (("ps", b),))
        COLS = L["COLS"][0:C]
        self.act(COLS, self.ps[0:C, b, 0:32], AF.Copy, (("ps", b),), ("cols",))
        DCOL = L["DCOL"][0:C]
        self.tt(DCOL, COLS[:, 16:32], COLS[:, 0:16], ALU.subtract, ("cols",), ("dcol",))
        self.act(DCOL, DCOL, AF.Exp, ("dcol",), ("dcol",))
        for hb in range(NH // HB):
            h0 = hb * HB
            gt = TOK[:, 16 + h0:16 + h0 + HB]
            RHS = t3("RHS", C)

            def bc_h(m):
                return m.unsqueeze(1).to_broadcast([C, HB, C])

            def bc_t(col, n=C):
                return col.unsqueeze(2).to_broadcast([C, HB, n])
            self.tt(RHS, bc_h(mk["TRI"]), bc_t(gt), ALU.mult, ("tok", "cm"), ("rhs",))
            b = self.bank()
            self.mm(self.ps[:, b, 0:W], ones_f[0:C, :], L["RHS"][0:C, 0:W], True, True, ("rhs", "cm"), (("ps", b),))
            self.act(L["GROW"][:, 0:W], self.ps[:, b, 0:W], AF.Copy, (("ps", b),), ("grow",))
            self.tt(RHS, bc_h(mk["ID"]), bc_t(NBTOK[:, h0:h0 + HB]), ALU.mult, ("nbtok", "cm", "rhs"), ("rhs",))
            b = self.bank()
            self.mm(self.ps[:, b, 0:W], ones_f[0:C, :], L["RHS"][0:C, 0:W], True, True, ("rhs", "cm"), (("ps", b),))
            self.act(L["NBROW"][:, 0:W], self.ps[:, b, 0:W], AF.Copy, (("ps", b),), ("nbrow",))
            R2 = L["RHS2"][0:C, 0:HB * nseg].rearrange("p (h s) -> p h s", h=HB)
            self.tt(R2, mk["SEG"].unsqueeze(1).to_broadcast([C, HB, nseg]), bc_t(gt, nseg), ALU.mult,
                    ("tok", "cm"), ("rhs2",))
            b = self.bank()
            self.mm(self.ps[:, b, 0:HB * nseg], ones_f[0:C, :], L["RHS2"][0:C, 0:HB * nseg], True, True,
                    ("rhs2", "cm"), (("ps", b),))
            GLR = L["GLR"][:, 0:HB * nseg]
            self.act(GLR, self.ps[:, b, 0:HB * nseg], AF.Exp, (("ps", b),), ("glr",))
            GLR3 = GLR.rearrange("p (h s) -> p h s", h=HB)
            self.act(L["GAM"][:, 0:W], L["GROW"][:, 0:W], AF.Exp, ("grow",), ("gam",))
            GAM = t3("GAM")
            kq = qkvc.rearrange("p (c t) -> p c t", c=48)
            self.tt(t3("KG"), kq[:, 16 + h0:16 + h0 + HB, :], GAM, ALU.mult, ("gam", qk), ("kg",))
            if need_o:
                self.tt(t3("QG"), kq[:, h0:h0 + HB, :], GAM, ALU.mult, ("gam", qk), ("qg",), eng="pool")
            GROWc = t3("GROW", C)
            gcol = COLS[:, h0:h0 + HB]
            E1 = t3("E1", C); W2 = t3("W2", C); E2 = t3("E2", C)
            self.tt(E1, GROWc, bc_t(gcol), ALU.subtract, ("grow", "cols"), ("e1",))
            self.ts(E1, E1, 0.0, None, ALU.min, None, ("e1",), ("e1",))
            self.act(E1, E1, AF.Exp, ("e1",), ("e1",))
            self.tt(W2, E1, bc_h(mk["US"]), ALU.mult, ("e1", "cm"), ("w2",))
            self.tt(W2, W2, t3("NBROW", C), ALU.mult, ("w2", "nbrow"), ("w2",))
            self.tt(E1, E1, bc_h(mk["UI"]), ALU.mult, ("e1", "cm"), ("e1",))
            self.tt(E2, GROWc, bc_t(gcol), ALU.subtract, ("grow", "cols"), ("e2",))
            self.ts(E2, E2, -1.0, 0.0, ALU.mult, ALU.min, ("e2",), ("e2",))
            self.act(E2, E2, AF.Exp, ("e2",), ("e2",))
            self.tt(E2, E2, bc_h(mk["LS"]), ALU.mult, ("e2", "cm"), ("e2",))
            self.tt(E2, E2, bc_t(NBTOK[:, h0:h0 + HB]), ALU.mult, ("e2", "nbtok"), ("e2",))
            bk = self.bank()
            bq = self.bank()
            bt = self.bank()
            psT = self.ps[:, bt, :].bitcast(BF16)
            for j in range(HB):
                kf = kq[:, 16 + h0 + j, :]
                qf = kq[:, h0 + j, :]
                self.mm(self.ps[0:C, bk, j * C:(j + 1) * C], kf, kf, True, True, (qk,), (("ps", bk),))
                if need_o:
                    self.mm(self.ps[0:C, bq, j * C:(j + 1) * C], kf, qf, True, True, (qk,), (("ps", bq),))
                self.tr(psT[0:C, j * 128:(j + 1) * 128], kf, self.ident_bf, (qk, "cb"), (("ps", bt),))
            pk = self.ps[0:C, bk, 0:W]
            RM = L["RM"][0:C, 0:W]
            self.tt(RM, pk, L["W2"][0:C, 0:W], ALU.mult, (("ps", bk), "w2"), ("rm",))
            Pc = L["P0"][0:C, 0:W]
            PTc = L["PT0"][0:C, 0:W]
            self.cp(Pc, RM, ("rm",), ("p0",), eng="pool")
            self.tt(PTc, pk, L["E2"][0:C, 0:W], ALU.mult, (("ps", bk), "e2"), ("pt0",))
            self.tt(t3("RM", C), t3("RM", C), bc_h(mk["ID"]), ALU.add, ("rm", "cm"), ("rm",))
            RB = L["RB"][0:C, 0:W]
            self.cp(RB, RM, ("rm",), ("rb",), eng="pool")
            if need_o:
                self.tt(L["QKM"][0:C, 0:W], self.ps[0:C, bq, 0:W], L["E1"][0:C, 0:W], ALU.mult,
                        (("ps", bq), "e1"), ("qkm",))
            KD = L["KDEC"][0:C, 0:HB * 128].rearrange("p (h d) -> p h d", h=HB)
            self.tt(KD, psT[0:C, 0:HB * 128].rearrange("p (h d) -> p h d", h=HB),
                    DCOL[:, h0:h0 + HB].unsqueeze(2).to_broadcast([C, HB, 128]), ALU.mult,
                    (("ps", bt), "dcol"), ("kdec",))
            cur = 0
            for lv in range(nlev):
                Pn = L["P%d" % (1 - cur)][0:C, 0:W]
                PTn = L["PT%d" % (1 - cur)][0:C, 0:W]
                pkey, ptkey = "p%d" % cur, "pt%d" % cur
                pnkey, ptnkey = "p%d" % (1 - cur), "pt%d" % (1 - cur)
                ba = self.bank()
                bb = self.bank()
                for j in range(HB):
                    sl = slice(j * C, (j + 1) * C)
                    if lv < nlev - 1:
                        self.mm(self.ps[0:C, ba, sl], PTc[:, sl], Pc[:, sl], True, True, (pkey, ptkey), (("ps", ba),))
                    self.mm(self.ps[0:C, bb, sl], Pc[:, sl], PTc[:, sl], True, True, (pkey, ptkey), (("ps", bb),))
                if lv < nlev - 1:
                    self.act(Pn, self.ps[0:C, ba, 0:W], AF.Copy, (("ps", ba),), (pnkey,))
                self.cp(PTn, self.ps[0:C, bb, 0:W], (("ps", bb),), (ptnkey,))
                bc = self.bank()
                for j in range(HB):
                    sl = slice(j * C, (j + 1) * C)
                    self.mm(self.ps[0:C, bc, sl], PTn[:, sl], RB[:, sl], True, True, (ptnkey, "rb"), (("ps", bc),))
                self.tt(RM, RM, self.ps[0:C, bc, 0:W], ALU.add, ("rm", ("ps", bc)), ("rm",))
                if lv < nlev - 1:
                    self.cp(RB, RM, ("rm",), ("rb",), eng="pool")
                Pc, PTc = Pn, PTn
                cur = 1 - cur
            self.tt(t3("TBT", C), t3("RM", C), bc_t(TOK[:, h0:h0 + HB]), ALU.mult, ("rm", "tok"), ("tbt",))
            for j in range(HB):
                h = h0 + j
                sl = slice(j * C, (j + 1) * C)
                KGh = L["KG"][:, sl]
                vf = kq[:, 32 + h, :]
                r = self.ring("chain", 4)
                DT = L["DT"][:, r, 0:C]
                DD = L["DD"][0:C, r, :]
                UU = L["UU"][0:C, r, :]
                OT = L["OT"][:, r, 0:C]
                if sample:
                    sr = h % 2
                    SSv = self.v_ss[sr]
                    SSB = L["SB"][:, 0:16, :]
                    self.dma("sp", SSv, self.sdelta[:, h].rearrange("s d v -> d s v"), ("ss", sr), (), (("ss", sr),))
                    self.cp(SSB, SSv, (("ss", sr),), ("ssb",), eng="pool")
                    bK = self.bank()
                    for sg in range(nseg):
                        self.mm(self.ps[:, bK, sg * LS:(sg + 1) * LS], SSB[:, sg, :], KGh[:, sg * LS:(sg + 1) * LS],
                                True, True, ("ssb", "kg"), (("ps", bK),))
                else:
                    SBh = L["SB"][:, h, :]
                    bK = self.bank()
                    self.mm(self.ps[:, bK, 0:C], SBh, KGh, True, True, (("sb", h), "kg"), (("ps", bK),))
                self.tt(DT, vf, self.ps[:, bK, 0:C], ALU.subtract, (qk, ("ps", bK)), (("dt", r),))
                bT = self.bank()
                pT = self.ps[:, bT, :].bitcast(BF16)
                self.tr(pT[0:C, 0:128], DT, self.ident_bf, (("dt", r), "cb"), (("ps", bT),))
                self.act(DD, pT[0:C, 0:128], AF.Copy, (("ps", bT),), (("dd", r),))
                bU = self.bank()
                self.mm(self.ps[0:C, bU, 0:128], L["TBT"][0:C, sl], DD, True, True, ("tbt", ("dd", r)), (("ps", bU),))
                self.act(UU, self.ps[0:C, bU, 0:128], AF.Copy, (("ps", bU),), (("uu", r),))
                if need_o:
                    bO = self.bank()
                    if sample:
                        for sg in range(nseg):
                            self.mm(self.ps[:, bO, sg * LS:(sg + 1) * LS], SSB[:, sg, :],
                                    L["QG"][:, j * C + sg * LS:j * C + (sg + 1) * LS], True, True,
                                    ("ssb", "qg"), (("ps", bO),))
                    else:
                        self.mm(self.ps[:, bO, 0:C], SBh, L["QG"][:, sl], True, True, (("sb", h), "qg"), (("ps", bO),))
                    self.mm(self.ps[:, bO, 128:128 + C], UU, L["QKM"][0:C, sl], True, True,
                            (("uu", r), "qkm"), (("ps", bO),))
                    self.act(OT, self.ps[:, bO, 0:C], AF.Copy, (("ps", bO),), (("ot", r),))
                    self.tt(self.OF[:, h, otok0:otok0 + C], OT, self.ps[:, bO, 128:128 + C], ALU.add,
                            (("ot", r), ("ps", bO)), ())
                KDh = L["KDEC"][0:C, j * 128:(j + 1) * 128]
                if sample:
                    UBD = L["UBD"]
                    self.tt(UBD, UU.unsqueeze(1).to_broadcast([128, 16, 128]),
                            mk["SEG"].unsqueeze(2).to_broadcast([128, 16, 128]), ALU.mult,
                            (("uu", r), "cm"), ("ubd",))
                    gl = GLR3[:, j, :]
                    self.tt(SSv, SSv, gl.unsqueeze(2).to_broadcast([128, 16, 128]), ALU.mult,
                            (("ss", sr), "glr"), (("ss", sr),))
                    for q4 in range(4):
                        bS = self.bank()
                        self.mm(self.ps[:, bS, :], KDh, UBD[:, 4 * q4:4 * q4 + 4, :].rearrange("p a b -> p (a b)"),
                                True, True, ("kdec", "ubd"), (("ps", bS),))
                        sv = SSv[:, 4 * q4:4 * q4 + 4, :].rearrange("p a b -> p (a b)")
                        self.tt(sv, sv, self.ps[:, bS, :], ALU.add, (("ss", sr), ("ps", bS)), (("ss", sr),))
                    self.dma("sp", self.nd_s[:, h].rearrange("s d v -> d s v"), SSv, ("ss", sr), (("ss", sr),), ())
                else:
                    bS = self.bank()
                    self.mm(self.ps[:, bS, 0:128], KDh, UU, True, True, ("kdec", ("uu", r)), (("ps", bS),))
                    Sh = L["S"][:, h, :]
                    self.stt(Sh, Sh, GLR[:, j:j + 1], self.ps[:, bS, 0:128], ALU.mult, ALU.add,
                             (("s", h), "glr", ("ps", bS)), (("s", h),))
                    self.cp(SBh, Sh, (("s", h),), (("sb", h),), eng="pool")

    def p_prep(self, c, hb, pb, ob, qkvc, qk, need_o, mk):
        C, HB, W, nlev = 64, 8, 512, 5
        L = self.L
        ones_f = self.cm(CM_ONE, 128)
        ident_f = self.cm(CM_ID, 128)
        tok0 = c * 64
        cp_ = c % 2

        def T(name):
            return L["%s_%d" % (name, pb)]

        def K(name):
            return (name, pb)

        def t3(name, rows=128):
            return T(name)[0:rows, 0:W].rearrange("p (h t) -> p h t", h=HB)

        def bc_h(m):
            return m.unsqueeze(1).to_broadcast([C, HB, C])

        def bc_t(col, n=C):
            return col.unsqueeze(2).to_broadcast([C, HB, n])
        TOK = L["TOK_%d" % cp_][0:C]
        NBTOK = L["NBTOK_%d" % cp_][0:C]
        COLS = L["COLS_%d" % cp_][0:C]
        DCOL = L["DCOL_%d" % cp_][0:C]
        kt, kn, kc_, kd = ("tok", cp_), ("nbtok", cp_), ("cols", cp_), ("dcol", cp_)
        if hb == 0:
            b = self.bank()
            self.tr(self.ps[0:C, b, 0:16], self.BETA[0:16, tok0:tok0 + C], ident_f[0:16, 0:16], ("cm",), (("ps", b),))
            self.tr(self.ps[0:C, b, 16:32], self.GG[0:16, tok0:tok0 + C], ident_f[0:16, 0:16], ("cm",), (("ps", b),))
            self.act(TOK, self.ps[0:C, b, 0:32], AF.Copy, (("ps", b),), (kt,))
            self.ts(NBTOK, TOK[:, 0:16], -1.0, None, ALU.mult, None, (kt,), (kn,), eng="pool")
            b = self.bank()
            self.mm(self.ps[0:C, b, 0:16], mk["TRI"], TOK[:, 16:32], True, True, (kt, "cm"), (("ps", b),))
            self.mm(self.ps[0:C, b, 16:32], mk["SAME"], TOK[:, 16:32], True, True, (kt, "cm"), (("ps", b),))
            self.act(COLS, self.ps[0:C, b, 0:32], AF.Copy, (("ps", b),), (kc_,))
            self.tt(DCOL, COLS[:, 16:32], COLS[:, 0:16], ALU.subtract, (kc_,), (kd,), eng="pool")
            self.act(DCOL, DCOL, AF.Exp, (kd,), (kd,))
            yield
        h0 = hb * HB
        gt = TOK[:, 16 + h0:16 + h0 + HB]
        kq = qkvc.rearrange("p (c t) -> p c t", c=48)
        bk = self.bank()
        bq = self.bank() if need_o else None
        bt = self.bank()
        psT = self.ps[:, bt, :].bitcast(BF16)
        for j in range(HB):
            kf = kq[:, 16 + h0 + j, :]
            qf = kq[:, h0 + j, :]
            self.mm(self.ps[0:C, bk, j * C:(j + 1) * C], kf, kf, True, True, (qk,), (("ps", bk),))
            if need_o:
                self.mm(self.ps[0:C, bq, j * C:(j + 1) * C], kf, qf, True, True, (qk,), (("ps", bq),))
            self.tr(psT[0:C, j * 128:(j + 1) * 128], kf, self.ident_bf, (qk, "cb"), (("ps", bt),))
        KK = T("KK")[0:C, 0:W]
        self.act(KK, self.ps[0:C, bk, 0:W], AF.Copy, (("ps", bk),), (K("kk"),))
        if need_o:
            QK = T("QK")[0:C, 0:W]
            self.act(QK, self.ps[0:C, bq, 0:W], AF.Copy, (("ps", bq),), (K("qk"),))
        RHS = t3("W2", C)
        self.tt(RHS, bc_h(mk["TRI"]), bc_t(gt), ALU.mult, (kt, "cm", K("w2")), (K("w2"),))
        b1 = self.bank()
        self.mm(self.ps[:, b1, 0:W], ones_f[0:C, :], T("W2")[0:C, 0:W], True, True, (K("w2"), "cm"), (("ps", b1),))
        self.act(T("GROW")[:, 0:W], self.ps[:, b1, 0:W], AF.Copy, (("ps", b1),), (K("grow"),))
        RHSb = t3("E2", C)
        self.tt(RHSb, bc_h(mk["ID"]), bc_t(NBTOK[:, h0:h0 + HB]), ALU.mult, (kn, "cm", K("e2")), (K("e2"),), eng="pool")
        b2 = self.bank()
        self.mm(self.ps[:, b2, 0:W], ones_f[0:C, :], T("E2")[0:C, 0:W], True, True, (K("e2"), "cm"), (("ps", b2),))
        MB = t3("NBROW", C)
        self.tt(MB, self.ps[0:C, b2, 0:W].rearrange("p (h t) -> p h t", h=HB), bc_h(mk["US"]), ALU.mult,
                (("ps", b2), "cm"), (K("mb"),))
        b3 = self.bank()
        self.mm(self.ps[:, b3, 0:HB], ones_f[0:C, :], gt, True, True, (kt, "cm"), (("ps", b3),))
        GLR = L["GLR%d" % ob][:, 0:HB]
        self.act(GLR, self.ps[:, b3, 0:HB], AF.Exp, (("ps", b3),), (("glr", ob),))
        KD = L["KDEC%d" % ob][0:C, 0:HB * 128].rearrange("p (h d) -> p h d", h=HB)
        self.tt(KD, psT[0:C, 0:HB * 128].rearrange("p (h d) -> p h d", h=HB),
                DCOL[:, h0:h0 + HB].unsqueeze(2).to_broadcast([C, HB, 128]), ALU.mult,
                (("ps", bt), kd), (("kdec", ob),))
        yield
        GROWc = t3("GROW", C)
        gcol = COLS[:, h0:h0 + HB]
        E1 = t3("E1", C); W2 = t3("W2", C); E2 = t3("E2", C)
        self.tt(E1, GROWc, bc_t(gcol), ALU.subtract, (K("grow"), kc_), (K("e1"),))
        self.tt(E2, bc_t(gcol), GROWc, ALU.subtract, (K("grow"), kc_, K("e2")), (K("e2"),))
        self.act(T("GAM")[:, 0:W], T("GROW")[:, 0:W], AF.Exp, (K("grow"),), (K("gam"),))
        self.ts(E1, E1, 0.0, None, ALU.min, None, (K("e1"),), (K("e1"),))
        self.ts(E2, E2, 0.0, None, ALU.min, None, (K("e2"),), (K("e2"),))
        yield
        self.act(E1, E1, AF.Exp, (K("e1"),), (K("e1"),))
        self.act(E2, E2, AF.Exp, (K("e2"),), (K("e2"),))
        GAM = t3("GAM")
        self.tt(L["KG%d" % ob][:, 0:W].rearrange("p (h t) -> p h t", h=HB),
                kq[:, 16 + h0:16 + h0 + HB, :], GAM, ALU.mult, (K("gam"), qk), (("kg", ob),))
        if need_o:
            self.tt(L["QG%d" % ob][:, 0:W].rearrange("p (h t) -> p h t", h=HB), kq[:, h0:h0 + HB, :], GAM, ALU.mult,
                    (K("gam"), qk), (("qg", ob),), eng="pool")
        yield
        self.stt(W2, E1, 1.0, MB, ALU.min, ALU.mult, (K("e1"), K("mb")), (K("w2"),))
        self.stt(E2, E2, 1.0, bc_h(mk["LS"]), ALU.min, ALU.mult, (K("e2"), "cm"), (K("e2"),))
        yield
        RM = T("RM")[0:C, 0:W]
        Pc = T("P0")[0:C, 0:W]
        PTc = T("PT0")[0:C, 0:W]
        self.tt(Pc, KK, T("W2")[0:C, 0:W], ALU.mult, (K("kk"), K("w2")), (K("p0"),), eng="pool")
        self.tt(t3("E2", C), t3("E2", C), bc_t(NBTOK[:, h0:h0 + HB]), ALU.mult, (K("e2"), kn), (K("e2"),), eng="pool")
        self.tt(RM, KK, T("W2")[0:C, 0:W], ALU.mult, (K("kk"), K("w2")), (K("rm"),))
        yield
        self.tt(PTc, KK, T("E2")[0:C, 0:W], ALU.mult, (K("kk"), K("e2")), (K("pt0"),))
        self.tt(t3("RM", C), t3("RM", C), bc_h(mk["ID"]), ALU.add, (K("rm"), "cm"), (K("rm"),), eng="pool")
        RB = T("RB")[0:C, 0:W]
        if need_o:
            self.stt(E1, E1, 1.0, bc_h(mk["UI"]), ALU.min, ALU.mult, (K("e1"), "cm"), (K("e1"),))
            self.tt(L["QKM%d" % ob][0:C, 0:W], QK, T("E1")[0:C, 0:W], ALU.mult, (K("qk"), K("e1")), (("qkm", ob),))
        yield "HALF"
        self.act(RB, RM, AF.Copy, (K("rm"),), (K("rb"),))
        cur = 0

        def squares(lv, Pc, PTc, cur):
            Pn = T("P%d" % (1 - cur))[0:C, 0:W]
            PTn = T("PT%d" % (1 - cur))[0:C, 0:W]
            pkey, ptkey = K("p%d" % cur), K("pt%d" % cur)
            ba = self.bank()
            bb = self.bank()
            for j in range(HB):
                sl = slice(j * C, (j + 1) * C)
                if lv < nlev - 1:
                    self.mm(self.ps[0:C, ba, sl], PTc[:, sl], Pc[:, sl], True, True, (pkey, ptkey), (("ps", ba),))
                self.mm(self.ps[0:C, bb, sl], Pc[:, sl], PTc[:, sl], True, True, (pkey, ptkey), (("ps", bb),))
            if lv < nlev - 1:
                self.act(Pn, self.ps[0:C, ba, 0:W], AF.Copy, (("ps", ba),), (K("p%d" % (1 - cur)),))
            self.act(PTn, self.ps[0:C, bb, 0:W], AF.Copy, (("ps", bb),), (K("pt%d" % (1 - cur)),))
            return Pn, PTn
        Pn, PTn = squares(0, Pc, PTc, cur)
        for lv in range(nlev):
            ptnkey = K("pt%d" % (1 - cur))
            Pc, PTc = Pn, PTn
            cur = 1 - cur
            yield
            bc = self.bank()
            for j in range(HB):
                sl = slice(j * C, (j + 1) * C)
                self.mm(self.ps[0:C, bc, sl], PTc[:, sl], RB[:, sl], True, True, (ptnkey, K("rb")), (("ps", bc),))
            if lv + 1 < nlev:
                Pn, PTn = squares(lv + 1, Pc, PTc, cur)
            self.tt(RM, RM, self.ps[0:C, bc, 0:W], ALU.add, (K("rm"), ("ps", bc)), (K("rm"),))
            if lv < nlev - 1:
                yield
                self.act(RB, RM, AF.Copy, (K("rm"),), (K("rb"),))
        yield
        self.tt(L["TBT%d" % pb][0:C, 0:W].rearrange("p (h t) -> p h t", h=HB), t3("RM", C),
                bc_t(TOK[:, h0:h0 + HB]), ALU.mult, (K("rm"), kt), (("tbt", pb),))

    def p_chain(self, c, hb, pb, ob, qkvc, qk, need_o):
        C, HB, W = 64, 8, 512
        L = self.L
        h0 = hb * HB
        kq = qkvc.rearrange("p (c t) -> p c t", c=48)
        SBk = [("sb", h0 + j) for j in range(HB)]
        Sk = [("s", h0 + j) for j in range(HB)]
        KG = L["KG%d" % ob]
        bK = self.bank()
        for j in range(HB):
            self.mm(self.ps[:, bK, j * C:(j + 1) * C], L["SB"][:, h0 + j, :], KG[:, j * C:(j + 1) * C], True, True,
                    (("sb", h0 + j), ("kg", ob)), (("ps", bK),))
        if need_o:
            bO1 = self.bank()
            for j in range(HB):
                self.mm(self.ps[:, bO1, j * C:(j + 1) * C], L["SB"][:, h0 + j, :], L["QG%d" % ob][:, j * C:(j + 1) * C],
                        True, True, (("sb", h0 + j), ("qg", ob)), (("ps", bO1),))
        DT = L["DT"].rearrange("p a b -> p (a b)")
        self.tt(DT.rearrange("p (h t) -> p h t", h=HB), kq[:, 32 + h0:32 + h0 + HB, :],
                self.ps[:, bK, 0:W].rearrange("p (h t) -> p h t", h=HB), ALU.subtract, (qk, ("ps", bK)), ("dt",))
        if need_o:
            OT = L["OTB"]
            self.act(OT, self.ps[:, bO1, 0:W], AF.Copy, (("ps", bO1),), ("ot",))
        yield
        bT = self.bank()
        pT = self.ps[:, bT, :].bitcast(BF16)
        for j in range(HB):
            self.tr(pT[0:C, j * 128:(j + 1) * 128], DT[:, j * C:(j + 1) * C], self.ident_bf, ("dt", "cb"), (("ps", bT),))
        DD = L["DDB"][0:C, :]
        self.act(DD, pT[0:C, 0:1024], AF.Copy, (("ps", bT),), ("dd",))
        yield
        TBT = L["TBT%d" % pb]
        bU = [self.bank(), self.bank()]
        for j in range(HB):
            self.mm(self.ps[0:C, bU[j // 4], (j % 4) * 128:(j % 4 + 1) * 128], TBT[0:C, j * C:(j + 1) * C],
                    DD[:, j * 128:(j + 1) * 128], True, True, (("tbt", pb), "dd"), (("ps", bU[j // 4]),))
        UU = L["UUB"][0:C, :]
        self.act(UU[:, 0:512], self.ps[0:C, bU[0], :], AF.Copy, (("ps", bU[0]),), ("uu0",))
        self.cp(UU[:, 512:1024], self.ps[0:C, bU[1], :], (("ps", bU[1]),), ("uu1",))
        yield
        KD = L["KDEC%d" % ob]
        bS = [self.bank(), self.bank()]
        for j in range(HB):
            self.mm(self.ps[:, bS[j // 4], (j % 4) * 128:(j % 4 + 1) * 128], KD[0:C, j * 128:(j + 1) * 128],
                    UU[:, j * 128:(j + 1) * 128], True, True, (("kdec", ob), "uu%d" % (j // 4)), (("ps", bS[j // 4]),))
        if need_o:
            bO2 = self.bank()
            for j in range(HB):
                self.mm(self.ps[:, bO2, j * C:(j + 1) * C], UU[:, j * 128:(j + 1) * 128],
                        L["QKM%d" % ob][0:C, j * C:(j + 1) * C], True, True,
                        ("uu%d" % (j // 4), ("qkm", ob)), (("ps", bO2),))
        S8 = L["S"][:, h0:h0 + HB, :]
        GLR = L["GLR%d" % ob][:, 0:HB]
        self.tt(S8, S8, GLR.unsqueeze(2).to_broadcast([128, HB, 128]), ALU.mult, Sk + [("glr", ob)], Sk, eng="pool")
        for q in range(2):
            s4 = L["S"][:, h0 + 4 * q:h0 + 4 * q + 4, :].rearrange("p a b -> p (a b)")
            self.tt(s4, s4, self.ps[:, bS[q], :], ALU.add, Sk[4 * q:4 * q + 4] + [("ps", bS[q])], Sk[4 * q:4 * q + 4])
        self.act(L["SB"][:, h0:h0 + HB, :], S8, AF.Copy, Sk, SBk)
        if need_o:
            otok0 = c * 64 - M0
            self.tt(self.OF[:, h0:h0 + HB, otok0:otok0 + C], OT.rearrange("p (h t) -> p h t", h=HB),
                    self.ps[:, bO2, 0:W].rearrange("p (h t) -> p h t", h=HB), ALU.add, ("ot", ("ps", bO2)), ())
        yield

    def delta_prompt_pipelined(self, mk):
        L = self.L
        self.rot = list(range(8))
        units = [(c, hb) for c in range(NTP // 64) for hb in range(2)]
        qslot = {}

        def step(g):
            try:
                return next(g) or True
            except StopIteration:
                return False
        preps = {}
        info = {}

        def start_prep(i):
            c, hb = units[i]
            if hb == 0:
                s = self.ring("qkvc", 2)
                qslot[c] = s
                self.dma("sp", L["QKVC"][:, s, :].rearrange("p (c t) -> p c t", c=48),
                         self.qkvF[:, :, c * 64:(c + 1) * 64].rearrange("c p t -> p c t"), ("qkvc", s), (), (("qkvc", s),))
            s = qslot[c]
            qc = L["QKVC"][:, s, :]
            need_o = c * 64 >= M0
            info[i] = (c, hb, i % 2, i % 3, qc, ("qkvc", s), need_o)
            preps[i] = self.p_prep(c, hb, i % 2, i % 3, qc, ("qkvc", s), need_o, mk)
        n = len(units)
        start_prep(0)
        while step(preps[0]) != "HALF":
            pass
        chain = None
        for i in range(n):
            if i + 1 < n:
                start_prep(i + 1)
            a_live = i + 1 < n
            b_live = True
            c_live = chain is not None
            while a_live or b_live or c_live:
                if b_live:
                    b_live = bool(step(preps[i]))
                if a_live:
                    if step(preps[i + 1]) == "HALF":
                        a_live = False
                if c_live:
                    c_live = bool(step(chain))
            c, hb, pb, ob, qc, qk, need_o = info[i]
            chain = self.p_chain(c, hb, pb, ob, qc, qk, need_o)
        while step(chain):
            pass

    def delta(self):
        self.delta_layout()
        L = self.L
        self.barrier()
        self.rot = list(range(8))
        self.memset(L["S"], 0.0, [("s", h) for h in range(NH)])
        self.memset(L["SB"], 0.0, [("sb", h) for h in range(NH)], eng="pool")
        mk = dict(TRI=self.cm(CM_TRI, 64, 64), UI=self.cm(CM_UI, 64, 64), US=self.cm(CM_US, 64, 64),
                  LS=self.cm(CM_LS, 64, 64), SAME=self.cm(CM_ONE, 64, 64), ID=self.cm(CM_ID, 64, 64),
                  SEG=self.cm(CM_ONE, 1, 64))
        self.delta_prompt_pipelined(mk)
        self.dma("sp", self.nd_p.rearrange("h d v -> d h v"), L["S"], ("sfin",), [("s", h) for h in range(NH)], ())
        self.barrier()
        self.rot = list(range(8))
        mk8 = dict(TRI=self.cm(CM_TRI8, 128), UI=self.cm(CM_UI8, 128), US=self.cm(CM_US8, 128),
                   LS=self.cm(CM_LS8, 128), SAME=self.cm(CM_SAME8, 128), ID=self.cm(CM_ID, 128),
                   SEG=self.cm(CM_SEG, 16))
        self.v_ss = [L["S"][:, 0:16, :], L["S2"]]
        qc = L["QKVC"].rearrange("p a b -> p (a b)")
        self.dma("sp", qc.rearrange("p (c t) -> p c t", c=48),
                 self.qkvF[:, :, NTP:NT].rearrange("c p t -> p c t"), ("qkvc", 0), (), (("qkvc", 0),))
        self.delta_block(128, 4, 16, NTP, qc, True, NMAIN, mk8, 2, sample=True, qk=("qkvc", 0))

    def mixer_tail(self):
        NU = NM + 32
        U0 = M0 - 32
        o = PH_OFF
        RSTD = self.v(o, [128, NU]); o += NU * 4
        SQ = self.v(o, [128, 2, 512], BF16); o += 2048
        MU = self.v(o, [128, NM]); o += NM * 4
        RS = self.v(o, [128, NM]); o += NM * 4
        assert o <= PH_OFF + 2 * NT * 4
        o = PH_OFF + 2 * NT * 4
        OF = self.OF; o += NH * NM * 2
        UN2 = self.v(o, [128, KC, NU], BF16); o += KC * NU * 2
        CB = self.v(o, [128, KC, NM], BF16); o += KC * NM * 2
        XT = self.v(o, [128, 4, 512]); o += 8192
        GPB = self.v(o, [128, 1056], BF16); o += 1056 * 2
        GP30 = self.v(o, [128, 32]); o += 128
        GSX = self.v(o, [128, 608]); o += 608 * 4
        DG31 = self.v(o, [128, 31, 128], BF16); o += 31 * 128 * 2
        CO = self.v(o, [128, NM]); o += NM * 4
        assert o <= ARENA_BYTES, o
        ident = self.cm(CM_ID, 128)
        win = self.w_in.rearrange("(kc p) n -> p kc n", p=128)
        self.barrier()
        self.rot = list(range(8))
        self.norm_load(self.h1, U0, NT, CV_G["mix_pre"], UN2, RSTD, XT, SQ)
        self.barrier()
        tlm = _tiles(0, NM)
        for zb in range(8):
            s = self.wslot()
            wv = self.wview(s, [128, KC, 256])
            self.wload(s, wv, win[:, :, O_Z + zb * 256:O_Z + (zb + 1) * 256])
            for cc in range(2):
                h = zb * 2 + cc
                SS = MU if cc == 0 else RS
                zbanks = []
                for (a, n) in tlm:
                    ok = ("of", h, a)
                    q = self.ring("sq", 2)
                    self.act(SQ[:, q, :n], OF[:, h, a:a + n], AF.Square, (ok,), (("sq", q),))
                    b1 = self.bank()
                    self.mm(self.ps[:, b1, :n], self.ones_bf, SQ[:, q, :n], True, True, (("sq", q), "cb"), (("ps", b1),))
                    b = self.bank()
                    for kc in range(KC):
                        self.mm(self.ps[:, b, :n], wv[:, kc, cc * 128:(cc + 1) * 128], UN2[:, kc, 32 + a:32 + a + n],
                                kc == 0, kc == KC - 1, (("w", s),), (("ps", b),))
                    zbanks.append(b)
                    self.act(SS[:, a:a + n], self.ps[:, b1, :n], AF.Copy, (("ps", b1),), (("ss", cc, a),))
                    r2 = self.ring("xt", 4)
                    self.act(XT[:, r2, :n], self.ps[:, b, :n], AF.Silu, (("ps", b),), (("xt", r2),))
                    self.tt(OF[:, h, a:a + n], OF[:, h, a:a + n], XT[:, r2, :n], ALU.mult, (ok, ("xt", r2)), (ok,))
                sk = [("ss", cc, a) for (a, n) in tlm]
                self.rstd(SS[:, 0:NM], SS[:, 0:NM], 1.0 / 128, sk, sk)
                self.stt(OF[:, h, :], OF[:, h, :], self.cv(CV_ON), SS[:, 0:NM], ALU.mult, ALU.mult,
                         sk + [("of", h, a) for (a, n) in tlm] + ["cv"], [("of", h, a) for (a, n) in tlm])
        self.barrier()
        self.rot = list(range(8))
        SUMB = [(2, 0), (3, 0), (4, 0)]
        SSQB = [(5, 0), (6, 0), (7, 0)]
        GS = GSX.rearrange("p (s j) -> p s j", j=38)
        tlu = _tiles(0, NU)
        def sgt_load(c_):
            for q4 in range(4):
                self.dma("sp", MU[0:120, (c_ % 2) * 512 + q4 * 128:(c_ % 2) * 512 + (q4 + 1) * 128],
                         self.sglu[q4 * 120:(q4 + 1) * 120, c_ * 128:(c_ + 1) * 128],
                         ("sgt", c_ % 2), (), (("sgt", c_ % 2),))
        sgt_load(0)

        def glu_wload(c_):
            s_ = self.wslot()
            wv_ = self.wview(s_, [128, KC, 256])
            self.wload(s_, wv_[:, :, 0:128], win[:, :, O_GLU + c_ * 128:O_GLU + (c_ + 1) * 128])
            self.wload(s_, wv_[:, :, 128:256], win[:, :, O_GLU + 2048 + c_ * 128:O_GLU + 2048 + (c_ + 1) * 128])
            return s_, wv_
        nxt_w = glu_wload(0)
        for c in range(KC):
            s, wv = nxt_w
            if c + 1 < KC:
                nxt_w = glu_wload(c + 1)
                sgt_load(c + 1)
            for q4 in range(4):
                sg_ = MU[0:120, (c % 2) * 512 + q4 * 128:(c % 2) * 512 + (q4 + 1) * 128]
                b = self.bank()
                self.tr(self.ps[:, b, 0:120], sg_, ident[0:120, 0:120], (("sgt", c % 2), "cm"), (("ps", b),))
                self.act(GS[:, 4 * q4:4 * q4 + 4, 0:30], self.ps[:, b, 0:120].rearrange("p (s j) -> p s j", j=30),
                         AF.Copy, (("ps", b),), ("glx",))
            for (a, n) in tlu:
                ba = self.bank()
                for kc in range(KC):
                    self.mm(self.ps[:, ba, :n], wv[:, kc, 0:128], UN2[:, kc, a:a + n], kc == 0, kc == KC - 1,
                            (("w", s),), (("ps", ba),))
                bb = self.bank()
                for kc in range(KC):
                    self.mm(self.ps[:, bb, :n], wv[:, kc, 128:256], UN2[:, kc, a:a + n], kc == 0, kc == KC - 1,
                            (("w", s),), (("ps", bb),))
                r = self.ring("xt", 4)
                self.act(XT[:, r, :n], self.ps[:, bb, :n], AF.Sigmoid, (("ps", bb),), (("xt", r),))
                lo = max(a, 2)
                hi = min(a + n, 1056)
                if hi > lo:
                    self.tt(GPB[:, lo - 2:hi - 2], self.ps[:, ba, lo - a:hi - a], XT[:, r, lo - a:hi - a], ALU.mult,
                            (("ps", ba), ("xt", r)), ("glx",))
                if a <= 1026 < a + n:
                    self.tt(GP30[:, 0:30], self.ps[:, ba, 1026 - a:1056 - a], XT[:, r, 1026 - a:1056 - a], ALU.mult,
                            (("ps", ba), ("xt", r)), ("gp30",))
                if a + n > 1056:
                    o0 = 1056 - a
                    self.tt(GS[:, :, 30:38], self.ps[:, ba, o0:o0 + 128].rearrange("p (s j) -> p s j", j=8),
                            XT[:, r, o0:o0 + 128].rearrange("p (s j) -> p s j", j=8), ALU.mult,
                            (("ps", ba), ("xt", r)), ("glx",))
            COs = CO[:, NMAIN:NM].rearrange("p (s j) -> p s j", j=8)
            bcol = self.cv(CV_G["b_dw"] + c)
            self.tt(DG31, self.ident_bf.unsqueeze(1).to_broadcast([128, 31, 128]),
                    self.cv(CV_DW + c * 31, 31).unsqueeze(2).to_broadcast([128, 31, 128]), ALU.mult,
                    ("cb", "cv"), ("dg31",))
            for t0_ in (0, 512):
                b = self.bank()
                for j in range(31):
                    self.mm(self.ps[:, b, :], DG31[:, j, :], GPB[:, t0_ + j:t0_ + j + 512], j == 0, j == 30,
                            ("dg31", "glx"), (("ps", b),))
                self.act(CO[:, t0_:t0_ + 512], self.ps[:, b, :], AF.Identity, (("ps", b), "cv"), ("co",), bias=bcol, scale=1.0)
            for j in range(31):
                wcol = self.cv(CV_DW + c * 31 + j)
                if j == 0:
                    self.ts(COs, GS[:, :, 0:8], wcol, bcol, ALU.mult, ALU.add, ("glx", "cv"), ("co",))
                else:
                    self.stt(COs, GS[:, :, j:j + 8], wcol, COs, ALU.mult, ALU.add, ("glx", "co", "cv"), ("co",))
            b = self.bank()
            self.tr(self.ps[0:30, b, 0:128], GP30[:, 0:30], ident, ("gp30", "cm"), (("ps", b),))
            k = self.ring("xt", 4)
            self.act(XT[0:30, k, 0:128], self.ps[0:30, b, 0:128], AF.Copy, (("ps", b),), (("xt", k),))
            self.dma("act", self.ng_p[:, c * 128:(c + 1) * 128], XT[0:30, k, 0:128], ("xt", k), (("xt", k),), ())
            for q4 in range(4):
                k = self.ring("xt", 4)
                self.cp(XT[:, k, 0:120].rearrange("p (s j) -> p s j", j=30), GS[:, 4 * q4:4 * q4 + 4, 8:38],
                        ("glx",), (("xt", k),), eng="pool")
                b = self.bank()
                self.tr(self.ps[0:120, b, 0:128], XT[:, k, 0:120], ident, (("xt", k), "cm"), (("ps", b),))
                self.act(XT[0:120, k, 128:256], self.ps[0:120, b, 0:128], AF.Copy, (("ps", b),), (("xt", k),))
                self.dma("act", self.ng_s[q4 * 120:(q4 + 1) * 120, c * 128:(c + 1) * 128], XT[0:120, k, 128:256],
                         ("xt", k), (("xt", k),), ())
            self.dma("sp", self.cT[c], CO, ("co",), ("co",), ())
        self.barrier()
        self.rot = [0, 1]
        for c in range(KC):
            for ti, (a, n) in enumerate(tlm):
                r = self.ring("xt", 4)
                self.dma("sp" if (c + ti) % 2 == 0 else "act", XT[:, r, :n], self.cT[c, :, a:a + n], ("xt", r), (), (("xt", r),))
                COt = XT[:, r, 0:512]
                a0 = a
                a = 0
                q = self.ring("sq", 2)
                self.act(SQ[:, q, :n], COt[:, a:a + n], AF.Square, (("xt", r),), (("sq", q),))
                self.mm(self.ps[:, SSQB[ti][0], SSQB[ti][1]:SSQB[ti][1] + n], self.ones_bf, SQ[:, q, :n], c == 0, c == KC - 1,
                        (("sq", q), "cb"), (("ps", SSQB[ti][0]),))
                q = self.ring("sq", 2)
                self.cp(SQ[:, q, :n], COt[:, a:a + n], (("xt", r),), (("sq", q),))
                a = a0
                self.mm(self.ps[:, SUMB[ti][0], SUMB[ti][1]:SUMB[ti][1] + n], self.ones_bf, SQ[:, q, :n], c == 0, c == KC - 1,
                        (("sq", q), "cb"), (("ps", SUMB[ti][0]),))
        for ti, (a, n) in enumerate(tlm):
            sb_, so_ = SUMB[ti]
            qb_, qo_ = SSQB[ti]
            self.act(MU[:, a:a + n], self.ps[:, sb_, so_:so_ + n], AF.Copy, (("ps", sb_),), (("mu", a),), scale=1.0 / D)
            self.tt(RS[:, a:a + n], MU[:, a:a + n], MU[:, a:a + n], ALU.mult, (("mu", a),), (("rs", a),))
            self.stt(RS[:, a:a + n], self.ps[:, qb_, qo_:qo_ + n], 1.0 / D, RS[:, a:a + n], ALU.mult, ALU.subtract,
                     (("ps", qb_), ("rs", a)), (("rs", a),))
            self.ts(RS[:, a:a + n], RS[:, a:a + n], 0.0, None, ALU.max, None, (("rs", a),), (("rs", a),))
            self.rstd(RS[:, a:a + n], RS[:, a:a + n], 1.0, (("rs", a),), (("rs", a),))
        self.barrier()
        self.rot = list(range(8))
        for c in range(KC):
            for (a, n) in tlm:
                r = self.ring("xt", 4)
                self.dma("sp", XT[:, r, :n], self.cT[c, :, a:a + n], ("xt", r), (), (("xt", r),))
                self.tt(XT[:, r, :n], XT[:, r, :n], MU[:, a:a + n], ALU.subtract, (("xt", r),), (("xt", r),))
                self.tt(XT[:, r, :n], XT[:, r, :n], RS[:, a:a + n], ALU.mult, (("xt", r),), (("xt", r),))
                self.ts(XT[:, r, :n], XT[:, r, :n], self.cv(CV_G["ln_g"] + c), self.cv(CV_G["ln_b"] + c),
                        ALU.mult, ALU.add, (("xt", r), "cv"), (("xt", r),))
                self.act(CB[:, c, a:a + n], XT[:, r, :n], AF.Silu, (("xt", r),), ())
        self.barrier()
        wba = self.w_ba.rearrange("(kc p) n -> p kc n", p=128)
        wbb = self.w_bb.rearrange("(kc p) n -> p kc n", p=128)
        LW = self.v(187392, [128, 2, KC, 256], BF16)
        bring = [0]

        def bslot():
            i = bring[0] % 5
            bring[0] += 1
            if i < 3:
                return self.wview(i, [128, KC, 256]), ("w", i)
            return LW[:, i - 3], ("wl", i - 3)

        def bload(dst, key, src):
            self.dma("pool", dst, src, key, (), (key,))
        for oc in range(KC):
            w1, s1 = bslot()
            bload(w1[:, :, 0:128], s1, wba[:, :, oc * 128:(oc + 1) * 128])
            bload(w1[:, :, 128:256], s1, wbb[:, :, oc * 128:(oc + 1) * 128])
            w2, s2 = bslot()
            bload(w2[:, :, 0:128], s2, win[:, :, O_GATE + oc * 128:O_GATE + (oc + 1) * 128])
            bload(w2[:, :, 128:256], s2, win[:, :, O_GATE + 2048 + oc * 128:O_GATE + 2048 + (oc + 1) * 128])
            for (a, n) in tlm:
                bs = []
                for (wv_, s_, col, src, off) in ((w1, s1, 0, OF, 0), (w1, s1, 128, CB, 0), (w2, s2, 0, UN2, 32), (w2, s2, 128, UN2, 32)):
                    b = self.bank()
                    for kc in range(KC):
                        self.mm(self.ps[:, b, :n], wv_[:, kc, col:col + 128], src[:, kc, off + a:off + a + n],
                                kc == 0, kc == KC - 1, (s_,), (("ps", b),))
                    bs.append(b)
                r1 = self.ring("xt", 4)
                r2 = self.ring("xt", 4)
                self.act(XT[:, r1, :n], self.ps[:, bs[2], :n], AF.Sigmoid, (("ps", bs[2]),), (("xt", r1),))
                self.act(XT[:, r2, :n], self.ps[:, bs[3], :n], AF.Sigmoid, (("ps", bs[3]),), (("xt", r2),))
                self.tt(XT[:, r1, :n], XT[:, r1, :n], self.ps[:, bs[0], :n], ALU.mult, (("xt", r1), ("ps", bs[0])), (("xt", r1),))
                self.tt(XT[:, r2, :n], XT[:, r2, :n], self.ps[:, bs[1], :n], ALU.mult, (("xt", r2), ("ps", bs[1])), (("xt", r2),))
                q = self.ring("sq", 2)
                self.tt(SQ[:, q, :n], XT[:, r1, :n], XT[:, r2, :n], ALU.add, (("xt", r1), ("xt", r2)), (("sq", q),))
                self.dma("sp", self.mgT[oc, :, a:a + n], SQ[:, q, :n], ("sq", q), (("sq", q),), ())

    def proj_post(self, kind):
        o = PH_OFF
        XN = self.v(o, [128, KC, NM], BF16); o += KC * NM * 2
        PB = self.v(o, [128, 2, NM], BF16); o += 2 * NM * 2
        RSTD = self.v(o, [128, NM]); o += NM * 4
        XT = self.v(o, [128, 4, 512]); o += 8192
        SQ = self.v(o, [128, 2, 512], BF16); o += 2048
        FT = self.v(o, [128, 2, 512]); o += 4096
        YST = self.v(o, [128, 2, 512]); o += 4096
        tlm = _tiles(0, NM)
        self.barrier()
        self.rot = list(range(5))
        if kind == "out":
            for kc in range(KC):
                self.dma("sp", XN[:, kc, :], self.mgT[kc], ("xnl", kc % 4), (), ())
            W = self.w_out.rearrange("(kc p) n -> p kc n", p=128)
            nk = KC
        else:
            self.norm_load(self.h3, M0, NT, CV_G["ple_pre"], XN, RSTD, XT, SQ, pre_ssq=[5, 6, 7])
            for kc in range(2):
                self.dma("pool", PB[:, kc, :], self.pT[kc], ("pbl", kc), (), ())
            W = self.w_pg.rearrange("(kc p) n -> p kc n", p=128)
            WP = self.w_pp.rearrange("(kc p) n -> p kc n", p=128)
            nk = KC
        self.barrier()
        ssq = [5, 6, 7]
        self.rot = list(range(5))
        pend_ssq = None
        for oc in range(KC):
            s = self.wslot()
            wv = self.wview(s, [128, KC + 2, 128])
            self.wload(s, wv[:, 0:KC, :], W[:, :, oc * 128:(oc + 1) * 128])
            if kind == "ple":
                self.wload(s, wv[:, KC:KC + 2, :], WP[:, :, oc * 128:(oc + 1) * 128])
            for ti, (a, n) in enumerate(tlm):
                b = self.bank()
                for kc in range(nk):
                    self.mm(self.ps[:, b, :n], wv[:, kc, :], XN[:, kc, a:a + n], kc == 0, kc == nk - 1,
                            (("w", s),), (("ps", b),))
                k = self.ring("ft", 2)
                if kind == "out":
                    self.act(FT[:, k, :n], self.ps[:, b, :n], AF.Copy, (("ps", b),), (("ft", k),))
                else:
                    b2 = self.bank()
                    for kc in range(2):
                        self.mm(self.ps[:, b2, :n], wv[:, KC + kc, :], PB[:, kc, a:a + n], kc == 0, kc == 1,
                                (("w", s),), (("ps", b2),))
                    self.act(FT[:, k, :n], self.ps[:, b, :n], AF.Sigmoid, (("ps", b),), (("ft", k),))
                    self.tt(FT[:, k, :n], FT[:, k, :n], self.ps[:, b2, :n], ALU.mult, (("ft", k), ("ps", b2)), (("ft", k),))
                self.dma("sp", self.fT[oc, :, M0 + a:M0 + a + n], FT[:, k, :n], ("ft", k), (("ft", k),), ("fscr",))
                q = self.ring("sq", 2)
                self.tt(SQ[:, q, :n], FT[:, k, :n], FT[:, k, :n], ALU.mult, (("ft", k),), (("sq", q),))
                if pend_ssq is not None:
                    pend_ssq()
                pend_ssq = (lambda ti=ti, n=n, q=q, oc=oc: self.mm(
                    self.ps[:, ssq[ti], :n], self.ones_bf, SQ[:, q, :n], oc == 0, oc == KC - 1,
                    (("sq", q), "cb"), (("ps", ssq[ti]),)))
        pend_ssq()
        if kind == "out":
            self.post_residual(self.fT, self.h1, self.h2, M0, NT, CV_G["mix_post"], ssq, RSTD, XT, SQ=SQ, nxt_ssq=True)
        else:
            self.post_residual(self.fT, self.h3, self.h4, M0, NT, CV_G["ple_post"], ssq, RSTD, XT, yout=self.y, YST=YST)
        self.rot = list(range(8))

    def write_y(self):
        self.barrier()
        self.rot = list(range(8))
        HIN = self.v(PH_OFF, [128, 2, 4, 128])
        YT = self.v(PH_OFF + 4096, [128, 2, D])
        ident = self.cm(CM_ID, 128)
        for tb in range(NM // 128):
            yk = self.ring("yt", 2)
            for g in range(4):
                k = self.ring("hin", 2)
                self.dma("sp", HIN[:, k], self.h4[g * 4:(g + 1) * 4, :, M0 + tb * 128:M0 + (tb + 1) * 128]
                         .rearrange("c p t -> p c t"), ("hin", k), (), (("hin", k),))
                b = self.bank()
                for q in range(4):
                    self.tr(self.ps[:, b, q * 128:(q + 1) * 128], HIN[:, k, q, :], ident, (("hin", k), "cm"), (("ps", b),))
                self.act(YT[:, yk, g * 512:(g + 1) * 512], self.ps[:, b, :], AF.Copy, (("ps", b),), (("yt", yk),))
            self.dma("act", self.y[tb * 128:(tb + 1) * 128, :], YT[:, yk, :], ("yt", yk), (("yt", yk),), ())

    def dbg_dump_bg(self):
        self.barrier()
        self.dma("sp", self.dbg_bg[0], self.BETA, ("dbg", 0), (), ())
        self.dma("sp", self.dbg_bg[1], self.GG, ("dbg", 0), (), ())

    def build(self):
        self.load_consts()
        self.transpose_in(self.xin, self.xT, NT, KC)
        if self.stop_after == "xT":
            return self.finish()
        self.ffn(self.xT, self.h1, 0, NPRE, self.w_gu1, self.w_dn1, CV_G["ffn1_pre"], CV_H1)
        self.ffn(self.xT, self.h1, M0, NT, self.w_gu1, self.w_dn1, CV_G["ffn1_pre"], CV_H1)
        if self.stop_after == "ffn1":
            return self.finish()
        self.mixer_qkv()
        if self.stop_after == "qkv":
            if self.debug:
                self.dbg_dump_bg()
            return self.finish()
        self.delta()
        if self.stop_after == "delta":
            if self.debug:
                self.barrier()
                self.dma("sp", self.dbg_of.rearrange("h p t -> p h t"), self.OF, ("dbg", 1), (), ())
            return self.finish()
        self.mixer_tail()
        self.transpose_in(self.pin, self.pT, NM, 2)
        self.proj_post("out")
        if self.stop_after == "mix":
            return self.finish()
        self.ffn(self.h2, self.h3, M0, NT, self.w_gu2, self.w_dn2, CV_G["ffn2_pre"], CV_H2, pre_ssq=[5, 6, 7], nxt_ssq=True)
        self.proj_post("ple")
        return self.finish()

    def finish(self):
        self.pg.emit()
        return self.nc


def _masks():
    m = np.zeros((128, NCM), np.float32)
    i = np.arange(128)
    m[:, CM_ID:CM_ID + 128] = np.eye(128, dtype=np.float32)
    m[:, CM_ONE:CM_ONE + 128] = 1.0
    j = np.arange(64)
    le = (j[:, None] <= j[None, :]).astype(np.float32)
    lt = (j[:, None] < j[None, :]).astype(np.float32)
    m[:64, CM_TRI:CM_TRI + 64] = le
    m[:64, CM_UI:CM_UI + 64] = le
    m[:64, CM_US:CM_US + 64] = lt
    m[:64, CM_LS:CM_LS + 64] = lt.T
    same = ((i[:, None] // LS) == (i[None, :] // LS)).astype(np.float32)
    le8 = (i[:, None] <= i[None, :]).astype(np.float32) * same
    lt8 = (i[:, None] < i[None, :]).astype(np.float32) * same
    m[:, CM_TRI8:CM_TRI8 + 128] = le8
    m[:, CM_UI8:CM_UI8 + 128] = le8
    m[:, CM_US8:CM_US8 + 128] = lt8
    m[:, CM_LS8:CM_LS8 + 128] = lt8.T
    m[:, CM_SAME8:CM_SAME8 + 128] = same
    m[:, CM_SEG:CM_SEG + 16] = (i[:, None] // LS == np.arange(16)[None, :]).astype(np.float32)
    return m


def _cvec(inp):
    c = np.zeros((128, NCV), np.float32)

    def fm(vec):
        return np.ascontiguousarray(np.asarray(vec, np.float32).reshape(-1, 128).T)
    for n, col in CV_G.items():
        key = {"b_dw": "b_dw_conv"}.get(n, n)
        c[:, col:col + 16] = fm(inp[key][0])
    wsc = np.asarray(inp["w_short_conv"][0], np.float32)
    c[:, CV_SC:CV_SC + 192] = wsc.reshape(4, 48, 128).transpose(2, 1, 0).reshape(128, 192)
    wdw = np.asarray(inp["w_dw_conv"][0], np.float32)
    c[:, CV_DW:CV_DW + 496] = wdw.reshape(31, 16, 128).transpose(2, 1, 0).reshape(128, 496)
    c[:, CV_ON] = np.asarray(inp["o_norm"][0], np.float32)
    c[:16, CV_AL] = np.asarray(inp["a_log"][0], np.float32)
    c[:16, CV_DT] = np.asarray(inp["dt_bias"][0], np.float32)
    return c


def make_in_maps(inp, cores=range(8)):
    xp = np.asarray(inp["x_prompt"], np.float32)
    xs = np.asarray(inp["x_sample"], np.float32)
    pp = np.asarray(inp["p_prompt"], np.float32)[0]
    psm = np.asarray(inp["p_sample"], np.float32)[0]
    sd = np.asarray(inp["state_delta"], np.float32)[0]
    sq = np.asarray(inp["state_qkv_conv"], np.float32)[0]
    sg = np.asarray(inp["state_glu_conv"], np.float32)[0]
    shared = {
        "cvec": _cvec(inp), "cmask": _masks(),
        "w_gu1": np.asarray(inp["ffn1_w_gu"][0]), "w_dn1": np.asarray(inp["ffn1_w_down"][0]),
        "w_in": np.asarray(inp["w_in"][0]), "w_ba": np.asarray(inp["w_branch_a"][0]),
        "w_bb": np.asarray(inp["w_branch_b"][0]), "w_out": np.asarray(inp["w_out"][0]),
        "w_gu2": np.asarray(inp["ffn2_w_gu"][0]), "w_dn2": np.asarray(inp["ffn2_w_down"][0]),
        "w_pg": np.asarray(inp["w_ple_gate"][0]), "w_pp": np.asarray(inp["w_ple_proj"][0]),
    }
    maps = []
    for c in cores:
        b, half = c // 2, c % 2
        main = xp[b, half * NMAIN:(half + 1) * NMAIN]
        pre = xp[b, 0:NPRE] if half == 1 else np.zeros((NPRE, D), np.float32)
        sl = slice(c * NSEQ, (c + 1) * NSEQ)
        m = dict(shared)
        m["xin"] = np.concatenate([pre, main, xs[sl].reshape(NS, D)], 0)
        m["pin"] = np.concatenate([pp[b, half * NMAIN:(half + 1) * NMAIN], psm[sl].reshape(NS, PLE)], 0)
        m["sdelta"] = np.ascontiguousarray(sd[sl])
        m["sqkv"] = np.ascontiguousarray(sq[sl].reshape(NSEQ * 3, QKV))
        m["sglu"] = np.ascontiguousarray(sg[sl].reshape(NSEQ * 30, D))
        maps.append(m)
    return maps


def kernel(**inputs):
    nc = Builder().build()
    maps = make_in_maps(inputs)
    res = run_bass_kernel_spmd(nc, maps, core_ids=list(range(8)))
    R = res.results
    yp = np.zeros((4, 2048, D), np.float32)
    ys = np.zeros((128, LS, D), np.float32)
    ndp = np.zeros((1, 4, NH, 128, 128), np.float32)
    nqp = np.zeros((1, 4, 3, QKV), np.float32)
    ngp = np.zeros((1, 4, 30, D), np.float32)
    nds = np.zeros((1, 128, NH, 128, 128), np.float32)
    nqs = np.zeros((1, 128, 3, QKV), np.float32)
    ngs = np.zeros((1, 128, 30, D), np.float32)
    for c in range(8):
        b, half = c // 2, c % 2
        r = R[c]
        yp[b, half * NMAIN:(half + 1) * NMAIN] = r["y"][:NMAIN]
        ys[c * NSEQ:(c + 1) * NSEQ] = r["y"][NMAIN:].reshape(NSEQ, LS, D)
        if half == 1:
            ndp[0, b] = r["nd_p"]
            nqp[0, b] = r["nq_p"]
            ngp[0, b] = r["ng_p"]
        nds[0, c * NSEQ:(c + 1) * NSEQ] = r["nd_s"]
        nqs[0, c * NSEQ:(c + 1) * NSEQ] = r["nq_s"].reshape(NSEQ, 3, QKV)
        ngs[0, c * NSEQ:(c + 1) * NSEQ] = r["ng_s"].reshape(NSEQ, 30, D)
    return (yp, ys, ndp, nqp, ngp, nds, nqs, ngs)
```

```python
import numpy as np
import concourse.bass as bass
import concourse.mybir as mybir
from concourse.bass_utils import run_bass_kernel_spmd

F32 = mybir.dt.float32
BF16 = mybir.dt.bfloat16
AF = mybir.ActivationFunctionType
ALU = mybir.AluOpType

ENGS = ("pe", "act", "dve", "pool", "sp")
EPOCH = 30000


class _Op:
    __slots__ = ("eng", "fn", "reads", "writes", "dma", "deps", "sig", "idx", "n", "bar")

    def __init__(self, eng, fn, reads, writes, dma):
        self.eng = eng
        self.fn = fn
        self.reads = reads
        self.writes = writes
        self.dma = dma
        self.deps = None
        self.sig = False
        self.idx = None
        self.n = 0
        self.bar = False


class Prog:
    def __init__(self, nc):
        self.nc = nc
        self.ops = []
        self.streams = {e: [] for e in ENGS}

    def add(self, eng, fn, reads=(), writes=(), dma=None):
        op = _Op(eng, fn, tuple(reads), tuple(writes), dma)
        op.n = len(self.ops)
        self.ops.append(op)
        self.streams[eng].append(op)
        return op

    def barrier(self, fn):
        op = self.add("sp", fn, dma=("bar",))
        op.bar = True
        return op

    def _analyze(self):
        last_w = {}
        readers = {}
        last_eng = {}
        last_dma = {}
        cur_bar = None
        need_bar = set()
        for op in self.ops:
            deps = {}
            if op.bar:
                for d in last_eng.values():
                    deps[d.n] = d
                for d in last_dma.values():
                    deps[d.n] = d
                last_w = {}
                readers = {}
                cur_bar = op
                need_bar = set(ENGS)
            else:
                if cur_bar is not None and op.eng in need_bar:
                    deps[cur_bar.n] = cur_bar
                    need_bar.discard(op.eng)
                for k in op.reads:
                    w = last_w.get(k)
                    if w is not None:
                        deps[w.n] = w
                for k in op.writes:
                    w = last_w.get(k)
                    if w is not None:
                        deps[w.n] = w
                    for r in readers.get(k, ()):
                        deps[r.n] = r
                for k in op.reads:
                    readers.setdefault(k, []).append(op)
                for k in op.writes:
                    last_w[k] = op
                    readers[k] = []
            if op.dma is not None:
                last_dma[op.dma] = op
            else:
                last_eng[op.eng] = op
            deps.pop(op.n, None)
            dl = []
            for d in deps.values():
                if d.dma is None and op.dma is None and d.eng == "pe" and op.eng == "pe":
                    continue
                dl.append(d)
            op.deps = dl
            for d in dl:
                d.sig = True
        cnt = {e: 0 for e in ENGS}
        dcnt = {}
        for op in self.ops:
            if op.dma is not None:
                dcnt[op.dma] = dcnt.get(op.dma, 0) + 16
                op.idx = ("d", op.dma, dcnt[op.dma])
            elif op.sig:
                c = cnt[op.eng]
                cnt[op.eng] += 1
                op.idx = ("e", (op.eng, c // EPOCH), c % EPOCH + 1)
        self.dma_final = dcnt

    def emit(self):
        nc = self.nc
        self._analyze()
        sems = {}

        def sem(kind, key):
            k = (kind, key)
            if k not in sems:
                sems[k] = nc.alloc_semaphore(name="s%d" % len(sems))
            return sems[k]

        waits = {}
        for e in ENGS:
            seen = {}
            for op in self.streams[e]:
                wl = {}
                for d in op.deps:
                    kind, key, val = d.idx
                    k = (kind, key)
                    if seen.get(k, 0) >= val:
                        continue
                    if wl.get(k, 0) < val:
                        wl[k] = val
                for k, v in wl.items():
                    seen[k] = v
                waits[op.n] = [(sem(*k), v) for k, v in wl.items()]
        final_waits = [(sem("d", k), v) for k, v in self.dma_final.items()]
        self.nsem = len(sems)

        def run_stream(engname, eng):
            for op in self.streams[engname]:
                for s, v in waits[op.n]:
                    eng.wait_ge(s, v)
                ins = op.fn(eng)
                if op.idx is not None:
                    kind, key, val = op.idx
                    ins.then_inc(sem(kind, key), 16 if kind == "d" else 1)
            if engname == "sp":
                for s, v in final_waits:
                    eng.wait_ge(s, v)

        with nc.Block() as block:
            @block.tensor
            def _(e):
                run_stream("pe", e)

            @block.scalar
            def _(e):
                run_stream("act", e)

            @block.vector
            def _(e):
                run_stream("dve", e)

            @block.gpsimd
            def _(e):
                run_stream("pool", e)

            @block.sync
            def _(e):
                run_stream("sp", e)


D = 2048
KC = 16
DFF = 5632
JC = 44
NH = 16
QKV = 6144
O_Z = QKV
O_BETA = O_Z + 2048
O_A = O_BETA + NH
O_GLU = O_A + NH
O_GATE = O_GLU + 4096
IN_DIM = O_GATE + 4096
PLE = 256
EPS = 1e-6
NPRE = 1024
NMAIN = 1024
NTP = NPRE + NMAIN
NSEQ = 16
LS = 8
NS = NSEQ * LS
NT = NTP + NS
NM = NMAIN + NS
M0 = NPRE

ARENA_BYTES = 206000
WR_OFF = 16384
WSLOT = 11264
NWS = 3
PH_OFF = WR_OFF + NWS * WSLOT

CV_G = {n: 16 * i for i, n in enumerate(
    ["ffn1_pre", "ffn1_post", "mix_pre", "mix_post", "ffn2_pre", "ffn2_post", "ple_pre", "ple_post",
     "b_dw", "ln_g", "ln_b"])}
CV_SC = 176
CV_DW = CV_SC + 192
CV_ON = CV_DW + 496
CV_AL = CV_ON + 1
CV_DT = CV_AL + 1
CV_H1 = CV_DT + 1
CV_H2 = CV_H1 + 16
CV_NA = CV_H2 + 16
NCV = CV_NA + 1
CM_ID = 0
CM_ONE = 128
CM_TRI = 256
CM_UI = 320
CM_US = 384
CM_LS = 448
CM_TRI8 = 512
CM_UI8 = 640
CM_US8 = 768
CM_LS8 = 896
CM_SAME8 = 1024
CM_SEG = 1152
NCM = CM_SEG + 16


def _tiles(t0, t1, n=512):
    return [(a, min(n, t1 - a)) for a in range(t0, t1, n)]


class Builder:
    def __init__(self, debug=False, stop_after=None):
        self.debug = debug
        self.stop_after = stop_after
        nc = bass.Bass("TRN2", target_bir_lowering=False)
        self.nc = nc
        self.pg = Prog(nc)
        self.arena = nc.alloc_sbuf_tensor("arena", [128, ARENA_BYTES // 4], F32)
        self.ps = nc.alloc_psum_tensor("ps", [128, 8, 512], F32)
        self.rot = list(range(8))
        self.rot_i = 0
        self.ws_i = 0
        self.ring_i = {}
        self._decl()

    def _in(self, name, shape, dt=F32):
        return self.nc.dram_tensor(name, list(shape), dt, kind="ExternalInput").ap()

    def _out(self, name, shape, dt=F32):
        return self.nc.dram_tensor(name, list(shape), dt, kind="ExternalOutput").ap()

    def _scr(self, name, shape, dt=F32):
        kind = "ExternalOutput" if self.debug else "Internal"
        return self.nc.dram_tensor(name, list(shape), dt, kind=kind).ap()

    def _decl(self):
        self.xin = self._in("xin", [NT, D])
        self.pin = self._in("pin", [NM, PLE])
        self.sdelta = self._in("sdelta", [NSEQ, NH, 128, 128])
        self.sqkv = self._in("sqkv", [NSEQ * 3, QKV])
        self.sglu = self._in("sglu", [NSEQ * 30, D])
        self.cvec = self._in("cvec", [128, NCV])
        self.cmask = self._in("cmask", [128, NCM])
        self.w_gu1 = self._in("w_gu1", [D, 2 * DFF])
        self.w_dn1 = self._in("w_dn1", [DFF, D])
        self.w_in = self._in("w_in", [D, IN_DIM])
        self.w_ba = self._in("w_ba", [D, D])
        self.w_bb = self._in("w_bb", [D, D])
        self.w_out = self._in("w_out", [D, D])
        self.w_gu2 = self._in("w_gu2", [D, 2 * DFF])
        self.w_dn2 = self._in("w_dn2", [DFF, D])
        self.w_pg = self._in("w_pg", [D, D])
        self.w_pp = self._in("w_pp", [PLE, D])
        self.y = self._out("y", [NM, D])
        self.nd_p = self._out("nd_p", [NH, 128, 128])
        self.nq_p = self._out("nq_p", [3, QKV])
        self.ng_p = self._out("ng_p", [30, D])
        self.nd_s = self._out("nd_s", [NSEQ, NH, 128, 128])
        self.nq_s = self._out("nq_s", [NSEQ * 3, QKV])
        self.ng_s = self._out("ng_s", [NSEQ * 30, D])
        self.xT = self._scr("xT", [KC, 128, NT])
        self.h1 = self._scr("h1", [KC, 128, NT])
        self.fT = self._scr("fT", [KC, 128, NT])
        self.qkvF = self._scr("qkvF", [48, 128, NT], BF16)
        self.cT = self._scr("cT", [KC, 128, NM])
        self.mgT = self._scr("mgT", [KC, 128, NM], BF16)
        self.h2 = self._scr("h2", [KC, 128, NT])
        self.h3 = self._scr("h3", [KC, 128, NT])
        self.h4 = self._scr("h4", [KC, 128, NT])
        self.pT = self._scr("pT", [2, 128, NM])
        if self.debug:
            self.dbg_bg = self._scr("dbg_bg", [2, 16, NT])
            self.dbg_of = self._scr("dbg_of", [NH, 128, NM], BF16)
        self.bar_a = self.nc.dram_tensor("bar_a", [1, 16], F32, kind="Internal").ap()
        self.bar_b = self.nc.dram_tensor("bar_b", [1, 16], F32, kind="Internal").ap()

    def v(self, off, shape, dt=F32):
        esz = 4 if dt == F32 else 2
        n = 1
        for s in shape[1:]:
            n *= s
        nb = n * esz
        assert off % 4 == 0 and nb % 4 == 0, (off, nb)
        assert off + nb <= ARENA_BYTES, (off, nb)
        a = self.arena[0:shape[0], off // 4:(off + nb) // 4]
        if dt != F32:
            a = a.bitcast(dt)
        if len(shape) == 3:
            a = a.rearrange("p (a b) -> p a b", a=shape[1])
        elif len(shape) == 4:
            a = a.rearrange("p (a b c) -> p a b c", a=shape[1], b=shape[2])
        return a

    def cv(self, col, n=1, rows=128):
        return self.arena[0:rows, col:col + n]

    def cm(self, col, n, rows=128, dt=F32):
        return self.arena[0:rows, 1024 + col:1024 + col + n]

    def bank(self):
        b = self.rot[self.rot_i % len(self.rot)]
        self.rot_i += 1
        return b

    def ring(self, name, n):
        i = self.ring_i.get(name, 0)
        self.ring_i[name] = i + 1
        return i % n

    def mm(self, out, lhsT, rhs, start, stop, reads, writes):
        return self.pg.add("pe", lambda e: e.matmul(out, lhsT, rhs, start=start, stop=stop), reads, writes)

    def tr(self, out, in_, ident, reads, writes):
        return self.pg.add("pe", lambda e: e.transpose(out, in_, ident), reads, writes)

    def act(self, out, in_, func, reads, writes, bias=None, scale=None):
        kw = {}
        if bias is not None:
            kw["bias"] = bias
        if scale is not None:
            kw["scale"] = scale
        return self.pg.add("act", lambda e: e.activation(out=out, in_=in_, func=func, **kw), reads, writes)

    def tt(self, out, a, b, op, reads, writes, eng="dve"):
        return self.pg.add(eng, lambda e: e.tensor_tensor(out, a, b, op), reads, writes)

    def ts(self, out, a, s1, s2, op0, op1, reads, writes, eng="dve"):
        if op1 is None:
            return self.pg.add(eng, lambda e: e.tensor_single_scalar(out, a, s1, op0), reads, writes)
        return self.pg.add(eng, lambda e: e.tensor_scalar(out, a, s1, s2, op0, op1), reads, writes)

    def stt(self, out, in0, scalar, in1, op0, op1, reads, writes, eng="dve"):
        return self.pg.add(eng, lambda e: e.scalar_tensor_tensor(out, in0, scalar, in1, op0, op1), reads, writes)

    def cp(self, out, in_, reads, writes, eng="dve"):
        return self.pg.add(eng, lambda e: e.tensor_copy(out, in_), reads, writes)

    def rstd(self, out, in_, scale, reads, writes):
        self.act(out, in_, AF.Ln, reads, writes, bias=EPS, scale=scale)
        return self.act(out, out, AF.Exp, writes, writes, scale=-0.5)

    def recip(self, out, in_, reads, writes):
        return self.pg.add("dve", lambda e: e.reciprocal(out, in_), reads, writes)

    def memset(self, ap, val, writes, eng="dve"):
        return self.pg.add(eng, lambda e: e.memset(ap, val), (), writes)

    def dma(self, eng, out, in_, key, reads, writes):
        return self.pg.add(eng, lambda e: e.dma_start(out=out, in_=in_), reads, writes, dma=key)

    def barrier(self):
        a, b = self.bar_a, self.bar_b
        self.bar_a, self.bar_b = b, a
        self.pg.barrier(lambda e: e.dma_start(out=b, in_=a))
        self.ring_i = {}

    def wslot(self):
        s = self.ws_i % NWS
        self.ws_i += 1
        return s

    def wview(self, s, shape):
        return self.v(WR_OFF + s * WSLOT, shape, BF16)

    def wload(self, s, dst, src):
        return self.dma("pool", dst, src, ("w", s), (), (("w", s),))

    def load_consts(self):
        self.dma("sp", self.arena[:, 0:NCV], self.cvec, ("c", 0), (), ("cv",))
        self.dma("sp", self.arena[:, 1024:1024 + NCM], self.cmask, ("c", 1), (), ("cm",))
        self.ident_bf = self.v(12288, [128, 128], BF16)
        self.ones_bf = self.v(12288 + 256, [128, 128], BF16)
        self.cp(self.ident_bf, self.cm(CM_ID, 128), ("cm",), ("cb",))
        self.cp(self.ones_bf, self.cm(CM_ONE, 128), ("cm",), ("cb",))
        self.ts(self.cv(CV_H1, 16), self.cv(CV_G["ffn1_post"], 16), 0.5, None, ALU.mult, None, ("cv",), ("cv2",))
        self.ts(self.cv(CV_H2, 16), self.cv(CV_G["ffn2_post"], 16), 0.5, None, ALU.mult, None, ("cv",), ("cv2",))
        self.act(self.cv(CV_NA, 1, 16), self.cv(CV_AL, 1, 16), AF.Exp, ("cv",), ("cv3",))
        self.ts(self.cv(CV_NA, 1, 16), self.cv(CV_NA, 1, 16), -1.0, None, ALU.mult, None, ("cv3",), ("cv3",))

    def transpose_in(self, src_tok, dst_fm, ntok, nfc, tok_off=0):
        self.barrier()
        XIN = self.v(PH_OFF, [128, 2, nfc * 128])
        XST = self.v(PH_OFF + 2 * nfc * 512, [128, 2, 4, 128])
        ident = self.cm(CM_ID, 128)
        for tb in range(ntok // 128):
            s = self.ring("xin", 2)
            self.dma("sp", XIN[:, s, :], src_tok[tb * 128:(tb + 1) * 128, :], ("xin", s), (), (("xin", s),))
            gsz = min(4, nfc)
            for g in range(nfc // gsz):
                b = self.bank()
                for q in range(gsz):
                    fc = g * gsz + q
                    self.tr(self.ps[:, b, q * 128:(q + 1) * 128], XIN[:, s, fc * 128:(fc + 1) * 128], ident,
                            (("xin", s), "cm"), (("ps", b),))
                k = self.ring("xst", 2)
                self.act(XST[:, k, 0:gsz].rearrange("p a b -> p (a b)"), self.ps[:, b, 0:gsz * 128], AF.Copy,
                         (("ps", b),), (("xst", k),))
                self.dma("act", dst_fm[g * gsz:(g + 1) * gsz, :, tok_off + tb * 128: tok_off + (tb + 1) * 128]
                         .rearrange("c p t -> p c t"), XST[:, k, 0:gsz], ("xst", k), (("xst", k),), ())

    def norm_load(self, src, t0, t1, gcol, XN, RSTD, XT, SQ, pre_ssq=None):
        G = t1 - t0
        for ti, (a, n) in enumerate(_tiles(0, G)):
            if pre_ssq is not None:
                b = pre_ssq[ti]
                self.rstd(RSTD[:, a:a + n], self.ps[:, b, :n], 1.0 / D, (("ps", b),), (("rstd", a),))
                continue
            b = self.bank()
            for fc in range(KC):
                s = self.ring("xt", 4)
                self.dma("sp" if fc % 2 == 0 else "act", XT[:, s, :n], src[fc, :, t0 + a:t0 + a + n], ("xt", s), (), (("xt", s),))
                q = self.ring("sq", 2)
                self.act(SQ[:, q, :n], XT[:, s, :n], AF.Square, (("xt", s),), (("sq", q),))
                self.mm(self.ps[:, b, :n], self.ones_bf, SQ[:, q, :n], fc == 0, fc == KC - 1,
                        (("sq", q), "cb"), (("ps", b),))
            self.rstd(RSTD[:, a:a + n], self.ps[:, b, :n], 1.0 / D, (("ps", b),), (("rstd", a),))
        for (a, n) in _tiles(0, G):
            for fc in range(KC):
                s = self.ring("xt", 4)
                self.dma("sp" if fc % 2 == 0 else "act", XT[:, s, :n], src[fc, :, t0 + a:t0 + a + n], ("xt", s), (), (("xt", s),))
                self.stt(XN[:, fc, a:a + n], XT[:, s, :n], self.cv(gcol + fc), RSTD[:, a:a + n], ALU.mult, ALU.mult,
                         (("xt", s), ("rstd", a), "cv"), (("xn", fc, a),))

    def post_residual(self, fsrc, rsrc, dst, t0, t1, gcol, ssq_banks, RSTD, XT, SQ=None, nxt_ssq=False, yout=None, YST=None):
        G = t1 - t0
        tl = _tiles(0, G)
        self.barrier()
        for ti, (a, n) in enumerate(tl):
            b = ssq_banks[ti]
            self.rstd(RSTD[:, a:a + n], self.ps[:, b, :n], 1.0 / D, (("ps", b),), (("rstd", a),))
        its = [(ti, a, n, fc) for ti, (a, n) in enumerate(tl) for fc in range(KC)]

        def load(i):
            ti, a, n, fc = its[i]
            s = self.ring("xt", 4)
            s2 = self.ring("xt", 4)
            self.dma("sp", XT[:, s, :n], fsrc[fc, :, t0 + a:t0 + a + n], ("xt", s), ("fscr",), (("xt", s),))
            self.dma("act", XT[:, s2, :n], rsrc[fc, :, t0 + a:t0 + a + n], ("xt", s2), (), (("xt", s2),))
            return s, s2
        nxt = load(0)
        for i, (ti, a, n, fc) in enumerate(its):
            s, s2 = nxt
            self.tt(XT[:, s, :n], XT[:, s, :n], RSTD[:, a:a + n], ALU.mult, (("xt", s), ("rstd", a)), (("xt", s),))
            self.stt(XT[:, s, :n], XT[:, s, :n], self.cv(gcol + fc), XT[:, s2, :n], ALU.mult, ALU.add,
                     (("xt", s), ("xt", s2), "cv", "cv2"), (("xt", s),))
            if i + 1 < len(its):
                nxt = load(i + 1)
            if nxt_ssq:
                q = self.ring("sq", 2)
                self.act(SQ[:, q, :n], XT[:, s, :n], AF.Square, (("xt", s),), (("sq", q),))
                self.mm(self.ps[:, ssq_banks[ti], :n], self.ones_bf, SQ[:, q, :n], fc == 0, fc == KC - 1,
                        (("sq", q), "cb"), (("ps", ssq_banks[ti]),))
            if yout is None:
                self.dma("sp", dst[fc, :, t0 + a:t0 + a + n], XT[:, s, :n], ("xt", s), (("xt", s),), ())
            else:
                nq = n // 128
                b = self.bank()
                for q4 in range(nq):
                    self.tr(self.ps[:, b, q4 * 128:(q4 + 1) * 128], XT[:, s, q4 * 128:(q4 + 1) * 128], self.cm(CM_ID, 128),
                            (("xt", s), "cm"), (("ps", b),))
                k = self.ring("yst", 2)
                self.act(YST[:, k, :n], self.ps[:, b, :n], AF.Copy, (("ps", b),), (("yst", k),))
                self.dma("sp", yout[a:a + n, fc * 128:(fc + 1) * 128].rearrange("(q p) f -> p q f", p=128),
                         YST[:, k, :n].rearrange("p (q f) -> p q f", f=128), ("yst", k), (("yst", k),), ())

    def ffn(self, src, dst, t0, t1, w_gu, w_dn, g_pre, g_post_half, pre_ssq=None, nxt_ssq=False):
        G = t1 - t0
        XN = self.v(PH_OFF, [128, KC, G], BF16)
        ACTB = self.v(PH_OFF + 36864, [128, JC, G], BF16)
        MO = PH_OFF + 36864 + 101376
        RSTD = self.v(MO, [128, 1152])
        XT = self.v(MO + 4608, [128, 4, 512])
        SQ = self.v(MO + 4608 + 8192, [128, 2, 512], BF16)
        FT = self.v(PH_OFF, [128, 2, 512])
        tl = _tiles(0, G)
        self.barrier()
        self.rot = list(range(8))
        self.norm_load(src, t0, t1, g_pre, XN, RSTD, XT, SQ, pre_ssq=pre_ssq)
        self.barrier()
        wgu = w_gu.rearrange("(kc p) n -> p kc n", p=128)
        for j in range(JC):
            s = self.wslot()
            wv = self.wview(s, [128, KC, 256])
            self.wload(s, wv[:, :, 0:128], wgu[:, :, j * 128:(j + 1) * 128])
            self.wload(s, wv[:, :, 128:256], wgu[:, :, DFF + j * 128:DFF + (j + 1) * 128])
            for ti, (a, n) in enumerate(tl):
                bg = self.bank()
                for kc in range(KC):
                    self.mm(self.ps[:, bg, :n], wv[:, kc, 0:128], XN[:, kc, a:a + n], kc == 0, kc == KC - 1,
                            (("w", s),), (("ps", bg),))
                bu = self.bank()
                for kc in range(KC):
                    self.mm(self.ps[:, bu, :n], wv[:, kc, 128:256], XN[:, kc, a:a + n], kc == 0, kc == KC - 1,
                            (("w", s),), (("ps", bu),))
                q = self.ring("sq", 2)
                self.act(SQ[:, q, :n], self.ps[:, bg, :n], AF.Silu, (("ps", bg),), (("sq", q),))
                self.tt(ACTB[:, j, a:a + n], SQ[:, q, :n], self.ps[:, bu, :n], ALU.mult,
                        (("sq", q), ("ps", bu)), ())
        self.barrier()
        nt = len(tl)
        ssq = list(range(8 - nt, 8))
        self.rot = list(range(8 - nt))
        wdn = w_dn.rearrange("(kc p) n -> p kc n", p=128)
        pend_ssq = None
        for oc in range(KC):
            s = self.wslot()
            wv = self.wview(s, [128, JC, 128])
            self.wload(s, wv[:, 0:22, :], wdn[:, 0:22, oc * 128:(oc + 1) * 128])
            self.wload(s, wv[:, 22:44, :], wdn[:, 22:44, oc * 128:(oc + 1) * 128])
            for ti, (a, n) in enumerate(tl):
                b = self.bank()
                for kc in range(JC):
                    self.mm(self.ps[:, b, :n], wv[:, kc, :], ACTB[:, kc, a:a + n], kc == 0, kc == JC - 1,
                            (("w", s),), (("ps", b),))
                k = self.ring("ft", 2)
                self.act(FT[:, k, :n], self.ps[:, b, :n], AF.Copy, (("ps", b),), (("ft", k),))
                self.dma("act", self.fT[oc, :, t0 + a:t0 + a + n], FT[:, k, :n], ("ft", k), (("ft", k),), ("fscr",))
                q = self.ring("sq", 2)
                self.tt(SQ[:, q, :n], FT[:, k, :n], FT[:, k, :n], ALU.mult, (("ft", k),), (("sq", q),))
                if pend_ssq is not None:
                    pend_ssq()
                pend_ssq = (lambda ti=ti, n=n, q=q, oc=oc: self.mm(
                    self.ps[:, ssq[ti], :n], self.ones_bf, SQ[:, q, :n], oc == 0, oc == KC - 1,
                    (("sq", q), "cb"), (("ps", ssq[ti]),)))
        pend_ssq()
        if nxt_ssq:
            self.rot = list(range(8 - nt))
        self.post_residual(self.fT, src, dst, t0, t1, g_post_half, ssq, RSTD, XT, SQ=SQ, nxt_ssq=nxt_ssq)
        self.rot = list(range(8))

    def mix_layout(self):
        o = PH_OFF
        self.BETA = self.v(o, [16, NT]); o += NT * 4
        self.GG = self.v(o, [16, NT]); o += NT * 4
        self.UN = self.v(o, [128, KC, NT], BF16); o += KC * NT * 2
        self.mix_free = o

    def mixer_qkv(self):
        self.mix_layout()
        o = self.mix_free
        RAW = self.v(o, [128, 2, 180]); o += 2 * 180 * 4
        RAWB = self.v(o, [128, 2, 2052], BF16); o += 2 * 2052 * 2
        DG = self.v(o, [128, 2, 4, 128], BF16); o += 2 * 4 * 128 * 2
        CVO = self.v(o, [128, 2, NT]); o += 2 * NT * 4
        QO = self.v(o, [128, 2, NT], BF16); o += 2 * NT * 2
        RSTD = self.v(o, [128, NT]); o += NT * 4
        XT = self.v(o, [128, 4, 512]); o += 8192
        SQ = self.v(o, [128, 2, 512], BF16); o += 2048
        ST = self.v(o, [64, 2, 256]); o += 2048
        CT = self.v(o, [128, 2, 48]); o += 384
        SSQ1 = self.v(o, [128, NT]); o += NT * 4
        SSQ = [RSTD, SSQ1]
        UN = self.UN
        ident = self.cm(CM_ID, 128)
        self.barrier()
        self.rot = list(range(8))
        self.norm_load(self.h1, 0, NT, CV_G["mix_pre"], UN, RSTD, XT, SQ)
        self.barrier()
        win = self.w_in.rearrange("(kc p) n -> p kc n", p=128)
        tl = _tiles(0, NT)
        tlp = [(a, n) for (a, n) in tl if a < NTP]
        allk = lambda nm, cc_: [(nm, cc_, a_) for (a_, _n) in tl]

        def emit_l2(cc_, a_, n_, q_):
            b3 = self.bank()
            self.mm(self.ps[:, b3, :n_], self.ones_bf, SQ[:, q_, :n_], True, True, (("sq", q_), "cb"), (("ps", b3),))
            self.act(SSQ[cc_][:, a_:a_ + n_], self.ps[:, b3, :n_], AF.Copy, (("ps", b3),), (("ssq", cc_, a_),))

        def emit_conv(c_, cc_, a_, n_):
            b2 = self.bank()
            for j in range(4):
                self.mm(self.ps[:, b2, :n_], DG[:, cc_, j, :], RAWB[:, cc_, a_ + j:a_ + j + n_], j == 0, j == 3,
                        (("dg", cc_), ("rawb", cc_, a_), ("rawb", cc_, a_ - 512)), (("ps", b2),))
            self.act(CVO[:, cc_, a_:a_ + n_], self.ps[:, b2, :n_], AF.Silu, (("ps", b2),), (("cvo", cc_, a_),))
            if c_ < 32:
                q_ = self.ring("sq", 2)
                self.act(SQ[:, q_, :n_], CVO[:, cc_, a_:a_ + n_], AF.Square, (("cvo", cc_, a_),), (("sq", q_),))
                return (cc_, a_, n_, q_)
            self.cp(QO[:, cc_, a_:a_ + n_], CVO[:, cc_, a_:a_ + n_], (("cvo", cc_, a_),), (("qo", cc_, a_),))
            return None

        def epilogue(c_, cc_, pend, pend_l2):
            RS = RAW[:, cc_, 0:176].rearrange("p (s j) -> p s j", j=11)
            if pend_l2 is not None:
                emit_l2(*pend_l2)
            if pend is not None:
                p2 = emit_conv(c_, *pend)
                if p2 is not None:
                    emit_l2(*p2)
            CS = CVO[:, cc_, NTP:NT].rearrange("p (s j) -> p s j", j=8)
            for j in range(4):
                wcol = self.cv(CV_SC + c_ * 4 + j)
                if j == 0:
                    self.ts(CS, RS[:, :, 0:8], wcol, None, ALU.mult, None, (("raw", cc_), "cv"), (("cvo", cc_, NTP),))
                else:
                    self.stt(CS, RS[:, :, j:j + 8], wcol, CS, ALU.mult, ALU.add,
                             (("raw", cc_), ("cvo", cc_, NTP), "cv"), (("cvo", cc_, NTP),))
            self.act(CVO[:, cc_, NTP:NT], CVO[:, cc_, NTP:NT], AF.Silu, (("cvo", cc_, NTP),), (("cvo", cc_, NTP),))
            if c_ < 32:
                q = self.ring("sq", 2)
                self.act(SQ[:, q, :NS], CVO[:, cc_, NTP:NT], AF.Square, (("cvo", cc_, NTP),), (("sq", q),))
                emit_l2(cc_, NTP, NS, q)
                keys = allk("ssq", cc_)
                self.rstd(SSQ[cc_][:, 0:NT], SSQ[cc_][:, 0:NT], 1.0, keys, keys)
                if c_ < 16:
                    self.stt(QO[:, cc_, :], CVO[:, cc_, :], float(128 ** -0.5), SSQ[cc_][:, 0:NT], ALU.mult, ALU.mult,
                             keys + allk("cvo", cc_), allk("qo", cc_))
                else:
                    self.tt(QO[:, cc_, :], CVO[:, cc_, :], SSQ[cc_][:, 0:NT], ALU.mult,
                            keys + allk("cvo", cc_), allk("qo", cc_))
            else:
                self.cp(QO[:, cc_, NTP:NT], CVO[:, cc_, NTP:NT], (("cvo", cc_, NTP),), (("qo", cc_, NTP),))
            self.dma("sp", self.qkvF[c_], QO[:, cc_, :], ("qo", cc_), allk("qo", cc_), ())
            b = self.bank()
            self.tr(self.ps[0:3, b, 0:128], RAW[:, cc_, 176:179], ident, (("raw", cc_), "cm"), (("ps", b),))
            kc_ = self.ring("ct", 2)
            self.cp(CT[:, kc_, :].rearrange("p (s j) -> p s j", j=3), RS[:, :, 8:11], (("raw", cc_),), (("ct", kc_),), eng="pool")
            self.tr(self.ps[0:48, b, 128:256], CT[:, kc_, :], ident, (("ct", kc_), "cm"), (("ps", b),))
            k = self.ring("st", 2)
            self.act(ST[0:3, k, 0:128], self.ps[0:3, b, 0:128], AF.Copy, (("ps", b),), (("st", k),))
            self.act(ST[0:48, k, 128:256], self.ps[0:48, b, 128:256], AF.Copy, (("ps", b),), (("st", k),))
            self.dma("sp", self.nq_p[:, c_ * 128:(c_ + 1) * 128], ST[0:3, k, 0:128], ("st", k), (("st", k),), ())
            self.dma("sp", self.nq_s[:, c_ * 128:(c_ + 1) * 128], ST[0:48, k, 128:256], ("st", k), (("st", k),), ())
        deferred = None

        def qkv_wload(bi_):
            s_ = self.wslot()
            wv_ = self.wview(s_, [128, KC, 256])
            self.wload(s_, wv_, win[:, :, bi_ * 256:(bi_ + 1) * 256])
            return s_, wv_
        nxt_w = qkv_wload(0)
        for bi in range(24):
            s, wv = nxt_w
            if bi + 1 < 24:
                nxt_w = qkv_wload(bi + 1)
            xs = self.ring("xt", 4)
            self.dma("sp", XT[0:48, xs, 0:256], self.sqkv[:, bi * 256:(bi + 1) * 256], ("xt", xs), (), (("xt", xs),))
            for cc in range(2):
                c = bi * 2 + cc
                RS = RAW[:, cc, 0:176].rearrange("p (s j) -> p s j", j=11)
                self.memset(RAWB[:, cc, 0:3], 0.0, (("rawb", cc, -512),), eng="pool")
                b = self.bank()
                self.tr(self.ps[:, b, 0:48], XT[0:48, xs, cc * 128:(cc + 1) * 128], ident[0:48, 0:48],
                        (("xt", xs), "cm"), (("ps", b),))
                self.act(RS[:, :, 0:3], self.ps[:, b, 0:48].rearrange("p (s j) -> p s j", j=3), AF.Copy,
                         (("ps", b),), (("raw", cc),))
                for j in range(4):
                    self.ts(DG[:, cc, j, :], self.ident_bf, self.cv(CV_SC + c * 4 + j), None, ALU.mult, None,
                            ("cb", "cv"), (("dg", cc),))
                pend = None
                pend_l2 = None
                for ti, (a, n) in enumerate(tl):
                    b = self.bank()
                    for kc in range(KC):
                        self.mm(self.ps[:, b, :n], wv[:, kc, cc * 128:(cc + 1) * 128], UN[:, kc, a:a + n],
                                kc == 0, kc == KC - 1, (("w", s),), (("ps", b),))
                    if a < NTP:
                        self.act(RAWB[:, cc, 3 + a:3 + a + n], self.ps[:, b, :n], AF.Copy, (("ps", b),), (("rawb", cc, a),))
                        if a + n == NTP:
                            self.act(RAW[:, cc, 176:179], self.ps[:, b, n - 3:n], AF.Copy, (("ps", b),), (("raw", cc),))
                    else:
                        self.act(RS[:, :, 3:11], self.ps[:, b, 0:128].rearrange("p (s j) -> p s j", j=8), AF.Copy,
                                 (("ps", b),), (("raw", cc),))
                    if ti == 1 and deferred is not None:
                        epilogue(*deferred)
                        deferred = None
                    if pend_l2 is not None:
                        emit_l2(*pend_l2)
                        pend_l2 = None
                    if pend is not None:
                        pend_l2 = emit_conv(c, *pend)
                    pend = (cc, a, n) if a < NTP else None
                deferred = (c, cc, pend, pend_l2)
        epilogue(*deferred)
        s = self.wslot()
        wv = self.wview(s, [128, KC, 32])
        self.wload(s, wv, win[:, :, O_BETA:O_BETA + 32])
        for (a, n) in tl:
            bb = self.bank()
            for kc in range(KC):
                self.mm(self.ps[0:16, bb, :n], wv[:, kc, 0:16], UN[:, kc, a:a + n], kc == 0, kc == KC - 1,
                        (("w", s),), (("ps", bb),))
            ba = self.bank()
            for kc in range(KC):
                self.mm(self.ps[0:16, ba, :n], wv[:, kc, 16:32], UN[:, kc, a:a + n], kc == 0, kc == KC - 1,
                        (("w", s),), (("ps", ba),))
            self.act(self.BETA[:, a:a + n], self.ps[0:16, bb, :n], AF.Sigmoid, (("ps", bb),), (("bg", a),))
            r = self.ring("xt", 4)
            self.act(XT[0:16, r, :n], self.ps[0:16, ba, :n], AF.Exp, (("ps", ba), "cv"), (("xt", r),),
                     bias=self.cv(CV_DT, 1, 16), scale=1.0)
            self.act(XT[0:16, r, :n], XT[0:16, r, :n], AF.Ln, (("xt", r),), (("xt", r),), bias=1.0, scale=1.0)
            self.ts(self.GG[:, a:a + n], XT[0:16, r, :n], self.cv(CV_NA, 1, 16), None, ALU.mult, None,
                    (("xt", r), "cv3"), (("bg", a),))

    def delta_layout(self):
        o = PH_OFF + 2 * NT * 4
        self.OF = self.v(o, [128, NH, NM], BF16); o += NH * NM * 2
        self.dl_free = o
        L = {}

        def al(name, shape, dt=F32):
            nonlocal o
            L[name] = self.v(o, shape, dt)
            n = 1
            for s in shape[1:]:
                n *= s
            o += ((n * (4 if dt == F32 else 2) + 31) // 32) * 32
        al("S", [128, NH, 128]); al("SB", [128, NH, 128], BF16)
        al("QKVC", [128, 2, 48 * 64], BF16)
        for nm in ("GROW", "NBROW", "GAM", "E1", "W2", "E2", "RM"):
            al(nm, [128, 512])
        self.alias_off = o
        for pb_ in range(2):
            for nm in ("GROW", "NBROW", "GAM", "E1", "W2", "E2", "RM"):
                if pb_ == 0:
                    L["%s_0" % nm] = L[nm]
                else:
                    al("%s_1" % nm, [128, 512])
        for nm in ("P0", "P1", "PT0", "PT1", "RB", "TBT0", "QKM0", "KG0", "QG0", "TBT1", "QKM1", "KG1", "QG1"):
            al(nm, [128, 512], BF16)
        al("KDEC0", [128, 1024], BF16); al("KDEC1", [128, 1024], BF16)
        al("DDB", [128, 1024], BF16); al("UUB", [128, 1024], BF16); al("OTB", [128, 512])
        al("TOK", [128, 32]); al("NBTOK", [128, 16]); al("COLS", [128, 32]); al("DCOL", [128, 16])
        al("TOK_1", [128, 32]); al("NBTOK_1", [128, 16]); al("COLS_1", [128, 32]); al("DCOL_1", [128, 16])
        for nm in ("TOK", "NBTOK", "COLS", "DCOL"):
            L[nm + "_0"] = L[nm]
        for nm in ("P0", "P1", "PT0", "PT1", "RB"):
            L[nm + "_0"] = L[nm]
            al(nm + "_1", [128, 512], BF16)
        for pb_ in range(2):
            al("KK_%d" % pb_, [128, 512], BF16); al("QK_%d" % pb_, [128, 512], BF16)
        al("GLR0", [128, 128]); al("GLR1", [128, 128]); al("RHS2", [128, 128])
        for nm in ("QKM2", "KG2", "QG2"):
            al(nm, [128, 512], BF16)
        al("KDEC2", [128, 1024], BF16); al("GLR2", [128, 128])
        for nm in ("TBT", "QKM", "KG", "QG", "KDEC", "GLR"):
            L[nm] = L[nm + "0"]
        al("DT", [128, 4, 128], BF16); al("DD", [128, 4, 128], BF16); al("UU", [128, 4, 128], BF16)
        al("OT", [128, 4, 128])
        o_save = o
        o = self.alias_off
        al("S2", [128, 16, 128])
        al("UBD", [128, 16, 128], BF16)
        assert o <= o_save
        o = o_save
        L["RHS"] = L["RM_1"]
        self.L = L
        print("delta layout end", o)
        assert o <= ARENA_BYTES, o

    def delta_block(self, C, HB, nseg, tok0, qkvc, need_o, otok0, mk, nlev, sample=False, qk="qkvc"):
        L = self.L
        ones_f = self.cm(CM_ONE, 128)
        ident_f = self.cm(CM_ID, 128)
        W = C * HB

        def t3(name, rows=128, dt=None):
            return L[name][0:rows, 0:W].rearrange("p (h t) -> p h t", h=HB)
        b = self.bank()
        self.tr(self.ps[0:C, b, 0:16], self.BETA[0:16, tok0:tok0 + C], ident_f[0:16, 0:16], ("cm",), (("ps", b),))
        self.tr(self.ps[0:C, b, 16:32], self.GG[0:16, tok0:tok0 + C], ident_f[0:16, 0:16], ("cm",), (("ps", b),))
        TOK = L["TOK"][0:C]
        self.act(TOK, self.ps[0:C, b, 0:32], AF.Copy, (("ps", b),), ("tok",))
        NBTOK = L["NBTOK"][0:C]
        self.ts(NBTOK, TOK[:, 0:16], -1.0, None, ALU.mult, None, ("tok",), ("nbtok",))
        b = self.bank()
        self.mm(self.ps[0:C, b, 0:16], mk["TRI"], TOK[:, 16:32], True, True, ("tok", "cm"), (("ps", b),))
        self.mm(self.ps[0:C, b, 16:32], mk["SAME"], TOK[:, 16:32], True, True, ("tok", "cm"), (("ps", b),))
        COLS = L["COLS"][0:C]
        self.act(COLS, self.ps[0:C, b, 0:32], AF.Copy, (("ps", b),), ("cols",))
        DCOL = L["DCOL"][0:C]
        self.tt(DCOL, COLS[:, 16:32], COLS[:, 0:16], ALU.subtract, ("cols",), ("dcol",))
        self.act(DCOL, DCOL, AF.Exp, ("dcol",), ("dcol",))
        for hb in range(NH // HB):
            h0 = hb * HB
            gt = TOK[:, 16 + h0:16 + h0 + HB]
            RHS = t3("RHS", C)

            def bc_h(m):
                return m.unsqueeze(1).to_broadcast([C, HB, C])

            def bc_t(col, n=C):
                return col.unsqueeze(2).to_broadcast([C, HB, n])
            self.tt(RHS, bc_h(mk["TRI"]), bc_t(gt), ALU.mult, ("tok", "cm"), ("rhs",))
            b = self.bank()
            self.mm(self.ps[:, b, 0:W], ones_f[0:C, :], L["RHS"][0:C, 0:W], True, True, ("rhs", "cm"), (("ps", b),))
            self.act(L["GROW"][:, 0:W], self.ps[:, b, 0:W], AF.Copy, (("ps", b),), ("grow",))
            self.tt(RHS, bc_h(mk["ID"]), bc_t(NBTOK[:, h0:h0 + HB]), ALU.mult, ("nbtok", "cm", "rhs"), ("rhs",))
            b = self.bank()
            self.mm(self.ps[:, b, 0:W], ones_f[0:C, :], L["RHS"][0:C, 0:W], True, True, ("rhs", "cm"), (("ps", b),))
            self.act(L["NBROW"][:, 0:W], self.ps[:, b, 0:W], AF.Copy, (("ps", b),), ("nbrow",))
            R2 = L["RHS2"][0:C, 0:HB * nseg].rearrange("p (h s) -> p h s", h=HB)
            self.tt(R2, mk["SEG"].unsqueeze(1).to_broadcast([C, HB, nseg]), bc_t(gt, nseg), ALU.mult,
                    ("tok", "cm"), ("rhs2",))
            b = self.bank()
            self.mm(self.ps[:, b, 0:HB * nseg], ones_f[0:C, :], L["RHS2"][0:C, 0:HB * nseg], True, True,
                    ("rhs2", "cm"), (("ps", b),))
            GLR = L["GLR"][:, 0:HB * nseg]
            self.act(GLR, self.ps[:, b, 0:HB * nseg], AF.Exp, (("ps", b),), ("glr",))
            GLR3 = GLR.rearrange("p (h s) -> p h s", h=HB)
            self.act(L["GAM"][:, 0:W], L["GROW"][:, 0:W], AF.Exp, ("grow",), ("gam",))
            GAM = t3("GAM")
            kq = qkvc.rearrange("p (c t) -> p c t", c=48)
            self.tt(t3("KG"), kq[:, 16 + h0:16 + h0 + HB, :], GAM, ALU.mult, ("gam", qk), ("kg",))
            if need_o:
                self.tt(t3("QG"), kq[:, h0:h0 + HB, :], GAM, ALU.mult, ("gam", qk), ("qg",), eng="pool")
            GROWc = t3("GROW", C)
            gcol = COLS[:, h0:h0 + HB]
            E1 = t3("E1", C); W2 = t3("W2", C); E2 = t3("E2", C)
            self.tt(E1, GROWc, bc_t(gcol), ALU.subtract, ("grow", "cols"), ("e1",))
            self.ts(E1, E1, 0.0, None, ALU.min, None, ("e1",), ("e1",))
            self.act(E1, E1, AF.Exp, ("e1",), ("e1",))
            self.tt(W2, E1, bc_h(mk["US"]), ALU.mult, ("e1", "cm"), ("w2",))
            self.tt(W2, W2, t3("NBROW", C), ALU.mult, ("w2", "nbrow"), ("w2",))
            self.tt(E1, E1, bc_h(mk["UI"]), ALU.mult, ("e1", "cm"), ("e1",))
            self.tt(E2, GROWc, bc_t(gcol), ALU.subtract, ("grow", "cols"), ("e2",))
            self.ts(E2, E2, -1.0, 0.0, ALU.mult, ALU.min, ("e2",), ("e2",))
            self.act(E2, E2, AF.Exp, ("e2",), ("e2",))
            self.tt(E2, E2, bc_h(mk["LS"]), ALU.mult, ("e2", "cm"), ("e2",))
            self.tt(E2, E2, bc_t(NBTOK[:, h0:h0 + HB]), ALU.mult, ("e2", "nbtok"), ("e2",))
            bk = self.bank()
            bq = self.bank()
            bt = self.bank()
            psT = self.ps[:, bt, :].bitcast(BF16)
            for j in range(HB):
                kf = kq[:, 16 + h0 + j, :]
                qf = kq[:, h0 + j, :]
                self.mm(self.ps[0:C, bk, j * C:(j + 1) * C], kf, kf, True, True, (qk,), (("ps", bk),))
                if need_o:
                    self.mm(self.ps[0:C, bq, j * C:(j + 1) * C], kf, qf, True, True, (qk,), (("ps", bq),))
                self.tr(psT[0:C, j * 128:(j + 1) * 128], kf, self.ident_bf, (qk, "cb"), (("ps", bt),))
            pk = self.ps[0:C, bk, 0:W]
            RM = L["RM"][0:C, 0:W]
            self.tt(RM, pk, L["W2"][0:C, 0:W], ALU.mult, (("ps", bk), "w2"), ("rm",))
            Pc = L["P0"][0:C, 0:W]
            PTc = L["PT0"][0:C, 0:W]
            self.cp(Pc, RM, ("rm",), ("p0",), eng="pool")
            self.tt(PTc, pk, L["E2"][0:C, 0:W], ALU.mult, (("ps", bk), "e2"), ("pt0",))
            self.tt(t3("RM", C), t3("RM", C), bc_h(mk["ID"]), ALU.add, ("rm", "cm"), ("rm",))
            RB = L["RB"][0:C, 0:W]
            self.cp(RB, RM, ("rm",), ("rb",), eng="pool")
            if need_o:
                self.tt(L["QKM"][0:C, 0:W], self.ps[0:C, bq, 0:W], L["E1"][0:C, 0:W], ALU.mult,
                        (("ps", bq), "e1"), ("qkm",))
            KD = L["KDEC"][0:C, 0:HB * 128].rearrange("p (h d) -> p h d", h=HB)
            self.tt(KD, psT[0:C, 0:HB * 128].rearrange("p (h d) -> p h d", h=HB),
                    DCOL[:, h0:h0 + HB].unsqueeze(2).to_broadcast([C, HB, 128]), ALU.mult,
                    (("ps", bt), "dcol"), ("kdec",))
            cur = 0
            for lv in range(nlev):
                Pn = L["P%d" % (1 - cur)][0:C, 0:W]
                PTn = L["PT%d" % (1 - cur)][0:C, 0:W]
                pkey, ptkey = "p%d" % cur, "pt%d" % cur
                pnkey, ptnkey = "p%d" % (1 - cur), "pt%d" % (1 - cur)
                ba = self.bank()
                bb = self.bank()
                for j in range(HB):
                    sl = slice(j * C, (j + 1) * C)
                    if lv < nlev - 1:
                        self.mm(self.ps[0:C, ba, sl], PTc[:, sl], Pc[:, sl], True, True, (pkey, ptkey), (("ps", ba),))
                    self.mm(self.ps[0:C, bb, sl], Pc[:, sl], PTc[:, sl], True, True, (pkey, ptkey), (("ps", bb),))
                if lv < nlev - 1:
                    self.act(Pn, self.ps[0:C, ba, 0:W], AF.Copy, (("ps", ba),), (pnkey,))
                self.cp(PTn, self.ps[0:C, bb, 0:W], (("ps", bb),), (ptnkey,))
                bc = self.bank()
                for j in range(HB):
                    sl = slice(j * C, (j + 1) * C)
                    self.mm(self.ps[0:C, bc, sl], PTn[:, sl], RB[:, sl], True, True, (ptnkey, "rb"), (("ps", bc),))
                self.tt(RM, RM, self.ps[0:C, bc, 0:W], ALU.add, ("rm", ("ps", bc)), ("rm",))
                if lv < nlev - 1:
                    self.cp(RB, RM, ("rm",), ("rb",), eng="pool")
                Pc, PTc = Pn, PTn
                cur = 1 - cur
            self.tt(t3("TBT", C), t3("RM", C), bc_t(TOK[:, h0:h0 + HB]), ALU.mult, ("rm", "tok"), ("tbt",))
            for j in range(HB):
                h = h0 + j
                sl = slice(j * C, (j + 1) * C)
                KGh = L["KG"][:, sl]
                vf = kq[:, 32 + h, :]
                r = self.ring("chain", 4)
                DT = L["DT"][:, r, 0:C]
                DD = L["DD"][0:C, r, :]
                UU = L["UU"][0:C, r, :]
                OT = L["OT"][:, r, 0:C]
                if sample:
                    sr = h % 2
                    SSv = self.v_ss[sr]
                    SSB = L["SB"][:, 0:16, :]
                    if h == 0:
                        self.dma("sp", SSv, self.sdelta[:, 0].rearrange("s d v -> d s v"), ("ss", 0), (), (("ss", 0),))
                    if h + 1 < NH:
                        self.dma("sp", self.v_ss[(h + 1) % 2], self.sdelta[:, h + 1].rearrange("s d v -> d s v"),
                                 ("ss", (h + 1) % 2), (), (("ss", (h + 1) % 2),))
                    self.cp(SSB, SSv, (("ss", sr),), ("ssb",), eng="pool")
                    bK = self.bank()
                    for sg in range(nseg):
                        self.mm(self.ps[:, bK, sg * LS:(sg + 1) * LS], SSB[:, sg, :], KGh[:, sg * LS:(sg + 1) * LS],
                                True, True, ("ssb", "kg"), (("ps", bK),))
                else:
                    SBh = L["SB"][:, h, :]
                    bK = self.bank()
                    self.mm(self.ps[:, bK, 0:C], SBh, KGh, True, True, (("sb", h), "kg"), (("ps", bK),))
                self.tt(DT, vf, self.ps[:, bK, 0:C], ALU.subtract, (qk, ("ps", bK)), (("dt", r),))
                bT = self.bank()
                pT = self.ps[:, bT, :].bitcast(BF16)
                self.tr(pT[0:C, 0:128], DT, self.ident_bf, (("dt", r), "cb"), (("ps", bT),))
                self.act(DD, pT[0:C, 0:128], AF.Copy, (("ps", bT),), (("dd", r),))
                bU = self.bank()
                self.mm(self.ps[0:C, bU, 0:128], L["TBT"][0:C, sl], DD, True, True, ("tbt", ("dd", r)), (("ps", bU),))
                self.act(UU, self.ps[0:C, bU, 0:128], AF.Copy, (("ps", bU),), (("uu", r),))
                if need_o:
                    bO = self.bank()
                    if sample:
                        for sg in range(nseg):
                            self.mm(self.ps[:, bO, sg * LS:(sg + 1) * LS], SSB[:, sg, :],
                                    L["QG"][:, j * C + sg * LS:j * C + (sg + 1) * LS], True, True,
                                    ("ssb", "qg"), (("ps", bO),))
                    else:
                        self.mm(self.ps[:, bO, 0:C], SBh, L["QG"][:, sl], True, True, (("sb", h), "qg"), (("ps", bO),))
                    self.mm(self.ps[:, bO, 128:128 + C], UU, L["QKM"][0:C, sl], True, True,
                            (("uu", r), "qkm"), (("ps", bO),))
                    self.act(OT, self.ps[:, bO, 0:C], AF.Copy, (("ps", bO),), (("ot", r),))
                    self.tt(self.OF[:, h, otok0:otok0 + C], OT, self.ps[:, bO, 128:128 + C], ALU.add,
                            (("ot", r), ("ps", bO)), ())
                KDh = L["KDEC"][0:C, j * 128:(j + 1) * 128]
                if sample:
                    UBD = L["UBD"]
                    self.tt(UBD, UU.unsqueeze(1).to_broadcast([128, 16, 128]),
                            mk["SEG"].unsqueeze(2).to_broadcast([128, 16, 128]), ALU.mult,
                            (("uu", r), "cm"), ("ubd",))
                    gl = GLR3[:, j, :]
                    self.tt(SSv, SSv, gl.unsqueeze(2).to_broadcast([128, 16, 128]), ALU.mult,
                            (("ss", sr), "glr"), (("ss", sr),))
                    for q4 in range(4):
                        bS = self.bank()
                        self.mm(self.ps[:, bS, :], KDh, UBD[:, 4 * q4:4 * q4 + 4, :].rearrange("p a b -> p (a b)"),
                                True, True, ("kdec", "ubd"), (("ps", bS),))
                        sv = SSv[:, 4 * q4:4 * q4 + 4, :].rearrange("p a b -> p (a b)")
                        self.tt(sv, sv, self.ps[:, bS, :], ALU.add, (("ss", sr), ("ps", bS)), (("ss", sr),))
                    self.dma("sp", self.nd_s[:, h].rearrange("s d v -> d s v"), SSv, ("ss", sr), (("ss", sr),), ())
                else:
                    bS = self.bank()
                    self.mm(self.ps[:, bS, 0:128], KDh, UU, True, True, ("kdec", ("uu", r)), (("ps", bS),))
                    Sh = L["S"][:, h, :]
                    self.stt(Sh, Sh, GLR[:, j:j + 1], self.ps[:, bS, 0:128], ALU.mult, ALU.add,
                             (("s", h), "glr", ("ps", bS)), (("s", h),))
                    self.cp(SBh, Sh, (("s", h),), (("sb", h),), eng="pool")

    def p_prep(self, c, hb, pb, ob, qkvc, qk, need_o, mk):
        C, HB, W, nlev = 64, 8, 512, 5
        L = self.L
        ones_f = self.cm(CM_ONE, 128)
        ident_f = self.cm(CM_ID, 128)
        tok0 = c * 64
        cp_ = c % 2

        def T(name):
            return L["%s_%d" % (name, pb)]

        def K(name):
            return (name, pb)

        def t3(name, rows=128):
            return T(name)[0:rows, 0:W].rearrange("p (h t) -> p h t", h=HB)

        def bc_h(m):
            return m.unsqueeze(1).to_broadcast([C, HB, C])

        def bc_t(col, n=C):
            return col.unsqueeze(2).to_broadcast([C, HB, n])
        TOK = L["TOK_%d" % cp_][0:C]
        NBTOK = L["NBTOK_%d" % cp_][0:C]
        COLS = L["COLS_%d" % cp_][0:C]
        DCOL = L["DCOL_%d" % cp_][0:C]
        kt, kn, kc_, kd = ("tok", cp_), ("nbtok", cp_), ("cols", cp_), ("dcol", cp_)
        if hb == 0:
            b = self.bank()
            self.tr(self.ps[0:C, b, 0:16], self.BETA[0:16, tok0:tok0 + C], ident_f[0:16, 0:16], ("cm",), (("ps", b),))
            self.tr(self.ps[0:C, b, 16:32], self.GG[0:16, tok0:tok0 + C], ident_f[0:16, 0:16], ("cm",), (("ps", b),))
            self.act(TOK, self.ps[0:C, b, 0:32], AF.Copy, (("ps", b),), (kt,))
            self.ts(NBTOK, TOK[:, 0:16], -1.0, None, ALU.mult, None, (kt,), (kn,), eng="pool")
            b = self.bank()
            self.mm(self.ps[0:C, b, 0:16], mk["TRI"], TOK[:, 16:32], True, True, (kt, "cm"), (("ps", b),))
            self.mm(self.ps[0:C, b, 16:32], mk["SAME"], TOK[:, 16:32], True, True, (kt, "cm"), (("ps", b),))
            self.act(COLS, self.ps[0:C, b, 0:32], AF.Copy, (("ps", b),), (kc_,))
            self.tt(DCOL, COLS[:, 16:32], COLS[:, 0:16], ALU.subtract, (kc_,), (kd,), eng="pool")
            self.act(DCOL, DCOL, AF.Exp, (kd,), (kd,))
            yield
        h0 = hb * HB
        gt = TOK[:, 16 + h0:16 + h0 + HB]
        kq = qkvc.rearrange("p (c t) -> p c t", c=48)
        bk = self.bank()
        bq = self.bank() if need_o else None
        bt = self.bank()
        psT = self.ps[:, bt, :].bitcast(BF16)
        for j in range(HB):
            kf = kq[:, 16 + h0 + j, :]
            qf = kq[:, h0 + j, :]
            self.mm(self.ps[0:C, bk, j * C:(j + 1) * C], kf, kf, True, True, (qk,), (("ps", bk),))
            if need_o:
                self.mm(self.ps[0:C, bq, j * C:(j + 1) * C], kf, qf, True, True, (qk,), (("ps", bq),))
            self.tr(psT[0:C, j * 128:(j + 1) * 128], kf, self.ident_bf, (qk, "cb"), (("ps", bt),))
        KK = T("KK")[0:C, 0:W]
        self.act(KK, self.ps[0:C, bk, 0:W], AF.Copy, (("ps", bk),), (K("kk"),))
        if need_o:
            QK = T("QK")[0:C, 0:W]
            self.act(QK, self.ps[0:C, bq, 0:W], AF.Copy, (("ps", bq),), (K("qk"),))
        RHS = t3("W2", C)
        self.tt(RHS, bc_h(mk["TRI"]), bc_t(gt), ALU.mult, (kt, "cm", K("w2")), (K("w2"),))
        b1 = self.bank()
        self.mm(self.ps[:, b1, 0:W], ones_f[0:C, :], T("W2")[0:C, 0:W], True, True, (K("w2"), "cm"), (("ps", b1),))
        self.act(T("GROW")[:, 0:W], self.ps[:, b1, 0:W], AF.Copy, (("ps", b1),), (K("grow"),))
        RHSb = t3("E2", C)
        self.tt(RHSb, bc_h(mk["ID"]), bc_t(NBTOK[:, h0:h0 + HB]), ALU.mult, (kn, "cm", K("e2")), (K("e2"),), eng="pool")
        b2 = self.bank()
        self.mm(self.ps[:, b2, 0:W], ones_f[0:C, :], T("E2")[0:C, 0:W], True, True, (K("e2"), "cm"), (("ps", b2),))
        MB = t3("NBROW", C)
        self.tt(MB, self.ps[0:C, b2, 0:W].rearrange("p (h t) -> p h t", h=HB), bc_h(mk["US"]), ALU.mult,
                (("ps", b2), "cm"), (K("mb"),))
        b3 = self.bank()
        self.mm(self.ps[:, b3, 0:HB], ones_f[0:C, :], gt, True, True, (kt, "cm"), (("ps", b3),))
        GLR = L["GLR%d" % ob][:, 0:HB]
        self.act(GLR, self.ps[:, b3, 0:HB], AF.Exp, (("ps", b3),), (("glr", ob),))
        KD = L["KDEC%d" % ob][0:C, 0:HB * 128].rearrange("p (h d) -> p h d", h=HB)
        self.tt(KD, psT[0:C, 0:HB * 128].rearrange("p (h d) -> p h d", h=HB),
                DCOL[:, h0:h0 + HB].unsqueeze(2).to_broadcast([C, HB, 128]), ALU.mult,
                (("ps", bt), kd), (("kdec", ob),))
        yield
        GROWc = t3("GROW", C)
        gcol = COLS[:, h0:h0 + HB]
        E1 = t3("E1", C); W2 = t3("W2", C); E2 = t3("E2", C)
        self.tt(E1, GROWc, bc_t(gcol), ALU.subtract, (K("grow"), kc_), (K("e1"),))
        self.tt(E2, bc_t(gcol), GROWc, ALU.subtract, (K("grow"), kc_, K("e2")), (K("e2"),))
        self.act(T("GAM")[:, 0:W], T("GROW")[:, 0:W], AF.Exp, (K("grow"),), (K("gam"),))
        self.ts(E1, E1, 0.0, None, ALU.min, None, (K("e1"),), (K("e1"),))
        self.ts(E2, E2, 0.0, None, ALU.min, None, (K("e2"),), (K("e2"),))
        yield
        self.act(E1, E1, AF.Exp, (K("e1"),), (K("e1"),))
        self.act(E2, E2, AF.Exp, (K("e2"),), (K("e2"),))
        GAM = t3("GAM")
        self.tt(L["KG%d" % ob][:, 0:W].rearrange("p (h t) -> p h t", h=HB),
                kq[:, 16 + h0:16 + h0 + HB, :], GAM, ALU.mult, (K("gam"), qk), (("kg", ob),))
        if need_o:
            self.tt(L["QG%d" % ob][:, 0:W].rearrange("p (h t) -> p h t", h=HB), kq[:, h0:h0 + HB, :], GAM, ALU.mult,
                    (K("gam"), qk), (("qg", ob),), eng="pool")
        yield
        self.stt(W2, E1, 1.0, MB, ALU.min, ALU.mult, (K("e1"), K("mb")), (K("w2"),))
        self.stt(E2, E2, 1.0, bc_h(mk["LS"]), ALU.min, ALU.mult, (K("e2"), "cm"), (K("e2"),))
        yield
        RM = T("RM")[0:C, 0:W]
        Pc = T("P0")[0:C, 0:W]
        PTc = T("PT0")[0:C, 0:W]
        self.tt(Pc, KK, T("W2")[0:C, 0:W], ALU.mult, (K("kk"), K("w2")), (K("p0"),), eng="pool")
        self.tt(t3("E2", C), t3("E2", C), bc_t(NBTOK[:, h0:h0 + HB]), ALU.mult, (K("e2"), kn), (K("e2"),), eng="pool")
        self.tt(RM, KK, T("W2")[0:C, 0:W], ALU.mult, (K("kk"), K("w2")), (K("rm"),))
        yield
        self.tt(PTc, KK, T("E2")[0:C, 0:W], ALU.mult, (K("kk"), K("e2")), (K("pt0"),))
        self.tt(t3("RM", C), t3("RM", C), bc_h(mk["ID"]), ALU.add, (K("rm"), "cm"), (K("rm"),), eng="pool")
        RB = T("RB")[0:C, 0:W]
        if need_o:
            self.stt(E1, E1, 1.0, bc_h(mk["UI"]), ALU.min, ALU.mult, (K("e1"), "cm"), (K("e1"),))
            self.tt(L["QKM%d" % ob][0:C, 0:W], QK, T("E1")[0:C, 0:W], ALU.mult, (K("qk"), K("e1")), (("qkm", ob),))
        yield "HALF"
        self.act(RB, RM, AF.Copy, (K("rm"),), (K("rb"),))
        cur = 0

        def squares(lv, Pc, PTc, cur):
            Pn = T("P%d" % (1 - cur))[0:C, 0:W]
            PTn = T("PT%d" % (1 - cur))[0:C, 0:W]
            pkey, ptkey = K("p%d" % cur), K("pt%d" % cur)
            ba = self.bank()
            bb = self.bank()
            for j in range(HB):
                sl = slice(j * C, (j + 1) * C)
                if lv < nlev - 1:
                    self.mm(self.ps[0:C, ba, sl], PTc[:, sl], Pc[:, sl], True, True, (pkey, ptkey), (("ps", ba),))
                self.mm(self.ps[0:C, bb, sl], Pc[:, sl], PTc[:, sl], True, True, (pkey, ptkey), (("ps", bb),))
            if lv < nlev - 1:
                self.act(Pn, self.ps[0:C, ba, 0:W], AF.Copy, (("ps", ba),), (K("p%d" % (1 - cur)),))
            self.act(PTn, self.ps[0:C, bb, 0:W], AF.Copy, (("ps", bb),), (K("pt%d" % (1 - cur)),))
            return Pn, PTn
        Pn, PTn = squares(0, Pc, PTc, cur)
        for lv in range(nlev):
            ptnkey = K("pt%d" % (1 - cur))
            Pc, PTc = Pn, PTn
            cur = 1 - cur
            yield
            bc = self.bank()
            for j in range(HB):
                sl = slice(j * C, (j + 1) * C)
                self.mm(self.ps[0:C, bc, sl], PTc[:, sl], RB[:, sl], True, True, (ptnkey, K("rb")), (("ps", bc),))
            if lv + 1 < nlev:
                Pn, PTn = squares(lv + 1, Pc, PTc, cur)
            self.tt(RM, RM, self.ps[0:C, bc, 0:W], ALU.add, (K("rm"), ("ps", bc)), (K("rm"),))
            if lv < nlev - 1:
                yield
                self.act(RB, RM, AF.Copy, (K("rm"),), (K("rb"),))
        yield
        self.tt(L["TBT%d" % pb][0:C, 0:W].rearrange("p (h t) -> p h t", h=HB), t3("RM", C),
                bc_t(TOK[:, h0:h0 + HB]), ALU.mult, (K("rm"), kt), (("tbt", pb),))

    def p_chain(self, c, hb, pb, ob, qkvc, qk, need_o):
        C, HB, W = 64, 8, 512
        L = self.L
        h0 = hb * HB
        kq = qkvc.rearrange("p (c t) -> p c t", c=48)
        SBk = [("sb", h0 + j) for j in range(HB)]
        Sk = [("s", h0 + j) for j in range(HB)]
        KG = L["KG%d" % ob]
        bK = self.bank()
        for j in range(HB):
            self.mm(self.ps[:, bK, j * C:(j + 1) * C], L["SB"][:, h0 + j, :], KG[:, j * C:(j + 1) * C], True, True,
                    (("sb", h0 + j), ("kg", ob)), (("ps", bK),))
        if need_o:
            bO1 = self.bank()
            for j in range(HB):
                self.mm(self.ps[:, bO1, j * C:(j + 1) * C], L["SB"][:, h0 + j, :], L["QG%d" % ob][:, j * C:(j + 1) * C],
                        True, True, (("sb", h0 + j), ("qg", ob)), (("ps", bO1),))
        DT = L["DT"].rearrange("p a b -> p (a b)")
        self.tt(DT.rearrange("p (h t) -> p h t", h=HB), kq[:, 32 + h0:32 + h0 + HB, :],
                self.ps[:, bK, 0:W].rearrange("p (h t) -> p h t", h=HB), ALU.subtract, (qk, ("ps", bK)), ("dt",))
        if need_o:
            OT = L["OTB"]
            self.act(OT, self.ps[:, bO1, 0:W], AF.Copy, (("ps", bO1),), ("ot",))
        yield
        bT = self.bank()
        pT = self.ps[:, bT, :].bitcast(BF16)
        for j in range(HB):
            self.tr(pT[0:C, j * 128:(j + 1) * 128], DT[:, j * C:(j + 1) * C], self.ident_bf, ("dt", "cb"), (("ps", bT),))
        DD = L["DDB"][0:C, :]
        self.act(DD, pT[0:C, 0:1024], AF.Copy, (("ps", bT),), ("dd",))
        yield
        TBT = L["TBT%d" % pb]
        bU = [self.bank(), self.bank()]
        for j in range(HB):
            self.mm(self.ps[0:C, bU[j // 4], (j % 4) * 128:(j % 4 + 1) * 128], TBT[0:C, j * C:(j + 1) * C],
                    DD[:, j * 128:(j + 1) * 128], True, True, (("tbt", pb), "dd"), (("ps", bU[j // 4]),))
        UU = L["UUB"][0:C, :]
        self.act(UU[:, 0:512], self.ps[0:C, bU[0], :], AF.Copy, (("ps", bU[0]),), ("uu0",))
        self.cp(UU[:, 512:1024], self.ps[0:C, bU[1], :], (("ps", bU[1]),), ("uu1",))
        yield
        KD = L["KDEC%d" % ob]
        bS = [self.bank(), self.bank()]
        for j in range(HB):
            self.mm(self.ps[:, bS[j // 4], (j % 4) * 128:(j % 4 + 1) * 128], KD[0:C, j * 128:(j + 1) * 128],
                    UU[:, j * 128:(j + 1) * 128], True, True, (("kdec", ob), "uu%d" % (j // 4)), (("ps", bS[j // 4]),))
        if need_o:
            bO2 = self.bank()
            for j in range(HB):
                self.mm(self.ps[:, bO2, j * C:(j + 1) * C], UU[:, j * 128:(j + 1) * 128],
                        L["QKM%d" % ob][0:C, j * C:(j + 1) * C], True, True,
                        ("uu%d" % (j // 4), ("qkm", ob)), (("ps", bO2),))
        S8 = L["S"][:, h0:h0 + HB, :]
        GLR = L["GLR%d" % ob][:, 0:HB]
        self.tt(S8, S8, GLR.unsqueeze(2).to_broadcast([128, HB, 128]), ALU.mult, Sk + [("glr", ob)], Sk, eng="pool")
        for q in range(2):
            s4 = L["S"][:, h0 + 4 * q:h0 + 4 * q + 4, :].rearrange("p a b -> p (a b)")
            self.tt(s4, s4, self.ps[:, bS[q], :], ALU.add, Sk[4 * q:4 * q + 4] + [("ps", bS[q])], Sk[4 * q:4 * q + 4])
        self.act(L["SB"][:, h0:h0 + HB, :], S8, AF.Copy, Sk, SBk)
        if need_o:
            otok0 = c * 64 - M0
            self.tt(self.OF[:, h0:h0 + HB, otok0:otok0 + C], OT.rearrange("p (h t) -> p h t", h=HB),
                    self.ps[:, bO2, 0:W].rearrange("p (h t) -> p h t", h=HB), ALU.add, ("ot", ("ps", bO2)), ())
        yield

    def delta_prompt_pipelined(self, mk):
        L = self.L
        self.rot = list(range(8))
        units = [(c, hb) for c in range(NTP // 64) for hb in range(2)]
        qslot = {}

        def step(g):
            try:
                return next(g) or True
            except StopIteration:
                return False
        preps = {}
        info = {}

        def start_prep(i):
            c, hb = units[i]
            if hb == 0:
                s = self.ring("qkvc", 2)
                qslot[c] = s
                self.dma("sp", L["QKVC"][:, s, :].rearrange("p (c t) -> p c t", c=48),
                         self.qkvF[:, :, c * 64:(c + 1) * 64].rearrange("c p t -> p c t"), ("qkvc", s), (), (("qkvc", s),))
            s = qslot[c]
            qc = L["QKVC"][:, s, :]
            need_o = c * 64 >= M0
            info[i] = (c, hb, i % 2, i % 3, qc, ("qkvc", s), need_o)
            preps[i] = self.p_prep(c, hb, i % 2, i % 3, qc, ("qkvc", s), need_o, mk)
        n = len(units)
        start_prep(0)
        while step(preps[0]) != "HALF":
            pass
        chain = None
        for i in range(n):
            if i + 1 < n:
                start_prep(i + 1)
            a_live = i + 1 < n
            b_live = True
            c_live = chain is not None
            while a_live or b_live or c_live:
                if b_live:
                    b_live = bool(step(preps[i]))
                if a_live:
                    if step(preps[i + 1]) == "HALF":
                        a_live = False
                if c_live:
                    c_live = bool(step(chain))
            c, hb, pb, ob, qc, qk, need_o = info[i]
            chain = self.p_chain(c, hb, pb, ob, qc, qk, need_o)
        while step(chain):
            pass

    def delta(self):
        self.delta_layout()
        L = self.L
        self.barrier()
        self.rot = list(range(8))
        self.memset(L["S"], 0.0, [("s", h) for h in range(NH)])
        self.memset(L["SB"], 0.0, [("sb", h) for h in range(NH)], eng="pool")
        mk = dict(TRI=self.cm(CM_TRI, 64, 64), UI=self.cm(CM_UI, 64, 64), US=self.cm(CM_US, 64, 64),
                  LS=self.cm(CM_LS, 64, 64), SAME=self.cm(CM_ONE, 64, 64), ID=self.cm(CM_ID, 64, 64),
                  SEG=self.cm(CM_ONE, 1, 64))
        self.delta_prompt_pipelined(mk)
        self.dma("sp", self.nd_p.rearrange("h d v -> d h v"), L["S"], ("sfin",), [("s", h) for h in range(NH)], ())
        self.barrier()
        self.rot = list(range(8))
        mk8 = dict(TRI=self.cm(CM_TRI8, 128), UI=self.cm(CM_UI8, 128), US=self.cm(CM_US8, 128),
                   LS=self.cm(CM_LS8, 128), SAME=self.cm(CM_SAME8, 128), ID=self.cm(CM_ID, 128),
                   SEG=self.cm(CM_SEG, 16))
        self.v_ss = [L["S"][:, 0:16, :], L["S2"]]
        qc = L["QKVC"].rearrange("p a b -> p (a b)")
        self.dma("sp", qc.rearrange("p (c t) -> p c t", c=48),
                 self.qkvF[:, :, NTP:NT].rearrange("c p t -> p c t"), ("qkvc", 0), (), (("qkvc", 0),))
        self.delta_block(128, 4, 16, NTP, qc, True, NMAIN, mk8, 2, sample=True, qk=("qkvc", 0))

    def mixer_tail(self):
        NU = NM + 32
        U0 = M0 - 32
        o = PH_OFF
        RSTD = self.v(o, [128, NU]); o += NU * 4
        SQ = self.v(o, [128, 2, 512], BF16); o += 2048
        MU = self.v(o, [128, NM]); o += NM * 4
        RS = self.v(o, [128, NM]); o += NM * 4
        assert o <= PH_OFF + 2 * NT * 4
        o = PH_OFF + 2 * NT * 4
        OF = self.OF; o += NH * NM * 2
        UN2 = self.v(o, [128, KC, NU], BF16); o += KC * NU * 2
        CB = self.v(o, [128, KC, NM], BF16); o += KC * NM * 2
        XT = self.v(o, [128, 4, 512]); o += 8192
        GPB = self.v(o, [128, 1056], BF16); o += 1056 * 2
        GP30 = self.v(o, [128, 32]); o += 128
        GSX = self.v(o, [128, 608]); o += 608 * 4
        DG31 = self.v(o, [128, 31, 128], BF16); o += 31 * 128 * 2
        CO = self.v(o, [128, NM]); o += NM * 4
        assert o <= ARENA_BYTES, o
        ident = self.cm(CM_ID, 128)
        win = self.w_in.rearrange("(kc p) n -> p kc n", p=128)
        self.barrier()
        self.rot = list(range(8))
        self.norm_load(self.h1, U0, NT, CV_G["mix_pre"], UN2, RSTD, XT, SQ)
        self.barrier()
        tlm = _tiles(0, NM)
        for zb in range(8):
            s = self.wslot()
            wv = self.wview(s, [128, KC, 256])
            self.wload(s, wv, win[:, :, O_Z + zb * 256:O_Z + (zb + 1) * 256])
            for cc in range(2):
                h = zb * 2 + cc
                SS = MU if cc == 0 else RS
                zbanks = []
                for (a, n) in tlm:
                    ok = ("of", h, a)
                    q = self.ring("sq", 2)
                    self.act(SQ[:, q, :n], OF[:, h, a:a + n], AF.Square, (ok,), (("sq", q),))
                    b1 = self.bank()
                    self.mm(self.ps[:, b1, :n], self.ones_bf, SQ[:, q, :n], True, True, (("sq", q), "cb"), (("ps", b1),))
                    b = self.bank()
                    for kc in range(KC):
                        self.mm(self.ps[:, b, :n], wv[:, kc, cc * 128:(cc + 1) * 128], UN2[:, kc, 32 + a:32 + a + n],
                                kc == 0, kc == KC - 1, (("w", s),), (("ps", b),))
                    zbanks.append(b)
                    self.act(SS[:, a:a + n], self.ps[:, b1, :n], AF.Copy, (("ps", b1),), (("ss", cc, a),))
                    r2 = self.ring("xt", 4)
                    self.act(XT[:, r2, :n], self.ps[:, b, :n], AF.Silu, (("ps", b),), (("xt", r2),))
                    self.tt(OF[:, h, a:a + n], OF[:, h, a:a + n], XT[:, r2, :n], ALU.mult, (ok, ("xt", r2)), (ok,))
                sk = [("ss", cc, a) for (a, n) in tlm]
                self.rstd(SS[:, 0:NM], SS[:, 0:NM], 1.0 / 128, sk, sk)
                self.stt(OF[:, h, :], OF[:, h, :], self.cv(CV_ON), SS[:, 0:NM], ALU.mult, ALU.mult,
                         sk + [("of", h, a) for (a, n) in tlm] + ["cv"], [("of", h, a) for (a, n) in tlm])
        self.barrier()
        self.rot = list(range(8))
        SUMB = [(2, 0), (3, 0), (4, 0)]
        SSQB = [(5, 0), (6, 0), (7, 0)]
        GS = GSX.rearrange("p (s j) -> p s j", j=38)
        tlu = _tiles(0, NU)
        def sgt_load(c_):
            for q4 in range(4):
                self.dma("sp", MU[0:120, (c_ % 2) * 512 + q4 * 128:(c_ % 2) * 512 + (q4 + 1) * 128],
                         self.sglu[q4 * 120:(q4 + 1) * 120, c_ * 128:(c_ + 1) * 128],
                         ("sgt", c_ % 2), (), (("sgt", c_ % 2),))
        sgt_load(0)

        def glu_wload(c_):
            s_ = self.wslot()
            wv_ = self.wview(s_, [128, KC, 256])
            self.wload(s_, wv_[:, :, 0:128], win[:, :, O_GLU + c_ * 128:O_GLU + (c_ + 1) * 128])
            self.wload(s_, wv_[:, :, 128:256], win[:, :, O_GLU + 2048 + c_ * 128:O_GLU + 2048 + (c_ + 1) * 128])
            return s_, wv_
        nxt_w = glu_wload(0)
        for c in range(KC):
            s, wv = nxt_w
            if c + 1 < KC:
                nxt_w = glu_wload(c + 1)
                sgt_load(c + 1)
            for q4 in range(4):
                sg_ = MU[0:120, (c % 2) * 512 + q4 * 128:(c % 2) * 512 + (q4 + 1) * 128]
                b = self.bank()
                self.tr(self.ps[:, b, 0:120], sg_, ident[0:120, 0:120], (("sgt", c % 2), "cm"), (("ps", b),))
                self.act(GS[:, 4 * q4:4 * q4 + 4, 0:30], self.ps[:, b, 0:120].rearrange("p (s j) -> p s j", j=30),
                         AF.Copy, (("ps", b),), ("glx",))
            for (a, n) in tlu:
                ba = self.bank()
                for kc in range(KC):
                    self.mm(self.ps[:, ba, :n], wv[:, kc, 0:128], UN2[:, kc, a:a + n], kc == 0, kc == KC - 1,
                            (("w", s),), (("ps", ba),))
                bb = self.bank()
                for kc in range(KC):
                    self.mm(self.ps[:, bb, :n], wv[:, kc, 128:256], UN2[:, kc, a:a + n], kc == 0, kc == KC - 1,
                            (("w", s),), (("ps", bb),))
                r = self.ring("xt", 4)
                self.act(XT[:, r, :n], self.ps[:, bb, :n], AF.Sigmoid, (("ps", bb),), (("xt", r),))
                lo = max(a, 2)
                hi = min(a + n, 1056)
                if hi > lo:
                    self.tt(GPB[:, lo - 2:hi - 2], self.ps[:, ba, lo - a:hi - a], XT[:, r, lo - a:hi - a], ALU.mult,
                            (("ps", ba), ("xt", r)), ("glx",))
                if a <= 1026 < a + n:
                    self.tt(GP30[:, 0:30], self.ps[:, ba, 1026 - a:1056 - a], XT[:, r, 1026 - a:1056 - a], ALU.mult,
                            (("ps", ba), ("xt", r)), ("gp30",))
                if a + n > 1056:
                    o0 = 1056 - a
                    self.tt(GS[:, :, 30:38], self.ps[:, ba, o0:o0 + 128].rearrange("p (s j) -> p s j", j=8),
                            XT[:, r, o0:o0 + 128].rearrange("p (s j) -> p s j", j=8), ALU.mult,
                            (("ps", ba), ("xt", r)), ("glx",))
            COs = CO[:, NMAIN:NM].rearrange("p (s j) -> p s j", j=8)
            bcol = self.cv(CV_G["b_dw"] + c)
            self.tt(DG31, self.ident_bf.unsqueeze(1).to_broadcast([128, 31, 128]),
                    self.cv(CV_DW + c * 31, 31).unsqueeze(2).to_broadcast([128, 31, 128]), ALU.mult,
                    ("cb", "cv"), ("dg31",))
            for t0_ in (0, 512):
                b = self.bank()
                for j in range(31):
                    self.mm(self.ps[:, b, :], DG31[:, j, :], GPB[:, t0_ + j:t0_ + j + 512], j == 0, j == 30,
                            ("dg31", "glx"), (("ps", b),))
                self.act(CO[:, t0_:t0_ + 512], self.ps[:, b, :], AF.Identity, (("ps", b), "cv"), ("co",), bias=bcol, scale=1.0)
            for j in range(31):
                wcol = self.cv(CV_DW + c * 31 + j)
                if j == 0:
                    self.ts(COs, GS[:, :, 0:8], wcol, bcol, ALU.mult, ALU.add, ("glx", "cv"), ("co",))
                else:
                    self.stt(COs, GS[:, :, j:j + 8], wcol, COs, ALU.mult, ALU.add, ("glx", "co", "cv"), ("co",))
            b = self.bank()
            self.tr(self.ps[0:30, b, 0:128], GP30[:, 0:30], ident, ("gp30", "cm"), (("ps", b),))
            k = self.ring("xt", 4)
            self.act(XT[0:30, k, 0:128], self.ps[0:30, b, 0:128], AF.Copy, (("ps", b),), (("xt", k),))
            self.dma("act", self.ng_p[:, c * 128:(c + 1) * 128], XT[0:30, k, 0:128], ("xt", k), (("xt", k),), ())
            for q4 in range(4):
                k = self.ring("xt", 4)
                self.cp(XT[:, k, 0:120].rearrange("p (s j) -> p s j", j=30), GS[:, 4 * q4:4 * q4 + 4, 8:38],
                        ("glx",), (("xt", k),), eng="pool")
                b = self.bank()
                self.tr(self.ps[0:120, b, 0:128], XT[:, k, 0:120], ident, (("xt", k), "cm"), (("ps", b),))
                self.act(XT[0:120, k, 128:256], self.ps[0:120, b, 0:128], AF.Copy, (("ps", b),), (("xt", k),))
                self.dma("act", self.ng_s[q4 * 120:(q4 + 1) * 120, c * 128:(c + 1) * 128], XT[0:120, k, 128:256],
                         ("xt", k), (("xt", k),), ())
            self.dma("sp", self.cT[c], CO, ("co",), ("co",), ())
        self.barrier()
        self.rot = [0, 1]
        for c in range(KC):
            for ti, (a, n) in enumerate(tlm):
                r = self.ring("xt", 4)
                self.dma("sp" if (c + ti) % 2 == 0 else "act", XT[:, r, :n], self.cT[c, :, a:a + n], ("xt", r), (), (("xt", r),))
                COt = XT[:, r, 0:512]
                a0 = a
                a = 0
                q = self.ring("sq", 2)
                self.act(SQ[:, q, :n], COt[:, a:a + n], AF.Square, (("xt", r),), (("sq", q),))
                self.mm(self.ps[:, SSQB[ti][0], SSQB[ti][1]:SSQB[ti][1] + n], self.ones_bf, SQ[:, q, :n], c == 0, c == KC - 1,
                        (("sq", q), "cb"), (("ps", SSQB[ti][0]),))
                q = self.ring("sq", 2)
                self.cp(SQ[:, q, :n], COt[:, a:a + n], (("xt", r),), (("sq", q),))
                a = a0
                self.mm(self.ps[:, SUMB[ti][0], SUMB[ti][1]:SUMB[ti][1] + n], self.ones_bf, SQ[:, q, :n], c == 0, c == KC - 1,
                        (("sq", q), "cb"), (("ps", SUMB[ti][0]),))
        for ti, (a, n) in enumerate(tlm):
            sb_, so_ = SUMB[ti]
            qb_, qo_ = SSQB[ti]
            self.act(MU[:, a:a + n], self.ps[:, sb_, so_:so_ + n], AF.Copy, (("ps", sb_),), (("mu", a),), scale=1.0 / D)
            self.tt(RS[:, a:a + n], MU[:, a:a + n], MU[:, a:a + n], ALU.mult, (("mu", a),), (("rs", a),))
            self.stt(RS[:, a:a + n], self.ps[:, qb_, qo_:qo_ + n], 1.0 / D, RS[:, a:a + n], ALU.mult, ALU.subtract,
                     (("ps", qb_), ("rs", a)), (("rs", a),))
            self.ts(RS[:, a:a + n], RS[:, a:a + n], 0.0, None, ALU.max, None, (("rs", a),), (("rs", a),))
            self.rstd(RS[:, a:a + n], RS[:, a:a + n], 1.0, (("rs", a),), (("rs", a),))
        self.barrier()
        self.rot = list(range(8))
        for c in range(KC):
            for (a, n) in tlm:
                r = self.ring("xt", 4)
                self.dma("sp", XT[:, r, :n], self.cT[c, :, a:a + n], ("xt", r), (), (("xt", r),))
                self.tt(XT[:, r, :n], XT[:, r, :n], MU[:, a:a + n], ALU.subtract, (("xt", r),), (("xt", r),))
                self.tt(XT[:, r, :n], XT[:, r, :n], RS[:, a:a + n], ALU.mult, (("xt", r),), (("xt", r),))
                self.ts(XT[:, r, :n], XT[:, r, :n], self.cv(CV_G["ln_g"] + c), self.cv(CV_G["ln_b"] + c),
                        ALU.mult, ALU.add, (("xt", r), "cv"), (("xt", r),))
                self.act(CB[:, c, a:a + n], XT[:, r, :n], AF.Silu, (("xt", r),), ())
        self.barrier()
        wba = self.w_ba.rearrange("(kc p) n -> p kc n", p=128)
        wbb = self.w_bb.rearrange("(kc p) n -> p kc n", p=128)
        LW = self.v(187392, [128, 2, KC, 256], BF16)
        bring = [0]

        def bslot():
            i = bring[0] % 5
            bring[0] += 1
            if i < 3:
                return self.wview(i, [128, KC, 256]), ("w", i)
            return LW[:, i - 3], ("wl", i - 3)

        def bload(dst, key, src):
            self.dma("pool", dst, src, key, (), (key,))
        for oc in range(KC):
            w1, s1 = bslot()
            bload(w1[:, :, 0:128], s1, wba[:, :, oc * 128:(oc + 1) * 128])
            bload(w1[:, :, 128:256], s1, wbb[:, :, oc * 128:(oc + 1) * 128])
            w2, s2 = bslot()
            bload(w2[:, :, 0:128], s2, win[:, :, O_GATE + oc * 128:O_GATE + (oc + 1) * 128])
            bload(w2[:, :, 128:256], s2, win[:, :, O_GATE + 2048 + oc * 128:O_GATE + 2048 + (oc + 1) * 128])
            for (a, n) in tlm:
                bs = []
                for (wv_, s_, col, src, off) in ((w1, s1, 0, OF, 0), (w1, s1, 128, CB, 0), (w2, s2, 0, UN2, 32), (w2, s2, 128, UN2, 32)):
                    b = self.bank()
                    for kc in range(KC):
                        self.mm(self.ps[:, b, :n], wv_[:, kc, col:col + 128], src[:, kc, off + a:off + a + n],
                                kc == 0, kc == KC - 1, (s_,), (("ps", b),))
                    bs.append(b)
                r1 = self.ring("xt", 4)
                r2 = self.ring("xt", 4)
                self.act(XT[:, r1, :n], self.ps[:, bs[2], :n], AF.Sigmoid, (("ps", bs[2]),), (("xt", r1),))
                self.act(XT[:, r2, :n], self.ps[:, bs[3], :n], AF.Sigmoid, (("ps", bs[3]),), (("xt", r2),))
                self.tt(XT[:, r1, :n], XT[:, r1, :n], self.ps[:, bs[0], :n], ALU.mult, (("xt", r1), ("ps", bs[0])), (("xt", r1),))
                self.tt(XT[:, r2, :n], XT[:, r2, :n], self.ps[:, bs[1], :n], ALU.mult, (("xt", r2), ("ps", bs[1])), (("xt", r2),))
                q = self.ring("sq", 2)
                self.tt(SQ[:, q, :n], XT[:, r1, :n], XT[:, r2, :n], ALU.add, (("xt", r1), ("xt", r2)), (("sq", q),))
                self.dma("sp", self.mgT[oc, :, a:a + n], SQ[:, q, :n], ("sq", q), (("sq", q),), ())

    def proj_post(self, kind):
        o = PH_OFF
        XN = self.v(o, [128, KC, NM], BF16); o += KC * NM * 2
        PB = self.v(o, [128, 2, NM], BF16); o += 2 * NM * 2
        RSTD = self.v(o, [128, NM]); o += NM * 4
        XT = self.v(o, [128, 4, 512]); o += 8192
        SQ = self.v(o, [128, 2, 512], BF16); o += 2048
        FT = self.v(o, [128, 2, 512]); o += 4096
        YST = self.v(o, [128, 2, 512]); o += 4096
        tlm = _tiles(0, NM)
        self.barrier()
        self.rot = list(range(5))
        if kind == "out":
            for kc in range(KC):
                self.dma("sp", XN[:, kc, :], self.mgT[kc], ("xnl", kc % 4), (), ())
            W = self.w_out.rearrange("(kc p) n -> p kc n", p=128)
            nk = KC
        else:
            self.norm_load(self.h3, M0, NT, CV_G["ple_pre"], XN, RSTD, XT, SQ, pre_ssq=[5, 6, 7])
            for kc in range(2):
                self.dma("pool", PB[:, kc, :], self.pT[kc], ("pbl", kc), (), ())
            W = self.w_pg.rearrange("(kc p) n -> p kc n", p=128)
            WP = self.w_pp.rearrange("(kc p) n -> p kc n", p=128)
            nk = KC
        self.barrier()
        ssq = [5, 6, 7]
        self.rot = list(range(5))
        pend_ssq = None
        for oc in range(KC):
            s = self.wslot()
            wv = self.wview(s, [128, KC + 2, 128])
            self.wload(s, wv[:, 0:KC, :], W[:, :, oc * 128:(oc + 1) * 128])
            if kind == "ple":
                self.wload(s, wv[:, KC:KC + 2, :], WP[:, :, oc * 128:(oc + 1) * 128])
            for ti, (a, n) in enumerate(tlm):
                b = self.bank()
                for kc in range(nk):
                    self.mm(self.ps[:, b, :n], wv[:, kc, :], XN[:, kc, a:a + n], kc == 0, kc == nk - 1,
                            (("w", s),), (("ps", b),))
                k = self.ring("ft", 2)
                if kind == "out":
                    self.act(FT[:, k, :n], self.ps[:, b, :n], AF.Copy, (("ps", b),), (("ft", k),))
                else:
                    b2 = self.bank()
                    for kc in range(2):
                        self.mm(self.ps[:, b2, :n], wv[:, KC + kc, :], PB[:, kc, a:a + n], kc == 0, kc == 1,
                                (("w", s),), (("ps", b2),))
                    self.act(FT[:, k, :n], self.ps[:, b, :n], AF.Sigmoid, (("ps", b),), (("ft", k),))
                    self.tt(FT[:, k, :n], FT[:, k, :n], self.ps[:, b2, :n], ALU.mult, (("ft", k), ("ps", b2)), (("ft", k),))
                self.dma("sp", self.fT[oc, :, M0 + a:M0 + a + n], FT[:, k, :n], ("ft", k), (("ft", k),), ("fscr",))
                q = self.ring("sq", 2)
                self.tt(SQ[:, q, :n], FT[:, k, :n], FT[:, k, :n], ALU.mult, (("ft", k),), (("sq", q),))
                if pend_ssq is not None:
                    pend_ssq()
                pend_ssq = (lambda ti=ti, n=n, q=q, oc=oc: self.mm(
                    self.ps[:, ssq[ti], :n], self.ones_bf, SQ[:, q, :n], oc == 0, oc == KC - 1,
                    (("sq", q), "cb"), (("ps", ssq[ti]),)))
        pend_ssq()
        if kind == "out":
            self.post_residual(self.fT, self.h1, self.h2, M0, NT, CV_G["mix_post"], ssq, RSTD, XT, SQ=SQ, nxt_ssq=True)
        else:
            self.post_residual(self.fT, self.h3, self.h4, M0, NT, CV_G["ple_post"], ssq, RSTD, XT, yout=self.y, YST=YST)
        self.rot = list(range(8))

    def write_y(self):
        self.barrier()
        self.rot = list(range(8))
        HIN = self.v(PH_OFF, [128, 2, 4, 128])
        YT = self.v(PH_OFF + 4096, [128, 2, D])
        ident = self.cm(CM_ID, 128)
        for tb in range(NM // 128):
            yk = self.ring("yt", 2)
            for g in range(4):
                k = self.ring("hin", 2)
                self.dma("sp", HIN[:, k], self.h4[g * 4:(g + 1) * 4, :, M0 + tb * 128:M0 + (tb + 1) * 128]
                         .rearrange("c p t -> p c t"), ("hin", k), (), (("hin", k),))
                b = self.bank()
                for q in range(4):
                    self.tr(self.ps[:, b, q * 128:(q + 1) * 128], HIN[:, k, q, :], ident, (("hin", k), "cm"), (("ps", b),))
                self.act(YT[:, yk, g * 512:(g + 1) * 512], self.ps[:, b, :], AF.Copy, (("ps", b),), (("yt", yk),))
            self.dma("act", self.y[tb * 128:(tb + 1) * 128, :], YT[:, yk, :], ("yt", yk), (("yt", yk),), ())

    def dbg_dump_bg(self):
        self.barrier()
        self.dma("sp", self.dbg_bg[0], self.BETA, ("dbg", 0), (), ())
        self.dma("sp", self.dbg_bg[1], self.GG, ("dbg", 0), (), ())

    def build(self):
        self.load_consts()
        self.transpose_in(self.xin, self.xT, NT, KC)
        if self.stop_after == "xT":
            return self.finish()
        self.ffn(self.xT, self.h1, 0, NPRE, self.w_gu1, self.w_dn1, CV_G["ffn1_pre"], CV_H1)
        self.ffn(self.xT, self.h1, M0, NT, self.w_gu1, self.w_dn1, CV_G["ffn1_pre"], CV_H1)
        if self.stop_after == "ffn1":
            return self.finish()
        self.mixer_qkv()
        if self.stop_after == "qkv":
            if self.debug:
                self.dbg_dump_bg()
            return self.finish()
        self.delta()
        if self.stop_after == "delta":
            if self.debug:
                self.barrier()
                self.dma("sp", self.dbg_of.rearrange("h p t -> p h t"), self.OF, ("dbg", 1), (), ())
            return self.finish()
        self.mixer_tail()
        self.transpose_in(self.pin, self.pT, NM, 2)
        self.proj_post("out")
        if self.stop_after == "mix":
            return self.finish()
        self.ffn(self.h2, self.h3, M0, NT, self.w_gu2, self.w_dn2, CV_G["ffn2_pre"], CV_H2, pre_ssq=[5, 6, 7], nxt_ssq=True)
        self.proj_post("ple")
        return self.finish()

    def finish(self):
        self.pg.emit()
        return self.nc


def _masks():
    m = np.zeros((128, NCM), np.float32)
    i = np.arange(128)
    m[:, CM_ID:CM_ID + 128] = np.eye(128, dtype=np.float32)
    m[:, CM_ONE:CM_ONE + 128] = 1.0
    j = np.arange(64)
    le = (j[:, None] <= j[None, :]).astype(np.float32)
    lt = (j[:, None] < j[None, :]).astype(np.float32)
    m[:64, CM_TRI:CM_TRI + 64] = le
    m[:64, CM_UI:CM_UI + 64] = le
    m[:64, CM_US:CM_US + 64] = lt
    m[:64, CM_LS:CM_LS + 64] = lt.T
    same = ((i[:, None] // LS) == (i[None, :] // LS)).astype(np.float32)
    le8 = (i[:, None] <= i[None, :]).astype(np.float32) * same
    lt8 = (i[:, None] < i[None, :]).astype(np.float32) * same
    m[:, CM_TRI8:CM_TRI8 + 128] = le8
    m[:, CM_UI8:CM_UI8 + 128] = le8
    m[:, CM_US8:CM_US8 + 128] = lt8
    m[:, CM_LS8:CM_LS8 + 128] = lt8.T
    m[:, CM_SAME8:CM_SAME8 + 128] = same
    m[:, CM_SEG:CM_SEG + 16] = (i[:, None] // LS == np.arange(16)[None, :]).astype(np.float32)
    return m


def _cvec(inp):
    c = np.zeros((128, NCV), np.float32)

    def fm(vec):
        return np.ascontiguousarray(np.asarray(vec, np.float32).reshape(-1, 128).T)
    for n, col in CV_G.items():
        key = {"b_dw": "b_dw_conv"}.get(n, n)
        c[:, col:col + 16] = fm(inp[key][0])
    wsc = np.asarray(inp["w_short_conv"][0], np.float32)
    c[:, CV_SC:CV_SC + 192] = wsc.reshape(4, 48, 128).transpose(2, 1, 0).reshape(128, 192)
    wdw = np.asarray(inp["w_dw_conv"][0], np.float32)
    c[:, CV_DW:CV_DW + 496] = wdw.reshape(31, 16, 128).transpose(2, 1, 0).reshape(128, 496)
    c[:, CV_ON] = np.asarray(inp["o_norm"][0], np.float32)
    c[:16, CV_AL] = np.asarray(inp["a_log"][0], np.float32)
    c[:16, CV_DT] = np.asarray(inp["dt_bias"][0], np.float32)
    return c


def make_in_maps(inp, cores=range(8)):
    xp = np.asarray(inp["x_prompt"], np.float32)
    xs = np.asarray(inp["x_sample"], np.float32)
    pp = np.asarray(inp["p_prompt"], np.float32)[0]
    psm = np.asarray(inp["p_sample"], np.float32)[0]
    sd = np.asarray(inp["state_delta"], np.float32)[0]
    sq = np.asarray(inp["state_qkv_conv"], np.float32)[0]
    sg = np.asarray(inp["state_glu_conv"], np.float32)[0]
    shared = {
        "cvec": _cvec(inp), "cmask": _masks(),
        "w_gu1": np.asarray(inp["ffn1_w_gu"][0]), "w_dn1": np.asarray(inp["ffn1_w_down"][0]),
        "w_in": np.asarray(inp["w_in"][0]), "w_ba": np.asarray(inp["w_branch_a"][0]),
        "w_bb": np.asarray(inp["w_branch_b"][0]), "w_out": np.asarray(inp["w_out"][0]),
        "w_gu2": np.asarray(inp["ffn2_w_gu"][0]), "w_dn2": np.asarray(inp["ffn2_w_down"][0]),
        "w_pg": np.asarray(inp["w_ple_gate"][0]), "w_pp": np.asarray(inp["w_ple_proj"][0]),
    }
    maps = []
    for c in cores:
        b, half = c // 2, c % 2
        main = xp[b, half * NMAIN:(half + 1) * NMAIN]
        pre = xp[b, 0:NPRE] if half == 1 else np.zeros((NPRE, D), np.float32)
        sl = slice(c * NSEQ, (c + 1) * NSEQ)
        m = dict(shared)
        m["xin"] = np.concatenate([pre, main, xs[sl].reshape(NS, D)], 0)
        m["pin"] = np.concatenate([pp[b, half * NMAIN:(half + 1) * NMAIN], psm[sl].reshape(NS, PLE)], 0)
        m["sdelta"] = np.ascontiguousarray(sd[sl])
        m["sqkv"] = np.ascontiguousarray(sq[sl].reshape(NSEQ * 3, QKV))
        m["sglu"] = np.ascontiguousarray(sg[sl].reshape(NSEQ * 30, D))
        maps.append(m)
    return maps


def kernel(**inputs):
    nc = Builder().build()
    maps = make_in_maps(inputs)
    res = run_bass_kernel_spmd(nc, maps, core_ids=list(range(8)))
    R = res.results
    yp = np.zeros((4, 2048, D), np.float32)
    ys = np.zeros((128, LS, D), np.float32)
    ndp = np.zeros((1, 4, NH, 128, 128), np.float32)
    nqp = np.zeros((1, 4, 3, QKV), np.float32)
    ngp = np.zeros((1, 4, 30, D), np.float32)
    nds = np.zeros((1, 128, NH, 128, 128), np.float32)
    nqs = np.zeros((1, 128, 3, QKV), np.float32)
    ngs = np.zeros((1, 128, 30, D), np.float32)
    for c in range(8):
        b, half = c // 2, c % 2
        r = R[c]
        yp[b, half * NMAIN:(half + 1) * NMAIN] = r["y"][:NMAIN]
        ys[c * NSEQ:(c + 1) * NSEQ] = r["y"][NMAIN:].reshape(NSEQ, LS, D)
        if half == 1:
            ndp[0, b] = r["nd_p"]
            nqp[0, b] = r["nq_p"]
            ngp[0, b] = r["ng_p"]
        nds[0, c * NSEQ:(c + 1) * NSEQ] = r["nd_s"]
        nqs[0, c * NSEQ:(c + 1) * NSEQ] = r["nq_s"].reshape(NSEQ, 3, QKV)
        ngs[0, c * NSEQ:(c + 1) * NSEQ] = r["ng_s"].reshape(NSEQ, 30, D)
    return (yp, ys, ndp, nqp, ngp, nds, nqs, ngs)
```
